# Optimizing a Trainium2 kernel written in Bass

```python
import math
import jax
import jax.numpy as jnp
from jax import lax
import numpy as np

D_MODEL = 1024
BATCH = 4
SEQ = 4096
DEPTH = 2

GRID_W = 64
CTX_LEN = 256
NORM_EPS = 1e-6
N_MOD = 6

DN_HEADS = 8
DN_HEAD_DIM = 128
DN_WIDTH = DN_HEADS * DN_HEAD_DIM
DN_CHUNK = 64
DN_CONV = 4

HY_WIDTH = 1024
HY_CONV = 3
HY_EMB = 33
HY_BANDS = (HY_EMB - 1) // 2
HY_FILTER_HIDDEN = 64
HY_FAST_DECAY_PCT = 0.3
HY_SLOW_DECAY_PCT = 1.5
HY_DECAY_TARGET = 1e-2

LRU_WIDTH = 1024
LRU_BLOCKS = 8
LRU_BLOCK = LRU_WIDTH // LRU_BLOCKS
LRU_CONV = 4
LRU_C = 8.0

FFN_HIDDEN = 2816
FFN_CONV = 3

N_BRANCH = 3
IN_SIZES = (3 * DN_WIDTH, DN_WIDTH, 4 * DN_HEADS, 3 * HY_WIDTH, LRU_WIDTH, LRU_WIDTH, N_BRANCH * D_MODEL)
N_IN = 3 * DN_WIDTH + DN_WIDTH + 4 * DN_HEADS + 3 * HY_WIDTH + 2 * LRU_WIDTH + N_BRANCH * D_MODEL

kernel_name = 'hybrid_deltanet_hyena_rglru_diffusion_block'


def rms_norm(x, gain):
    xf = x.astype(jnp.float32)
    y = xf * lax.rsqrt(jnp.mean(xf * xf, axis=-1, keepdims=True) + NORM_EPS)
    return (y * gain.astype(jnp.float32)).astype(x.dtype)


def modulate(h, shift, scale):
    return h * (1.0 + scale) + shift


def l2_normalize(t):
    return t * lax.rsqrt(jnp.sum(t * t, axis=-1, keepdims=True) + 1e-6)


def dwconv1d(x, w):
    k = w.shape[0]
    n = x.shape[1]
    xp = jnp.pad(x, ((0, 0), ((k - 1) // 2, k // 2), (0, 0)))
    return sum(xp[:, j:j + n] * w[j] for j in range(k))


def dwconv_grid(u, w, rows, cols):
    bsz, n, ch = u.shape
    k = w.shape[0]
    p = k // 2
    img = jnp.pad(u.reshape(bsz, rows, cols, ch), ((0, 0), (p, p), (p, p), (0, 0)))
    out = sum(img[:, i:i + rows, j:j + cols] * w[i, j] for i in range(k) for j in range(k))
    return out.reshape(bsz, n, ch)


def block_diag_linear(x, w, b):
    bsz, n, _ = x.shape
    xb = x.reshape(bsz, n, LRU_BLOCKS, LRU_BLOCK)
    y = jnp.einsum('blgi,gij->blgj', xb, w.astype(x.dtype))
    return y.reshape(bsz, n, LRU_WIDTH) + b.astype(x.dtype)


def linear_scan(a, b, h0):
    def combine(e1, e2):
        a1, b1 = e1
        a2, b2 = e2
        return a1 * a2, a2 * b1 + b2
    a_cum, b_cum = lax.associative_scan(combine, (a, b), axis=1)
    h = b_cum + a_cum * h0[:, None, :]
    return h, h[:, -1]


def gdn_chunked(q, k, v, g, beta, s0):
    bsz, nh, n, dk = q.shape
    dv = v.shape[-1]
    cs = DN_CHUNK
    nc = n // cs

    def chunks(t):
        return t.reshape(bsz, nh, nc, cs, *t.shape[3:])

    q = chunks(q * dk ** -0.5)
    k = chunks(k)
    v = chunks(v)
    beta = chunks(beta)
    gc = jnp.cumsum(chunks(g), axis=-1)
    pos = jnp.arange(cs)
    lower = pos[:, None] >= pos[None, :]
    strict = pos[:, None] > pos[None, :]
    diff = gc[..., :, None] - gc[..., None, :]
    decay = jnp.where(lower, jnp.exp(jnp.where(lower, diff, 0.0)), 0.0)
    kb = k * beta[..., None]
    m = jnp.where(strict, jnp.einsum('bhnid,bhnjd->bhnij', kb, k) * decay, 0.0)
    eye = jnp.eye(cs, dtype=m.dtype)
    t_inv = lax.linalg.triangular_solve(eye + m, jnp.broadcast_to(eye, m.shape), left_side=True, lower=True)
    u = jnp.einsum('bhnij,bhnjd->bhnid', t_inv, v * beta[..., None])
    w = jnp.einsum('bhnij,bhnjd->bhnid', t_inv, kb * jnp.exp(gc)[..., None])
    attn = jnp.einsum('bhnid,bhnjd->bhnij', q, k) * decay
    q_dec = q * jnp.exp(gc)[..., None]
    k_dec = k * jnp.exp(gc[..., -1:] - gc)[..., None]
    chunk_decay = jnp.exp(gc[..., -1])
    xs = tuple(jnp.moveaxis(t, 2, 0) for t in (q_dec, k_dec, u, w, attn, chunk_decay))

    def step(state, inp):
        q_i, k_i, u_i, w_i, a_i, d_i = inp
        v_new = u_i - jnp.einsum('bhcd,bhde->bhce', w_i, state)
        o_i = jnp.einsum('bhcd,bhde->bhce', q_i, state) + jnp.einsum('bhcs,bhse->bhce', a_i, v_new)
        state = state * d_i[..., None, None] + jnp.einsum('bhcd,bhce->bhde', k_i, v_new)
        return state, o_i

    s_final, o = lax.scan(step, s0, xs)
    return jnp.moveaxis(o, 0, 2).reshape(bsz, nh, n, dv), s_final


def delta_net_branch(p_qkv, p_z, p_ab, s0_f, s0_b, conv_w, a_log, dt_bias, norm_g):
    f32 = jnp.float32
    bsz, n, _ = p_qkv.shape
    qkv = jax.nn.silu(dwconv1d(p_qkv, conv_w)).astype(f32)
    qkv = qkv.reshape(bsz, n, 3, DN_HEADS, DN_HEAD_DIM).transpose(2, 0, 3, 1, 4)
    q = l2_normalize(qkv[0])
    k = l2_normalize(qkv[1])
    v = qkv[2]
    ab = p_ab.astype(f32).reshape(bsz, n, 2, 2, DN_HEADS)
    g = -jnp.exp(a_log.astype(f32)) * jax.nn.softplus(ab[:, :, :, 0] + dt_bias.astype(f32))
    beta = jax.nn.sigmoid(ab[:, :, :, 1])
    g = g.transpose(2, 0, 3, 1)
    beta = beta.transpose(2, 0, 3, 1)

    def rev(t):
        return jnp.flip(t, axis=2)

    o_f, s_f = gdn_chunked(q, k, v, g[0], beta[0], s0_f)
    o_b, s_b = gdn_chunked(rev(q), rev(k), rev(v), rev(g[1]), rev(beta[1]), s0_b)
    o = (o_f + rev(o_b)).transpose(0, 2, 1, 3)
    z = p_z.astype(f32).reshape(bsz, n, DN_HEADS, DN_HEAD_DIM)
    o = rms_norm(o, norm_g) * jax.nn.silu(z)
    return o.reshape(bsz, n, DN_WIDTH).astype(p_qkv.dtype), s_f, s_b


def hyena_kernel(n, w1, b1, f1, w2, b2, f2, w3):
    f32 = jnp.float32
    t = jnp.linspace(0.0, 1.0, n, dtype=f32)[:, None]
    omega = 2.0 * math.pi * jnp.arange(n, dtype=f32)[:, None] / n
    bands = jnp.linspace(1e-4, HY_BANDS - 1, HY_BANDS, dtype=f32)[None, :]
    z = jnp.concatenate([t, jnp.cos(bands * omega), -jnp.sin(bands * omega)], axis=-1)
    hid = jnp.sin(f1.astype(f32) * (z @ w1.astype(f32) + b1.astype(f32)))
    hid = jnp.sin(f2.astype(f32) * (hid @ w2.astype(f32) + b2.astype(f32)))
    filt = (hid @ w3.astype(f32)).reshape(n, 2, HY_WIDTH)
    log_target = math.log(HY_DECAY_TARGET)
    deltas = jnp.abs(jnp.linspace(log_target / HY_SLOW_DECAY_PCT, log_target / HY_FAST_DECAY_PCT, HY_WIDTH, dtype=f32))
    filt = filt * jnp.exp(-t * deltas)[:, None, :]
    h_fwd = filt[:, 0]
    h_bwd = filt[:, 1]
    return jnp.concatenate([h_fwd.at[0].add(h_bwd[0]), jnp.zeros_like(h_fwd[:1]), h_bwd[:0:-1]], axis=0)


def long_conv(u, kern):
    n = u.shape[1]
    uf = jnp.fft.rfft(u, n=2 * n, axis=1)
    kf = jnp.fft.rfft(kern, axis=0)
    return jnp.fft.irfft(uf * kf[None], n=2 * n, axis=1)[:, :n]


def hyena_branch(p_hy, conv_w, conv_b, w1, b1, f1, w2, b2, f2, w3, bias):
    f32 = jnp.float32
    n = p_hy.shape[1]
    uc = (dwconv1d(p_hy, conv_w) + conv_b).astype(f32)
    x0, x1, v = jnp.split(uc, 3, axis=-1)
    kern = hyena_kernel(n, w1, b1, f1, w2, b2, f2, w3)
    zz = x1 * v
    y = long_conv(zz, kern) + zz * bias.astype(f32)
    return (x0 * y).astype(p_hy.dtype)


def rglru_direction(x, h0, w_a, b_a, w_x, b_x, lam):
    r = jax.nn.sigmoid(block_diag_linear(x, w_a, b_a))
    i = jax.nn.sigmoid(block_diag_linear(x, w_x, b_x))
    log_a = -LRU_C * r * jax.nn.softplus(-lam.astype(jnp.float32))
    b = jnp.sqrt(-jnp.expm1(2.0 * log_a)) * (i * x)
    return linear_scan(jnp.exp(log_a), b, h0)


def rglru_branch(p_x, p_y, h0_f, h0_b, conv_w, conv_b, w_a, b_a, w_x, b_x, lam):
    f32 = jnp.float32
    xs = (dwconv1d(p_x, conv_w) + conv_b).astype(f32)
    h_f, last_f = rglru_direction(xs, h0_f, w_a[0], b_a[0], w_x[0], b_x[0], lam[0])
    h_b, last_b = rglru_direction(jnp.flip(xs, 1), h0_b, w_a[1], b_a[1], w_x[1], b_x[1], lam[1])
    out = (h_f + jnp.flip(h_b, 1)) * jax.nn.gelu(p_y.astype(f32))
    return out.astype(p_x.dtype), last_f, last_b


def token_mixer(h, s_dn_f, s_dn_b, s_lru_f, s_lru_b, with_output,
                w_in, dn_conv_w, dn_a_log, dn_dt_bias, dn_norm_g,
                hy_conv_w, hy_conv_b, hy_w1, hy_b1, hy_f1, hy_w2, hy_b2, hy_f2, hy_w3, hy_bias,
                lru_conv_w, lru_conv_b, lru_w_a, lru_b_a, lru_w_x, lru_b_x, lru_lambda,
                w_proj_dn, w_proj_hy, w_proj_lru, w_out):
    proj = h @ w_in
    offsets = np.cumsum(IN_SIZES)[:-1].tolist()
    p_qkv, p_z, p_ab, p_hy, p_lx, p_ly, p_gate = jnp.split(proj, offsets, axis=-1)
    o_dn, s_dn_f, s_dn_b = delta_net_branch(p_qkv, p_z, p_ab, s_dn_f, s_dn_b, dn_conv_w, dn_a_log, dn_dt_bias, dn_norm_g)
    o_lru, s_lru_f, s_lru_b = rglru_branch(p_lx, p_ly, s_lru_f, s_lru_b, lru_conv_w, lru_conv_b,
                                          lru_w_a, lru_b_a, lru_w_x, lru_b_x, lru_lambda)
    states = (s_dn_f, s_dn_b, s_lru_f, s_lru_b)
    if not with_output:
        return None, states
    o_hy = hyena_branch(p_hy, hy_conv_w, hy_conv_b, hy_w1, hy_b1, hy_f1, hy_w2, hy_b2, hy_f2, hy_w3, hy_bias)
    g_dn, g_hy, g_lru = jnp.split(jax.nn.sigmoid(p_gate), N_BRANCH, axis=-1)
    merged = g_dn * (o_dn @ w_proj_dn) + g_hy * (o_hy @ w_proj_hy) + g_lru * (o_lru @ w_proj_lru)
    return merged @ w_out, states


def conv_ffn(h, rows, cols, w_up, conv_w, w_down):
    u = dwconv_grid(h @ w_up, conv_w, rows, cols)
    gate, val = jnp.split(u, 2, axis=-1)
    return (jax.nn.silu(gate) * val) @ w_down


def setup_inputs(seed: int = 0) -> dict:
    key = jax.random.key(seed)
    ks = iter(jax.random.split(key, 48))
    f32 = jnp.float32
    dp = DEPTH
    d = D_MODEL

    def nrm(shape, scale):
        return jax.random.normal(next(ks), shape, f32) * scale

    def gain(shape):
        return 1.0 + nrm(shape, 0.02)

    a_log = jnp.log(jax.random.uniform(next(ks), (dp, 2, DN_HEADS), f32, 1.0, 16.0))
    dt = jnp.exp(jax.random.uniform(next(ks), (dp, 2, DN_HEADS), f32, math.log(1e-3), math.log(1e-1)))
    dt_bias = dt + jnp.log(-jnp.expm1(-dt))
    a0 = jax.random.uniform(next(ks), (dp, 2, LRU_WIDTH), f32, 0.9, 0.999) ** (1.0 / LRU_C)
    lru_lambda = jnp.log(a0) - jnp.log1p(-a0)
    return {
        'x': nrm((BATCH, SEQ, d), 1.0),
        'c': nrm((BATCH, d), 1.0),
        'ctx': nrm((BATCH, CTX_LEN, d), 1.0),
        'c_ctx': nrm((d,), 1.0),
        'w_mod': nrm((dp, d, N_MOD * d), 0.5 * d ** -0.5),
        'b_mod': nrm((dp, N_MOD * d), 0.02),
        'norm1_g': gain((dp, d)),
        'norm2_g': gain((dp, d)),
        'w_in': nrm((dp, d, N_IN), d ** -0.5),
        'dn_conv_w': nrm((dp, DN_CONV, 3 * DN_WIDTH), DN_CONV ** -0.5),
        'dn_a_log': a_log,
        'dn_dt_bias': dt_bias,
        'dn_norm_g': gain((dp, DN_HEAD_DIM)),
        'hy_conv_w': nrm((dp, HY_CONV, 3 * HY_WIDTH), HY_CONV ** -0.5),
        'hy_conv_b': nrm((dp, 3 * HY_WIDTH), 0.02),
        'hy_w1': nrm((dp, HY_EMB, HY_FILTER_HIDDEN), HY_EMB ** -0.5),
        'hy_b1': nrm((dp, HY_FILTER_HIDDEN), 0.02),
        'hy_f1': gain((dp, HY_FILTER_HIDDEN)),
        'hy_w2': nrm((dp, HY_FILTER_HIDDEN, HY_FILTER_HIDDEN), HY_FILTER_HIDDEN ** -0.5),
        'hy_b2': nrm((dp, HY_FILTER_HIDDEN), 0.02),
        'hy_f2': gain((dp, HY_FILTER_HIDDEN)),
        'hy_w3': nrm((dp, HY_FILTER_HIDDEN, 2 * HY_WIDTH), 0.05 * HY_FILTER_HIDDEN ** -0.5),
        'hy_bias': nrm((dp, HY_WIDTH), 0.1),
        'lru_conv_w': nrm((dp, LRU_CONV, LRU_WIDTH), LRU_CONV ** -0.5),
        'lru_conv_b': nrm((dp, LRU_WIDTH), 0.02),
        'lru_w_a': nrm((dp, 2, LRU_BLOCKS, LRU_BLOCK, LRU_BLOCK), LRU_BLOCK ** -0.5),
        'lru_b_a': nrm((dp, 2, LRU_WIDTH), 0.02),
        'lru_w_x': nrm((dp, 2, LRU_BLOCKS, LRU_BLOCK, LRU_BLOCK), LRU_BLOCK ** -0.5),
        'lru_b_x': nrm((dp, 2, LRU_WIDTH), 0.02),
        'lru_lambda': lru_lambda,
        'w_proj_dn': nrm((dp, DN_WIDTH, d), DN_WIDTH ** -0.5),
        'w_proj_hy': nrm((dp, HY_WIDTH, d), HY_WIDTH ** -0.5),
        'w_proj_lru': nrm((dp, LRU_WIDTH, d), LRU_WIDTH ** -0.5),
        'w_out': nrm((dp, d, d), d ** -0.5),
        'ffn_up': nrm((dp, d, 2 * FFN_HIDDEN), d ** -0.5),
        'ffn_conv_w': nrm((dp, FFN_CONV, FFN_CONV, 2 * FFN_HIDDEN), 1.0 / FFN_CONV),
        'ffn_down': nrm((dp, FFN_HIDDEN, d), FFN_HIDDEN ** -0.5),
        'final_norm_g': gain((d,)),
    }


def reference(x, c, ctx, c_ctx, w_mod, b_mod, norm1_g, norm2_g, w_in, dn_conv_w, dn_a_log, dn_dt_bias,
              dn_norm_g, hy_conv_w, hy_conv_b, hy_w1, hy_b1, hy_f1, hy_w2, hy_b2, hy_f2, hy_w3, hy_bias,
              lru_conv_w, lru_conv_b, lru_w_a, lru_b_a, lru_w_x, lru_b_x, lru_lambda,
              w_proj_dn, w_proj_hy, w_proj_lru, w_out, ffn_up, ffn_conv_w, ffn_down, final_norm_g):
    n_lat = x.shape[1]
    rows = n_lat // GRID_W
    bsz, n_ctx = ctx.shape[0], ctx.shape[1]
    silu_c = jax.nn.silu(c)[:, None, :]
    silu_cc = jax.nn.silu(c_ctx)[None, None, :]
    zero_dn = jnp.zeros((bsz, DN_HEADS, DN_HEAD_DIM, DN_HEAD_DIM), jnp.float32)
    zero_lru = jnp.zeros((bsz, LRU_WIDTH), jnp.float32)
    xc = ctx
    for l in range(DEPTH):
        ctx_needed = l < DEPTH - 1
        mixer_params = (w_in[l], dn_conv_w[l], dn_a_log[l], dn_dt_bias[l], dn_norm_g[l],
                        hy_conv_w[l], hy_conv_b[l], hy_w1[l], hy_b1[l], hy_f1[l], hy_w2[l], hy_b2[l],
                        hy_f2[l], hy_w3[l], hy_bias[l],
                        lru_conv_w[l], lru_conv_b[l], lru_w_a[l], lru_b_a[l], lru_w_x[l], lru_b_x[l],
                        lru_lambda[l], w_proj_dn[l], w_proj_hy[l], w_proj_lru[l], w_out[l])
        lat_mod = jnp.split(silu_c @ w_mod[l] + b_mod[l], N_MOD, axis=-1)
        ctx_mod = jnp.split(silu_cc @ w_mod[l] + b_mod[l], N_MOD, axis=-1)
        hc = modulate(rms_norm(xc, norm1_g[l]), ctx_mod[0], ctx_mod[1])
        yc, ctx_states = token_mixer(hc, zero_dn, zero_dn, zero_lru, zero_lru, ctx_needed, *mixer_params)
        h = modulate(rms_norm(x, norm1_g[l]), lat_mod[0], lat_mod[1])
        y, _ = token_mixer(h, *ctx_states, True, *mixer_params)
        x = x + lat_mod[2] * y
        h = modulate(rms_norm(x, norm2_g[l]), lat_mod[3], lat_mod[4])
        x = x + lat_mod[5] * conv_ffn(h, rows, GRID_W, ffn_up[l], ffn_conv_w[l], ffn_down[l])
        if ctx_needed:
            xc = xc + ctx_mod[2] * yc
            hc = modulate(rms_norm(xc, norm2_g[l]), ctx_mod[3], ctx_mod[4])
            xc = xc + ctx_mod[5] * conv_ffn(hc, 1, n_ctx, ffn_up[l], ffn_conv_w[l], ffn_down[l])
    return rms_norm(x, final_norm_g)
```

```python
import math
from contextlib import ExitStack

import numpy as np
import ml_dtypes
import concourse.bass as bass
import concourse.mybir as mybir
from concourse.bass_utils import run_bass_kernel_spmd

F32 = mybir.dt.float32
F32R = mybir.dt.float32r
BF16 = mybir.dt.bfloat16
ALU = mybir.AluOpType
AF = mybir.ActivationFunctionType
AX = mybir.AxisListType

D = 1024
L = 4096
LC = 256
U = 4355
LT0 = 259
UP = 4360
DEPTH = 2
NIN = 12320
OFF_QKV, OFF_Z, OFF_AB, OFF_HY, OFF_LX, OFF_LY, OFF_GATE = 0, 3072, 4096, 4128, 7200, 8224, 9248
FH = 2816
TT = [(0, 256)] + [(LT0 + 512 * i, 512) for i in range(8)]
NCORES = 4


class Unit:
    __slots__ = ("w", "r", "excl", "wd")

    def __init__(self, excl=False):
        self.w = None
        self.r = []
        self.wd = []
        self.excl = excl


class Op:
    __slots__ = ("eng", "fn", "deps", "dma", "need_inc", "sem", "val")

    def __init__(self, eng, fn, dma):
        self.eng = eng
        self.fn = fn
        self.dma = dma
        self.deps = []
        self.need_inc = False
        self.sem = None
        self.val = 0


class Sched:
    EPOCH = 30000
    NDMA = 24

    def __init__(self, nc, es):
        self.nc = nc
        self.es = es
        self.ops = []
        self.engs = {"pe": nc.tensor, "act": nc.scalar, "dve": nc.vector,
                     "pool": nc.gpsimd, "sp": nc.sync}
        self.last = {k: None for k in self.engs}
        self.dmas_since_barrier = []
        self.bar_deps = {k: [] for k in self.engs}
        self.nsem = 0

    def add(self, eng, fn, reads=(), writes=(), dma=False):
        op = Op(eng, fn, dma)
        deps = []
        for u in reads:
            if u.w is not None:
                deps.append(u.w)
            deps.extend(u.wd)
            if u.excl:
                deps.extend(o for o in u.r if o.eng != eng)
        for u in writes:
            if u.w is not None:
                deps.append(u.w)
            deps.extend(u.wd)
            deps.extend(u.r)
        if self.bar_deps[eng]:
            deps.extend(self.bar_deps[eng])
            self.bar_deps[eng] = []
        seen = set()
        for d in deps:
            if d is op or id(d) in seen:
                continue
            if d.eng == "pe" and eng == "pe" and not d.dma and not dma:
                continue
            seen.add(id(d))
            d.need_inc = True
            op.deps.append(d)
        for u in reads:
            if not dma:
                u.r = [o for o in u.r if o.dma or o.eng != eng]
            u.r.append(op)
        for u in writes:
            u.w = op
            u.r = []
            if dma:
                u.wd.append(op)
                if len(u.wd) > 48:
                    u.wd = u.wd[-48:]
            else:
                u.wd = []
        if dma:
            op.need_inc = True
            self.dmas_since_barrier.append(op)
        self.ops.append(op)
        self.last[eng] = op
        return op

    def barrier(self):
        deps = [o for o in self.last.values() if o is not None] + self.dmas_since_barrier
        self.dmas_since_barrier = []
        for k in self.engs:
            self.bar_deps[k] = list(deps)

    def _newsem(self):
        self.nsem += 1
        return self.es.enter_context(self.nc.semaphore("s%d" % self.nsem))

    def emit(self):
        self.barrier()
        self.add("sp", lambda e: None)
        esem, ecount = {}, {}
        dsem = [self._newsem() for _ in range(self.NDMA)]
        dcount = [0] * self.NDMA
        nd = 0
        seen = {k: {} for k in self.engs}
        for op in self.ops:
            e = self.engs[op.eng]
            waits = []
            if op.dma:
                j = nd % self.NDMA
                nd += 1
                if dcount[j]:
                    waits.append((dsem[j], dcount[j]))
                if dcount[j] >= 30000:
                    dsem[j] = self._newsem()
                    dcount[j] = 0
                dcount[j] += 16
                op.sem, op.val = dsem[j], dcount[j]
            elif op.need_inc:
                if op.eng not in esem or ecount[op.eng] >= self.EPOCH:
                    esem[op.eng] = self._newsem()
                    ecount[op.eng] = 0
                ecount[op.eng] += 1
                op.sem, op.val = esem[op.eng], ecount[op.eng]
            for d in op.deps:
                waits.append((d.sem, d.val))
            sn = seen[op.eng]
            for (s, v) in waits:
                if sn.get(id(s), 0) >= v:
                    continue
                sn[id(s)] = v
                e.wait_ge(s, v)
            ins = op.fn(e)
            if op.sem is not None and ins is not None:
                ins.then_inc(op.sem, 16 if op.dma else 1)
        return len(self.ops)


_UID = [0]


def uname(name):
    _UID[0] += 1
    return "%s_%d" % (name, _UID[0])


class TPool:
    def __init__(self, nc, es, name, shape, dtype, n, psum=False):
        self.t = []
        for i in range(n):
            mk = nc.psum_tensor if psum else nc.sbuf_tensor
            self.t.append((es.enter_context(mk(uname(name), shape, dtype)), Unit(excl=psum)))
        self.i = 0

    def get(self):
        r = self.t[self.i % len(self.t)]
        self.i += 1
        return r


def _pk(v):
    return np.ascontiguousarray(v.reshape(-1, 128).T)


VEC_FIELDS = [("g1", 8), ("g2", 8), ("bmod", 48), ("dncw", 96), ("hycw", 72), ("hycb", 24),
              ("lrucw", 32), ("lrucb", 8), ("lba", 16), ("lbx", 16), ("llam", 16), ("hybias", 8),
              ("ffncw", 396), ("fng", 8), ("dnng", 1), ("alog", 1), ("dtb", 1),
              ("hyb1", 1), ("hyf1", 1), ("hyb2", 1), ("hyf2", 1), ("hyb32", 32)]
VOFF = {}
_o = 0
for _n, _w in VEC_FIELDS:
    VOFF[_n] = (_o, _w)
    _o += _w
NV = _o


def build_vec(inp, l):
    v = np.zeros((128, NV), np.float32)

    def put(name, arr):
        o, w = VOFF[name]
        v[:arr.shape[0], o:o + w] = arr.reshape(arr.shape[0], w)
    put("g1", _pk(inp["norm1_g"][l]))
    put("g2", _pk(inp["norm2_g"][l]))
    put("bmod", _pk(inp["b_mod"][l]))
    put("dncw", inp["dn_conv_w"][l].reshape(4, 24, 128).transpose(2, 1, 0))
    put("hycw", inp["hy_conv_w"][l].reshape(3, 24, 128).transpose(2, 1, 0))
    put("hycb", _pk(inp["hy_conv_b"][l]))
    put("lrucw", inp["lru_conv_w"][l].reshape(4, 8, 128).transpose(2, 1, 0))
    put("lrucb", _pk(inp["lru_conv_b"][l]))
    put("lba", inp["lru_b_a"][l].reshape(2, 8, 128).transpose(2, 0, 1))
    put("lbx", inp["lru_b_x"][l].reshape(2, 8, 128).transpose(2, 0, 1))
    put("llam", inp["lru_lambda"][l].reshape(2, 8, 128).transpose(2, 0, 1))
    put("hybias", _pk(inp["hy_bias"][l]))
    put("ffncw", inp["ffn_conv_w"][l].reshape(9, 44, 128).transpose(2, 1, 0))
    put("fng", _pk(inp["final_norm_g"]))
    put("dnng", inp["dn_norm_g"][l].reshape(128, 1))
    al = np.zeros((40, 1), np.float32)
    db = np.zeros((40, 1), np.float32)
    for d in range(2):
        al[d * 32:d * 32 + 8, 0] = inp["dn_a_log"][l][d]
        db[d * 32:d * 32 + 8, 0] = inp["dn_dt_bias"][l][d]
    put("alog", al)
    put("dtb", db)
    put("hyb1", inp["hy_b1"][l].reshape(64, 1))
    put("hyf1", inp["hy_f1"][l].reshape(64, 1))
    put("hyb2", inp["hy_b2"][l].reshape(64, 1))
    put("hyf2", inp["hy_f2"][l].reshape(64, 1))
    put("hyb32", np.ascontiguousarray(inp["hy_bias"][l].reshape(32, 32).T))
    return v


CST_FIELDS = [("ident", 128), ("lowi", 128), ("lows", 128), ("uppi", 128), ("upps", 128),
              ("deltas", 8)]
COFF = {}
_o = 0
for _n, _w in CST_FIELDS:
    COFF[_n] = (_o, _w)
    _o += _w
NCST = _o


def build_cst():
    c = np.zeros((128, NCST), np.float32)
    i = np.arange(128)[:, None]
    j = np.arange(128)[None, :]
    same = (i // 64) == (j // 64)

    def put(name, arr):
        o, w = COFF[name]
        c[:, o:o + w] = arr
    put("ident", (i == j).astype(np.float32))
    put("lowi", ((i >= j) & same).astype(np.float32))
    put("lows", ((i > j) & same).astype(np.float32))
    put("uppi", ((i <= j) & same).astype(np.float32))
    put("upps", ((i < j) & same).astype(np.float32))
    lt = math.log(1e-2)
    deltas = np.abs(np.linspace(lt / 1.5, lt / 0.3, 1024, dtype=np.float32))
    put("deltas", _pk(deltas))
    return c


def build_rmask():
    m = np.ones((2, U), np.float32)
    starts = list(range(0, 256, 64)) + list(range(LT0, U, 64))
    for s in starts:
        m[0, s] = 0.0
        m[1, s + 63] = 0.0
    out = np.zeros((40, U), np.float32)
    out[0:8] = m[0]
    out[32:40] = m[1]
    return out


class Builder:
    def __init__(self, debug=False, stop=None, only=None, feed=()):
        self.debug = debug
        self.stop = stop
        self.only = only
        self.feed = set(feed)
        self.nc = bass.Bass("TRN2", target_bir_lowering=False)
        self.dbg_names = []

    def din(self, name, shape, dt=F32):
        return self.nc.dram_tensor(name, list(shape), dt, kind="ExternalInput").ap()

    def dscr(self, name, shape, dt=F32):
        kind = "ExternalOutput" if self.debug else "Internal"
        if name in self.feed:
            kind = "ExternalInput"
        elif self.debug:
            self.dbg_names.append(name)
        return self.nc.dram_tensor(name, list(shape), dt, kind=kind).ap()

    def vs(self, name, k=None):
        o, w = VOFF[name]
        if k is None:
            return self.vec[:, o:o + w]
        return self.vec[:, o + k:o + k + 1]

    def cs(self, name):
        o, w = COFF[name]
        return self.cst[:, o:o + w]

    def dma(self, out, in_, reads=(), writes=(), eng="sp"):
        return self.S.add(eng, lambda e: e.dma_start(out=out, in_=in_), reads=reads, writes=writes, dma=True)

    def build(self):
        nc = self.nc
        I = {}
        I["xT0"] = self.din("xT0", [D, U])
        I["cc"] = self.din("cc", [128, 16])
        I["vec"] = self.din("vec", [DEPTH, 128, NV])
        I["cst"] = self.din("cst", [128, NCST])
        I["rmask"] = self.din("rmask", [40, U])
        I["hy_w1"] = self.din("hy_w1", [DEPTH, 33, 64])
        I["hy_w2"] = self.din("hy_w2", [DEPTH, 64, 64])
        I["hy_w3"] = self.din("hy_w3", [DEPTH, 64, 2048])
        if self.only == "mg":
            I["w_mod"] = self.din("w_mod", [DEPTH, D, 6 * D])
            for n in ("w_proj_dn", "w_proj_hy", "w_proj_lru", "w_out"):
                I[n] = self.din(n, [DEPTH, D, D])
            I["ffn_up"] = self.din("ffn_up", [DEPTH, D, 2 * FH])
            I["ffn_down"] = self.din("ffn_down", [DEPTH, FH, D])
        if self.only:
            self.I = I
            return self.build2()
        I["w_mod"] = self.din("w_mod", [DEPTH, D, 6 * D])
        I["w_in"] = self.din("w_in", [DEPTH, D, NIN])
        I["lru_w_a"] = self.din("lru_w_a", [DEPTH, 2, 8, 128, 128])
        I["lru_w_x"] = self.din("lru_w_x", [DEPTH, 2, 8, 128, 128])
        for n in ("w_proj_dn", "w_proj_hy", "w_proj_lru", "w_out"):
            I[n] = self.din(n, [DEPTH, D, D])
        I["ffn_up"] = self.din("ffn_up", [DEPTH, D, 2 * FH])
        I["ffn_down"] = self.din("ffn_down", [DEPTH, FH, D])
        self.I = I
        return self.build2()

    def build2(self):
        nc, I = self.nc, self.I
        self.outT = nc.dram_tensor("outT", [D, L], F32, kind="ExternalOutput").ap()
        Sx = {}
        Sx["qT"] = self.dscr("qT", [8, 128, U])
        Sx["kT"] = self.dscr("kT", [8, 128, U])
        Sx["vT"] = self.dscr("vT", [8, 128, U])
        Sx["szT"] = self.dscr("szT", [D, U])
        Sx["gab"] = self.dscr("gab", [3, 40, U])
        Sx["zzT"] = self.dscr("zzT", [D, U])
        Sx["x0T"] = self.dscr("x0T", [D, U])
        Sx["olru"] = self.dscr("olru", [D, U], BF16)
        Sx["odn"] = self.dscr("odn", [D, U], BF16)
        Sx["ohy"] = self.dscr("ohy", [D, U], BF16)
        Sx["gT"] = self.dscr("gT", [3 * D, U], BF16)
        Sx["xA"] = self.dscr("xA", [D, U])
        Sx["xB"] = self.dscr("xB", [D, U])
        Sx["actT"] = self.dscr("actT", [FH, U], BF16)
        Sx["hp_lat"] = self.dscr("hp_lat", [2, D, L])
        Sx["hp_ctx"] = self.dscr("hp_ctx", [2, D, LC])
        for seg, n_ in (("lat", L), ("ctx", LC)):
            N_, NS1_, NF1_, NFH_ = hy_dims(n_)
            I["hy_zT_" + seg] = self.din("hy_zT_" + seg, [33, n_])
            I["hy_negt_" + seg] = self.din("hy_negt_" + seg, [128, n_])
            I["hy_F1_" + seg] = self.din("hy_F1_" + seg, [NS1_, 2 * NFH_])
            I["hy_G_" + seg] = self.din("hy_G_" + seg, [128, NFH_, 2, 128])
            I["hy_Tout_" + seg] = self.din("hy_Tout_" + seg, [NFH_, 128, 2, NS1_])
        I["hy_Winv"] = self.din("hy_Winv", [128, 2, 256])
        self.Sx = Sx
        self.Ux = {k: Unit() for k in Sx}

        with ExitStack() as es:
            self.es = es
            self.S = Sched(nc, es)
            S = self.S
            self.cst = es.enter_context(nc.sbuf_tensor("cst_sb", [128, NCST], F32))
            self.vec = es.enter_context(nc.sbuf_tensor("vec_sb", [128, NV], F32))
            self.ones = es.enter_context(nc.sbuf_tensor("ones", [128, 128], F32R))
            self.sc = es.enter_context(nc.sbuf_tensor("sc", [128, 16], F32))
            self.modv = es.enter_context(nc.sbuf_tensor("modv", [128, 48, 2], F32))
            self.der = es.enter_context(nc.sbuf_tensor("der", [128, 2, 2, 8], F32))
            self.u_cst, self.u_vec, self.u_ones, self.u_sc = Unit(), Unit(), Unit(), Unit()
            self.u_modv, self.u_der = Unit(), Unit()
            self.PS = TPool(nc, es, "ps", [128, 512], F32, 8, psum=True)
            self.dma(self.cst[:], I["cst"], writes=[self.u_cst])
            self.eps6 = es.enter_context(nc.sbuf_tensor("eps6", [128, 1], F32))
            S.add("dve", lambda e: e.memset(self.eps6[:], 1e-6), writes=[self.u_cst])
            ones32 = es.enter_context(nc.sbuf_tensor("ones32", [128, 128], F32))
            u_o32 = Unit()
            S.add("dve", lambda e: e.memset(ones32[:], 1.0), writes=[u_o32])
            S.add("act", lambda e: e.activation(out=self.ones[:], in_=ones32[:], func=AF.Copy), reads=[u_o32], writes=[self.u_ones])
            self.ones_bf = es.enter_context(nc.sbuf_tensor("ones_bf", [128, 128], BF16))
            S.add("act", lambda e: e.activation(out=self.ones_bf[:], in_=ones32[:], func=AF.Copy), reads=[u_o32], writes=[self.u_ones])
            self.dma(self.sc[:], I["cc"], writes=[self.u_sc])
            S.add("act", lambda e: e.activation(out=self.sc[:], in_=self.sc[:], func=AF.Silu),
                  reads=[self.u_sc], writes=[self.u_sc])
            xin = I["xT0"]
            u_xin = Unit()
            for l in range(DEPTH):
                self.l = l
                self.dma(self.vec[:], I["vec"][l], writes=[self.u_vec])
                if self.only == "dn":
                    with ExitStack() as es2:
                        dn_phase(self, l, es2)
                        S.barrier()
                    break
                if self.only == "mg":
                    self.phase_mod(l)
                    with ExitStack() as es2:
                        merge_phase(self, l, es2, xin, u_xin, TT)
                        S.barrier()
                    with ExitStack() as es2:
                        ffn_phase(self, l, es2, TT)
                        S.barrier()
                    with ExitStack() as es2:
                        final_phase(self, es2)
                        S.barrier()
                    break
                if self.only == "hy":
                    for seg in (("lat", "ctx") if "ctx" in self.stop else ("lat",)):
                        with ExitStack() as es2:
                            hyena_phase(self, l, es2, seg)
                            S.barrier()
                    break
                self.phase_mod(l)
                if self.stop == "mod":
                    break
                with ExitStack() as es2:
                    self.phase_mixer_pre(l, es2, xin, u_xin)
                    S.barrier()
                if self.stop and self.stop.startswith("pre"):
                    break
                with ExitStack() as es2:
                    dn_phase(self, l, es2)
                    S.barrier()
                if self.stop and self.stop.startswith("dn"):
                    break
                for seg in (("lat", "ctx") if l < DEPTH - 1 else ("lat",)):
                    with ExitStack() as es2:
                        hyena_phase(self, l, es2, seg)
                        S.barrier()
                if self.stop and self.stop.startswith("hy"):
                    break
                tiles = TT if l < DEPTH - 1 else TT[1:]
                with ExitStack() as es2:
                    merge_phase(self, l, es2, xin, u_xin, tiles)
                    S.barrier()
                if self.stop and self.stop.startswith("mg"):
                    break
                with ExitStack() as es2:
                    ffn_phase(self, l, es2, tiles)
                    S.barrier()
                xin, u_xin = Sx["xB"], self.Ux["xB"]
                if self.stop and self.stop.startswith("ffn"):
                    break
                if l == DEPTH - 1:
                    with ExitStack() as es2:
                        final_phase(self, es2)
                        S.barrier()
            n = S.emit()
        self.n_ops = n
        return nc

    def phase_mod(self, l):
        nc, S = self.nc, self.S
        with ExitStack() as es:
            wm_pool = TPool(nc, es, "wm", [128, 8, 512], F32, 2)
            ps, ups = self.PS.get()
            for pn in range(12):
                wm, uwm = wm_pool.get()
                self.dma(wm[:], self.I["w_mod"][l][:, pn * 512:(pn + 1) * 512].rearrange("(k p) n -> p k n", p=128),
                         writes=[uwm])
                for cc in range(4):
                    c = pn * 4 + cc
                    for k in range(8):
                        S.add("pe", (lambda e, c=c, cc=cc, k=k, wm=wm: e.matmul(
                            ps[:, 2 * c:2 * c + 2], wm[:, k, cc * 128:(cc + 1) * 128], self.sc[:, 2 * k:2 * k + 2],
                            start=(k == 0), stop=(k == 7))), reads=[uwm, self.u_sc], writes=[ups])
            bm = self.vs("bmod")
            for s in range(2):
                S.add("dve", (lambda e, s=s: e.tensor_tensor(out=self.modv[:, :, s], in0=ps[:, s:96:2], in1=bm, op=ALU.add)),
                      reads=[ups, self.u_vec], writes=[self.u_modv])
            for w, (gname, j) in enumerate((("g1", 1), ("g2", 4))):
                for s in range(2):
                    S.add("dve", (lambda e, w=w, s=s, j=j, gname=gname: e.scalar_tensor_tensor(
                        out=self.der[:, w, s, :], in0=self.modv[:, j * 8:(j + 1) * 8, s], scalar=1.0, in1=self.vs(gname),
                        op0=ALU.add, op1=ALU.mult)), reads=[self.u_modv, self.u_vec], writes=[self.u_der])
            S.barrier()

    def modcol(self, j, s, k):
        return self.modv[:, j * 8 + k, s:s + 1]

    def norm_to_hT(self, xsrc, u_xsrc, which, hT, u_hT, tiles, es, tile_ids=None):
        nc, S = self.nc, self.S
        xp = TPool(nc, es, "nx", [128, 8, 512], F32, 2)
        sqp = TPool(nc, es, "nsq", [128, 8, 512], F32R, 1)
        rsp = TPool(nc, es, "nrs", [128, 512], F32, 2)
        shj = 0 if which == 0 else 3
        for ti_, (u0, n) in enumerate(tiles):
            ti = tile_ids[ti_] if tile_ids is not None else ti_
            seg = 1 if u0 == 0 else 0
            x, ux = xp.get()
            self.dma(x[:, :, :n], xsrc.rearrange("(k p) t -> p k t", p=128)[:, :, u0:u0 + n], reads=[u_xsrc], writes=[ux])
            sq, usq = sqp.get()
            S.add("act", (lambda e, x=x, sq=sq, n=n: e.activation(out=sq[:, :, :n], in_=x[:, :, :n], func=AF.Square)),
                  reads=[ux], writes=[usq])
            ps, ups = self.PS.get()
            for k in range(8):
                S.add("pe", (lambda e, k=k, sq=sq, ps=ps, n=n: e.matmul(ps[:, :n], self.ones[:], sq[:, k, :n],
                                                                         start=(k == 0), stop=(k == 7))),
                      reads=[usq, self.u_ones], writes=[ups])
            rs, urs = rsp.get()
            S.add("act", (lambda e, rs=rs, ps=ps, n=n: e.activation(out=rs[:, :n], in_=ps[:, :n], func=AF.Sqrt, scale=1.0 / D, bias=self.eps6[:])),
                  reads=[ups], writes=[urs])
            S.add("dve", (lambda e, rs=rs, n=n: e.reciprocal(out=rs[:, :n], in_=rs[:, :n])), reads=[urs], writes=[urs])
            S.add("dve", (lambda e, x=x, rs=rs, n=n: e.tensor_tensor(out=x[:, :, :n], in0=x[:, :, :n],
                                                                      in1=rs[:, :n].unsqueeze(1).to_broadcast([128, 8, n]), op=ALU.mult)),
                  reads=[urs, ux], writes=[ux])
            for k in range(8):
                S.add("act", (lambda e, k=k, x=x, n=n, u0=u0, seg=seg: e.activation(
                    out=hT[:, k, u0:u0 + n], in_=x[:, k, :n], func=AF.Identity,
                    scale=self.der[:, which, seg, k:k + 1], bias=self.modcol(shj, seg, k))),
                    reads=[ux, self.u_der, self.u_modv], writes=[u_hT[ti]])

    def load_w_bf16(self, wsrc, M, wst_pool, wbf_pool, eng="pool"):
        S = self.S
        wst, uws = wst_pool.get()
        self.dma(wst[:, :, :M], wsrc.rearrange("(k p) m -> p k m", p=128), writes=[uws])
        wb, uwb = wbf_pool.get()
        S.add(eng, (lambda e, wb=wb, wst=wst, M=M: e.tensor_copy(out=wb[:, :, :M], in_=wst[:, :, :M])),
              reads=[uws], writes=[uwb])
        return wb, uwb

    def mm_tile(self, wb, uwb, M, hT, u_hT, ti):
        S = self.S
        u0, n = TT[ti]
        ps, ups = self.PS.get()
        for k in range(8):
            S.add("pe", (lambda e, k=k, ps=ps, wb=wb: e.matmul(ps[:M, :n], wb[:, k, :M], hT[:, k, u0:u0 + n],
                                                                start=(k == 0), stop=(k == 7))),
                  reads=[uwb, u_hT[ti]], writes=[ups])
        return ps, ups

    def conv(self, pp, upp, cw, ntap, bias, out_ap, uout):
        S = self.S
        acc, uacc = self._acc, self._uacc
        S.add("act", (lambda e: e.activation(out=acc[:, :U], in_=pp[:, 0:U], func=AF.Identity,
                                             scale=cw(0), bias=(bias if bias is not None else 0.0))),
              reads=[upp, self.u_vec], writes=[uacc])
        for j in range(1, ntap):
            last = j == ntap - 1
            o = out_ap if last else acc[:, :U]
            uo = uout if last else uacc
            S.add("dve", (lambda e, j=j, o=o: e.scalar_tensor_tensor(out=o, in0=pp[:, j:j + U], scalar=cw(j),
                                                                    in1=acc[:, :U], op0=ALU.mult, op1=ALU.add)),
                  reads=[upp, uacc, self.u_vec], writes=[uo])

    def phase_mixer_pre(self, l, es, xin, u_xin):
        nc, S, I, Sx, Ux = self.nc, self.S, self.I, self.Sx, self.Ux
        hT = es.enter_context(nc.sbuf_tensor(uname("hT"), [128, 8, U], BF16))
        u_hT = [Unit() for _ in TT]
        with ExitStack() as es3:
            self.norm_to_hT(xin, u_xin, 0, hT, u_hT, TT, es3)
            S.barrier()
        if self.stop == "norm":
            dbg = self.nc.dram_tensor(uname("dbg_hT"), [128, 8, U], BF16, kind="ExternalOutput").ap()
            self.dma(dbg, hT[:], reads=u_hT)
            return
        WK = TPool(nc, es, "wk", [128, UP], F32, 4)
        PP = TPool(nc, es, "pp", [128, UP], F32, 1)
        wst_pool = TPool(nc, es, "wst", [128, 8, 128], F32, 2)
        wbf_pool = TPool(nc, es, "wbf", [128, 8, 128], BF16, 2)
        rsp = TPool(nc, es, "rs", [128, 512], F32, 2)
        gtp = TPool(nc, es, "gt", [128, 512], BF16, 4)
        ztp = TPool(nc, es, "zt", [128, 512], F32, 3)
        lwp = TPool(nc, es, "lw", [128, 128], F32, 2)
        lwr = TPool(nc, es, "lwr", [128, 128], BF16, 2)
        sm = es.enter_context(nc.sbuf_tensor(uname("sm"), [128, 40], F32))
        u_sm = Unit()
        pp, upp = PP.get()
        S.add("pool", lambda e: e.memset(pp[:], 0.0), writes=[upp])
        sqb = es.enter_context(nc.sbuf_tensor(uname("sqb"), [128, UP], BF16))
        usqb = Unit()
        self._cnt = 0

        def evac_copy(ps, ups, out_ap, uo, M=128, n=512):
            self._cnt += 1
            if self._cnt % 2:
                S.add("act", (lambda e: e.activation(out=out_ap, in_=ps[:M, :n], func=AF.Copy)), reads=[ups], writes=[uo])
            else:
                S.add("dve", (lambda e: e.tensor_copy(out=out_ap, in_=ps[:M, :n])), reads=[ups], writes=[uo])

        def proj_to_pp(col0):
            wb, uwb = self.load_w_bf16(I["w_in"][l][:, col0:col0 + 128], 128, wst_pool, wbf_pool)
            for ti, (u0, n) in enumerate(TT):
                ps, ups = self.mm_tile(wb, uwb, 128, hT, u_hT, ti)
                evac_copy(ps, ups, pp[:, u0 + 1:u0 + 1 + n], upp, 128, n)

        def proj_act(col0, func, out_tile_fn, bias=None):
            wb, uwb = self.load_w_bf16(I["w_in"][l][:, col0:col0 + 128], 128, wst_pool, wbf_pool)
            for ti, (u0, n) in enumerate(TT):
                ps, ups = self.mm_tile(wb, uwb, 128, hT, u_hT, ti)
                o, uo = out_tile_fn(ti, u0, n)
                S.add("act", (lambda e, ps=ps, o=o, n=n: e.activation(out=o, in_=ps[:, :n], func=func)),
                      reads=[ups], writes=[uo])

        S.add("act", lambda e: e.activation(out=sm[:40, 0:1], in_=self.vs("alog")[:40], func=AF.Exp),
              reads=[self.u_vec], writes=[u_sm])
        S.add("dve", lambda e: e.tensor_scalar(out=sm[:40, 0:1], in0=sm[:40, 0:1], scalar1=-1.0, scalar2=None, op0=ALU.mult),
              reads=[u_sm], writes=[u_sm])
        S.add("act", lambda e: e.activation(out=sm[:, 8:24], in_=self.vs("llam"), func=AF.Exp, scale=-1.0),
              reads=[self.u_vec, u_sm], writes=[u_sm])
        S.add("act", lambda e: e.activation(out=sm[:, 8:24], in_=sm[:, 8:24], func=AF.Ln, bias=1.0),
              reads=[u_sm], writes=[u_sm])
        S.add("dve", lambda e: e.tensor_scalar(out=sm[:, 8:24], in0=sm[:, 8:24], scalar1=-8.0, scalar2=None, op0=ALU.mult),
              reads=[u_sm], writes=[u_sm])

        for w3 in range(3):
            for h in range(8):
                c = w3 * 8 + h
                proj_to_pp(OFF_QKV + c * 128)
                (q, uq) = WK.get()
                self._acc, self._uacc = q, uq
                self.conv(pp, upp, (lambda j, c=c: self.vs("dncw", c * 4 + j)), 4, None, q[:, :U], uq)
                S.add("act", (lambda e, q=q: e.activation(out=q[:, :U], in_=q[:, :U], func=AF.Silu)), reads=[uq], writes=[uq])
                if w3 < 2:
                    (sq, usq) = (sqb, usqb)
                    S.add("act", (lambda e, q=q, sq=sq: e.activation(out=sq[:, :U], in_=q[:, :U], func=AF.Square)),
                          reads=[uq], writes=[usq])
                    for (u0, n) in TT:
                        ps, ups = self.PS.get()
                        S.add("pe", (lambda e, ps=ps, sq=sq, u0=u0, n=n: e.matmul(ps[:, :n], self.ones_bf[:], sq[:, u0:u0 + n],
                                                                                   start=True, stop=True)),
                              reads=[usq, self.u_ones], writes=[ups])
                        rs, urs = rsp.get()
                        S.add("act", (lambda e, rs=rs, ps=ps, n=n: e.activation(out=rs[:, :n], in_=ps[:, :n], func=AF.Sqrt, bias=self.eps6[:])),
                              reads=[ups], writes=[urs])
                        S.add("dve", (lambda e, rs=rs, n=n: e.reciprocal(out=rs[:, :n], in_=rs[:, :n])), reads=[urs], writes=[urs])
                        S.add("pool", (lambda e, q=q, rs=rs, u0=u0, n=n: e.tensor_tensor(out=q[:, u0:u0 + n], in0=q[:, u0:u0 + n],
                                                                                         in1=rs[:, :n], op=ALU.mult)),
                              reads=[urs, uq], writes=[uq])
                dst = (Sx["qT"], Sx["kT"], Sx["vT"])[w3]
                ud = (Ux["qT"], Ux["kT"], Ux["vT"])[w3]
                self.dma(dst[h], q[:, :U], reads=[uq], writes=[ud])
            if w3 == 0:
                for c in range(8):
                    def zt(ti, u0, n, c=c):
                        t, ut = ztp.get()
                        self._zt = (t, ut, u0, n)
                        return t[:, :n], ut
                    wb, uwb = self.load_w_bf16(I["w_in"][l][:, OFF_Z + c * 128:OFF_Z + (c + 1) * 128], 128, wst_pool, wbf_pool)
                    for ti, (u0, n) in enumerate(TT):
                        ps, ups = self.mm_tile(wb, uwb, 128, hT, u_hT, ti)
                        t, ut = ztp.get()
                        S.add("act", (lambda e, ps=ps, t=t, n=n: e.activation(out=t[:, :n], in_=ps[:, :n], func=AF.Silu)),
                              reads=[ups], writes=[ut])
                        self.dma(Sx["szT"][c * 128:(c + 1) * 128, u0:u0 + n], t[:, :n], reads=[ut], writes=[Ux["szT"]])
        if self.stop == "pre1":
            return
        for c in range(24):
            wb, uwb = self.load_w_bf16(I["w_in"][l][:, OFF_GATE + c * 128:OFF_GATE + (c + 1) * 128], 128, wst_pool, wbf_pool)
            for ti, (u0, n) in enumerate(TT):
                ps, ups = self.mm_tile(wb, uwb, 128, hT, u_hT, ti)
                t, ut = gtp.get()
                S.add("act", (lambda e, ps=ps, t=t, n=n: e.activation(out=t[:, :n], in_=ps[:, :n], func=AF.Sigmoid)),
                      reads=[ups], writes=[ut])
                self.dma(Sx["gT"][c * 128:(c + 1) * 128, u0:u0 + n], t[:, :n], reads=[ut], writes=[Ux["gT"]])
        if self.stop == "pre2":
            return
        rm, urm = WK.get()
        self.dma(rm[:40, :U], I["rmask"], writes=[urm])
        G, uG = WK.get()
        LNB, uLNB = WK.get()
        for which, (dst, udst) in enumerate(((G, uG), (LNB, uLNB))):
            wst, uws = wst_pool.get()
            S.add("pool", (lambda e, wst=wst: e.memset(wst[:], 0.0)), writes=[uws])
            for d in range(2):
                c0 = OFF_AB + d * 16 + which * 8
                self.dma(wst[:, :, d * 32:d * 32 + 8], I["w_in"][l][:, c0:c0 + 8].rearrange("(k p) m -> p k m", p=128),
                         reads=[uws], writes=[uws])
            wb, uwb = wbf_pool.get()
            S.add("pool", (lambda e, wb=wb, wst=wst: e.tensor_copy(out=wb[:, :, :40], in_=wst[:, :, :40])), reads=[uws], writes=[uwb])
            for ti, (u0, n) in enumerate(TT):
                ps, ups = self.mm_tile(wb, uwb, 40, hT, u_hT, ti)
                evac_copy(ps, ups, dst[:40, u0:u0 + n], udst, 40, n)
        for (t_, ut_) in ((G, uG), (LNB, uLNB)):
            S.add("pool", (lambda e, t_=t_: e.memset(t_[:40, 256:259], 0.0)), reads=[ut_], writes=[ut_])
        S.add("act", lambda e: e.activation(out=G[:40, :U], in_=G[:40, :U], func=AF.Exp, bias=self.vs("dtb")[:40]),
              reads=[uG, self.u_vec], writes=[uG])
        S.add("act", lambda e: e.activation(out=G[:40, :U], in_=G[:40, :U], func=AF.Ln, bias=1.0), reads=[uG], writes=[uG])
        S.add("dve", lambda e: e.tensor_scalar(out=G[:40, :U], in0=G[:40, :U], scalar1=sm[:40, 0:1], scalar2=None, op0=ALU.mult),
              reads=[uG, u_sm], writes=[uG])
        S.add("act", lambda e: e.activation(out=LNB[:40, :U], in_=LNB[:40, :U], func=AF.Exp, scale=-1.0), reads=[uLNB], writes=[uLNB])
        S.add("act", lambda e: e.activation(out=LNB[:40, :U], in_=LNB[:40, :U], func=AF.Ln, bias=1.0), reads=[uLNB], writes=[uLNB])
        S.add("dve", lambda e: e.tensor_scalar(out=LNB[:40, :U], in0=LNB[:40, :U], scalar1=-1.0, scalar2=None, op0=ALU.mult),
              reads=[uLNB], writes=[uLNB])
        GC, uGC = WK.get()
        S.add("pool", lambda e: e.memset(GC[:40, :U], 0.0), writes=[uGC])
        S.add("dve", lambda e: e.tensor_tensor_scan(out=GC[0:8, :U], data0=rm[0:8, :U], data1=G[0:8, :U], initial=0.0,
                                                    op0=ALU.mult, op1=ALU.add), reads=[urm, uG, uGC], writes=[uGC])
        S.add("dve", lambda e: e.tensor_tensor_scan(out=GC[32:40, U - 1::-1], data0=rm[32:40, U - 1::-1], data1=G[32:40, U - 1::-1],
                                                    initial=0.0, op0=ALU.mult, op1=ALU.add), reads=[urm, uG, uGC], writes=[uGC])
        self.dma(Sx["gab"][0], GC[:40, :U], reads=[uGC], writes=[Ux["gab"]])
        self.dma(Sx["gab"][2], LNB[:40, :U], reads=[uLNB], writes=[Ux["gab"]])
        S.add("dve", lambda e: e.tensor_tensor(out=G[:40, :U], in0=GC[:40, :U], in1=LNB[:40, :U], op=ALU.add),
              reads=[uGC, uLNB, uG], writes=[uG])
        self.dma(Sx["gab"][1], G[:40, :U], reads=[uG], writes=[Ux["gab"]])
        if self.stop == "pre3":
            return
        for c in range(8):
            tl = []
            for part in (1, 2, 0):
                cidx = part * 8 + c
                proj_to_pp(OFF_HY + cidx * 128)
                t, ut = WK.get()
                self._acc, self._uacc = t, ut
                self.conv(pp, upp, (lambda j, cidx=cidx: self.vs("hycw", cidx * 3 + j)), 3, self.vs("hycb", cidx), t[:, :U], ut)
                tl.append((t, ut))
            (a1, ua1), (a2, ua2), (a0, ua0) = tl
            S.add("pool", (lambda e, a1=a1, a2=a2: e.tensor_tensor(out=a1[:, :U], in0=a1[:, :U], in1=a2[:, :U], op=ALU.mult)),
                  reads=[ua1, ua2], writes=[ua1])
            self.dma(Sx["zzT"][c * 128:(c + 1) * 128], a1[:, :U], reads=[ua1], writes=[Ux["zzT"]])
            self.dma(Sx["x0T"][c * 128:(c + 1) * 128], a0[:, :U], reads=[ua0], writes=[Ux["x0T"]])
        if self.stop == "pre4":
            return
        for g in range(8):
            proj_to_pp(OFF_LX + g * 128)
            (xs, uxs), (H, uH), (A, uA), (Bt, uB) = WK.t
            self._acc, self._uacc = H, uH
            self.conv(pp, upp, (lambda j, g=g: self.vs("lrucw", g * 4 + j)), 4, self.vs("lrucb", g), xs[:, :U], uxs)
            S.add("pool", (lambda e: e.tensor_copy(out=sqb[:, :U], in_=xs[:, :U])), reads=[uxs], writes=[usqb])
            for d in range(2):
                tt_, utt = (H, uH) if d == 0 else (pp, upp)
                for (wname, bname, dst, udst) in (("lru_w_a", "lba", A, uA), ("lru_w_x", "lbx", Bt, uB)):
                    lw, ulw = lwp.get()
                    self.dma(lw[:], I[wname][l, d, g], writes=[ulw])
                    lr, ulr = lwr.get()
                    S.add("act", (lambda e, lw=lw, lr=lr: e.activation(out=lr[:], in_=lw[:], func=AF.Copy)), reads=[ulw], writes=[ulr])
                    for (u0, n) in TT:
                        ps, ups = self.PS.get()
                        S.add("pe", (lambda e, ps=ps, lr=lr, u0=u0, n=n: e.matmul(ps[:, :n], lr[:], sqb[:, u0:u0 + n],
                                                                                   start=True, stop=True)),
                              reads=[ulr, usqb], writes=[ups])
                        S.add("act", (lambda e, ps=ps, dst=dst, u0=u0, n=n, bname=bname, d=d, g=g: e.activation(
                            out=dst[:, u0:u0 + n], in_=ps[:, :n], func=AF.Sigmoid, bias=self.vs(bname, d * 8 + g))),
                            reads=[ups, self.u_vec], writes=[udst])
                if self.stop == "pre5":
                    return
                S.add("pool", (lambda e, A=A: e.memset(A[:, 256:259], 0.0)), reads=[uA], writes=[uA])
                S.add("act", (lambda e, A=A, d=d, g=g: e.activation(out=A[:, :U], in_=A[:, :U], func=AF.Exp,
                                                                     scale=sm[:, 8 + d * 8 + g:9 + d * 8 + g])),
                      reads=[uA, u_sm], writes=[uA])
                S.add("dve", (lambda e, A=A, t=tt_: e.tensor_tensor(out=t[:, :U], in0=A[:, :U], in1=A[:, :U], op=ALU.mult)),
                      reads=[uA, utt], writes=[utt])
                S.add("dve", (lambda e, t=tt_: e.tensor_scalar(out=t[:, :U], in0=t[:, :U], scalar1=-1.0, scalar2=1.0,
                                                               op0=ALU.mult, op1=ALU.add)), reads=[utt], writes=[utt])
                S.add("act", (lambda e, t=tt_: e.activation(out=t[:, :U], in_=t[:, :U], func=AF.Sqrt)), reads=[utt], writes=[utt])
                S.add("pool", (lambda e, Bt=Bt, t=tt_: e.tensor_tensor(out=Bt[:, :U], in0=Bt[:, :U], in1=t[:, :U], op=ALU.mult)),
                      reads=[uB, utt], writes=[uB])
                S.add("pool", (lambda e, Bt=Bt: e.tensor_tensor(out=Bt[:, :U], in0=Bt[:, :U], in1=xs[:, :U], op=ALU.mult)),
                      reads=[uB, uxs], writes=[uB])
                S.add("pool", (lambda e, Bt=Bt: e.memset(Bt[:, 256:259], 0.0)), reads=[uB], writes=[uB])
                if self.stop == "pre6":
                    return
                if d == 0:
                    S.add("dve", (lambda e, A=A, Bt=Bt, t=tt_: e.tensor_tensor_scan(out=t[:, 0:U], data0=A[:, 0:U], data1=Bt[:, 0:U],
                                                                                    initial=0.0, op0=ALU.mult, op1=ALU.add)),
                          reads=[uA, uB, utt], writes=[utt])
                else:
                    S.add("dve", (lambda e, A=A, Bt=Bt, t=tt_: e.tensor_tensor_scan(out=t[:, 255::-1], data0=A[:, 255::-1], data1=Bt[:, 255::-1],
                                                                                    initial=0.0, op0=ALU.mult, op1=ALU.add)),
                          reads=[uA, uB, utt], writes=[utt])
                    S.add("dve", (lambda e, A=A, Bt=Bt, t=tt_: e.tensor_tensor_scan(out=t[:, U - 1:LT0 - 1:-1], data0=A[:, U - 1:LT0 - 1:-1],
                                                                                    data1=Bt[:, U - 1:LT0 - 1:-1], initial=t[:, 0:1],
                                                                                    op0=ALU.mult, op1=ALU.add)),
                          reads=[uA, uB, utt], writes=[utt])
                    S.add("pool", (lambda e, t=tt_: e.tensor_tensor(out=H[:, :U], in0=H[:, :U], in1=t[:, :U], op=ALU.add)),
                          reads=[utt, uH], writes=[uH])
            if self.stop == "pre7":
                return
            S.add("pool", lambda e: e.memset(pp[:, 0:1], 0.0), reads=[upp], writes=[upp])
            S.add("pool", lambda e: e.memset(pp[:, 257:260], 0.0), reads=[upp], writes=[upp])
            S.add("pool", lambda e: e.memset(pp[:, 4356:UP], 0.0), reads=[upp], writes=[upp])
            (Y, uY), (T2, uT2) = WK.t[2], WK.t[3]
            wb, uwb = self.load_w_bf16(I["w_in"][l][:, OFF_LY + g * 128:OFF_LY + (g + 1) * 128], 128, wst_pool, wbf_pool)
            for ti, (u0, n) in enumerate(TT):
                ps, ups = self.mm_tile(wb, uwb, 128, hT, u_hT, ti)
                evac_copy(ps, ups, Y[:, u0:u0 + n], uY, 128, n)
            S.add("pool", (lambda e, Y=Y: e.memset(Y[:, 256:259], 0.0)), reads=[uY], writes=[uY])
            S.add("act", (lambda e, Y=Y, T2=T2: e.activation(out=T2[:, :U], in_=Y[:, :U], func=AF.Square)), reads=[uY], writes=[uT2])
            S.add("dve", (lambda e, T2=T2: e.tensor_scalar(out=T2[:, :U], in0=T2[:, :U], scalar1=0.044715, scalar2=1.0,
                                                           op0=ALU.mult, op1=ALU.add)), reads=[uT2], writes=[uT2])
            S.add("pool", (lambda e, Y=Y, T2=T2: e.tensor_tensor(out=T2[:, :U], in0=T2[:, :U], in1=Y[:, :U], op=ALU.mult)),
                  reads=[uT2, uY], writes=[uT2])
            S.add("act", (lambda e, T2=T2: e.activation(out=T2[:, :U], in_=T2[:, :U], func=AF.Sigmoid, scale=1.5957691216057308)),
                  reads=[uT2], writes=[uT2])
            S.add("dve", (lambda e, Y=Y, T2=T2: e.tensor_tensor(out=Y[:, :U], in0=Y[:, :U], in1=T2[:, :U], op=ALU.mult)),
                  reads=[uT2, uY], writes=[uY])
            S.add("dve", (lambda e, Y=Y: e.tensor_tensor(out=T2[:, :U].bitcast(BF16)[:, :U], in0=Y[:, :U], in1=H[:, :U], op=ALU.mult)),
                  reads=[uY, uH, uT2], writes=[uT2])
            self.dma(Sx["olru"][g * 128:(g + 1) * 128], T2[:, :U].bitcast(BF16)[:, :U], reads=[uT2], writes=[Ux["olru"]])
            if self.stop == "pre8":
                return


def prep_inputs(inp, b):
    m = {}
    xT = np.zeros((D, U), np.float32)
    xT[:, 0:LC] = inp["ctx"][b].T
    xT[:, LT0:] = inp["x"][b].T
    m["xT0"] = xT
    cc = np.zeros((128, 8, 2), np.float32)
    cc[:, :, 0] = _pk(inp["c"][b])
    cc[:, :, 1] = _pk(inp["c_ctx"])
    m["cc"] = cc.reshape(128, 16)
    m["vec"] = np.stack([build_vec(inp, l) for l in range(DEPTH)])
    m["cst"] = build_cst()
    m["rmask"] = build_rmask()
    for seg, n_ in (("lat", L), ("ctx", LC)):
        tbs = hyena_tables(n_)
        for k in ("zT", "negt", "F1", "G", "Tout"):
            m["hy_%s_%s" % (k, seg)] = tbs[k]
        m["hy_Winv"] = tbs["Winv"]
    for n in ("w_mod", "w_in", "lru_w_a", "lru_w_x", "w_proj_dn", "w_proj_hy", "w_proj_lru", "w_out",
              "ffn_up", "ffn_down", "hy_w1", "hy_w2", "hy_w3"):
        m[n] = np.ascontiguousarray(inp[n], dtype=np.float32)
    return m


DN_BLOCKS = [0, 128] + [LT0 + 128 * i for i in range(32)]


class T128:
    def __init__(self, nc, es, names, dtype=F32):
        self.t = {}
        for n in names:
            self.t[n] = (es.enter_context(nc.sbuf_tensor(uname(n), [128, 128], dtype)), Unit())

    def __getitem__(self, n):
        return self.t[n]


def dn_phase(B, l, es):
    nc, S, Sx, Ux = B.nc, B.S, B.Sx, B.Ux
    ident = B.cs("ident")
    NB = len(DN_BLOCKS)
    GR = es.enter_context(nc.sbuf_tensor(uname("GR"), [40, U], F32))
    uGR = Unit()
    B.dma(GR[:], Sx["gab"][0], reads=[Ux["gab"]], writes=[uGR])
    TG = es.enter_context(nc.sbuf_tensor(uname("TG"), [128, NB, 3, 40], F32))
    TE = es.enter_context(nc.sbuf_tensor(uname("TE"), [128, NB, 2, 40], F32))
    uTG = Unit()
    gl_pool = TPool(nc, es, "gl", [40, 3, 128], F32, 2)
    for bi, ub in enumerate(DN_BLOCKS):
        gl, ugl = gl_pool.get()
        B.dma(gl[:], Sx["gab"][:, :, ub:ub + 128].rearrange("k r u -> r k u"), reads=[Ux["gab"]], writes=[ugl])
        ps, ups = B.PS.get()
        for k in range(3):
            S.add("pe", (lambda e, ps=ps, k=k, gl=gl: e.transpose(ps[:, k * 40:(k + 1) * 40], gl[:, k, :], ident[:40, :40])),
                  reads=[ugl, B.u_cst], writes=[ups])
        S.add("dve", (lambda e, ps=ps, bi=bi: e.tensor_copy(out=TG[:, bi].rearrange("p k r -> p (k r)"), in_=ps[:, 0:120])),
              reads=[ups], writes=[uTG])
        S.add("act", (lambda e, ps=ps, bi=bi: e.activation(out=TE[:, bi].rearrange("p k r -> p (k r)"), in_=ps[:, 40:120], func=AF.Exp)),
              reads=[ups], writes=[uTG])
    SELN = es.enter_context(nc.sbuf_tensor(uname("SELN"), [40, 16, 128], F32))
    uSEL = Unit()
    for q in range(16):
        r = (q // 8) * 32 + (q % 8)
        S.add("dve", (lambda e, q=q, r=r: e.tensor_scalar(out=SELN[:, q, :], in0=ident[:40, r:r + 1].to_broadcast([40, 128]),
                                                          scalar1=-1.0, scalar2=None, op0=ALU.mult)),
              reads=[B.u_cst], writes=[uSEL])
    zero = es.enter_context(nc.sbuf_tensor(uname("zero"), [128, 128], F32))
    uzero = Unit()
    S.add("pool", lambda e: e.memset(zero[:], 0.0), writes=[uzero])

    if B.stop == "dn0":
        dbg = nc.dram_tensor(uname("dbg_TG"), [128, NB, 3, 40], F32, kind="ExternalOutput").ap()
        B.dma(dbg, TG[:], reads=[uTG])
        return
    NCH = 4
    chains_res = []
    for ci in range(NCH):
        res = {}
        res["f32"] = T128(nc, es, ["qt", "kt", "vt", "Dm", "A", "E1", "M", "AT", "MT", "Y", "P0", "P1", "Q0", "Q1", "Us", "EG"])
        res["f32b"] = T128(nc, es, ["qt", "kt", "vt"])
        res["r"] = T128(nc, es, ["ktr", "qtr", "attnT", "Kd0", "Kd1", "Ktb", "Vb", "QdT", "YR", "WTs", "VN", "S"], F32R)
        res["cd"] = (es.enter_context(nc.sbuf_tensor(uname("cd"), [128, 2], F32)), Unit())
        if ci % 2 == 0:
            res["O"] = (es.enter_context(nc.sbuf_tensor(uname("O"), [128, NB, 128], F32)), [Unit() for _ in range(NB)])
        else:
            res["O"] = chains_res[ci - 1]["O"]
        res["ps"] = [B.PS.t[2 * ci], B.PS.t[2 * ci + 1]]
        chains_res.append(res)
    OD = es.enter_context(nc.sbuf_tensor(uname("OD"), [128, U], BF16))
    uOD = Unit()
    S.add("pool", lambda e: e.memset(OD[:, 256:259], 0.0), writes=[uOD])
    pp_t = TPool(nc, es, "dnpost", [128, 128], F32, 3)
    pp_s = TPool(nc, es, "dnsm", [128, 2], F32, 3)
    DK = 128.0 ** -0.5

    def chain(h, d, res):
        q = d * 8 + h
        r = d * 32 + h
        f, fb, rr = res["f32"], res["f32b"], res["r"]
        cd, ucd = res["cd"]
        O, uO = res["O"]
        psl = res["ps"]
        slot_i = [0]

        def slot():
            k = slot_i[0] % 8
            slot_i[0] += 1
            t, u = psl[k // 4]
            qd = k % 4
            return t[:, qd * 128:(qd + 1) * 128], u
        incl = B.cs("lowi") if d == 0 else B.cs("uppi")
        strict = B.cs("lows") if d == 0 else B.cs("upps")
        iend = (lambda c: c * 64 + 63) if d == 0 else (lambda c: c * 64)
        (Sst, uS), (VN, uVN) = rr["S"], rr["VN"]
        S.add("dve", (lambda e: e.tensor_copy(out=Sst[:], in_=zero[:])), reads=[uzero], writes=[uS])
        S.add("dve", (lambda e: e.tensor_copy(out=VN[:], in_=zero[:])), reads=[uzero], writes=[uVN])
        order = list(range(NB)) if d == 0 else [1, 0] + list(range(NB - 1, 1, -1))
        for oi, bi in enumerate(order):
            ub = DN_BLOCKS[bi]
            ld = f if oi % 2 == 0 else fb
            (qt, uqt), (kt, ukt), (vt, uvt) = ld["qt"], ld["kt"], ld["vt"]
            B.dma(qt[:], Sx["qT"][h][:, ub:ub + 128], reads=[Ux["qT"]], writes=[uqt])
            B.dma(kt[:], Sx["kT"][h][:, ub:ub + 128], reads=[Ux["kT"]], writes=[ukt])
            B.dma(vt[:], Sx["vT"][h][:, ub:ub + 128], reads=[Ux["vT"]], writes=[uvt])
            (ktr, uktr), (qtr, uqtr) = rr["ktr"], rr["qtr"]
            S.add("act", (lambda e, kt=kt: e.activation(out=ktr[:], in_=kt[:], func=AF.Copy)), reads=[ukt], writes=[uktr])
            S.add("act", (lambda e, qt=qt: e.activation(out=qtr[:], in_=qt[:], func=AF.Copy)), reads=[uqt], writes=[uqtr])
            yield
            if B.stop.endswith(":A"):
                return
            pKK, uKK = slot()
            S.add("pe", (lambda e, o=pKK: e.matmul(o, ktr[:], ktr[:], start=True, stop=True)), reads=[uktr], writes=[uKK])
            pQK, uQK = slot()
            S.add("pe", (lambda e, o=pQK: e.matmul(o, ktr[:], qtr[:], start=True, stop=True)), reads=[uktr, uqtr], writes=[uQK])
            pbc, ubc = slot()
            S.add("pe", (lambda e, o=pbc, ub=ub: e.matmul(o, SELN[:, q, :], GR[:, ub:ub + 128], start=True, stop=True)),
                  reads=[uSEL, uGR], writes=[ubc])
            pvt, uvtk = slot()
            S.add("pe", (lambda e, o=pvt, vt=vt: e.transpose(o, vt[:], ident)), reads=[uvt, B.u_cst], writes=[uvtk])
            pkt, uktk = slot()
            S.add("pe", (lambda e, o=pkt, kt=kt: e.transpose(o, kt[:], ident)), reads=[ukt, B.u_cst], writes=[uktk])
            yield
            if B.stop.endswith(":B"):
                return
            (Dm, uDm), (A, uA), (E1, uE1), (M, uM), (EG, uEG) = f["Dm"], f["A"], f["E1"], f["M"], f["EG"]
            gcol = TG[:, bi, 0, r:r + 1]
            climit = int(B.stop.split(":K")[1]) if ":K" in B.stop else 99
            if 0 < climit:
                S.add("dve", (lambda e, o=pbc, gcol=gcol: e.tensor_scalar(out=Dm[:], in0=o, scalar1=gcol, scalar2=0.0, op0=ALU.add, op1=ALU.min)),
                      reads=[ubc, uTG], writes=[uDm])
            if 2 < climit:
                S.add("act", (lambda e, o=pbc: e.activation(out=EG[:], in_=o, func=AF.Exp, scale=-1.0)), reads=[ubc], writes=[uEG])
            if 3 < climit:
                S.add("act", (lambda e: e.activation(out=A[:], in_=Dm[:], func=AF.Exp)), reads=[uDm], writes=[uA])
            if 4 < climit:
                S.add("dve", (lambda e: e.tensor_tensor(out=A[:], in0=A[:], in1=incl, op=ALU.mult)), reads=[uA, B.u_cst], writes=[uA])
            bcol = TE[:, bi, 1, r:r + 1]
            if 5 < climit:
                S.add("dve", (lambda e, bcol=bcol: e.scalar_tensor_tensor(out=E1[:], in0=A[:], scalar=bcol, in1=strict, op0=ALU.mult, op1=ALU.mult)),
                      reads=[uA, uTG, B.u_cst], writes=[uE1])
            if 6 < climit:
                S.add("dve", (lambda e, o=pKK: e.tensor_tensor(out=M[:], in0=o, in1=E1[:], op=ALU.mult)), reads=[uKK, uE1], writes=[uM])
            (QdT, uQdT) = rr["QdT"]
            if 7 < climit:
                S.add("dve", (lambda e, qt=qt: e.scalar_tensor_tensor(out=QdT[:], in0=qt[:], scalar=DK, in1=EG[:], op0=ALU.mult, op1=ALU.mult)),
                      reads=[uqt, uEG], writes=[uQdT])
            yield
            if B.stop.endswith(":C") or ":K" in B.stop:
                return
            pAT, uAT_ = slot()
            S.add("pe", (lambda e, o=pAT: e.transpose(o, A[:], ident)), reads=[uA, B.u_cst], writes=[uAT_])
            pMT, uMT_ = slot()
            S.add("pe", (lambda e, o=pMT: e.transpose(o, M[:], ident)), reads=[uM, B.u_cst], writes=[uMT_])
            yield
            (AT, uAT), (MT, uMT), (Y, uY) = f["AT"], f["MT"], f["Y"]
            S.add("act", (lambda e, o=pAT: e.activation(out=AT[:], in_=o, func=AF.Copy)), reads=[uAT_], writes=[uAT])
            S.add("act", (lambda e, o=pMT: e.activation(out=MT[:], in_=o, func=AF.Copy)), reads=[uMT_], writes=[uMT])
            S.add("dve", (lambda e, o=pMT: e.tensor_tensor(out=Y[:], in0=ident, in1=o, op=ALU.subtract)), reads=[uMT_, B.u_cst], writes=[uY])
            (attnT, uattn), (Kd0, uKd0), (Kd1, uKd1), (Ktb, uKtb), (Vb, uVb) = rr["attnT"], rr["Kd0"], rr["Kd1"], rr["Ktb"], rr["Vb"]
            S.add("dve", (lambda e, o=pQK: e.scalar_tensor_tensor(out=attnT[:], in0=o, scalar=DK, in1=AT[:], op0=ALU.mult, op1=ALU.mult)),
                  reads=[uQK, uAT], writes=[uattn])
            for c, (Kd, uKd) in enumerate(((Kd0, uKd0), (Kd1, uKd1))):
                S.add("act", (lambda e, o=pkt, Kd=Kd, c=c: e.activation(out=Kd[:], in_=o, func=AF.Copy, scale=AT[:, iend(c):iend(c) + 1])),
                      reads=[uktk, uAT], writes=[uKd])
            wcol = TE[:, bi, 0, r:r + 1]
            S.add("dve", (lambda e, o=pkt, wcol=wcol: e.tensor_scalar(out=Ktb[:], in0=o, scalar1=wcol, scalar2=None, op0=ALU.mult)),
                  reads=[uktk, uTG], writes=[uKtb])
            S.add("dve", (lambda e, o=pvt, bcol=bcol: e.tensor_scalar(out=Vb[:], in0=o, scalar1=bcol, scalar2=None, op0=ALU.mult)),
                  reads=[uvtk, uTG], writes=[uVb])
            yield
            if B.stop.endswith(":E"):
                return
            P, uP = M, uM
            PT, uPT = MT, uMT
            bufs = [(f["P0"], f["Q0"]), (f["P1"], f["Q1"])]
            (YR, uYR) = rr["YR"]
            pend = None
            for lev in range(1, 7):
                cur = None
                if lev <= 5:
                    (Pn, uPn), (PnT, uPnT) = bufs[lev % 2]
                    pP, upP = slot()
                    S.add("pe", (lambda e, o=pP, PT=PT, P=P: e.matmul(o, PT[:], P[:], start=True, stop=True)), reads=[uPT, uP], writes=[upP])
                    pPT = None
                    if lev < 5:
                        pPT, upPT = slot()
                        S.add("pe", (lambda e, o=pPT, PT=PT, P=P: e.matmul(o, P[:], PT[:], start=True, stop=True)), reads=[uPT, uP], writes=[upPT])
                    cur = (Pn, uPn, PnT, uPnT, pP, upP, pPT, upPT if lev < 5 else None)
                pY = None
                if pend is not None:
                    pY, upY = slot()
                    S.add("pe", (lambda e, o=pY, Pq=pend[0]: e.matmul(o, Pq[:], Y[:], start=True, stop=True)), reads=[pend[1], uY], writes=[upY])
                yield
                if pY is not None:
                    if lev <= 5:
                        S.add("dve", (lambda e, o=pY: e.tensor_tensor(out=Y[:], in0=Y[:], in1=o, op=ALU.add)), reads=[upY, uY], writes=[uY])
                    else:
                        S.add("dve", (lambda e, o=pY: e.tensor_tensor(out=YR[:], in0=Y[:], in1=o, op=ALU.add)), reads=[upY, uY], writes=[uYR])
                if cur is not None:
                    (Pn, uPn, PnT, uPnT, pP, upP, pPT, upPT) = cur
                    S.add("act", (lambda e, o=pP, Pn=Pn: e.activation(out=Pn[:], in_=o, func=AF.Copy)), reads=[upP], writes=[uPn])
                    if pPT is not None:
                        S.add("act", (lambda e, o=pPT, PnT=PnT: e.activation(out=PnT[:], in_=o, func=AF.Copy)), reads=[upPT], writes=[uPnT])
                    pend = (Pn, uPn)
                    P, uP, PT, uPT = Pn, uPn, PnT, uPnT
                yield
            if B.stop.endswith(":G"):
                return
            (Us, uUs), (WTs, uWTs) = f["Us"], rr["WTs"]
            pU, upU = slot()
            S.add("pe", (lambda e, o=pU: e.matmul(o, YR[:], Vb[:], start=True, stop=True)), reads=[uYR, uVb], writes=[upU])
            pW, upW = slot()
            S.add("pe", (lambda e, o=pW: e.matmul(o, Ktb[:], YR[:], start=True, stop=True)), reads=[uYR, uKtb], writes=[upW])
            yield
            S.add("act", (lambda e, o=pU: e.activation(out=Us[:], in_=o, func=AF.Copy)), reads=[upU], writes=[uUs])
            S.add("dve", (lambda e, o=pW: e.tensor_copy(out=WTs[:], in_=o)), reads=[upW], writes=[uWTs])
            yield
            if B.stop.endswith(":H"):
                return
            for c in ((0, 1) if d == 0 else (1, 0)):
                rows = slice(c * 64, (c + 1) * 64)
                Kd, uKd = (Kd0, uKd0) if c == 0 else (Kd1, uKd1)
                p1, up1 = slot()
                S.add("pe", (lambda e, o=p1: e.matmul(o, WTs[:], Sst[:], start=True, stop=True)), reads=[uWTs, uS], writes=[up1])
                yield
                S.add("dve", (lambda e, o=p1, rows=rows: e.tensor_tensor(out=VN[rows, :], in0=Us[rows, :], in1=o[rows, :], op=ALU.subtract)),
                      reads=[up1, uUs, uVN], writes=[uVN])
                yield
                p2, up2 = slot()
                S.add("pe", (lambda e, o=p2: e.matmul(o, QdT[:], Sst[:], start=True, stop=False)), reads=[uQdT, uS], writes=[up2])
                S.add("pe", (lambda e, o=p2: e.matmul(o, attnT[:], VN[:], start=False, stop=True)), reads=[uattn, uVN], writes=[up2])
                p3, up3 = slot()
                S.add("pe", (lambda e, o=p3, Kd=Kd: e.matmul(o, Kd[:], VN[:], start=True, stop=True)), reads=[uKd, uVN], writes=[up3])
                yield
                oi_other = (bi if d == 1 else ([1, 0] + list(range(NB - 1, 1, -1))).index(bi))
                first = (oi < oi_other) or (oi == oi_other and d == 0)
                if first:
                    S.add("act", (lambda e, o=p2, rows=rows, bi=bi: e.activation(out=O[rows, bi, :], in_=o[rows, :], func=AF.Copy)),
                          reads=[up2], writes=[uO[bi]])
                else:
                    S.add("dve", (lambda e, o=p2, rows=rows, bi=bi: e.tensor_tensor(out=O[rows, bi, :], in0=O[rows, bi, :], in1=o[rows, :], op=ALU.add)),
                          reads=[up2, uO[bi]], writes=[uO[bi]])
                S.add("dve", (lambda e, o=p3, c=c: e.scalar_tensor_tensor(out=Sst[:], in0=Sst[:], scalar=EG[:, iend(c):iend(c) + 1], in1=o,
                                                                         op0=ALU.mult, op1=ALU.add)), reads=[up3, uS, uEG], writes=[uS])
                yield

    def post(h, resf, resb):
        (Of, uOf) = resf["O"]
        for bi, ub in enumerate(DN_BLOCKS):
            t, ut = pp_t.get()
            sm_, usm = pp_s.get()
            S.add("pool", (lambda e, t=t, bi=bi: e.tensor_copy(out=t[:], in_=Of[:, bi, :])),
                  reads=[uOf[bi]], writes=[ut])
            t2, ut2 = pp_t.get()
            S.add("act", (lambda e, t=t, t2=t2, sm_=sm_: e.activation(out=t2[:], in_=t[:], func=AF.Square, accum_out=sm_[:, 0:1])),
                  reads=[ut], writes=[ut2, usm])
            S.add("act", (lambda e, sm_=sm_: e.activation(out=sm_[:, 1:2], in_=sm_[:, 0:1], func=AF.Sqrt, scale=1.0 / 128, bias=B.eps6[:])),
                  reads=[usm], writes=[usm])
            S.add("dve", (lambda e, sm_=sm_: e.reciprocal(out=sm_[:, 1:2], in_=sm_[:, 1:2])), reads=[usm], writes=[usm])
            S.add("dve", (lambda e, t=t, sm_=sm_: e.tensor_scalar(out=t[:], in0=t[:], scalar1=sm_[:, 1:2], scalar2=None, op0=ALU.mult)),
                  reads=[usm, ut], writes=[ut])
            ps, ups = B.PS.get()
            S.add("pe", (lambda e, ps=ps, t=t: e.transpose(ps[:, 0:128], t[:], ident)), reads=[ut, B.u_cst], writes=[ups])
            sz, usz = pp_t.get()
            B.dma(sz[:], Sx["szT"][h * 128:(h + 1) * 128, ub:ub + 128], reads=[Ux["szT"]], writes=[usz])
            S.add("dve", (lambda e, ps=ps, sz=sz, ub=ub: e.scalar_tensor_tensor(out=OD[:, ub:ub + 128], in0=ps[:, 0:128], scalar=B.vs("dnng"),
                                                                              in1=sz[:], op0=ALU.mult, op1=ALU.mult)),
                  reads=[ups, usz, B.u_vec, uOD], writes=[uOD])
        B.dma(Sx["odn"][h * 128:(h + 1) * 128], OD[:], reads=[uOD], writes=[Ux["odn"]])

    nheads = 8 if not (B.stop or "").startswith("dn1") else 2
    if B.stop is None:
        B.stop = ""
    for h0 in range(0, nheads, 2):
        gens = []
        for j in range(2):
            for d in range(2):
                gens.append(chain(h0 + j, d, chains_res[j * 2 + d]))
        active = list(gens)
        while active:
            nxt = []
            for g in active:
                try:
                    next(g)
                    nxt.append(g)
                except StopIteration:
                    pass
            active = nxt
        if ":" in B.stop:
            return
        for j in range(2):
            post(h0 + j, chains_res[j * 2], chains_res[j * 2 + 1])


def hy_dims(n):
    N = 2 * n
    NS1 = n // 128
    NF1 = N // 128
    NFH = NF1 // 2 + 1
    return N, NS1, NF1, NFH


def hyena_tables(n):
    N, NS1, NF1, NFH = hy_dims(n)
    f64 = np.float64
    s1 = np.arange(NS1, dtype=f64)[:, None]
    f1 = np.arange(NFH, dtype=f64)[None, :]
    ang = 2 * np.pi * f1 * s1 / NF1
    F1 = np.concatenate([np.cos(ang), -np.sin(ang)], axis=1)
    s2 = np.arange(128, dtype=f64)[:, None, None]
    f1b = np.arange(NFH, dtype=f64)[None, :, None]
    f2 = np.arange(128, dtype=f64)[None, None, :]
    ang = 2 * np.pi * (f1b + NF1 * f2) * s2 / N
    G = np.stack([np.cos(ang), -np.sin(ang)], axis=2)
    f2c = np.arange(128, dtype=f64)[:, None]
    t2 = np.arange(128, dtype=f64)[None, :]
    th = 2 * np.pi * f2c * t2 / 128
    Winv = np.stack([np.concatenate([np.cos(th), np.sin(th)], 1), np.concatenate([-np.sin(th), np.cos(th)], 1)], axis=1)
    NT1 = NS1
    f1c = np.arange(NFH, dtype=f64)[:, None, None]
    t2c = np.arange(128, dtype=f64)[None, :, None]
    t1c = np.arange(NT1, dtype=f64)[None, None, :]
    ph = 2 * np.pi * f1c * (128 * t1c + t2c) / N
    w = np.full((NFH, 1, 1), 2.0)
    w[0] = 1.0
    w[NFH - 1] = 1.0
    Tout = np.stack([w / N * np.cos(ph), -w / N * np.sin(ph)], axis=2)
    t = np.linspace(0.0, 1.0, n, dtype=np.float32)[:, None]
    omega = (2.0 * math.pi * np.arange(n, dtype=np.float32)[:, None] / n).astype(np.float32)
    bands = np.linspace(1e-4, 15, 16, dtype=np.float32)[None, :]
    z = np.concatenate([t, np.cos(bands * omega), -np.sin(bands * omega)], axis=-1).astype(np.float32)
    negt = np.broadcast_to(-t[:, 0][None, :], (128, n))
    return dict(F1=F1.astype(np.float32), G=G.astype(np.float32), Winv=Winv.astype(np.float32),
                Tout=Tout.astype(np.float32), zT=np.ascontiguousarray(z.T), negt=np.ascontiguousarray(negt, dtype=np.float32))


HY_C = 32


def hyena_phase(B, l, es, seg):
    nc, S, Sx, Ux, I = B.nc, B.S, B.Sx, B.Ux, B.I
    n, base = (L, LT0) if seg == "lat" else (LC, 0)
    N, NS1, NF1, NFH = hy_dims(n)
    NT1 = NS1
    W2 = 2 * NFH
    C = HY_C
    tb = lambda k: I["hy_%s_%s" % (k, seg)]
    hp, uhp = Sx["hp_" + seg], Ux["hp_" + seg]
    NTL = [(i * 512, min(512, n - i * 512)) for i in range((n + 511) // 512)]
    MAGIC = 12582912.0

    with ExitStack() as e1:
        zT = e1.enter_context(nc.sbuf_tensor(uname("zT"), [33, n], F32))
        h1 = e1.enter_context(nc.sbuf_tensor(uname("h1"), [64, n], F32))
        h2 = e1.enter_context(nc.sbuf_tensor(uname("h2"), [64, n], F32))
        negt = e1.enter_context(nc.sbuf_tensor(uname("negt"), [128, n], F32))
        dec = e1.enter_context(nc.sbuf_tensor(uname("dec"), [128, n], F32))
        w1 = e1.enter_context(nc.sbuf_tensor(uname("w1"), [33, 64], F32))
        w2 = e1.enter_context(nc.sbuf_tensor(uname("w2"), [64, 64], F32))
        w3 = e1.enter_context(nc.sbuf_tensor(uname("w3"), [64, 2048], F32))
        fb = e1.enter_context(nc.sbuf_tensor(uname("fb"), [64, 2], F32))
        uz, uh1, uh2, unegt, udec, uw, ufb = Unit(), Unit(), Unit(), Unit(), Unit(), Unit(), Unit()
        tmp_p = TPool(nc, e1, "hyt", [128, 512], F32, 3)
        out_p = TPool(nc, e1, "hyo", [128, 2, 512], F32, 2)
        B.dma(zT[:], tb("zT"), writes=[uz])
        B.dma(negt[:], tb("negt"), writes=[unegt])
        B.dma(w1[:], I["hy_w1"][l], writes=[uw])
        B.dma(w2[:], I["hy_w2"][l], writes=[uw])
        B.dma(w3[:], I["hy_w3"][l], writes=[uw])
        S.add("dve", lambda e: e.tensor_tensor(out=fb[:, 0:1], in0=B.vs("hyf1")[:64], in1=B.vs("hyb1")[:64], op=ALU.mult),
              reads=[B.u_vec], writes=[ufb])
        S.add("dve", lambda e: e.tensor_tensor(out=fb[:, 1:2], in0=B.vs("hyf2")[:64], in1=B.vs("hyb2")[:64], op=ALU.mult),
              reads=[B.u_vec, ufb], writes=[ufb])
        for li, (wt, K, src, usrc, dst, udst, fname) in enumerate(((w1, 33, zT, uz, h1, uh1, "hyf1"), (w2, 64, h1, uh1, h2, uh2, "hyf2"))):
            for (t0, tn) in NTL:
                ps, ups = B.PS.get()
                S.add("pe", (lambda e, ps=ps, wt=wt, K=K, src=src, t0=t0, tn=tn: e.matmul(ps[:64, :tn], wt[:K, :], src[:K, t0:t0 + tn],
                                                                                          start=True, stop=True)),
                      reads=[uw, usrc], writes=[ups])
                x, ux = tmp_p.get()
                k2, uk2 = tmp_p.get()
                S.add("dve", (lambda e, ps=ps, x=x, tn=tn, li=li, fname=fname: e.tensor_scalar(
                    out=x[:64, :tn], in0=ps[:64, :tn], scalar1=B.vs(fname)[:64], scalar2=fb[:, li:li + 1], op0=ALU.mult, op1=ALU.add)),
                    reads=[ups, B.u_vec, ufb], writes=[ux])
                S.add("dve", (lambda e, x=x, k2=k2, tn=tn: e.tensor_scalar(out=k2[:64, :tn], in0=x[:64, :tn], scalar1=1.0 / (2 * math.pi),
                                                                           scalar2=MAGIC, op0=ALU.mult, op1=ALU.add)), reads=[ux], writes=[uk2])
                S.add("dve", (lambda e, k2=k2, tn=tn: e.tensor_scalar(out=k2[:64, :tn], in0=k2[:64, :tn], scalar1=-MAGIC, scalar2=-2 * math.pi,
                                                                      op0=ALU.add, op1=ALU.mult)), reads=[uk2], writes=[uk2])
                S.add("dve", (lambda e, x=x, k2=k2, tn=tn: e.tensor_tensor(out=x[:64, :tn], in0=x[:64, :tn], in1=k2[:64, :tn], op=ALU.add)),
                      reads=[ux, uk2], writes=[ux])
                S.add("act", (lambda e, x=x, dst=dst, t0=t0, tn=tn: e.activation(out=dst[:, t0:t0 + tn], in_=x[:64, :tn], func=AF.Sin)),
                      reads=[ux], writes=[udst])
        for cch in range(8):
            S.add("act", (lambda e, cch=cch: e.activation(out=dec[:], in_=negt[:], func=AF.Exp, scale=B.cs("deltas")[:, cch:cch + 1])),
                  reads=[unegt, B.u_cst], writes=[udec])
            for (t0, tn) in NTL:
                psf, upsf = B.PS.get()
                psb, upsb = B.PS.get()
                for (ps_, ups_, dr) in ((psf, upsf, 0), (psb, upsb, 1)):
                    S.add("pe", (lambda e, ps_=ps_, dr=dr, cch=cch, t0=t0, tn=tn: e.matmul(
                        ps_[:, :tn], w3[:, dr * 1024 + cch * 128:dr * 1024 + (cch + 1) * 128], h2[:, t0:t0 + tn], start=True, stop=True)),
                        reads=[uw, uh2], writes=[ups_])
                a1, ua1 = tmp_p.get()
                S.add("act", (lambda e, a1=a1, psf=psf, tn=tn: e.activation(out=a1[:, :tn], in_=psf[:, :tn], func=AF.Copy)), reads=[upsf], writes=[ua1])
                o, uo = out_p.get()
                for k_, op_ in ((0, ALU.add), (1, ALU.subtract)):
                    S.add("dve", (lambda e, o=o, a1=a1, psb=psb, tn=tn, k_=k_, op_=op_: e.tensor_tensor(out=o[:, k_, :tn], in0=a1[:, :tn], in1=psb[:, :tn], op=op_)),
                          reads=[ua1, upsb, uo], writes=[uo])
                    S.add("pool", (lambda e, o=o, tn=tn, t0=t0, k_=k_: e.tensor_tensor(out=o[:, k_, :tn], in0=o[:, k_, :tn], in1=dec[:, t0:t0 + tn], op=ALU.mult)),
                          reads=[uo, udec], writes=[uo])
                B.dma(hp[:, cch * 128:(cch + 1) * 128, t0:t0 + tn].rearrange("k c t -> c k t"), o[:, :, :tn], reads=[uo], writes=[uhp])
        S.barrier()
    if B.stop and B.stop.endswith("hyf"):
        return

    with ExitStack() as e2:
        cvt_pool = TPool(nc, e2, "cvtst", [128, 1024], F32, 2)

        def cvt(name, shape, src_ap):
            t = e2.enter_context(nc.sbuf_tensor(uname(name), shape, F32R))
            ut = Unit()
            flat = 1
            for s_ in shape[1:]:
                flat *= s_
            step = 1024
            st_pool = cvt_pool
            tf = t[:].rearrange(" ".join(["p"] + ["a%d" % i for i in range(len(shape) - 1)]) + " -> p (" + " ".join("a%d" % i for i in range(len(shape) - 1)) + ")") if len(shape) > 2 else t[:]
            sf = src_ap.rearrange(" ".join(["p"] + ["a%d" % i for i in range(len(shape) - 1)]) + " -> p (" + " ".join("a%d" % i for i in range(len(shape) - 1)) + ")") if len(shape) > 2 else src_ap
            P_ = shape[0]
            for o_ in range(0, flat, step):
                w_ = min(step, flat - o_)
                st, ust = st_pool.get()
                B.dma(st[:P_, :w_], sf[:, o_:o_ + w_], writes=[ust])
                S.add("act", (lambda e, st=st, o_=o_, w_=w_: e.activation(out=tf[:, o_:o_ + w_], in_=st[:P_, :w_], func=AF.Copy)),
                      reads=[ust], writes=[ut])
            return t, ut
        F1 = e2.enter_context(nc.sbuf_tensor(uname("F1"), [NS1, W2], F32))
        uF1 = Unit()
        B.dma(F1[:], tb("F1"), writes=[uF1])
        G, uG = cvt("G", [128, NFH, 2, 128], tb("G"))
        Wv, uWv = cvt("Wv", [128, 2, 256], I["hy_Winv"])
        To, uTo = cvt("To", [NFH, 128, 2, NT1], tb("Tout"))
        xin_p = TPool(nc, e2, "xin", [NS1, C, 128], F32, 1)
        Yb = e2.enter_context(nc.sbuf_tensor(uname("Yb"), [128, NFH, 3, C], F32R))
        Kb = e2.enter_context(nc.sbuf_tensor(uname("Kb"), [128, NFH, 2, C], F32))
        XP = e2.enter_context(nc.sbuf_tensor(uname("XP"), [128, NFH, 2, C], F32R))
        Vb = e2.enter_context(nc.sbuf_tensor(uname("Vb"), [NFH, 128, 2, C], F32R))
        yb = e2.enter_context(nc.sbuf_tensor(uname("yb"), [C, n], F32))
        uYb, uKb, uXP, uVb, uyb = Unit(), Unit(), Unit(), Unit(), Unit()
        tp = TPool(nc, e2, "hytp", [128, 4, C], F32, 4)
        xz_p = TPool(nc, e2, "hyxz", [C, 2, 1024], F32, 1)
        ob_p = TPool(nc, e2, "hyob", [C, 1024], BF16, 2)
        NPB = 512 // W2
        XP2 = e2.enter_context(nc.sbuf_tensor(uname("XP2"), [128, NFH, 2, C], F32R))
        uXP2 = Unit()
        XPS = [(XP, uXP), (XP2, uXP2)]

        def fwd(g):
            XP, uXP = XPS[g % 2]
            c0 = g * C
            for sig in ("p", "m", "zz"):
                xin, uxin = xin_p.get()
                if sig == "zz":
                    src, usrc = Sx["zzT"][c0:c0 + C, base:base + n], Ux["zzT"]
                else:
                    src, usrc = hp[0 if sig == "p" else 1, c0:c0 + C, :], uhp
                B.dma(xin[:], src.rearrange("c (a b) -> a c b", b=128), reads=[usrc], writes=[uxin])
                for cb in range(0, C, NPB):
                    nb = min(NPB, C - cb)
                    ps, ups = B.PS.get()
                    for ci in range(nb):
                        S.add("pe", (lambda e, ps=ps, xin=xin, ci=ci, cb=cb: e.matmul(ps[:, ci * W2:(ci + 1) * W2], xin[:, cb + ci, :], F1[:, :],
                                                                                      start=True, stop=True)),
                              reads=[uxin, uF1], writes=[ups])
                    pv = ps[:, :nb * W2].rearrange("p (c r f) -> p c r f", c=nb, r=2)
                    for r_ in range(2):
                        S.add("act" if r_ == 0 else "dve",
                              (lambda e, pv=pv, r_=r_, cb=cb, nb=nb: (e.activation(out=Yb[:, :, r_, cb:cb + nb], in_=pv[:, :, r_, :].rearrange("p c f -> p f c"), func=AF.Copy)
                                                                       if r_ == 0 else
                                                                       e.tensor_copy(out=Yb[:, :, r_, cb:cb + nb], in_=pv[:, :, r_, :].rearrange("p c f -> p f c")))),
                              reads=[ups], writes=[uYb])
                    S.add("act", (lambda e, pv=pv, cb=cb, nb=nb: e.activation(out=Yb[:, :, 2, cb:cb + nb], in_=pv[:, :, 1, :].rearrange("p c f -> p f c"),
                                                                              func=AF.Copy, scale=-1.0)), reads=[ups], writes=[uYb])
                    yield
                for fq in range(0, NFH, 4):
                    nf = min(4, NFH - fq)
                    ps, ups = B.PS.get()
                    for fi in range(nf):
                        f1_ = fq + fi
                        if sig != "m":
                            S.add("pe", (lambda e, ps=ps, fi=fi, f1_=f1_: e.matmul(ps[:, (fi * 2) * C:(fi * 2 + 1) * C], G[:, f1_, 0, :], Yb[:, f1_, 0, :], start=True, stop=False)),
                                  reads=[uG, uYb], writes=[ups])
                            S.add("pe", (lambda e, ps=ps, fi=fi, f1_=f1_: e.matmul(ps[:, (fi * 2) * C:(fi * 2 + 1) * C], G[:, f1_, 1, :], Yb[:, f1_, 2, :], start=False, stop=True)),
                                  reads=[uG, uYb], writes=[ups])
                        if sig != "p":
                            S.add("pe", (lambda e, ps=ps, fi=fi, f1_=f1_: e.matmul(ps[:, (fi * 2 + 1) * C:(fi * 2 + 2) * C], G[:, f1_, 1, :], Yb[:, f1_, 0, :], start=True, stop=False)),
                                  reads=[uG, uYb], writes=[ups])
                            S.add("pe", (lambda e, ps=ps, fi=fi, f1_=f1_: e.matmul(ps[:, (fi * 2 + 1) * C:(fi * 2 + 2) * C], G[:, f1_, 0, :], Yb[:, f1_, 1, :], start=False, stop=True)),
                                  reads=[uG, uYb], writes=[ups])
                    pv = ps[:, :nf * 2 * C].rearrange("p (f r c) -> p f r c", f=nf, r=2)
                    if sig == "p":
                        S.add("act", (lambda e, pv=pv, fq=fq, nf=nf: e.activation(out=Kb[:, fq:fq + nf, 0, :], in_=pv[:, :, 0, :], func=AF.Copy)), reads=[ups], writes=[uKb])
                    elif sig == "m":
                        S.add("act", (lambda e, pv=pv, fq=fq, nf=nf: e.activation(out=Kb[:, fq:fq + nf, 1, :], in_=pv[:, :, 1, :], func=AF.Copy)), reads=[ups], writes=[uKb])
                    else:
                        (t1, u1), (t2, u2), (t3, u3), (t4, u4) = tp.get(), tp.get(), tp.get(), tp.get()
                        kr, ki = Kb[:, fq:fq + nf, 0, :], Kb[:, fq:fq + nf, 1, :]
                        S.add("dve", (lambda e, pv=pv, t1=t1, nf=nf, kr=kr: e.tensor_tensor(out=t1[:, :nf, :], in0=pv[:, :, 0, :], in1=kr, op=ALU.mult)), reads=[ups, uKb], writes=[u1])
                        S.add("dve", (lambda e, pv=pv, t2=t2, nf=nf, ki=ki: e.tensor_tensor(out=t2[:, :nf, :], in0=pv[:, :, 1, :], in1=ki, op=ALU.mult)), reads=[ups, uKb], writes=[u2])
                        S.add("dve", (lambda e, pv=pv, t3=t3, nf=nf, ki=ki: e.tensor_tensor(out=t3[:, :nf, :], in0=pv[:, :, 0, :], in1=ki, op=ALU.mult)), reads=[ups, uKb], writes=[u3])
                        S.add("dve", (lambda e, pv=pv, t4=t4, nf=nf, kr=kr: e.tensor_tensor(out=t4[:, :nf, :], in0=pv[:, :, 1, :], in1=kr, op=ALU.mult)), reads=[ups, uKb], writes=[u4])
                        S.add("dve", (lambda e, t1=t1, t2=t2, fq=fq, nf=nf: e.tensor_tensor(out=XP[:, fq:fq + nf, 0, :], in0=t1[:, :nf, :], in1=t2[:, :nf, :], op=ALU.subtract)),
                              reads=[u1, u2], writes=[uXP])
                        S.add("dve", (lambda e, t3=t3, t4=t4, fq=fq, nf=nf: e.tensor_tensor(out=XP[:, fq:fq + nf, 1, :], in0=t3[:, :nf, :], in1=t4[:, :nf, :], op=ALU.add)),
                              reads=[u3, u4], writes=[uXP])
                    yield

        def inv(g):
            XP, uXP = XPS[g % 2]
            c0 = g * C
            for cb in range(0, C, 2):
                ps, ups = B.PS.get()
                for ci in range(2):
                    c_ = cb + ci
                    S.add("pe", (lambda e, ps=ps, ci=ci, c_=c_: e.matmul(ps[:NFH, ci * 256:(ci + 1) * 256], XP[:, :, 0, c_], Wv[:, 0, :], start=True, stop=False)),
                          reads=[uXP, uWv], writes=[ups])
                    S.add("pe", (lambda e, ps=ps, ci=ci, c_=c_: e.matmul(ps[:NFH, ci * 256:(ci + 1) * 256], XP[:, :, 1, c_], Wv[:, 1, :], start=False, stop=True)),
                          reads=[uXP, uWv], writes=[ups])
                pv = ps[:NFH, :].rearrange("p (c r t) -> p c r t", c=2, r=2)
                S.add("act" if (cb // 2) % 2 == 0 else "dve",
                      (lambda e, pv=pv, cb=cb: (e.activation(out=Vb[:, :, :, cb:cb + 2], in_=pv.rearrange("p c r t -> p t r c"), func=AF.Copy)
                                                if (cb // 2) % 2 == 0 else e.tensor_copy(out=Vb[:, :, :, cb:cb + 2], in_=pv.rearrange("p c r t -> p t r c")))),
                      reads=[ups], writes=[uVb])
                yield
            TB = 512 // NT1 if NT1 * 16 > 512 else 16
            for tq in range(0, 128, TB):
                ps, ups = B.PS.get()
                for ti in range(TB):
                    t2_ = tq + ti
                    S.add("pe", (lambda e, ps=ps, ti=ti, t2_=t2_: e.matmul(ps[:C, ti * NT1:(ti + 1) * NT1], Vb[:, t2_, 0, :], To[:, t2_, 0, :], start=True, stop=False)),
                          reads=[uVb, uTo], writes=[ups])
                    S.add("pe", (lambda e, ps=ps, ti=ti, t2_=t2_: e.matmul(ps[:C, ti * NT1:(ti + 1) * NT1], Vb[:, t2_, 1, :], To[:, t2_, 1, :], start=False, stop=True)),
                          reads=[uVb, uTo], writes=[ups])
                S.add("act", (lambda e, ps=ps, tq=tq: e.activation(out=yb[:, :].rearrange("c (a b) -> c a b", b=128)[:, :, tq:tq + TB],
                                                                   in_=ps[:C, :TB * NT1].rearrange("c (b a) -> c a b", a=NT1), func=AF.Copy)),
                      reads=[ups], writes=[uyb])
                yield
            for q0 in range(0, n, 1024):
                qn = min(1024, n - q0)
                xz, uxz = xz_p.get()
                B.dma(xz[:, 0, :qn], Sx["x0T"][c0:c0 + C, base + q0:base + q0 + qn], reads=[Ux["x0T"]], writes=[uxz])
                B.dma(xz[:, 1, :qn], Sx["zzT"][c0:c0 + C, base + q0:base + q0 + qn], reads=[Ux["zzT"]], writes=[uxz])
                S.add("dve", (lambda e, xz=xz, q0=q0, qn=qn, c0=c0: e.scalar_tensor_tensor(out=xz[:, 1, :qn], in0=xz[:, 1, :qn], scalar=B.vs("hyb32")[:C, c0 // C:c0 // C + 1],
                                                                                      in1=yb[:, q0:q0 + qn], op0=ALU.mult, op1=ALU.add)),
                      reads=[uxz, uyb, B.u_vec], writes=[uxz])
                ob, uob = ob_p.get()
                S.add("pool", (lambda e, xz=xz, ob=ob, qn=qn: e.tensor_tensor(out=ob[:, :qn], in0=xz[:, 0, :qn], in1=xz[:, 1, :qn], op=ALU.mult)),
                      reads=[uxz], writes=[uob])
                B.dma(Sx["ohy"][c0:c0 + C, base + q0:base + q0 + qn], ob[:, :qn], reads=[uob], writes=[Ux["ohy"]])
                yield

        NG = D // C
        if B.stop and B.stop.endswith("hy1"):
            NG = 2
        for g in range(NG + 1):
            gens = []
            if g < NG:
                gens.append(fwd(g))
            if g >= 1:
                gens.append(inv(g - 1))
            while gens:
                nxt = []
                for gen in gens:
                    try:
                        next(gen)
                        nxt.append(gen)
                    except StopIteration:
                        pass
                gens = nxt
        S.barrier()


def load_wres(B, es, name, src, kparts, ncols, st_pool, eng="pool"):
    nc, S = B.nc, B.S
    w = es.enter_context(nc.sbuf_tensor(uname(name), [128, kparts, ncols], BF16))
    uw = Unit()
    srcv = src.rearrange("(k p) n -> p k n", p=128)
    for k0 in range(0, kparts, 4):
        kn = min(4, kparts - k0)
        for c0 in range(0, ncols, 512):
            st, ust = st_pool.get()
            B.dma(st[:, :kn, :], srcv[:, k0:k0 + kn, c0:c0 + 512], writes=[ust])
            S.add(eng, (lambda e, st=st, k0=k0, kn=kn, c0=c0: e.tensor_copy(out=w[:, k0:k0 + kn, c0:c0 + 512], in_=st[:, :kn, :])),
                  reads=[ust], writes=[uw])
    return w, uw


def merge_phase(B, l, es, xin, u_xin, tiles):
    nc, S, Sx, Ux, I = B.nc, B.S, B.Sx, B.Ux, B.I
    st_pool = TPool(nc, es, "mst", [128, 4, 512], F32, 2)
    Ws = []
    for nm in ("w_proj_dn", "w_proj_hy", "w_proj_lru", "w_out"):
        Ws.append(load_wres(B, es, nm + "_sb", I[nm][l], 8, D, st_pool))
    ot_p = TPool(nc, es, "mot", [128, 3, 8, 512], BF16, 1)
    gt_p = TPool(nc, es, "mgt", [128, 24, 512], BF16, 1)
    x_p = TPool(nc, es, "mx", [128, 8, 512], F32, 2)
    mg_p = TPool(nc, es, "mmg", [128, 8, 512], BF16, 1)
    t_p = TPool(nc, es, "mt", [128, 512], F32, 4)
    for (u0, n) in tiles:
        seg = 1 if u0 == 0 else 0
        ot, uot = ot_p.get()
        for bi, nm in enumerate(("odn", "ohy", "olru")):
            B.dma(ot[:, bi, :, :n], Sx[nm].rearrange("(k p) u -> p k u", p=128)[:, :, u0:u0 + n], reads=[Ux[nm]], writes=[uot])
        gt, ugt = gt_p.get()
        B.dma(gt[:, :, :n], Sx["gT"].rearrange("(k p) u -> p k u", p=128)[:, :, u0:u0 + n], reads=[Ux["gT"]], writes=[ugt])
        x, ux = x_p.get()
        B.dma(x[:, :, :n], xin.rearrange("(k p) u -> p k u", p=128)[:, :, u0:u0 + n], reads=[u_xin], writes=[ux])
        mg, umg = mg_p.get()
        for m in range(8):
            pss = []
            for bi in range(3):
                ps, ups = B.PS.get()
                w, uw = Ws[bi]
                for k in range(8):
                    S.add("pe", (lambda e, ps=ps, w=w, k=k, m=m, bi=bi, ot=ot, n=n: e.matmul(ps[:, :n], w[:, k, m * 128:(m + 1) * 128], ot[:, bi, k, :n],
                                                                                           start=(k == 0), stop=(k == 7))),
                          reads=[uw, uot], writes=[ups])
                pss.append((ps, ups))
            ta, uta = t_p.get()
            tb_, utb = t_p.get()
            S.add("dve", (lambda e, ta=ta, ps=pss[0][0], gt=gt, m=m, n=n: e.tensor_tensor(out=ta[:, :n], in0=ps[:, :n], in1=gt[:, m, :n], op=ALU.mult)),
                  reads=[pss[0][1], ugt], writes=[uta])
            S.add("dve", (lambda e, tb_=tb_, ps=pss[1][0], gt=gt, m=m, n=n: e.tensor_tensor(out=tb_[:, :n], in0=ps[:, :n], in1=gt[:, 8 + m, :n], op=ALU.mult)),
                  reads=[pss[1][1], ugt], writes=[utb])
            S.add("pool", (lambda e, ta=ta, tb_=tb_, n=n: e.tensor_tensor(out=ta[:, :n], in0=ta[:, :n], in1=tb_[:, :n], op=ALU.add)),
                  reads=[uta, utb], writes=[uta])
            tc_, utc = t_p.get()
            S.add("dve", (lambda e, tc_=tc_, ps=pss[2][0], gt=gt, m=m, n=n: e.tensor_tensor(out=tc_[:, :n], in0=ps[:, :n], in1=gt[:, 16 + m, :n], op=ALU.mult)),
                  reads=[pss[2][1], ugt], writes=[utc])
            S.add("pool", (lambda e, ta=ta, tc_=tc_, mg=mg, m=m, n=n: e.tensor_tensor(out=mg[:, m, :n], in0=ta[:, :n], in1=tc_[:, :n], op=ALU.add)),
                  reads=[uta, utc, umg], writes=[umg])
        w, uw = Ws[3]
        for nn in range(8):
            ps, ups = B.PS.get()
            for m in range(8):
                S.add("pe", (lambda e, ps=ps, m=m, nn=nn, mg=mg, n=n: e.matmul(ps[:, :n], w[:, m, nn * 128:(nn + 1) * 128], mg[:, m, :n],
                                                                              start=(m == 0), stop=(m == 7))),
                      reads=[uw, umg], writes=[ups])
            S.add("dve", (lambda e, ps=ps, x=x, nn=nn, n=n, seg=seg: e.scalar_tensor_tensor(out=x[:, nn, :n], in0=ps[:, :n], scalar=B.modcol(2, seg, nn),
                                                                                       in1=x[:, nn, :n], op0=ALU.mult, op1=ALU.add)),
                  reads=[ups, ux, B.u_modv], writes=[ux])
        B.dma(Sx["xA"].rearrange("(k p) u -> p k u", p=128)[:, :, u0:u0 + n], x[:, :, :n], reads=[ux], writes=[Ux["xA"]])


def ffn_phase(B, l, es, tiles):
    nc, S, Sx, Ux, I = B.nc, B.S, B.Sx, B.Ux, B.I
    with_ctx = tiles[0][0] == 0
    esu = ExitStack()
    hT = esu.enter_context(nc.sbuf_tensor(uname("hT2"), [128, 8, U], BF16))
    u_hT = [Unit() for _ in TT]
    with ExitStack() as es3:
        B.norm_to_hT(Sx["xA"], Ux["xA"], 1, hT, u_hT, tiles, es3, tile_ids=[TT.index(t) for t in tiles])
        S.barrier()
    with ExitStack() as es4:
        wst_pool = TPool(nc, es4, "fwst", [128, 8, 128], F32, 2)
        wbf_pool = TPool(nc, es4, "fwbf", [128, 8, 128], BF16, 2)
        up_p = TPool(nc, es4, "fup", [128, 66, 66], F32, 2)
        cp_p = TPool(nc, es4, "fcp", [128, 260], F32, 2)
        acc_p = TPool(nc, es4, "facc", [128, U], F32, 2)
        ab_p = TPool(nc, es4, "fab", [128, U], BF16, 2)
        for (t, ut) in up_p.t:
            S.add("pool", (lambda e, t=t: e.memset(t[:], 0.0)), writes=[ut])
        for (t, ut) in cp_p.t:
            S.add("pool", (lambda e, t=t: e.memset(t[:], 0.0)), writes=[ut])
        for j in range(FH // 128):
            accs = []
            for part in range(2):
                cidx = part * (FH // 128) + j
                wb, uwb = B.load_w_bf16(I["ffn_up"][l][:, cidx * 128:(cidx + 1) * 128], 128, wst_pool, wbf_pool)
                up, uup = up_p.get()
                cp, ucp = cp_p.get()
                for (u0, n) in tiles:
                    ti = TT.index((u0, n))
                    ps, ups = B.mm_tile(wb, uwb, 128, hT, u_hT, ti)
                    if u0 == 0:
                        S.add("act", (lambda e, ps=ps, cp=cp, n=n: e.activation(out=cp[:, 1:1 + n], in_=ps[:, :n], func=AF.Copy)), reads=[ups], writes=[ucp])
                    else:
                        r0 = (u0 - LT0) // 64
                        S.add("act", (lambda e, ps=ps, up=up, r0=r0: e.activation(out=up[:, 1 + r0:9 + r0, 1:65], in_=ps[:, :512].rearrange("p (r c) -> p r c", c=64),
                                                                                  func=AF.Copy)), reads=[ups], writes=[uup])
                acc, uacc = acc_p.get()
                cw = lambda tap, cidx=cidx: B.vs("ffncw", cidx * 9 + tap)
                av = acc[:, LT0:U].rearrange("p (r c) -> p r c", c=64)
                S.add("act", (lambda e, up=up, av=av, cw=cw: e.activation(out=av, in_=up[:, 0:64, 0:64], func=AF.Identity, scale=cw(0))),
                      reads=[uup, B.u_vec], writes=[uacc])
                for tap in range(1, 9):
                    di, dj = tap // 3, tap % 3
                    S.add("dve", (lambda e, up=up, av=av, cw=cw, tap=tap, di=di, dj=dj: e.scalar_tensor_tensor(
                        out=av, in0=up[:, di:di + 64, dj:dj + 64], scalar=cw(tap), in1=av, op0=ALU.mult, op1=ALU.add)),
                        reads=[uup, uacc, B.u_vec], writes=[uacc])
                if with_ctx:
                    S.add("act", (lambda e, cp=cp, acc=acc, cw=cw: e.activation(out=acc[:, 0:256], in_=cp[:, 0:256], func=AF.Identity, scale=cw(3))),
                          reads=[ucp, B.u_vec, uacc], writes=[uacc])
                    for tap in (4, 5):
                        S.add("dve", (lambda e, cp=cp, acc=acc, cw=cw, tap=tap: e.scalar_tensor_tensor(
                            out=acc[:, 0:256], in0=cp[:, tap - 3:tap - 3 + 256], scalar=cw(tap), in1=acc[:, 0:256], op0=ALU.mult, op1=ALU.add)),
                            reads=[ucp, uacc, B.u_vec], writes=[uacc])
                accs.append((acc, uacc))
            (ag, uag), (av_, uav) = accs
            lo = 0 if with_ctx else LT0
            S.add("act", (lambda e, ag=ag, lo=lo: e.activation(out=ag[:, lo:U], in_=ag[:, lo:U], func=AF.Silu)), reads=[uag], writes=[uag])
            ab, uab = ab_p.get()
            S.add("pool", (lambda e, ab=ab: e.memset(ab[:, 0:LT0], 0.0)), writes=[uab])
            S.add("dve", (lambda e, ag=ag, av_=av_, ab=ab, lo=lo: e.tensor_tensor(out=ab[:, lo:U], in0=ag[:, lo:U], in1=av_[:, lo:U], op=ALU.mult)),
                  reads=[uag, uav, uab], writes=[uab])
            B.dma(Sx["actT"][j * 128:(j + 1) * 128], ab[:], reads=[uab], writes=[Ux["actT"]])
        S.barrier()
    esu.close()
    st_pool = TPool(nc, es, "dst", [128, 4, 512], F32, 2)
    wd, uwd = load_wres(B, es, "wdown", I["ffn_down"][l], FH // 128, D, st_pool)
    at_p = TPool(nc, es, "dat", [128, FH // 128, 512], BF16, 2)
    x_p = TPool(nc, es, "dx", [128, 8, 512], F32, 2)
    for (u0, n) in tiles:
        seg = 1 if u0 == 0 else 0
        at, uat = at_p.get()
        B.dma(at[:, :, :n], Sx["actT"].rearrange("(k p) u -> p k u", p=128)[:, :, u0:u0 + n], reads=[Ux["actT"]], writes=[uat])
        x, ux = x_p.get()
        B.dma(x[:, :, :n], Sx["xA"].rearrange("(k p) u -> p k u", p=128)[:, :, u0:u0 + n], reads=[Ux["xA"]], writes=[ux])
        for nn in range(8):
            ps, ups = B.PS.get()
            for j in range(FH // 128):
                S.add("pe", (lambda e, ps=ps, j=j, nn=nn, at=at, n=n: e.matmul(ps[:, :n], wd[:, j, nn * 128:(nn + 1) * 128], at[:, j, :n],
                                                                              start=(j == 0), stop=(j == FH // 128 - 1))),
                      reads=[uwd, uat], writes=[ups])
            S.add("dve", (lambda e, ps=ps, x=x, nn=nn, n=n, seg=seg: e.scalar_tensor_tensor(out=x[:, nn, :n], in0=ps[:, :n], scalar=B.modcol(5, seg, nn),
                                                                                       in1=x[:, nn, :n], op0=ALU.mult, op1=ALU.add)),
                  reads=[ups, ux, B.u_modv], writes=[ux])
        B.dma(Sx["xB"].rearrange("(k p) u -> p k u", p=128)[:, :, u0:u0 + n], x[:, :, :n], reads=[ux], writes=[Ux["xB"]])


def final_phase(B, es):
    nc, S, Sx, Ux = B.nc, B.S, B.Sx, B.Ux
    xp = TPool(nc, es, "fx", [128, 8, 512], F32, 2)
    sqp = TPool(nc, es, "fsq", [128, 8, 512], F32R, 1)
    rsp = TPool(nc, es, "frs", [128, 512], F32, 2)
    for (u0, n) in TT[1:]:
        x, ux = xp.get()
        B.dma(x[:], Sx["xB"].rearrange("(k p) u -> p k u", p=128)[:, :, u0:u0 + n], reads=[Ux["xB"]], writes=[ux])
        sq, usq = sqp.get()
        S.add("act", (lambda e, x=x, sq=sq: e.activation(out=sq[:], in_=x[:], func=AF.Square)), reads=[ux], writes=[usq])
        ps, ups = B.PS.get()
        for k in range(8):
            S.add("pe", (lambda e, k=k, sq=sq, ps=ps: e.matmul(ps[:], B.ones[:], sq[:, k, :], start=(k == 0), stop=(k == 7))),
                  reads=[usq, B.u_ones], writes=[ups])
        rs, urs = rsp.get()
        S.add("act", (lambda e, rs=rs, ps=ps: e.activation(out=rs[:], in_=ps[:], func=AF.Sqrt, scale=1.0 / D, bias=B.eps6[:])), reads=[ups], writes=[urs])
        S.add("dve", (lambda e, rs=rs: e.reciprocal(out=rs[:], in_=rs[:])), reads=[urs], writes=[urs])
        S.add("dve", (lambda e, x=x, rs=rs: e.tensor_tensor(out=x[:], in0=x[:], in1=rs[:].unsqueeze(1).to_broadcast([128, 8, 512]), op=ALU.mult)),
              reads=[urs, ux], writes=[ux])
        S.add("pool", (lambda e, x=x: e.tensor_tensor(out=x[:], in0=x[:], in1=B.vs("fng").unsqueeze(2).to_broadcast([128, 8, 512]), op=ALU.mult)),
              reads=[ux, B.u_vec], writes=[ux])
        t0 = u0 - LT0
        B.dma(B.outT.rearrange("(k p) t -> p k t", p=128)[:, :, t0:t0 + n], x[:], reads=[ux])


_CACHE = {}


def kernel(**inputs):
    inp = {k: np.asarray(v) for k, v in inputs.items()}
    if "nc" not in _CACHE:
        b = Builder(debug=False)
        _CACHE["nc"] = b.build()
    nc = _CACHE["nc"]
    bsz = inp["x"].shape[0]
    in_maps = [prep_inputs(inp, b_) for b_ in range(bsz)]
    res = run_bass_kernel_spmd(nc, in_maps, core_ids=list(range(bsz)))
    out = np.stack([np.ascontiguousarray(np.asarray(r["outT"]).T) for r in res.results])
    return out.astype(np.float32)
```

```python
import math
from contextlib import ExitStack

import numpy as np
import ml_dtypes
import concourse.bass as bass
import concourse.mybir as mybir
from concourse.bass_utils import run_bass_kernel_spmd

F32 = mybir.dt.float32
F32R = mybir.dt.float32r
BF16 = mybir.dt.bfloat16
ALU = mybir.AluOpType
AF = mybir.ActivationFunctionType
AX = mybir.AxisListType

D = 1024
L = 4096
LC = 256
U = 4355
LT0 = 259
UP = 4360
DEPTH = 2
NIN = 12320
OFF_QKV, OFF_Z, OFF_AB, OFF_HY, OFF_LX, OFF_LY, OFF_GATE = 0, 3072, 4096, 4128, 7200, 8224, 9248
FH = 2816
TT = [(0, 256)] + [(LT0 + 512 * i, 512) for i in range(8)]
NCORES = 4


class Unit:
    __slots__ = ("w", "r", "excl", "wd")

    def __init__(self, excl=False):
        self.w = None
        self.r = []
        self.wd = []
        self.excl = excl


class Op:
    __slots__ = ("eng", "fn", "deps", "dma", "need_inc", "sem", "val")

    def __init__(self, eng, fn, dma):
        self.eng = eng
        self.fn = fn
        self.dma = dma
        self.deps = []
        self.need_inc = False
        self.sem = None
        self.val = 0


class Sched:
    EPOCH = 30000
    NDMA = 24

    def __init__(self, nc, es):
        self.nc = nc
        self.es = es
        self.ops = []
        self.engs = {"pe": nc.tensor, "act": nc.scalar, "dve": nc.vector,
                     "pool": nc.gpsimd, "sp": nc.sync}
        self.last = {k: None for k in self.engs}
        self.dmas_since_barrier = []
        self.bar_deps = {k: [] for k in self.engs}
        self.nsem = 0

    def add(self, eng, fn, reads=(), writes=(), dma=False):
        op = Op(eng, fn, dma)
        deps = []
        for u in reads:
            if u.w is not None:
                deps.append(u.w)
            deps.extend(u.wd)
            if u.excl:
                deps.extend(o for o in u.r if o.eng != eng)
        for u in writes:
            if u.w is not None:
                deps.append(u.w)
            deps.extend(u.wd)
            deps.extend(u.r)
        if self.bar_deps[eng]:
            deps.extend(self.bar_deps[eng])
            self.bar_deps[eng] = []
        seen = set()
        for d in deps:
            if d is op or id(d) in seen:
                continue
            if d.eng == "pe" and eng == "pe" and not d.dma and not dma:
                continue
            seen.add(id(d))
            d.need_inc = True
            op.deps.append(d)
        for u in reads:
            if not dma:
                u.r = [o for o in u.r if o.dma or o.eng != eng]
            u.r.append(op)
        for u in writes:
            u.w = op
            u.r = []
            if dma:
                u.wd.append(op)
                if len(u.wd) > 48:
                    u.wd = u.wd[-48:]
            else:
                u.wd = []
        if dma:
            op.need_inc = True
            self.dmas_since_barrier.append(op)
        self.ops.append(op)
        self.last[eng] = op
        return op

    def barrier(self):
        deps = [o for o in self.last.values() if o is not None] + self.dmas_since_barrier
        self.dmas_since_barrier = []
        for k in self.engs:
            self.bar_deps[k] = list(deps)

    def _newsem(self):
        self.nsem += 1
        return self.es.enter_context(self.nc.semaphore("s%d" % self.nsem))

    def emit(self):
        self.barrier()
        self.add("sp", lambda e: None)
        esem, ecount = {}, {}
        dsem = [self._newsem() for _ in range(self.NDMA)]
        dcount = [0] * self.NDMA
        nd = 0
        seen = {k: {} for k in self.engs}
        for op in self.ops:
            e = self.engs[op.eng]
            waits = []
            if op.dma:
                j = nd % self.NDMA
                nd += 1
                if dcount[j]:
                    waits.append((dsem[j], dcount[j]))
                if dcount[j] >= 30000:
                    dsem[j] = self._newsem()
                    dcount[j] = 0
                dcount[j] += 16
                op.sem, op.val = dsem[j], dcount[j]
            elif op.need_inc:
                if op.eng not in esem or ecount[op.eng] >= self.EPOCH:
                    esem[op.eng] = self._newsem()
                    ecount[op.eng] = 0
                ecount[op.eng] += 1
                op.sem, op.val = esem[op.eng], ecount[op.eng]
            for d in op.deps:
                waits.append((d.sem, d.val))
            sn = seen[op.eng]
            for (s, v) in waits:
                if sn.get(id(s), 0) >= v:
                    continue
                sn[id(s)] = v
                e.wait_ge(s, v)
            ins = op.fn(e)
            if op.sem is not None and ins is not None:
                ins.then_inc(op.sem, 16 if op.dma else 1)
        return len(self.ops)


_UID = [0]


def uname(name):
    _UID[0] += 1
    return "%s_%d" % (name, _UID[0])


class TPool:
    def __init__(self, nc, es, name, shape, dtype, n, psum=False):
        self.t = []
        for i in range(n):
            mk = nc.psum_tensor if psum else nc.sbuf_tensor
            self.t.append((es.enter_context(mk(uname(name), shape, dtype)), Unit(excl=psum)))
        self.i = 0

    def get(self):
        r = self.t[self.i % len(self.t)]
        self.i += 1
        return r


def _pk(v):
    return np.ascontiguousarray(v.reshape(-1, 128).T)


VEC_FIELDS = [("g1", 8), ("g2", 8), ("bmod", 48), ("dncw", 96), ("hycw", 72), ("hycb", 24),
              ("lrucw", 32), ("lrucb", 8), ("lba", 16), ("lbx", 16), ("llam", 16), ("hybias", 8),
              ("ffncw", 396), ("fng", 8), ("dnng", 1), ("alog", 1), ("dtb", 1),
              ("hyb1", 1), ("hyf1", 1), ("hyb2", 1), ("hyf2", 1), ("hyb32", 32)]
VOFF = {}
_o = 0
for _n, _w in VEC_FIELDS:
    VOFF[_n] = (_o, _w)
    _o += _w
NV = _o


def build_vec(inp, l):
    v = np.zeros((128, NV), np.float32)

    def put(name, arr):
        o, w = VOFF[name]
        v[:arr.shape[0], o:o + w] = arr.reshape(arr.shape[0], w)
    put("g1", _pk(inp["norm1_g"][l]))
    put("g2", _pk(inp["norm2_g"][l]))
    put("bmod", _pk(inp["b_mod"][l]))
    put("dncw", inp["dn_conv_w"][l].reshape(4, 24, 128).transpose(2, 1, 0))
    put("hycw", inp["hy_conv_w"][l].reshape(3, 24, 128).transpose(2, 1, 0))
    put("hycb", _pk(inp["hy_conv_b"][l]))
    put("lrucw", inp["lru_conv_w"][l].reshape(4, 8, 128).transpose(2, 1, 0))
    put("lrucb", _pk(inp["lru_conv_b"][l]))
    put("lba", inp["lru_b_a"][l].reshape(2, 8, 128).transpose(2, 0, 1))
    put("lbx", inp["lru_b_x"][l].reshape(2, 8, 128).transpose(2, 0, 1))
    put("llam", inp["lru_lambda"][l].reshape(2, 8, 128).transpose(2, 0, 1))
    put("hybias", _pk(inp["hy_bias"][l]))
    put("ffncw", inp["ffn_conv_w"][l].reshape(9, 44, 128).transpose(2, 1, 0))
    put("fng", _pk(inp["final_norm_g"]))
    put("dnng", inp["dn_norm_g"][l].reshape(128, 1))
    al = np.zeros((40, 1), np.float32)
    db = np.zeros((40, 1), np.float32)
    for d in range(2):
        al[d * 32:d * 32 + 8, 0] = inp["dn_a_log"][l][d]
        db[d * 32:d * 32 + 8, 0] = inp["dn_dt_bias"][l][d]
    put("alog", al)
    put("dtb", db)
    put("hyb1", inp["hy_b1"][l].reshape(64, 1))
    put("hyf1", inp["hy_f1"][l].reshape(64, 1))
    put("hyb2", inp["hy_b2"][l].reshape(64, 1))
    put("hyf2", inp["hy_f2"][l].reshape(64, 1))
    put("hyb32", np.ascontiguousarray(inp["hy_bias"][l].reshape(32, 32).T))
    return v


CST_FIELDS = [("ident", 128), ("lowi", 128), ("lows", 128), ("uppi", 128), ("upps", 128),
              ("deltas", 8)]
COFF = {}
_o = 0
for _n, _w in CST_FIELDS:
    COFF[_n] = (_o, _w)
    _o += _w
NCST = _o


def build_cst():
    c = np.zeros((128, NCST), np.float32)
    i = np.arange(128)[:, None]
    j = np.arange(128)[None, :]
    same = (i // 64) == (j // 64)

    def put(name, arr):
        o, w = COFF[name]
        c[:, o:o + w] = arr
    put("ident", (i == j).astype(np.float32))
    put("lowi", ((i >= j) & same).astype(np.float32))
    put("lows", ((i > j) & same).astype(np.float32))
    put("uppi", ((i <= j) & same).astype(np.float32))
    put("upps", ((i < j) & same).astype(np.float32))
    lt = math.log(1e-2)
    deltas = np.abs(np.linspace(lt / 1.5, lt / 0.3, 1024, dtype=np.float32))
    put("deltas", _pk(deltas))
    return c


def build_rmask():
    m = np.ones((2, U), np.float32)
    starts = list(range(0, 256, 64)) + list(range(LT0, U, 64))
    for s in starts:
        m[0, s] = 0.0
        m[1, s + 63] = 0.0
    out = np.zeros((40, U), np.float32)
    out[0:8] = m[0]
    out[32:40] = m[1]
    return out


class Builder:
    def __init__(self, debug=False, stop=None, only=None, feed=()):
        self.debug = debug
        self.stop = stop
        self.only = only
        self.feed = set(feed)
        self.nc = bass.Bass("TRN2", target_bir_lowering=False)
        self.dbg_names = []

    def din(self, name, shape, dt=F32):
        return self.nc.dram_tensor(name, list(shape), dt, kind="ExternalInput").ap()

    def dscr(self, name, shape, dt=F32):
        kind = "ExternalOutput" if self.debug else "Internal"
        if name in self.feed:
            kind = "ExternalInput"
        elif self.debug:
            self.dbg_names.append(name)
        return self.nc.dram_tensor(name, list(shape), dt, kind=kind).ap()

    def vs(self, name, k=None):
        o, w = VOFF[name]
        if k is None:
            return self.vec[:, o:o + w]
        return self.vec[:, o + k:o + k + 1]

    def cs(self, name):
        o, w = COFF[name]
        return self.cst[:, o:o + w]

    def dma(self, out, in_, reads=(), writes=(), eng="sp"):
        return self.S.add(eng, lambda e: e.dma_start(out=out, in_=in_), reads=reads, writes=writes, dma=True)

    def build(self):
        nc = self.nc
        I = {}
        I["xT0"] = self.din("xT0", [D, U])
        I["cc"] = self.din("cc", [128, 16])
        I["vec"] = self.din("vec", [DEPTH, 128, NV])
        I["cst"] = self.din("cst", [128, NCST])
        I["rmask"] = self.din("rmask", [40, U])
        I["hy_w1"] = self.din("hy_w1", [DEPTH, 33, 64])
        I["hy_w2"] = self.din("hy_w2", [DEPTH, 64, 64])
        I["hy_w3"] = self.din("hy_w3", [DEPTH, 64, 2048])
        if self.only == "mg":
            I["w_mod"] = self.din("w_mod", [DEPTH, D, 6 * D])
            for n in ("w_proj_dn", "w_proj_hy", "w_proj_lru", "w_out"):
                I[n] = self.din(n, [DEPTH, D, D])
            I["ffn_up"] = self.din("ffn_up", [DEPTH, D, 2 * FH])
            I["ffn_down"] = self.din("ffn_down", [DEPTH, FH, D])
        if self.only:
            self.I = I
            return self.build2()
        I["w_mod"] = self.din("w_mod", [DEPTH, D, 6 * D])
        I["w_in"] = self.din("w_in", [DEPTH, D, NIN])
        I["lru_w_a"] = self.din("lru_w_a", [DEPTH, 2, 8, 128, 128])
        I["lru_w_x"] = self.din("lru_w_x", [DEPTH, 2, 8, 128, 128])
        for n in ("w_proj_dn", "w_proj_hy", "w_proj_lru", "w_out"):
            I[n] = self.din(n, [DEPTH, D, D])
        I["ffn_up"] = self.din("ffn_up", [DEPTH, D, 2 * FH])
        I["ffn_down"] = self.din("ffn_down", [DEPTH, FH, D])
        self.I = I
        return self.build2()

    def build2(self):
        nc, I = self.nc, self.I
        self.outT = nc.dram_tensor("outT", [D, L], F32, kind="ExternalOutput").ap()
        Sx = {}
        Sx["qT"] = self.dscr("qT", [8, 128, U])
        Sx["kT"] = self.dscr("kT", [8, 128, U])
        Sx["vT"] = self.dscr("vT", [8, 128, U])
        Sx["szT"] = self.dscr("szT", [D, U])
        Sx["gab"] = self.dscr("gab", [3, 40, U])
        Sx["zzT"] = self.dscr("zzT", [D, U])
        Sx["x0T"] = self.dscr("x0T", [D, U])
        Sx["olru"] = self.dscr("olru", [D, U], BF16)
        Sx["odn"] = self.dscr("odn", [D, U], BF16)
        Sx["ohy"] = self.dscr("ohy", [D, U], BF16)
        Sx["gT"] = self.dscr("gT", [3 * D, U], BF16)
        Sx["xA"] = self.dscr("xA", [D, U])
        Sx["xB"] = self.dscr("xB", [D, U])
        Sx["actT"] = self.dscr("actT", [FH, U], BF16)
        Sx["hp_lat"] = self.dscr("hp_lat", [2, D, L])
        Sx["hp_ctx"] = self.dscr("hp_ctx", [2, D, LC])
        for seg, n_ in (("lat", L), ("ctx", LC)):
            N_, NS1_, NF1_, NFH_ = hy_dims(n_)
            I["hy_zT_" + seg] = self.din("hy_zT_" + seg, [33, n_])
            I["hy_negt_" + seg] = self.din("hy_negt_" + seg, [128, n_])
            I["hy_F1_" + seg] = self.din("hy_F1_" + seg, [NS1_, 2 * NFH_])
            I["hy_G_" + seg] = self.din("hy_G_" + seg, [128, NFH_, 2, 128])
            I["hy_Tout_" + seg] = self.din("hy_Tout_" + seg, [NFH_, 128, 2, NS1_])
        I["hy_Winv"] = self.din("hy_Winv", [128, 2, 256])
        self.Sx = Sx
        self.Ux = {k: Unit() for k in Sx}

        with ExitStack() as es:
            self.es = es
            self.S = Sched(nc, es)
            S = self.S
            self.cst = es.enter_context(nc.sbuf_tensor("cst_sb", [128, NCST], F32))
            self.vec = es.enter_context(nc.sbuf_tensor("vec_sb", [128, NV], F32))
            self.ones = es.enter_context(nc.sbuf_tensor("ones", [128, 128], F32R))
            self.sc = es.enter_context(nc.sbuf_tensor("sc", [128, 16], F32))
            self.modv = es.enter_context(nc.sbuf_tensor("modv", [128, 48, 2], F32))
            self.der = es.enter_context(nc.sbuf_tensor("der", [128, 2, 2, 8], F32))
            self.u_cst, self.u_vec, self.u_ones, self.u_sc = Unit(), Unit(), Unit(), Unit()
            self.u_modv, self.u_der = Unit(), Unit()
            self.PS = TPool(nc, es, "ps", [128, 512], F32, 8, psum=True)
            self.dma(self.cst[:], I["cst"], writes=[self.u_cst])
            self.eps6 = es.enter_context(nc.sbuf_tensor("eps6", [128, 1], F32))
            S.add("dve", lambda e: e.memset(self.eps6[:], 1e-6), writes=[self.u_cst])
            self.one1 = es.enter_context(nc.sbuf_tensor("one1", [128, 1], F32))
            S.add("dve", lambda e: e.memset(self.one1[:], 1.0), writes=[self.u_cst])
            ones32 = es.enter_context(nc.sbuf_tensor("ones32", [128, 128], F32))
            u_o32 = Unit()
            S.add("dve", lambda e: e.memset(ones32[:], 1.0), writes=[u_o32])
            S.add("act", lambda e: e.activation(out=self.ones[:], in_=ones32[:], func=AF.Copy), reads=[u_o32], writes=[self.u_ones])
            self.ones_bf = es.enter_context(nc.sbuf_tensor("ones_bf", [128, 128], BF16))
            S.add("act", lambda e: e.activation(out=self.ones_bf[:], in_=ones32[:], func=AF.Copy), reads=[u_o32], writes=[self.u_ones])
            self.dma(self.sc[:], I["cc"], writes=[self.u_sc])
            S.add("act", lambda e: e.activation(out=self.sc[:], in_=self.sc[:], func=AF.Silu),
                  reads=[self.u_sc], writes=[self.u_sc])
            xin = I["xT0"]
            u_xin = Unit()
            for l in range(DEPTH):
                self.l = l
                self.dma(self.vec[:], I["vec"][l], writes=[self.u_vec])
                if self.only == "dn":
                    with ExitStack() as es2:
                        dn_phase(self, l, es2)
                        S.barrier()
                    break
                if self.only == "mg":
                    self.phase_mod(l)
                    with ExitStack() as es2:
                        merge_phase(self, l, es2, xin, u_xin, TT)
                        S.barrier()
                    with ExitStack() as es2:
                        ffn_phase(self, l, es2, TT)
                        S.barrier()
                    with ExitStack() as es2:
                        final_phase(self, es2)
                        S.barrier()
                    break
                if self.only == "hy":
                    for seg in (("lat", "ctx") if "ctx" in self.stop else ("lat",)):
                        with ExitStack() as es2:
                            hyena_phase(self, l, es2, seg)
                            S.barrier()
                    break
                self.phase_mod(l)
                if self.stop == "mod":
                    break
                with ExitStack() as es2:
                    self.phase_mixer_pre(l, es2, xin, u_xin)
                    S.barrier()
                if self.stop and self.stop.startswith("pre"):
                    break
                with ExitStack() as es2:
                    dn_phase(self, l, es2)
                    S.barrier()
                if self.stop and self.stop.startswith("dn"):
                    break
                for seg in (("lat", "ctx") if l < DEPTH - 1 else ("lat",)):
                    with ExitStack() as es2:
                        hyena_phase(self, l, es2, seg)
                        S.barrier()
                if self.stop and self.stop.startswith("hy"):
                    break
                tiles = TT if l < DEPTH - 1 else TT[1:]
                with ExitStack() as es2:
                    merge_phase(self, l, es2, xin, u_xin, tiles)
                    S.barrier()
                if self.stop and self.stop.startswith("mg"):
                    break
                with ExitStack() as es2:
                    ffn_phase(self, l, es2, tiles)
                    S.barrier()
                xin, u_xin = Sx["xB"], self.Ux["xB"]
                if self.stop and self.stop.startswith("ffn"):
                    break
                if l == DEPTH - 1:
                    with ExitStack() as es2:
                        final_phase(self, es2)
                        S.barrier()
            n = S.emit()
        self.n_ops = n
        return nc

    def phase_mod(self, l):
        nc, S = self.nc, self.S
        with ExitStack() as es:
            wm_pool = TPool(nc, es, "wm", [128, 8, 512], F32, 2)
            ps, ups = self.PS.get()
            for pn in range(12):
                wm, uwm = wm_pool.get()
                self.dma(wm[:], self.I["w_mod"][l][:, pn * 512:(pn + 1) * 512].rearrange("(k p) n -> p k n", p=128),
                         writes=[uwm])
                for cc in range(4):
                    c = pn * 4 + cc
                    for k in range(8):
                        S.add("pe", (lambda e, c=c, cc=cc, k=k, wm=wm: e.matmul(
                            ps[:, 2 * c:2 * c + 2], wm[:, k, cc * 128:(cc + 1) * 128], self.sc[:, 2 * k:2 * k + 2],
                            start=(k == 0), stop=(k == 7))), reads=[uwm, self.u_sc], writes=[ups])
            bm = self.vs("bmod")
            for s in range(2):
                S.add("dve", (lambda e, s=s: e.tensor_tensor(out=self.modv[:, :, s], in0=ps[:, s:96:2], in1=bm, op=ALU.add)),
                      reads=[ups, self.u_vec], writes=[self.u_modv])
            for w, (gname, j) in enumerate((("g1", 1), ("g2", 4))):
                for s in range(2):
                    S.add("dve", (lambda e, w=w, s=s, j=j, gname=gname: e.scalar_tensor_tensor(
                        out=self.der[:, w, s, :], in0=self.modv[:, j * 8:(j + 1) * 8, s], scalar=1.0, in1=self.vs(gname),
                        op0=ALU.add, op1=ALU.mult)), reads=[self.u_modv, self.u_vec], writes=[self.u_der])
            S.barrier()

    def modcol(self, j, s, k):
        return self.modv[:, j * 8 + k, s:s + 1]

    def norm_to_hT(self, xsrc, u_xsrc, which, hT, u_hT, tiles, es, tile_ids=None):
        nc, S = self.nc, self.S
        xp = TPool(nc, es, "nx", [128, 8, 512], F32, 2)
        sqp = TPool(nc, es, "nsq", [128, 8, 512], F32R, 1)
        rsp = TPool(nc, es, "nrs", [128, 512], F32, 2)
        shj = 0 if which == 0 else 3
        for ti_, (u0, n) in enumerate(tiles):
            ti = tile_ids[ti_] if tile_ids is not None else ti_
            seg = 1 if u0 == 0 else 0
            x, ux = xp.get()
            self.dma(x[:, :, :n], xsrc.rearrange("(k p) t -> p k t", p=128)[:, :, u0:u0 + n], reads=[u_xsrc], writes=[ux])
            sq, usq = sqp.get()
            S.add("act", (lambda e, x=x, sq=sq, n=n: e.activation(out=sq[:, :, :n], in_=x[:, :, :n], func=AF.Square)),
                  reads=[ux], writes=[usq])
            ps, ups = self.PS.get()
            for k in range(8):
                S.add("pe", (lambda e, k=k, sq=sq, ps=ps, n=n: e.matmul(ps[:, :n], self.ones[:], sq[:, k, :n],
                                                                         start=(k == 0), stop=(k == 7))),
                      reads=[usq, self.u_ones], writes=[ups])
            rs, urs = rsp.get()
            S.add("act", (lambda e, rs=rs, ps=ps, n=n: e.activation(out=rs[:, :n], in_=ps[:, :n], func=AF.Sqrt, scale=1.0 / D, bias=self.eps6[:])),
                  reads=[ups], writes=[urs])
            S.add("dve", (lambda e, rs=rs, n=n: e.reciprocal(out=rs[:, :n], in_=rs[:, :n])), reads=[urs], writes=[urs])
            S.add("dve", (lambda e, x=x, rs=rs, n=n: e.tensor_tensor(out=x[:, :, :n], in0=x[:, :, :n],
                                                                      in1=rs[:, :n].unsqueeze(1).to_broadcast([128, 8, n]), op=ALU.mult)),
                  reads=[urs, ux], writes=[ux])
            for k in range(8):
                S.add("act", (lambda e, k=k, x=x, n=n, u0=u0, seg=seg: e.activation(
                    out=hT[:, k, u0:u0 + n], in_=x[:, k, :n], func=AF.Identity,
                    scale=self.der[:, which, seg, k:k + 1], bias=self.modcol(shj, seg, k))),
                    reads=[ux, self.u_der, self.u_modv], writes=[u_hT[ti]])

    def load_w_bf16(self, wsrc, M, wst_pool, wbf_pool, eng="pool"):
        S = self.S
        wst, uws = wst_pool.get()
        self.dma(wst[:, :, :M], wsrc.rearrange("(k p) m -> p k m", p=128), writes=[uws])
        wb, uwb = wbf_pool.get()
        S.add(eng, (lambda e, wb=wb, wst=wst, M=M: e.tensor_copy(out=wb[:, :, :M], in_=wst[:, :, :M])),
              reads=[uws], writes=[uwb])
        return wb, uwb

    def mm_tile(self, wb, uwb, M, hT, u_hT, ti):
        S = self.S
        u0, n = TT[ti]
        ps, ups = self.PS.get()
        for k in range(8):
            S.add("pe", (lambda e, k=k, ps=ps, wb=wb: e.matmul(ps[:M, :n], wb[:, k, :M], hT[:, k, u0:u0 + n],
                                                                start=(k == 0), stop=(k == 7))),
                  reads=[uwb, u_hT[ti]], writes=[ups])
        return ps, ups

    def conv(self, pp, upp, cw, ntap, bias, out_ap, uout):
        S = self.S
        acc, uacc = self._acc, self._uacc
        S.add("act", (lambda e: e.activation(out=acc[:, :U], in_=pp[:, 0:U], func=AF.Identity,
                                             scale=cw(0), bias=(bias if bias is not None else 0.0))),
              reads=[upp, self.u_vec], writes=[uacc])
        for j in range(1, ntap):
            last = j == ntap - 1
            o = out_ap if last else acc[:, :U]
            uo = uout if last else uacc
            S.add("dve", (lambda e, j=j, o=o: e.scalar_tensor_tensor(out=o, in0=pp[:, j:j + U], scalar=cw(j),
                                                                    in1=acc[:, :U], op0=ALU.mult, op1=ALU.add)),
                  reads=[upp, uacc, self.u_vec], writes=[uo])

    def phase_mixer_pre(self, l, es, xin, u_xin):
        nc, S, I, Sx, Ux = self.nc, self.S, self.I, self.Sx, self.Ux
        hT = es.enter_context(nc.sbuf_tensor(uname("hT"), [128, 8, U], BF16))
        u_hT = [Unit() for _ in TT]
        with ExitStack() as es3:
            self.norm_to_hT(xin, u_xin, 0, hT, u_hT, TT, es3)
            S.barrier()
        if self.stop == "norm":
            dbg = self.nc.dram_tensor(uname("dbg_hT"), [128, 8, U], BF16, kind="ExternalOutput").ap()
            self.dma(dbg, hT[:], reads=u_hT)
            return
        WK = TPool(nc, es, "wk", [128, UP], F32, 4)
        PP = TPool(nc, es, "pp", [128, UP], F32, 1)
        wst_pool = TPool(nc, es, "wst", [128, 8, 128], F32, 2)
        wbf_pool = TPool(nc, es, "wbf", [128, 8, 128], BF16, 2)
        rsp = TPool(nc, es, "rs", [128, 512], F32, 2)
        gtp = TPool(nc, es, "gt", [128, 512], BF16, 4)
        ztp = TPool(nc, es, "zt", [128, 512], F32, 3)
        lwp = TPool(nc, es, "lw", [128, 128], F32, 2)
        lwr = TPool(nc, es, "lwr", [128, 128], BF16, 2)
        sm = es.enter_context(nc.sbuf_tensor(uname("sm"), [128, 40], F32))
        u_sm = Unit()
        pp, upp = PP.get()
        S.add("pool", lambda e: e.memset(pp[:], 0.0), writes=[upp])
        sqb = es.enter_context(nc.sbuf_tensor(uname("sqb"), [128, UP], BF16))
        usqb = Unit()
        self._cnt = 0

        def evac_copy(ps, ups, out_ap, uo, M=128, n=512):
            self._cnt += 1
            if self._cnt % 2:
                S.add("act", (lambda e: e.activation(out=out_ap, in_=ps[:M, :n], func=AF.Copy)), reads=[ups], writes=[uo])
            else:
                S.add("dve", (lambda e: e.tensor_copy(out=out_ap, in_=ps[:M, :n])), reads=[ups], writes=[uo])

        def proj_to_pp(col0):
            wb, uwb = self.load_w_bf16(I["w_in"][l][:, col0:col0 + 128], 128, wst_pool, wbf_pool)
            for ti, (u0, n) in enumerate(TT):
                ps, ups = self.mm_tile(wb, uwb, 128, hT, u_hT, ti)
                evac_copy(ps, ups, pp[:, u0 + 1:u0 + 1 + n], upp, 128, n)

        def proj_act(col0, func, out_tile_fn, bias=None):
            wb, uwb = self.load_w_bf16(I["w_in"][l][:, col0:col0 + 128], 128, wst_pool, wbf_pool)
            for ti, (u0, n) in enumerate(TT):
                ps, ups = self.mm_tile(wb, uwb, 128, hT, u_hT, ti)
                o, uo = out_tile_fn(ti, u0, n)
                S.add("act", (lambda e, ps=ps, o=o, n=n: e.activation(out=o, in_=ps[:, :n], func=func)),
                      reads=[ups], writes=[uo])

        S.add("act", lambda e: e.activation(out=sm[:40, 0:1], in_=self.vs("alog")[:40], func=AF.Exp),
              reads=[self.u_vec], writes=[u_sm])
        S.add("dve", lambda e: e.tensor_scalar(out=sm[:40, 0:1], in0=sm[:40, 0:1], scalar1=-1.0, scalar2=None, op0=ALU.mult),
              reads=[u_sm], writes=[u_sm])
        S.add("act", lambda e: e.activation(out=sm[:, 8:24], in_=self.vs("llam"), func=AF.Exp, scale=-1.0),
              reads=[self.u_vec, u_sm], writes=[u_sm])
        S.add("act", lambda e: e.activation(out=sm[:, 8:24], in_=sm[:, 8:24], func=AF.Ln, bias=1.0),
              reads=[u_sm], writes=[u_sm])
        S.add("dve", lambda e: e.tensor_scalar(out=sm[:, 8:24], in0=sm[:, 8:24], scalar1=-8.0, scalar2=None, op0=ALU.mult),
              reads=[u_sm], writes=[u_sm])
        S.add("dve", lambda e: e.tensor_scalar(out=sm[:, 24:40], in0=sm[:, 8:24], scalar1=2.0, scalar2=None, op0=ALU.mult),
              reads=[u_sm], writes=[u_sm])

        def z_chunk(c):
            wb, uwb = self.load_w_bf16(I["w_in"][l][:, OFF_Z + c * 128:OFF_Z + (c + 1) * 128], 128, wst_pool, wbf_pool)
            for ti, (u0, n) in enumerate(TT):
                ps, ups = self.mm_tile(wb, uwb, 128, hT, u_hT, ti)
                t, ut = ztp.get()
                S.add("act", (lambda e, ps=ps, t=t, n=n: e.activation(out=t[:, :n], in_=ps[:, :n], func=AF.Silu)),
                      reads=[ups], writes=[ut])
                self.dma(Sx["szT"][c * 128:(c + 1) * 128, u0:u0 + n], t[:, :n], reads=[ut], writes=[Ux["szT"]])

        def gate_chunk(c):
            wb, uwb = self.load_w_bf16(I["w_in"][l][:, OFF_GATE + c * 128:OFF_GATE + (c + 1) * 128], 128, wst_pool, wbf_pool)
            for ti, (u0, n) in enumerate(TT):
                ps, ups = self.mm_tile(wb, uwb, 128, hT, u_hT, ti)
                t, ut = gtp.get()
                S.add("act", (lambda e, ps=ps, t=t, n=n: e.activation(out=t[:, :n], in_=ps[:, :n], func=AF.Sigmoid)),
                      reads=[ups], writes=[ut])
                self.dma(Sx["gT"][c * 128:(c + 1) * 128, u0:u0 + n], t[:, :n], reads=[ut], writes=[Ux["gT"]])
        fillers = [(z_chunk, c) for c in range(8)] + [(gate_chunk, c) for c in range(24)]

        def fill():
            if fillers:
                fn, c = fillers.pop(0)
                fn(c)

        for w3 in range(3):
            for h in range(8):
                c = w3 * 8 + h
                proj_to_pp(OFF_QKV + c * 128)
                (q, uq) = WK.get()
                self._acc, self._uacc = q, uq
                self.conv(pp, upp, (lambda j, c=c: self.vs("dncw", c * 4 + j)), 4, None, q[:, :U], uq)
                S.add("act", (lambda e, q=q: e.activation(out=q[:, :U], in_=q[:, :U], func=AF.Silu)), reads=[uq], writes=[uq])
                if w3 < 2:
                    (sq, usq) = (sqb, usqb)
                    S.add("act", (lambda e, q=q, sq=sq: e.activation(out=sq[:, :U], in_=q[:, :U], func=AF.Square)),
                          reads=[uq], writes=[usq])
                    for (u0, n) in TT:
                        ps, ups = self.PS.get()
                        S.add("pe", (lambda e, ps=ps, sq=sq, u0=u0, n=n: e.matmul(ps[:, :n], self.ones_bf[:], sq[:, u0:u0 + n],
                                                                                   start=True, stop=True)),
                              reads=[usq, self.u_ones], writes=[ups])
                        rs, urs = rsp.get()
                        S.add("act", (lambda e, rs=rs, ps=ps, n=n: e.activation(out=rs[:, :n], in_=ps[:, :n], func=AF.Sqrt, bias=self.eps6[:])),
                              reads=[ups], writes=[urs])
                        S.add("dve", (lambda e, rs=rs, n=n: e.reciprocal(out=rs[:, :n], in_=rs[:, :n])), reads=[urs], writes=[urs])
                        S.add("pool", (lambda e, q=q, rs=rs, u0=u0, n=n: e.tensor_tensor(out=q[:, u0:u0 + n], in0=q[:, u0:u0 + n],
                                                                                         in1=rs[:, :n], op=ALU.mult)),
                              reads=[urs, uq], writes=[uq])
                dst = (Sx["qT"], Sx["kT"], Sx["vT"])[w3]
                ud = (Ux["qT"], Ux["kT"], Ux["vT"])[w3]
                self.dma(dst[h], q[:, :U], reads=[uq], writes=[ud])
                fill()
        if self.stop == "pre1":
            return
        if self.stop == "pre2":
            return
        rm, urm = WK.get()
        self.dma(rm[:40, :U], I["rmask"], writes=[urm])
        G, uG = WK.get()
        LNB, uLNB = WK.get()
        for which, (dst, udst) in enumerate(((G, uG), (LNB, uLNB))):
            wst, uws = wst_pool.get()
            S.add("pool", (lambda e, wst=wst: e.memset(wst[:], 0.0)), writes=[uws])
            for d in range(2):
                c0 = OFF_AB + d * 16 + which * 8
                self.dma(wst[:, :, d * 32:d * 32 + 8], I["w_in"][l][:, c0:c0 + 8].rearrange("(k p) m -> p k m", p=128),
                         reads=[uws], writes=[uws])
            wb, uwb = wbf_pool.get()
            S.add("pool", (lambda e, wb=wb, wst=wst: e.tensor_copy(out=wb[:, :, :40], in_=wst[:, :, :40])), reads=[uws], writes=[uwb])
            for ti, (u0, n) in enumerate(TT):
                ps, ups = self.mm_tile(wb, uwb, 40, hT, u_hT, ti)
                evac_copy(ps, ups, dst[:40, u0:u0 + n], udst, 40, n)
        for (t_, ut_) in ((G, uG), (LNB, uLNB)):
            S.add("pool", (lambda e, t_=t_: e.memset(t_[:40, 256:259], 0.0)), reads=[ut_], writes=[ut_])
        S.add("act", lambda e: e.activation(out=G[:40, :U], in_=G[:40, :U], func=AF.Exp, bias=self.vs("dtb")[:40]),
              reads=[uG, self.u_vec], writes=[uG])
        S.add("act", lambda e: e.activation(out=G[:40, :U], in_=G[:40, :U], func=AF.Ln, bias=1.0), reads=[uG], writes=[uG])
        S.add("dve", lambda e: e.tensor_scalar(out=G[:40, :U], in0=G[:40, :U], scalar1=sm[:40, 0:1], scalar2=None, op0=ALU.mult),
              reads=[uG, u_sm], writes=[uG])
        S.add("act", lambda e: e.activation(out=LNB[:40, :U], in_=LNB[:40, :U], func=AF.Exp, scale=-1.0), reads=[uLNB], writes=[uLNB])
        S.add("act", lambda e: e.activation(out=LNB[:40, :U], in_=LNB[:40, :U], func=AF.Ln, bias=1.0), reads=[uLNB], writes=[uLNB])
        S.add("dve", lambda e: e.tensor_scalar(out=LNB[:40, :U], in0=LNB[:40, :U], scalar1=-1.0, scalar2=None, op0=ALU.mult),
              reads=[uLNB], writes=[uLNB])
        GC, uGC = WK.get()
        S.add("pool", lambda e: e.memset(GC[:40, :U], 0.0), writes=[uGC])
        S.add("dve", lambda e: e.tensor_tensor_scan(out=GC[0:8, :U], data0=rm[0:8, :U], data1=G[0:8, :U], initial=0.0,
                                                    op0=ALU.mult, op1=ALU.add), reads=[urm, uG, uGC], writes=[uGC])
        S.add("dve", lambda e: e.tensor_tensor_scan(out=GC[32:40, U - 1::-1], data0=rm[32:40, U - 1::-1], data1=G[32:40, U - 1::-1],
                                                    initial=0.0, op0=ALU.mult, op1=ALU.add), reads=[urm, uG, uGC], writes=[uGC])
        self.dma(Sx["gab"][0], GC[:40, :U], reads=[uGC], writes=[Ux["gab"]])
        self.dma(Sx["gab"][2], LNB[:40, :U], reads=[uLNB], writes=[Ux["gab"]])
        S.add("dve", lambda e: e.tensor_tensor(out=G[:40, :U], in0=GC[:40, :U], in1=LNB[:40, :U], op=ALU.add),
              reads=[uGC, uLNB, uG], writes=[uG])
        self.dma(Sx["gab"][1], G[:40, :U], reads=[uG], writes=[Ux["gab"]])
        if self.stop == "pre3":
            return
        for c in range(8):
            tl = []
            for part in (1, 2, 0):
                cidx = part * 8 + c
                proj_to_pp(OFF_HY + cidx * 128)
                t, ut = WK.get()
                self._acc, self._uacc = t, ut
                self.conv(pp, upp, (lambda j, cidx=cidx: self.vs("hycw", cidx * 3 + j)), 3, self.vs("hycb", cidx), t[:, :U], ut)
                tl.append((t, ut))
                fill()
            (a1, ua1), (a2, ua2), (a0, ua0) = tl
            S.add("pool", (lambda e, a1=a1, a2=a2: e.tensor_tensor(out=a1[:, :U], in0=a1[:, :U], in1=a2[:, :U], op=ALU.mult)),
                  reads=[ua1, ua2], writes=[ua1])
            self.dma(Sx["zzT"][c * 128:(c + 1) * 128], a1[:, :U], reads=[ua1], writes=[Ux["zzT"]])
            self.dma(Sx["x0T"][c * 128:(c + 1) * 128], a0[:, :U], reads=[ua0], writes=[Ux["x0T"]])
        if self.stop == "pre4":
            return
        while fillers:
            fill()
        for g in range(8):
            proj_to_pp(OFF_LX + g * 128)
            (xs, uxs), (H, uH), (A, uA), (Bt, uB) = WK.t
            self._acc, self._uacc = H, uH
            self.conv(pp, upp, (lambda j, g=g: self.vs("lrucw", g * 4 + j)), 4, self.vs("lrucb", g), xs[:, :U], uxs)
            S.add("act", (lambda e: e.activation(out=sqb[:, :U], in_=xs[:, :U], func=AF.Copy)), reads=[uxs], writes=[usqb])
            for d in range(2):
                tt_, utt = (H, uH) if d == 0 else (pp, upp)
                for (wname, bname, dst, udst) in (("lru_w_a", "lba", A, uA), ("lru_w_x", "lbx", Bt, uB)):
                    lw, ulw = lwp.get()
                    self.dma(lw[:], I[wname][l, d, g], writes=[ulw])
                    lr, ulr = lwr.get()
                    S.add("act", (lambda e, lw=lw, lr=lr: e.activation(out=lr[:], in_=lw[:], func=AF.Copy)), reads=[ulw], writes=[ulr])
                    for (u0, n) in TT:
                        ps, ups = self.PS.get()
                        S.add("pe", (lambda e, ps=ps, lr=lr, u0=u0, n=n: e.matmul(ps[:, :n], lr[:], sqb[:, u0:u0 + n],
                                                                                   start=True, stop=True)),
                              reads=[ulr, usqb], writes=[ups])
                        S.add("act", (lambda e, ps=ps, dst=dst, u0=u0, n=n, bname=bname, d=d, g=g: e.activation(
                            out=dst[:, u0:u0 + n], in_=ps[:, :n], func=AF.Sigmoid, bias=self.vs(bname, d * 8 + g))),
                            reads=[ups, self.u_vec], writes=[udst])
                if self.stop == "pre5":
                    return
                S.add("pool", (lambda e, A=A: e.memset(A[:, 256:259], 0.0)), reads=[uA], writes=[uA])
                S.add("act", (lambda e, A=A, t=tt_, d=d, g=g: e.activation(out=t[:, :U], in_=A[:, :U], func=AF.Exp,
                                                                           scale=sm[:, 24 + d * 8 + g:25 + d * 8 + g])),
                      reads=[uA, u_sm, utt], writes=[utt])
                S.add("act", (lambda e, A=A, d=d, g=g: e.activation(out=A[:, :U], in_=A[:, :U], func=AF.Exp,
                                                                     scale=sm[:, 8 + d * 8 + g:9 + d * 8 + g])),
                      reads=[uA, u_sm], writes=[uA])
                S.add("act", (lambda e, t=tt_: e.activation(out=t[:, :U], in_=t[:, :U], func=AF.Sqrt, scale=-1.0, bias=self.one1[:])),
                      reads=[utt], writes=[utt])
                S.add("dve", (lambda e, Bt=Bt, t=tt_: e.tensor_tensor(out=Bt[:, :U], in0=Bt[:, :U], in1=t[:, :U], op=ALU.mult)),
                      reads=[uB, utt], writes=[uB])
                S.add("dve", (lambda e, Bt=Bt: e.tensor_tensor(out=Bt[:, :U], in0=Bt[:, :U], in1=xs[:, :U], op=ALU.mult)),
                      reads=[uB, uxs], writes=[uB])
                S.add("pool", (lambda e, Bt=Bt: e.memset(Bt[:, 256:259], 0.0)), reads=[uB], writes=[uB])
                if self.stop == "pre6":
                    return
                if d == 0:
                    S.add("dve", (lambda e, A=A, Bt=Bt, t=tt_: e.tensor_tensor_scan(out=t[:, 0:U], data0=A[:, 0:U], data1=Bt[:, 0:U],
                                                                                    initial=0.0, op0=ALU.mult, op1=ALU.add)),
                          reads=[uA, uB, utt], writes=[utt])
                else:
                    S.add("dve", (lambda e, A=A, Bt=Bt, t=tt_: e.tensor_tensor_scan(out=t[:, 255::-1], data0=A[:, 255::-1], data1=Bt[:, 255::-1],
                                                                                    initial=0.0, op0=ALU.mult, op1=ALU.add)),
                          reads=[uA, uB, utt], writes=[utt])
                    S.add("dve", (lambda e, A=A, Bt=Bt, t=tt_: e.tensor_tensor_scan(out=t[:, U - 1:LT0 - 1:-1], data0=A[:, U - 1:LT0 - 1:-1],
                                                                                    data1=Bt[:, U - 1:LT0 - 1:-1], initial=t[:, 0:1],
                                                                                    op0=ALU.mult, op1=ALU.add)),
                          reads=[uA, uB, utt], writes=[utt])
                    S.add("dve", (lambda e, t=tt_: e.tensor_tensor(out=H[:, :U], in0=H[:, :U], in1=t[:, :U], op=ALU.add)),
                          reads=[utt, uH], writes=[uH])
            if self.stop == "pre7":
                return
            S.add("pool", lambda e: e.memset(pp[:, 0:1], 0.0), reads=[upp], writes=[upp])
            S.add("pool", lambda e: e.memset(pp[:, 257:260], 0.0), reads=[upp], writes=[upp])
            S.add("pool", lambda e: e.memset(pp[:, 4356:UP], 0.0), reads=[upp], writes=[upp])
            (Y, uY), (T2, uT2) = WK.t[2], WK.t[3]
            wb, uwb = self.load_w_bf16(I["w_in"][l][:, OFF_LY + g * 128:OFF_LY + (g + 1) * 128], 128, wst_pool, wbf_pool)
            for ti, (u0, n) in enumerate(TT):
                ps, ups = self.mm_tile(wb, uwb, 128, hT, u_hT, ti)
                evac_copy(ps, ups, Y[:, u0:u0 + n], uY, 128, n)
            S.add("pool", (lambda e, Y=Y: e.memset(Y[:, 256:259], 0.0)), reads=[uY], writes=[uY])
            S.add("act", (lambda e, Y=Y, T2=T2: e.activation(out=T2[:, :U], in_=Y[:, :U], func=AF.Square)), reads=[uY], writes=[uT2])
            S.add("dve", (lambda e, T2=T2: e.tensor_scalar(out=T2[:, :U], in0=T2[:, :U], scalar1=0.044715, scalar2=1.0,
                                                           op0=ALU.mult, op1=ALU.add)), reads=[uT2], writes=[uT2])
            S.add("dve", (lambda e, Y=Y, T2=T2: e.tensor_tensor(out=T2[:, :U], in0=T2[:, :U], in1=Y[:, :U], op=ALU.mult)),
                  reads=[uT2, uY], writes=[uT2])
            S.add("act", (lambda e, T2=T2: e.activation(out=T2[:, :U], in_=T2[:, :U], func=AF.Sigmoid, scale=1.5957691216057308)),
                  reads=[uT2], writes=[uT2])
            S.add("dve", (lambda e, Y=Y, T2=T2: e.tensor_tensor(out=Y[:, :U], in0=Y[:, :U], in1=T2[:, :U], op=ALU.mult)),
                  reads=[uT2, uY], writes=[uY])
            S.add("dve", (lambda e, Y=Y: e.tensor_tensor(out=T2[:, :U].bitcast(BF16)[:, :U], in0=Y[:, :U], in1=H[:, :U], op=ALU.mult)),
                  reads=[uY, uH, uT2], writes=[uT2])
            self.dma(Sx["olru"][g * 128:(g + 1) * 128], T2[:, :U].bitcast(BF16)[:, :U], reads=[uT2], writes=[Ux["olru"]])
            if self.stop == "pre8":
                return


def prep_inputs(inp, b):
    m = {}
    xT = np.zeros((D, U), np.float32)
    xT[:, 0:LC] = inp["ctx"][b].T
    xT[:, LT0:] = inp["x"][b].T
    m["xT0"] = xT
    cc = np.zeros((128, 8, 2), np.float32)
    cc[:, :, 0] = _pk(inp["c"][b])
    cc[:, :, 1] = _pk(inp["c_ctx"])
    m["cc"] = cc.reshape(128, 16)
    m["vec"] = np.stack([build_vec(inp, l) for l in range(DEPTH)])
    m["cst"] = build_cst()
    m["rmask"] = build_rmask()
    for seg, n_ in (("lat", L), ("ctx", LC)):
        tbs = hyena_tables(n_)
        for k in ("zT", "negt", "F1", "G", "Tout"):
            m["hy_%s_%s" % (k, seg)] = tbs[k]
        m["hy_Winv"] = tbs["Winv"]
    for n in ("w_mod", "w_in", "lru_w_a", "lru_w_x", "w_proj_dn", "w_proj_hy", "w_proj_lru", "w_out",
              "ffn_up", "ffn_down", "hy_w1", "hy_w2", "hy_w3"):
        m[n] = np.ascontiguousarray(inp[n], dtype=np.float32)
    return m


DN_BLOCKS = [0, 128] + [LT0 + 128 * i for i in range(32)]


class T128:
    def __init__(self, nc, es, names, dtype=F32):
        self.t = {}
        for n in names:
            self.t[n] = (es.enter_context(nc.sbuf_tensor(uname(n), [128, 128], dtype)), Unit())

    def __getitem__(self, n):
        return self.t[n]


def dn_phase(B, l, es):
    nc, S, Sx, Ux = B.nc, B.S, B.Sx, B.Ux
    ident = B.cs("ident")
    NB = len(DN_BLOCKS)
    GR = es.enter_context(nc.sbuf_tensor(uname("GR"), [40, U], F32))
    uGR = Unit()
    B.dma(GR[:], Sx["gab"][0], reads=[Ux["gab"]], writes=[uGR])
    TG = es.enter_context(nc.sbuf_tensor(uname("TG"), [128, NB, 3, 40], F32))
    TE = es.enter_context(nc.sbuf_tensor(uname("TE"), [128, NB, 2, 40], F32))
    uTG = Unit()
    gl_pool = TPool(nc, es, "gl", [40, 3, 128], F32, 2)
    for bi, ub in enumerate(DN_BLOCKS):
        gl, ugl = gl_pool.get()
        B.dma(gl[:], Sx["gab"][:, :, ub:ub + 128].rearrange("k r u -> r k u"), reads=[Ux["gab"]], writes=[ugl])
        ps, ups = B.PS.get()
        for k in range(3):
            S.add("pe", (lambda e, ps=ps, k=k, gl=gl: e.transpose(ps[:, k * 40:(k + 1) * 40], gl[:, k, :], ident[:40, :40])),
                  reads=[ugl, B.u_cst], writes=[ups])
        S.add("dve", (lambda e, ps=ps, bi=bi: e.tensor_copy(out=TG[:, bi].rearrange("p k r -> p (k r)"), in_=ps[:, 0:120])),
              reads=[ups], writes=[uTG])
        S.add("act", (lambda e, ps=ps, bi=bi: e.activation(out=TE[:, bi].rearrange("p k r -> p (k r)"), in_=ps[:, 40:120], func=AF.Exp)),
              reads=[ups], writes=[uTG])
    SELN = es.enter_context(nc.sbuf_tensor(uname("SELN"), [40, 16, 128], F32))
    uSEL = Unit()
    for q in range(16):
        r = (q // 8) * 32 + (q % 8)
        S.add("dve", (lambda e, q=q, r=r: e.tensor_scalar(out=SELN[:, q, :], in0=ident[:40, r:r + 1].to_broadcast([40, 128]),
                                                          scalar1=-1.0, scalar2=None, op0=ALU.mult)),
              reads=[B.u_cst], writes=[uSEL])
    zero = es.enter_context(nc.sbuf_tensor(uname("zero"), [128, 128], F32))
    uzero = Unit()
    S.add("pool", lambda e: e.memset(zero[:], 0.0), writes=[uzero])

    if B.stop == "dn0":
        dbg = nc.dram_tensor(uname("dbg_TG"), [128, NB, 3, 40], F32, kind="ExternalOutput").ap()
        B.dma(dbg, TG[:], reads=[uTG])
        return
    NCH = 4
    chains_res = []
    for ci in range(NCH):
        res = {}
        res["f32"] = T128(nc, es, ["qt", "kt", "vt", "Dm", "A", "E1", "M", "AT", "MT", "Y", "P0", "P1", "Q0", "Q1", "Us", "EG"])
        res["f32b"] = T128(nc, es, ["qt", "kt", "vt"])
        res["r"] = T128(nc, es, ["ktr", "qtr", "attnT", "Kd0", "Kd1", "Ktb", "Vb", "QdT", "YR", "WTs", "VN", "S"], F32R)
        res["cd"] = (es.enter_context(nc.sbuf_tensor(uname("cd"), [128, 2], F32)), Unit())
        if ci % 2 == 0:
            res["O"] = (es.enter_context(nc.sbuf_tensor(uname("O"), [128, NB, 128], F32)), [Unit() for _ in range(NB)])
        else:
            res["O"] = chains_res[ci - 1]["O"]
        res["ps"] = [B.PS.t[2 * ci], B.PS.t[2 * ci + 1]]
        chains_res.append(res)
    OD = es.enter_context(nc.sbuf_tensor(uname("OD"), [128, U], BF16))
    uOD = Unit()
    S.add("pool", lambda e: e.memset(OD[:, 256:259], 0.0), writes=[uOD])
    pp_t = TPool(nc, es, "dnpost", [128, 128], F32, 3)
    pp_s = TPool(nc, es, "dnsm", [128, 2], F32, 3)
    DK = 128.0 ** -0.5

    def chain(h, d, res):
        q = d * 8 + h
        r = d * 32 + h
        f, fb, rr = res["f32"], res["f32b"], res["r"]
        cd, ucd = res["cd"]
        O, uO = res["O"]
        psl = res["ps"]
        slot_i = [0]

        def slot():
            k = slot_i[0] % 8
            slot_i[0] += 1
            t, u = psl[k // 4]
            qd = k % 4
            return t[:, qd * 128:(qd + 1) * 128], u
        incl = B.cs("lowi") if d == 0 else B.cs("uppi")
        strict = B.cs("lows") if d == 0 else B.cs("upps")
        iend = (lambda c: c * 64 + 63) if d == 0 else (lambda c: c * 64)
        (Sst, uS), (VN, uVN) = rr["S"], rr["VN"]
        S.add("dve", (lambda e: e.tensor_copy(out=Sst[:], in_=zero[:])), reads=[uzero], writes=[uS])
        S.add("dve", (lambda e: e.tensor_copy(out=VN[:], in_=zero[:])), reads=[uzero], writes=[uVN])
        order = list(range(NB)) if d == 0 else [1, 0] + list(range(NB - 1, 1, -1))
        for oi, bi in enumerate(order):
            ub = DN_BLOCKS[bi]
            ld = f if oi % 2 == 0 else fb
            (qt, uqt), (kt, ukt), (vt, uvt) = ld["qt"], ld["kt"], ld["vt"]
            B.dma(qt[:], Sx["qT"][h][:, ub:ub + 128], reads=[Ux["qT"]], writes=[uqt])
            B.dma(kt[:], Sx["kT"][h][:, ub:ub + 128], reads=[Ux["kT"]], writes=[ukt])
            B.dma(vt[:], Sx["vT"][h][:, ub:ub + 128], reads=[Ux["vT"]], writes=[uvt])
            (ktr, uktr), (qtr, uqtr) = rr["ktr"], rr["qtr"]
            S.add("act", (lambda e, kt=kt: e.activation(out=ktr[:], in_=kt[:], func=AF.Copy)), reads=[ukt], writes=[uktr])
            S.add("act", (lambda e, qt=qt: e.activation(out=qtr[:], in_=qt[:], func=AF.Copy)), reads=[uqt], writes=[uqtr])
            yield
            if B.stop.endswith(":A"):
                return
            pKK, uKK = slot()
            S.add("pe", (lambda e, o=pKK: e.matmul(o, ktr[:], ktr[:], start=True, stop=True)), reads=[uktr], writes=[uKK])
            pQK, uQK = slot()
            S.add("pe", (lambda e, o=pQK: e.matmul(o, ktr[:], qtr[:], start=True, stop=True)), reads=[uktr, uqtr], writes=[uQK])
            pbc, ubc = slot()
            S.add("pe", (lambda e, o=pbc, ub=ub: e.matmul(o, SELN[:, q, :], GR[:, ub:ub + 128], start=True, stop=True)),
                  reads=[uSEL, uGR], writes=[ubc])
            pvt, uvtk = slot()
            S.add("pe", (lambda e, o=pvt, vt=vt: e.transpose(o, vt[:], ident)), reads=[uvt, B.u_cst], writes=[uvtk])
            pkt, uktk = slot()
            S.add("pe", (lambda e, o=pkt, kt=kt: e.transpose(o, kt[:], ident)), reads=[ukt, B.u_cst], writes=[uktk])
            yield
            if B.stop.endswith(":B"):
                return
            (Dm, uDm), (A, uA), (E1, uE1), (M, uM), (EG, uEG) = f["Dm"], f["A"], f["E1"], f["M"], f["EG"]
            gcol = TG[:, bi, 0, r:r + 1]
            climit = int(B.stop.split(":K")[1]) if ":K" in B.stop else 99
            if 0 < climit:
                S.add("dve", (lambda e, o=pbc, gcol=gcol: e.tensor_scalar(out=Dm[:], in0=o, scalar1=gcol, scalar2=0.0, op0=ALU.add, op1=ALU.min)),
                      reads=[ubc, uTG], writes=[uDm])
            if 2 < climit:
                S.add("act", (lambda e, o=pbc: e.activation(out=EG[:], in_=o, func=AF.Exp, scale=-1.0)), reads=[ubc], writes=[uEG])
            if 3 < climit:
                S.add("act", (lambda e: e.activation(out=A[:], in_=Dm[:], func=AF.Exp)), reads=[uDm], writes=[uA])
            if 4 < climit:
                S.add("dve", (lambda e: e.tensor_tensor(out=A[:], in0=A[:], in1=incl, op=ALU.mult)), reads=[uA, B.u_cst], writes=[uA])
            bcol = TE[:, bi, 1, r:r + 1]
            if 5 < climit:
                S.add("dve", (lambda e, bcol=bcol: e.scalar_tensor_tensor(out=E1[:], in0=A[:], scalar=bcol, in1=strict, op0=ALU.mult, op1=ALU.mult)),
                      reads=[uA, uTG, B.u_cst], writes=[uE1])
            if 6 < climit:
                S.add("dve", (lambda e, o=pKK: e.tensor_tensor(out=M[:], in0=o, in1=E1[:], op=ALU.mult)), reads=[uKK, uE1], writes=[uM])
            (QdT, uQdT) = rr["QdT"]
            if 7 < climit:
                S.add("dve", (lambda e, qt=qt: e.scalar_tensor_tensor(out=QdT[:], in0=qt[:], scalar=DK, in1=EG[:], op0=ALU.mult, op1=ALU.mult)),
                      reads=[uqt, uEG], writes=[uQdT])
            yield
            if B.stop.endswith(":C") or ":K" in B.stop:
                return
            pAT, uAT_ = slot()
            S.add("pe", (lambda e, o=pAT: e.transpose(o, A[:], ident)), reads=[uA, B.u_cst], writes=[uAT_])
            pMT, uMT_ = slot()
            S.add("pe", (lambda e, o=pMT: e.transpose(o, M[:], ident)), reads=[uM, B.u_cst], writes=[uMT_])
            yield
            (AT, uAT), (MT, uMT), (Y, uY) = f["AT"], f["MT"], f["Y"]
            S.add("act", (lambda e, o=pAT: e.activation(out=AT[:], in_=o, func=AF.Copy)), reads=[uAT_], writes=[uAT])
            S.add("act", (lambda e, o=pMT: e.activation(out=MT[:], in_=o, func=AF.Copy)), reads=[uMT_], writes=[uMT])
            S.add("dve", (lambda e, o=pMT: e.tensor_tensor(out=Y[:], in0=ident, in1=o, op=ALU.subtract)), reads=[uMT_, B.u_cst], writes=[uY])
            (attnT, uattn), (Kd0, uKd0), (Kd1, uKd1), (Ktb, uKtb), (Vb, uVb) = rr["attnT"], rr["Kd0"], rr["Kd1"], rr["Ktb"], rr["Vb"]
            S.add("dve", (lambda e, o=pQK: e.scalar_tensor_tensor(out=attnT[:], in0=o, scalar=DK, in1=AT[:], op0=ALU.mult, op1=ALU.mult)),
                  reads=[uQK, uAT], writes=[uattn])
            for c, (Kd, uKd) in enumerate(((Kd0, uKd0), (Kd1, uKd1))):
                S.add("act", (lambda e, o=pkt, Kd=Kd, c=c: e.activation(out=Kd[:], in_=o, func=AF.Copy, scale=AT[:, iend(c):iend(c) + 1])),
                      reads=[uktk, uAT], writes=[uKd])
            wcol = TE[:, bi, 0, r:r + 1]
            S.add("dve", (lambda e, o=pkt, wcol=wcol: e.tensor_scalar(out=Ktb[:], in0=o, scalar1=wcol, scalar2=None, op0=ALU.mult)),
                  reads=[uktk, uTG], writes=[uKtb])
            S.add("dve", (lambda e, o=pvt, bcol=bcol: e.tensor_scalar(out=Vb[:], in0=o, scalar1=bcol, scalar2=None, op0=ALU.mult)),
                  reads=[uvtk, uTG], writes=[uVb])
            yield
            if B.stop.endswith(":E"):
                return
            P, uP = M, uM
            PT, uPT = MT, uMT
            bufs = [(f["P0"], f["Q0"]), (f["P1"], f["Q1"])]
            (YR, uYR) = rr["YR"]
            pend = None
            for lev in range(1, 7):
                cur = None
                if lev <= 5:
                    (Pn, uPn), (PnT, uPnT) = bufs[lev % 2]
                    pP, upP = slot()
                    S.add("pe", (lambda e, o=pP, PT=PT, P=P: e.matmul(o, PT[:], P[:], start=True, stop=True)), reads=[uPT, uP], writes=[upP])
                    pPT = None
                    if lev < 5:
                        pPT, upPT = slot()
                        S.add("pe", (lambda e, o=pPT, PT=PT, P=P: e.matmul(o, P[:], PT[:], start=True, stop=True)), reads=[uPT, uP], writes=[upPT])
                    cur = (Pn, uPn, PnT, uPnT, pP, upP, pPT, upPT if lev < 5 else None)
                pY = None
                if pend is not None:
                    pY, upY = slot()
                    S.add("pe", (lambda e, o=pY, Pq=pend[0]: e.matmul(o, Pq[:], Y[:], start=True, stop=True)), reads=[pend[1], uY], writes=[upY])
                yield
                if pY is not None:
                    if lev <= 5:
                        S.add("dve", (lambda e, o=pY: e.tensor_tensor(out=Y[:], in0=Y[:], in1=o, op=ALU.add)), reads=[upY, uY], writes=[uY])
                    else:
                        S.add("dve", (lambda e, o=pY: e.tensor_tensor(out=YR[:], in0=Y[:], in1=o, op=ALU.add)), reads=[upY, uY], writes=[uYR])
                if cur is not None:
                    (Pn, uPn, PnT, uPnT, pP, upP, pPT, upPT) = cur
                    S.add("act", (lambda e, o=pP, Pn=Pn: e.activation(out=Pn[:], in_=o, func=AF.Copy)), reads=[upP], writes=[uPn])
                    if pPT is not None:
                        S.add("act", (lambda e, o=pPT, PnT=PnT: e.activation(out=PnT[:], in_=o, func=AF.Copy)), reads=[upPT], writes=[uPnT])
                    pend = (Pn, uPn)
                    P, uP, PT, uPT = Pn, uPn, PnT, uPnT
                yield
            if B.stop.endswith(":G"):
                return
            (Us, uUs), (WTs, uWTs) = f["Us"], rr["WTs"]
            pU, upU = slot()
            S.add("pe", (lambda e, o=pU: e.matmul(o, YR[:], Vb[:], start=True, stop=True)), reads=[uYR, uVb], writes=[upU])
            pW, upW = slot()
            S.add("pe", (lambda e, o=pW: e.matmul(o, Ktb[:], YR[:], start=True, stop=True)), reads=[uYR, uKtb], writes=[upW])
            yield
            S.add("act", (lambda e, o=pU: e.activation(out=Us[:], in_=o, func=AF.Copy)), reads=[upU], writes=[uUs])
            S.add("dve", (lambda e, o=pW: e.tensor_copy(out=WTs[:], in_=o)), reads=[upW], writes=[uWTs])
            yield
            if B.stop.endswith(":H"):
                return
            for c in ((0, 1) if d == 0 else (1, 0)):
                rows = slice(c * 64, (c + 1) * 64)
                Kd, uKd = (Kd0, uKd0) if c == 0 else (Kd1, uKd1)
                p1, up1 = slot()
                S.add("pe", (lambda e, o=p1: e.matmul(o, WTs[:], Sst[:], start=True, stop=True)), reads=[uWTs, uS], writes=[up1])
                yield
                S.add("dve", (lambda e, o=p1, rows=rows: e.tensor_tensor(out=VN[rows, :], in0=Us[rows, :], in1=o[rows, :], op=ALU.subtract)),
                      reads=[up1, uUs, uVN], writes=[uVN])
                yield
                p2, up2 = slot()
                S.add("pe", (lambda e, o=p2: e.matmul(o, QdT[:], Sst[:], start=True, stop=False)), reads=[uQdT, uS], writes=[up2])
                S.add("pe", (lambda e, o=p2: e.matmul(o, attnT[:], VN[:], start=False, stop=True)), reads=[uattn, uVN], writes=[up2])
                p3, up3 = slot()
                S.add("pe", (lambda e, o=p3, Kd=Kd: e.matmul(o, Kd[:], VN[:], start=True, stop=True)), reads=[uKd, uVN], writes=[up3])
                yield
                oi_other = (bi if d == 1 else ([1, 0] + list(range(NB - 1, 1, -1))).index(bi))
                first = (oi < oi_other) or (oi == oi_other and d == 0)
                if first:
                    S.add("act", (lambda e, o=p2, rows=rows, bi=bi: e.activation(out=O[rows, bi, :], in_=o[rows, :], func=AF.Copy)),
                          reads=[up2], writes=[uO[bi]])
                else:
                    S.add("dve", (lambda e, o=p2, rows=rows, bi=bi: e.tensor_tensor(out=O[rows, bi, :], in0=O[rows, bi, :], in1=o[rows, :], op=ALU.add)),
                          reads=[up2, uO[bi]], writes=[uO[bi]])
                S.add("dve", (lambda e, o=p3, c=c: e.scalar_tensor_tensor(out=Sst[:], in0=Sst[:], scalar=EG[:, iend(c):iend(c) + 1], in1=o,
                                                                         op0=ALU.mult, op1=ALU.add)), reads=[up3, uS, uEG], writes=[uS])
                yield

    def post(h, resf, resb):
        (Of, uOf) = resf["O"]
        for bi, ub in enumerate(DN_BLOCKS):
            t, ut = pp_t.get()
            sm_, usm = pp_s.get()
            S.add("pool", (lambda e, t=t, bi=bi: e.tensor_copy(out=t[:], in_=Of[:, bi, :])),
                  reads=[uOf[bi]], writes=[ut])
            t2, ut2 = pp_t.get()
            S.add("act", (lambda e, t=t, t2=t2, sm_=sm_: e.activation(out=t2[:], in_=t[:], func=AF.Square, accum_out=sm_[:, 0:1])),
                  reads=[ut], writes=[ut2, usm])
            S.add("act", (lambda e, sm_=sm_: e.activation(out=sm_[:, 1:2], in_=sm_[:, 0:1], func=AF.Sqrt, scale=1.0 / 128, bias=B.eps6[:])),
                  reads=[usm], writes=[usm])
            S.add("dve", (lambda e, sm_=sm_: e.reciprocal(out=sm_[:, 1:2], in_=sm_[:, 1:2])), reads=[usm], writes=[usm])
            S.add("dve", (lambda e, t=t, sm_=sm_: e.tensor_scalar(out=t[:], in0=t[:], scalar1=sm_[:, 1:2], scalar2=None, op0=ALU.mult)),
                  reads=[usm, ut], writes=[ut])
            ps, ups = B.PS.get()
            S.add("pe", (lambda e, ps=ps, t=t: e.transpose(ps[:, 0:128], t[:], ident)), reads=[ut, B.u_cst], writes=[ups])
            sz, usz = pp_t.get()
            B.dma(sz[:], Sx["szT"][h * 128:(h + 1) * 128, ub:ub + 128], reads=[Ux["szT"]], writes=[usz])
            S.add("dve", (lambda e, ps=ps, sz=sz, ub=ub: e.scalar_tensor_tensor(out=OD[:, ub:ub + 128], in0=ps[:, 0:128], scalar=B.vs("dnng"),
                                                                              in1=sz[:], op0=ALU.mult, op1=ALU.mult)),
                  reads=[ups, usz, B.u_vec, uOD], writes=[uOD])
        B.dma(Sx["odn"][h * 128:(h + 1) * 128], OD[:], reads=[uOD], writes=[Ux["odn"]])

    nheads = 8 if not (B.stop or "").startswith("dn1") else 2
    if B.stop is None:
        B.stop = ""
    for h0 in range(0, nheads, 2):
        gens = []
        for j in range(2):
            for d in range(2):
                gens.append(chain(h0 + j, d, chains_res[j * 2 + d]))
        active = list(gens)
        while active:
            nxt = []
            for g in active:
                try:
                    next(g)
                    nxt.append(g)
                except StopIteration:
                    pass
            active = nxt
        if ":" in B.stop:
            return
        for j in range(2):
            post(h0 + j, chains_res[j * 2], chains_res[j * 2 + 1])


def hy_dims(n):
    N = 2 * n
    NS1 = n // 128
    NF1 = N // 128
    NFH = NF1 // 2 + 1
    return N, NS1, NF1, NFH


def hyena_tables(n):
    N, NS1, NF1, NFH = hy_dims(n)
    f64 = np.float64
    s1 = np.arange(NS1, dtype=f64)[:, None]
    f1 = np.arange(NFH, dtype=f64)[None, :]
    ang = 2 * np.pi * f1 * s1 / NF1
    F1 = np.concatenate([np.cos(ang), -np.sin(ang)], axis=1)
    s2 = np.arange(128, dtype=f64)[:, None, None]
    f1b = np.arange(NFH, dtype=f64)[None, :, None]
    f2 = np.arange(128, dtype=f64)[None, None, :]
    ang = 2 * np.pi * (f1b + NF1 * f2) * s2 / N
    G = np.stack([np.cos(ang), -np.sin(ang)], axis=2)
    f2c = np.arange(128, dtype=f64)[:, None]
    t2 = np.arange(128, dtype=f64)[None, :]
    th = 2 * np.pi * f2c * t2 / 128
    Winv = np.stack([np.concatenate([np.cos(th), np.sin(th)], 1), np.concatenate([-np.sin(th), np.cos(th)], 1)], axis=1)
    NT1 = NS1
    f1c = np.arange(NFH, dtype=f64)[:, None, None]
    t2c = np.arange(128, dtype=f64)[None, :, None]
    t1c = np.arange(NT1, dtype=f64)[None, None, :]
    ph = 2 * np.pi * f1c * (128 * t1c + t2c) / N
    w = np.full((NFH, 1, 1), 2.0)
    w[0] = 1.0
    w[NFH - 1] = 1.0
    Tout = np.stack([w / N * np.cos(ph), -w / N * np.sin(ph)], axis=2)
    t = np.linspace(0.0, 1.0, n, dtype=np.float32)[:, None]
    omega = (2.0 * math.pi * np.arange(n, dtype=np.float32)[:, None] / n).astype(np.float32)
    bands = np.linspace(1e-4, 15, 16, dtype=np.float32)[None, :]
    z = np.concatenate([t, np.cos(bands * omega), -np.sin(bands * omega)], axis=-1).astype(np.float32)
    negt = np.broadcast_to(-t[:, 0][None, :], (128, n))
    return dict(F1=F1.astype(np.float32), G=G.astype(np.float32), Winv=Winv.astype(np.float32),
                Tout=Tout.astype(np.float32), zT=np.ascontiguousarray(z.T), negt=np.ascontiguousarray(negt, dtype=np.float32))


HY_C = 32


def hyena_phase(B, l, es, seg):
    nc, S, Sx, Ux, I = B.nc, B.S, B.Sx, B.Ux, B.I
    n, base = (L, LT0) if seg == "lat" else (LC, 0)
    N, NS1, NF1, NFH = hy_dims(n)
    NT1 = NS1
    W2 = 2 * NFH
    C = HY_C
    tb = lambda k: I["hy_%s_%s" % (k, seg)]
    hp, uhp = Sx["hp_" + seg], Ux["hp_" + seg]
    NTL = [(i * 512, min(512, n - i * 512)) for i in range((n + 511) // 512)]
    MAGIC = 12582912.0

    with ExitStack() as e1:
        zT = e1.enter_context(nc.sbuf_tensor(uname("zT"), [33, n], F32))
        h1 = e1.enter_context(nc.sbuf_tensor(uname("h1"), [64, n], F32))
        h2 = e1.enter_context(nc.sbuf_tensor(uname("h2"), [64, n], F32))
        negt = e1.enter_context(nc.sbuf_tensor(uname("negt"), [128, n], F32))
        dec = e1.enter_context(nc.sbuf_tensor(uname("dec"), [128, n], F32))
        w1 = e1.enter_context(nc.sbuf_tensor(uname("w1"), [33, 64], F32))
        w2 = e1.enter_context(nc.sbuf_tensor(uname("w2"), [64, 64], F32))
        w3 = e1.enter_context(nc.sbuf_tensor(uname("w3"), [64, 2048], F32))
        fb = e1.enter_context(nc.sbuf_tensor(uname("fb"), [64, 2], F32))
        uz, uh1, uh2, unegt, udec, uw, ufb = Unit(), Unit(), Unit(), Unit(), Unit(), Unit(), Unit()
        tmp_p = TPool(nc, e1, "hyt", [128, 512], F32, 3)
        out_p = TPool(nc, e1, "hyo", [128, 2, 512], F32, 2)
        B.dma(zT[:], tb("zT"), writes=[uz])
        B.dma(negt[:], tb("negt"), writes=[unegt])
        B.dma(w1[:], I["hy_w1"][l], writes=[uw])
        B.dma(w2[:], I["hy_w2"][l], writes=[uw])
        B.dma(w3[:], I["hy_w3"][l], writes=[uw])
        S.add("dve", lambda e: e.tensor_tensor(out=fb[:, 0:1], in0=B.vs("hyf1")[:64], in1=B.vs("hyb1")[:64], op=ALU.mult),
              reads=[B.u_vec], writes=[ufb])
        S.add("dve", lambda e: e.tensor_tensor(out=fb[:, 1:2], in0=B.vs("hyf2")[:64], in1=B.vs("hyb2")[:64], op=ALU.mult),
              reads=[B.u_vec, ufb], writes=[ufb])
        for li, (wt, K, src, usrc, dst, udst, fname) in enumerate(((w1, 33, zT, uz, h1, uh1, "hyf1"), (w2, 64, h1, uh1, h2, uh2, "hyf2"))):
            for (t0, tn) in NTL:
                ps, ups = B.PS.get()
                S.add("pe", (lambda e, ps=ps, wt=wt, K=K, src=src, t0=t0, tn=tn: e.matmul(ps[:64, :tn], wt[:K, :], src[:K, t0:t0 + tn],
                                                                                          start=True, stop=True)),
                      reads=[uw, usrc], writes=[ups])
                x, ux = tmp_p.get()
                k2, uk2 = tmp_p.get()
                S.add("dve", (lambda e, ps=ps, x=x, tn=tn, li=li, fname=fname: e.tensor_scalar(
                    out=x[:64, :tn], in0=ps[:64, :tn], scalar1=B.vs(fname)[:64], scalar2=fb[:, li:li + 1], op0=ALU.mult, op1=ALU.add)),
                    reads=[ups, B.u_vec, ufb], writes=[ux])
                S.add("dve", (lambda e, x=x, k2=k2, tn=tn: e.tensor_scalar(out=k2[:64, :tn], in0=x[:64, :tn], scalar1=1.0 / (2 * math.pi),
                                                                           scalar2=MAGIC, op0=ALU.mult, op1=ALU.add)), reads=[ux], writes=[uk2])
                S.add("dve", (lambda e, k2=k2, tn=tn: e.tensor_scalar(out=k2[:64, :tn], in0=k2[:64, :tn], scalar1=-MAGIC, scalar2=-2 * math.pi,
                                                                      op0=ALU.add, op1=ALU.mult)), reads=[uk2], writes=[uk2])
                S.add("dve", (lambda e, x=x, k2=k2, tn=tn: e.tensor_tensor(out=x[:64, :tn], in0=x[:64, :tn], in1=k2[:64, :tn], op=ALU.add)),
                      reads=[ux, uk2], writes=[ux])
                S.add("act", (lambda e, x=x, dst=dst, t0=t0, tn=tn: e.activation(out=dst[:, t0:t0 + tn], in_=x[:64, :tn], func=AF.Sin)),
                      reads=[ux], writes=[udst])
        for cch in range(8):
            S.add("act", (lambda e, cch=cch: e.activation(out=dec[:], in_=negt[:], func=AF.Exp, scale=B.cs("deltas")[:, cch:cch + 1])),
                  reads=[unegt, B.u_cst], writes=[udec])
            for (t0, tn) in NTL:
                psf, upsf = B.PS.get()
                psb, upsb = B.PS.get()
                for (ps_, ups_, dr) in ((psf, upsf, 0), (psb, upsb, 1)):
                    S.add("pe", (lambda e, ps_=ps_, dr=dr, cch=cch, t0=t0, tn=tn: e.matmul(
                        ps_[:, :tn], w3[:, dr * 1024 + cch * 128:dr * 1024 + (cch + 1) * 128], h2[:, t0:t0 + tn], start=True, stop=True)),
                        reads=[uw, uh2], writes=[ups_])
                a1, ua1 = tmp_p.get()
                S.add("act", (lambda e, a1=a1, psf=psf, tn=tn: e.activation(out=a1[:, :tn], in_=psf[:, :tn], func=AF.Copy)), reads=[upsf], writes=[ua1])
                o, uo = out_p.get()
                for k_, op_ in ((0, ALU.add), (1, ALU.subtract)):
                    S.add("dve", (lambda e, o=o, a1=a1, psb=psb, tn=tn, k_=k_, op_=op_: e.tensor_tensor(out=o[:, k_, :tn], in0=a1[:, :tn], in1=psb[:, :tn], op=op_)),
                          reads=[ua1, upsb, uo], writes=[uo])
                    S.add("pool", (lambda e, o=o, tn=tn, t0=t0, k_=k_: e.tensor_tensor(out=o[:, k_, :tn], in0=o[:, k_, :tn], in1=dec[:, t0:t0 + tn], op=ALU.mult)),
                          reads=[uo, udec], writes=[uo])
                B.dma(hp[:, cch * 128:(cch + 1) * 128, t0:t0 + tn].rearrange("k c t -> c k t"), o[:, :, :tn], reads=[uo], writes=[uhp])
        S.barrier()
    if B.stop and B.stop.endswith("hyf"):
        return

    with ExitStack() as e2:
        cvt_pool = TPool(nc, e2, "cvtst", [128, 1024], F32, 2)

        def cvt(name, shape, src_ap):
            t = e2.enter_context(nc.sbuf_tensor(uname(name), shape, F32R))
            ut = Unit()
            flat = 1
            for s_ in shape[1:]:
                flat *= s_
            step = 1024
            st_pool = cvt_pool
            tf = t[:].rearrange(" ".join(["p"] + ["a%d" % i for i in range(len(shape) - 1)]) + " -> p (" + " ".join("a%d" % i for i in range(len(shape) - 1)) + ")") if len(shape) > 2 else t[:]
            sf = src_ap.rearrange(" ".join(["p"] + ["a%d" % i for i in range(len(shape) - 1)]) + " -> p (" + " ".join("a%d" % i for i in range(len(shape) - 1)) + ")") if len(shape) > 2 else src_ap
            P_ = shape[0]
            for o_ in range(0, flat, step):
                w_ = min(step, flat - o_)
                st, ust = st_pool.get()
                B.dma(st[:P_, :w_], sf[:, o_:o_ + w_], writes=[ust])
                S.add("act", (lambda e, st=st, o_=o_, w_=w_: e.activation(out=tf[:, o_:o_ + w_], in_=st[:P_, :w_], func=AF.Copy)),
                      reads=[ust], writes=[ut])
            return t, ut
        F1 = e2.enter_context(nc.sbuf_tensor(uname("F1"), [NS1, W2], F32))
        uF1 = Unit()
        B.dma(F1[:], tb("F1"), writes=[uF1])
        G, uG = cvt("G", [128, NFH, 2, 128], tb("G"))
        Wv, uWv = cvt("Wv", [128, 2, 256], I["hy_Winv"])
        To, uTo = cvt("To", [NFH, 128, 2, NT1], tb("Tout"))
        xin_p = TPool(nc, e2, "xin", [NS1, C, 128], F32, 1)
        Yb = e2.enter_context(nc.sbuf_tensor(uname("Yb"), [128, NFH, 3, C], F32R))
        Kb = e2.enter_context(nc.sbuf_tensor(uname("Kb"), [128, NFH, 2, C], F32))
        XP = e2.enter_context(nc.sbuf_tensor(uname("XP"), [128, NFH, 2, C], F32R))
        Vb = e2.enter_context(nc.sbuf_tensor(uname("Vb"), [NFH, 128, 2, C], F32R))
        yb = e2.enter_context(nc.sbuf_tensor(uname("yb"), [C, n], F32))
        uYb, uKb, uXP, uVb, uyb = Unit(), Unit(), Unit(), Unit(), Unit()
        tp = TPool(nc, e2, "hytp", [128, 8, C], F32, 4)
        xz_p = TPool(nc, e2, "hyxz", [C, 2, 1024], F32, 1)
        ob_p = TPool(nc, e2, "hyob", [C, 1024], BF16, 2)
        NPB = 512 // W2
        XP2 = e2.enter_context(nc.sbuf_tensor(uname("XP2"), [128, NFH, 2, C], F32R))
        uXP2 = Unit()
        XPS = [(XP, uXP), (XP2, uXP2)]

        def fwd(g):
            XP, uXP = XPS[g % 2]
            c0 = g * C
            for sig in ("p", "m", "zz"):
                xin, uxin = xin_p.get()
                if sig == "zz":
                    src, usrc = Sx["zzT"][c0:c0 + C, base:base + n], Ux["zzT"]
                else:
                    src, usrc = hp[0 if sig == "p" else 1, c0:c0 + C, :], uhp
                B.dma(xin[:], src.rearrange("c (a b) -> a c b", b=128), reads=[usrc], writes=[uxin])
                for cb in range(0, C, NPB):
                    nb = min(NPB, C - cb)
                    ps, ups = B.PS.get()
                    for ci in range(nb):
                        S.add("pe", (lambda e, ps=ps, xin=xin, ci=ci, cb=cb: e.matmul(ps[:, ci * W2:(ci + 1) * W2], xin[:, cb + ci, :], F1[:, :],
                                                                                      start=True, stop=True)),
                              reads=[uxin, uF1], writes=[ups])
                    pv = ps[:, :nb * W2].rearrange("p (c r f) -> p c r f", c=nb, r=2)
                    for r_ in range(2):
                        S.add("act" if r_ == 0 else "dve",
                              (lambda e, pv=pv, r_=r_, cb=cb, nb=nb: (e.activation(out=Yb[:, :, r_, cb:cb + nb], in_=pv[:, :, r_, :].rearrange("p c f -> p f c"), func=AF.Copy)
                                                                       if r_ == 0 else
                                                                       e.tensor_copy(out=Yb[:, :, r_, cb:cb + nb], in_=pv[:, :, r_, :].rearrange("p c f -> p f c")))),
                              reads=[ups], writes=[uYb])
                    S.add("act", (lambda e, pv=pv, cb=cb, nb=nb: e.activation(out=Yb[:, :, 2, cb:cb + nb], in_=pv[:, :, 1, :].rearrange("p c f -> p f c"),
                                                                              func=AF.Copy, scale=-1.0)), reads=[ups], writes=[uYb])
                    yield
                for fq in range(0, NFH, 8):
                    nf = min(8, NFH - fq)
                    ps, ups = B.PS.get()
                    for fi in range(nf):
                        f1_ = fq + fi
                        if sig != "m":
                            S.add("pe", (lambda e, ps=ps, fi=fi, f1_=f1_: e.matmul(ps[:, (fi * 2) * C:(fi * 2 + 1) * C], G[:, f1_, 0, :], Yb[:, f1_, 0, :], start=True, stop=False)),
                                  reads=[uG, uYb], writes=[ups])
                            S.add("pe", (lambda e, ps=ps, fi=fi, f1_=f1_: e.matmul(ps[:, (fi * 2) * C:(fi * 2 + 1) * C], G[:, f1_, 1, :], Yb[:, f1_, 2, :], start=False, stop=True)),
                                  reads=[uG, uYb], writes=[ups])
                        if sig != "p":
                            S.add("pe", (lambda e, ps=ps, fi=fi, f1_=f1_: e.matmul(ps[:, (fi * 2 + 1) * C:(fi * 2 + 2) * C], G[:, f1_, 1, :], Yb[:, f1_, 0, :], start=True, stop=False)),
                                  reads=[uG, uYb], writes=[ups])
                            S.add("pe", (lambda e, ps=ps, fi=fi, f1_=f1_: e.matmul(ps[:, (fi * 2 + 1) * C:(fi * 2 + 2) * C], G[:, f1_, 0, :], Yb[:, f1_, 1, :], start=False, stop=True)),
                                  reads=[uG, uYb], writes=[ups])
                    pv = ps[:, :nf * 2 * C].rearrange("p (f r c) -> p f r c", f=nf, r=2)
                    if sig == "p":
                        S.add("act", (lambda e, pv=pv, fq=fq, nf=nf: e.activation(out=Kb[:, fq:fq + nf, 0, :], in_=pv[:, :, 0, :], func=AF.Copy)), reads=[ups], writes=[uKb])
                    elif sig == "m":
                        S.add("act", (lambda e, pv=pv, fq=fq, nf=nf: e.activation(out=Kb[:, fq:fq + nf, 1, :], in_=pv[:, :, 1, :], func=AF.Copy)), reads=[ups], writes=[uKb])
                    else:
                        (t1, u1), (t2, u2), (t3, u3), (t4, u4) = tp.get(), tp.get(), tp.get(), tp.get()
                        kr, ki = Kb[:, fq:fq + nf, 0, :], Kb[:, fq:fq + nf, 1, :]
                        S.add("dve", (lambda e, pv=pv, t1=t1, nf=nf, kr=kr: e.tensor_tensor(out=t1[:, :nf, :], in0=pv[:, :, 0, :], in1=kr, op=ALU.mult)), reads=[ups, uKb], writes=[u1])
                        S.add("dve", (lambda e, pv=pv, t2=t2, nf=nf, ki=ki: e.tensor_tensor(out=t2[:, :nf, :], in0=pv[:, :, 1, :], in1=ki, op=ALU.mult)), reads=[ups, uKb], writes=[u2])
                        S.add("dve", (lambda e, pv=pv, t3=t3, nf=nf, ki=ki: e.tensor_tensor(out=t3[:, :nf, :], in0=pv[:, :, 0, :], in1=ki, op=ALU.mult)), reads=[ups, uKb], writes=[u3])
                        S.add("dve", (lambda e, pv=pv, t4=t4, nf=nf, kr=kr: e.tensor_tensor(out=t4[:, :nf, :], in0=pv[:, :, 1, :], in1=kr, op=ALU.mult)), reads=[ups, uKb], writes=[u4])
                        S.add("dve", (lambda e, t1=t1, t2=t2, fq=fq, nf=nf: e.tensor_tensor(out=XP[:, fq:fq + nf, 0, :], in0=t1[:, :nf, :], in1=t2[:, :nf, :], op=ALU.subtract)),
                              reads=[u1, u2], writes=[uXP])
                        S.add("dve", (lambda e, t3=t3, t4=t4, fq=fq, nf=nf: e.tensor_tensor(out=XP[:, fq:fq + nf, 1, :], in0=t3[:, :nf, :], in1=t4[:, :nf, :], op=ALU.add)),
                              reads=[u3, u4], writes=[uXP])
                    yield

        def inv(g):
            XP, uXP = XPS[g % 2]
            c0 = g * C
            for cb in range(0, C, 2):
                ps, ups = B.PS.get()
                for ci in range(2):
                    c_ = cb + ci
                    S.add("pe", (lambda e, ps=ps, ci=ci, c_=c_: e.matmul(ps[:NFH, ci * 256:(ci + 1) * 256], XP[:, :, 0, c_], Wv[:, 0, :], start=True, stop=False)),
                          reads=[uXP, uWv], writes=[ups])
                    S.add("pe", (lambda e, ps=ps, ci=ci, c_=c_: e.matmul(ps[:NFH, ci * 256:(ci + 1) * 256], XP[:, :, 1, c_], Wv[:, 1, :], start=False, stop=True)),
                          reads=[uXP, uWv], writes=[ups])
                pv = ps[:NFH, :].rearrange("p (c r t) -> p c r t", c=2, r=2)
                S.add("act" if (cb // 2) % 2 == 0 else "dve",
                      (lambda e, pv=pv, cb=cb: (e.activation(out=Vb[:, :, :, cb:cb + 2], in_=pv.rearrange("p c r t -> p t r c"), func=AF.Copy)
                                                if (cb // 2) % 2 == 0 else e.tensor_copy(out=Vb[:, :, :, cb:cb + 2], in_=pv.rearrange("p c r t -> p t r c")))),
                      reads=[ups], writes=[uVb])
                yield
            TB = 512 // NT1 if NT1 * 16 > 512 else 16
            for tq in range(0, 128, TB):
                ps, ups = B.PS.get()
                for ti in range(TB):
                    t2_ = tq + ti
                    S.add("pe", (lambda e, ps=ps, ti=ti, t2_=t2_: e.matmul(ps[:C, ti * NT1:(ti + 1) * NT1], Vb[:, t2_, 0, :], To[:, t2_, 0, :], start=True, stop=False)),
                          reads=[uVb, uTo], writes=[ups])
                    S.add("pe", (lambda e, ps=ps, ti=ti, t2_=t2_: e.matmul(ps[:C, ti * NT1:(ti + 1) * NT1], Vb[:, t2_, 1, :], To[:, t2_, 1, :], start=False, stop=True)),
                          reads=[uVb, uTo], writes=[ups])
                S.add("act", (lambda e, ps=ps, tq=tq: e.activation(out=yb[:, :].rearrange("c (a b) -> c a b", b=128)[:, :, tq:tq + TB],
                                                                   in_=ps[:C, :TB * NT1].rearrange("c (b a) -> c a b", a=NT1), func=AF.Copy)),
                      reads=[ups], writes=[uyb])
                yield
            for q0 in range(0, n, 1024):
                qn = min(1024, n - q0)
                xz, uxz = xz_p.get()
                B.dma(xz[:, 0, :qn], Sx["x0T"][c0:c0 + C, base + q0:base + q0 + qn], reads=[Ux["x0T"]], writes=[uxz])
                B.dma(xz[:, 1, :qn], Sx["zzT"][c0:c0 + C, base + q0:base + q0 + qn], reads=[Ux["zzT"]], writes=[uxz])
                S.add("dve", (lambda e, xz=xz, q0=q0, qn=qn, c0=c0: e.scalar_tensor_tensor(out=xz[:, 1, :qn], in0=xz[:, 1, :qn], scalar=B.vs("hyb32")[:C, c0 // C:c0 // C + 1],
                                                                                      in1=yb[:, q0:q0 + qn], op0=ALU.mult, op1=ALU.add)),
                      reads=[uxz, uyb, B.u_vec], writes=[uxz])
                ob, uob = ob_p.get()
                S.add("pool", (lambda e, xz=xz, ob=ob, qn=qn: e.tensor_tensor(out=ob[:, :qn], in0=xz[:, 0, :qn], in1=xz[:, 1, :qn], op=ALU.mult)),
                      reads=[uxz], writes=[uob])
                B.dma(Sx["ohy"][c0:c0 + C, base + q0:base + q0 + qn], ob[:, :qn], reads=[uob], writes=[Ux["ohy"]])
                yield

        NG = D // C
        if B.stop and B.stop.endswith("hy1"):
            NG = 2
        for g in range(NG + 1):
            gens = []
            if g < NG:
                gens.append(fwd(g))
            if g >= 1:
                gens.append(inv(g - 1))
            while gens:
                nxt = []
                for gen in gens:
                    try:
                        next(gen)
                        nxt.append(gen)
                    except StopIteration:
                        pass
                gens = nxt
        S.barrier()


def load_wres(B, es, name, src, kparts, ncols, st_pool, eng="pool"):
    nc, S = B.nc, B.S
    w = es.enter_context(nc.sbuf_tensor(uname(name), [128, kparts, ncols], BF16))
    uw = Unit()
    srcv = src.rearrange("(k p) n -> p k n", p=128)
    for k0 in range(0, kparts, 4):
        kn = min(4, kparts - k0)
        for c0 in range(0, ncols, 512):
            st, ust = st_pool.get()
            B.dma(st[:, :kn, :], srcv[:, k0:k0 + kn, c0:c0 + 512], writes=[ust])
            S.add(eng, (lambda e, st=st, k0=k0, kn=kn, c0=c0: e.tensor_copy(out=w[:, k0:k0 + kn, c0:c0 + 512], in_=st[:, :kn, :])),
                  reads=[ust], writes=[uw])
    return w, uw


def merge_phase(B, l, es, xin, u_xin, tiles):
    nc, S, Sx, Ux, I = B.nc, B.S, B.Sx, B.Ux, B.I
    st_pool = TPool(nc, es, "mst", [128, 4, 512], F32, 2)
    Ws = []
    for nm in ("w_proj_dn", "w_proj_hy", "w_proj_lru", "w_out"):
        Ws.append(load_wres(B, es, nm + "_sb", I[nm][l], 8, D, st_pool))
    ot_p = TPool(nc, es, "mot", [128, 3, 8, 512], BF16, 1)
    gt_p = TPool(nc, es, "mgt", [128, 24, 512], BF16, 1)
    x_p = TPool(nc, es, "mx", [128, 8, 512], F32, 2)
    mg_p = TPool(nc, es, "mmg", [128, 8, 512], BF16, 1)
    t_p = TPool(nc, es, "mt", [128, 512], F32, 4)
    for (u0, n) in tiles:
        seg = 1 if u0 == 0 else 0
        ot, uot = ot_p.get()
        for bi, nm in enumerate(("odn", "ohy", "olru")):
            B.dma(ot[:, bi, :, :n], Sx[nm].rearrange("(k p) u -> p k u", p=128)[:, :, u0:u0 + n], reads=[Ux[nm]], writes=[uot])
        gt, ugt = gt_p.get()
        B.dma(gt[:, :, :n], Sx["gT"].rearrange("(k p) u -> p k u", p=128)[:, :, u0:u0 + n], reads=[Ux["gT"]], writes=[ugt])
        x, ux = x_p.get()
        B.dma(x[:, :, :n], xin.rearrange("(k p) u -> p k u", p=128)[:, :, u0:u0 + n], reads=[u_xin], writes=[ux])
        mg, umg = mg_p.get()
        for m in range(8):
            pss = []
            for bi in range(3):
                ps, ups = B.PS.get()
                w, uw = Ws[bi]
                for k in range(8):
                    S.add("pe", (lambda e, ps=ps, w=w, k=k, m=m, bi=bi, ot=ot, n=n: e.matmul(ps[:, :n], w[:, k, m * 128:(m + 1) * 128], ot[:, bi, k, :n],
                                                                                           start=(k == 0), stop=(k == 7))),
                          reads=[uw, uot], writes=[ups])
                pss.append((ps, ups))
            ta, uta = t_p.get()
            tb_, utb = t_p.get()
            S.add("dve", (lambda e, ta=ta, ps=pss[0][0], gt=gt, m=m, n=n: e.tensor_tensor(out=ta[:, :n], in0=ps[:, :n], in1=gt[:, m, :n], op=ALU.mult)),
                  reads=[pss[0][1], ugt], writes=[uta])
            S.add("dve", (lambda e, tb_=tb_, ps=pss[1][0], gt=gt, m=m, n=n: e.tensor_tensor(out=tb_[:, :n], in0=ps[:, :n], in1=gt[:, 8 + m, :n], op=ALU.mult)),
                  reads=[pss[1][1], ugt], writes=[utb])
            S.add("pool", (lambda e, ta=ta, tb_=tb_, n=n: e.tensor_tensor(out=ta[:, :n], in0=ta[:, :n], in1=tb_[:, :n], op=ALU.add)),
                  reads=[uta, utb], writes=[uta])
            tc_, utc = t_p.get()
            S.add("dve", (lambda e, tc_=tc_, ps=pss[2][0], gt=gt, m=m, n=n: e.tensor_tensor(out=tc_[:, :n], in0=ps[:, :n], in1=gt[:, 16 + m, :n], op=ALU.mult)),
                  reads=[pss[2][1], ugt], writes=[utc])
            S.add("pool", (lambda e, ta=ta, tc_=tc_, mg=mg, m=m, n=n: e.tensor_tensor(out=mg[:, m, :n], in0=ta[:, :n], in1=tc_[:, :n], op=ALU.add)),
                  reads=[uta, utc, umg], writes=[umg])
        w, uw = Ws[3]
        for nn in range(8):
            ps, ups = B.PS.get()
            for m in range(8):
                S.add("pe", (lambda e, ps=ps, m=m, nn=nn, mg=mg, n=n: e.matmul(ps[:, :n], w[:, m, nn * 128:(nn + 1) * 128], mg[:, m, :n],
                                                                              start=(m == 0), stop=(m == 7))),
                      reads=[uw, umg], writes=[ups])
            S.add("dve", (lambda e, ps=ps, x=x, nn=nn, n=n, seg=seg: e.scalar_tensor_tensor(out=x[:, nn, :n], in0=ps[:, :n], scalar=B.modcol(2, seg, nn),
                                                                                       in1=x[:, nn, :n], op0=ALU.mult, op1=ALU.add)),
                  reads=[ups, ux, B.u_modv], writes=[ux])
        B.dma(Sx["xA"].rearrange("(k p) u -> p k u", p=128)[:, :, u0:u0 + n], x[:, :, :n], reads=[ux], writes=[Ux["xA"]])


def ffn_phase(B, l, es, tiles):
    nc, S, Sx, Ux, I = B.nc, B.S, B.Sx, B.Ux, B.I
    with_ctx = tiles[0][0] == 0
    esu = ExitStack()
    hT = esu.enter_context(nc.sbuf_tensor(uname("hT2"), [128, 8, U], BF16))
    u_hT = [Unit() for _ in TT]
    with ExitStack() as es3:
        B.norm_to_hT(Sx["xA"], Ux["xA"], 1, hT, u_hT, tiles, es3, tile_ids=[TT.index(t) for t in tiles])
        S.barrier()
    with ExitStack() as es4:
        wst_pool = TPool(nc, es4, "fwst", [128, 8, 128], F32, 2)
        wbf_pool = TPool(nc, es4, "fwbf", [128, 8, 128], BF16, 2)
        up_p = TPool(nc, es4, "fup", [128, 66, 66], F32, 2)
        cp_p = TPool(nc, es4, "fcp", [128, 260], F32, 2)
        acc_p = TPool(nc, es4, "facc", [128, U], F32, 2)
        ab_p = TPool(nc, es4, "fab", [128, U], BF16, 2)
        for (t, ut) in up_p.t:
            S.add("pool", (lambda e, t=t: e.memset(t[:], 0.0)), writes=[ut])
        for (t, ut) in cp_p.t:
            S.add("pool", (lambda e, t=t: e.memset(t[:], 0.0)), writes=[ut])
        for j in range(FH // 128):
            accs = []
            for part in range(2):
                cidx = part * (FH // 128) + j
                wb, uwb = B.load_w_bf16(I["ffn_up"][l][:, cidx * 128:(cidx + 1) * 128], 128, wst_pool, wbf_pool)
                up, uup = up_p.get()
                cp, ucp = cp_p.get()
                for (u0, n) in tiles:
                    ti = TT.index((u0, n))
                    ps, ups = B.mm_tile(wb, uwb, 128, hT, u_hT, ti)
                    if u0 == 0:
                        S.add("act", (lambda e, ps=ps, cp=cp, n=n: e.activation(out=cp[:, 1:1 + n], in_=ps[:, :n], func=AF.Copy)), reads=[ups], writes=[ucp])
                    else:
                        r0 = (u0 - LT0) // 64
                        S.add("act", (lambda e, ps=ps, up=up, r0=r0: e.activation(out=up[:, 1 + r0:9 + r0, 1:65], in_=ps[:, :512].rearrange("p (r c) -> p r c", c=64),
                                                                                  func=AF.Copy)), reads=[ups], writes=[uup])
                acc, uacc = acc_p.get()
                cw = lambda tap, cidx=cidx: B.vs("ffncw", cidx * 9 + tap)
                av = acc[:, LT0:U].rearrange("p (r c) -> p r c", c=64)
                S.add("act", (lambda e, up=up, av=av, cw=cw: e.activation(out=av, in_=up[:, 0:64, 0:64], func=AF.Identity, scale=cw(0))),
                      reads=[uup, B.u_vec], writes=[uacc])
                for tap in range(1, 9):
                    di, dj = tap // 3, tap % 3
                    S.add("dve", (lambda e, up=up, av=av, cw=cw, tap=tap, di=di, dj=dj: e.scalar_tensor_tensor(
                        out=av, in0=up[:, di:di + 64, dj:dj + 64], scalar=cw(tap), in1=av, op0=ALU.mult, op1=ALU.add)),
                        reads=[uup, uacc, B.u_vec], writes=[uacc])
                if with_ctx:
                    S.add("act", (lambda e, cp=cp, acc=acc, cw=cw: e.activation(out=acc[:, 0:256], in_=cp[:, 0:256], func=AF.Identity, scale=cw(3))),
                          reads=[ucp, B.u_vec, uacc], writes=[uacc])
                    for tap in (4, 5):
                        S.add("dve", (lambda e, cp=cp, acc=acc, cw=cw, tap=tap: e.scalar_tensor_tensor(
                            out=acc[:, 0:256], in0=cp[:, tap - 3:tap - 3 + 256], scalar=cw(tap), in1=acc[:, 0:256], op0=ALU.mult, op1=ALU.add)),
                            reads=[ucp, uacc, B.u_vec], writes=[uacc])
                accs.append((acc, uacc))
            (ag, uag), (av_, uav) = accs
            lo = 0 if with_ctx else LT0
            S.add("act", (lambda e, ag=ag, lo=lo: e.activation(out=ag[:, lo:U], in_=ag[:, lo:U], func=AF.Silu)), reads=[uag], writes=[uag])
            ab, uab = ab_p.get()
            S.add("pool", (lambda e, ab=ab: e.memset(ab[:, 0:LT0], 0.0)), writes=[uab])
            S.add("dve", (lambda e, ag=ag, av_=av_, ab=ab, lo=lo: e.tensor_tensor(out=ab[:, lo:U], in0=ag[:, lo:U], in1=av_[:, lo:U], op=ALU.mult)),
                  reads=[uag, uav, uab], writes=[uab])
            B.dma(Sx["actT"][j * 128:(j + 1) * 128], ab[:], reads=[uab], writes=[Ux["actT"]])
        S.barrier()
    esu.close()
    st_pool = TPool(nc, es, "dst", [128, 4, 512], F32, 2)
    wd, uwd = load_wres(B, es, "wdown", I["ffn_down"][l], FH // 128, D, st_pool)
    at_p = TPool(nc, es, "dat", [128, FH // 128, 512], BF16, 2)
    x_p = TPool(nc, es, "dx", [128, 8, 512], F32, 2)
    for (u0, n) in tiles:
        seg = 1 if u0 == 0 else 0
        at, uat = at_p.get()
        B.dma(at[:, :, :n], Sx["actT"].rearrange("(k p) u -> p k u", p=128)[:, :, u0:u0 + n], reads=[Ux["actT"]], writes=[uat])
        x, ux = x_p.get()
        B.dma(x[:, :, :n], Sx["xA"].rearrange("(k p) u -> p k u", p=128)[:, :, u0:u0 + n], reads=[Ux["xA"]], writes=[ux])
        for nn in range(8):
            ps, ups = B.PS.get()
            for j in range(FH // 128):
                S.add("pe", (lambda e, ps=ps, j=j, nn=nn, at=at, n=n: e.matmul(ps[:, :n], wd[:, j, nn * 128:(nn + 1) * 128], at[:, j, :n],
                                                                              start=(j == 0), stop=(j == FH // 128 - 1))),
                      reads=[uwd, uat], writes=[ups])
            S.add("dve", (lambda e, ps=ps, x=x, nn=nn, n=n, seg=seg: e.scalar_tensor_tensor(out=x[:, nn, :n], in0=ps[:, :n], scalar=B.modcol(5, seg, nn),
                                                                                       in1=x[:, nn, :n], op0=ALU.mult, op1=ALU.add)),
                  reads=[ups, ux, B.u_modv], writes=[ux])
        B.dma(Sx["xB"].rearrange("(k p) u -> p k u", p=128)[:, :, u0:u0 + n], x[:, :, :n], reads=[ux], writes=[Ux["xB"]])


def final_phase(B, es):
    nc, S, Sx, Ux = B.nc, B.S, B.Sx, B.Ux
    xp = TPool(nc, es, "fx", [128, 8, 512], F32, 2)
    sqp = TPool(nc, es, "fsq", [128, 8, 512], F32R, 1)
    rsp = TPool(nc, es, "frs", [128, 512], F32, 2)
    for (u0, n) in TT[1:]:
        x, ux = xp.get()
        B.dma(x[:], Sx["xB"].rearrange("(k p) u -> p k u", p=128)[:, :, u0:u0 + n], reads=[Ux["xB"]], writes=[ux])
        sq, usq = sqp.get()
        S.add("act", (lambda e, x=x, sq=sq: e.activation(out=sq[:], in_=x[:], func=AF.Square)), reads=[ux], writes=[usq])
        ps, ups = B.PS.get()
        for k in range(8):
            S.add("pe", (lambda e, k=k, sq=sq, ps=ps: e.matmul(ps[:], B.ones[:], sq[:, k, :], start=(k == 0), stop=(k == 7))),
                  reads=[usq, B.u_ones], writes=[ups])
        rs, urs = rsp.get()
        S.add("act", (lambda e, rs=rs, ps=ps: e.activation(out=rs[:], in_=ps[:], func=AF.Sqrt, scale=1.0 / D, bias=B.eps6[:])), reads=[ups], writes=[urs])
        S.add("dve", (lambda e, rs=rs: e.reciprocal(out=rs[:], in_=rs[:])), reads=[urs], writes=[urs])
        S.add("dve", (lambda e, x=x, rs=rs: e.tensor_tensor(out=x[:], in0=x[:], in1=rs[:].unsqueeze(1).to_broadcast([128, 8, 512]), op=ALU.mult)),
              reads=[urs, ux], writes=[ux])
        S.add("pool", (lambda e, x=x: e.tensor_tensor(out=x[:], in0=x[:], in1=B.vs("fng").unsqueeze(2).to_broadcast([128, 8, 512]), op=ALU.mult)),
              reads=[ux, B.u_vec], writes=[ux])
        t0 = u0 - LT0
        B.dma(B.outT.rearrange("(k p) t -> p k t", p=128)[:, :, t0:t0 + n], x[:], reads=[ux])


_CACHE = {}


def kernel(**inputs):
    inp = {k: np.asarray(v) for k, v in inputs.items()}
    if "nc" not in _CACHE:
        b = Builder(debug=False)
        _CACHE["nc"] = b.build()
    nc = _CACHE["nc"]
    bsz = inp["x"].shape[0]
    in_maps = [prep_inputs(inp, b_) for b_ in range(bsz)]
    res = run_bass_kernel_spmd(nc, in_maps, core_ids=list(range(bsz)))
    out = np.stack([np.ascontiguousarray(np.asarray(r["outT"]).T) for r in res.results])
    return out.astype(np.float32)
```

```python
import math
from contextlib import ExitStack

import numpy as np
import ml_dtypes
import concourse.bass as bass
import concourse.mybir as mybir
from concourse.bass_utils import run_bass_kernel_spmd

F32 = mybir.dt.float32
F32R = mybir.dt.float32r
BF16 = mybir.dt.bfloat16
ALU = mybir.AluOpType
AF = mybir.ActivationFunctionType
AX = mybir.AxisListType

D = 1024
L = 4096
LC = 256
U = 4355
LT0 = 259
UP = 4360
DEPTH = 2
NIN = 12320
OFF_QKV, OFF_Z, OFF_AB, OFF_HY, OFF_LX, OFF_LY, OFF_GATE = 0, 3072, 4096, 4128, 7200, 8224, 9248
FH = 2816
TT = [(0, 256)] + [(LT0 + 512 * i, 512) for i in range(8)]
NCORES = 4


class Unit:
    __slots__ = ("w", "r", "excl", "wd")

    def __init__(self, excl=False):
        self.w = None
        self.r = []
        self.wd = []
        self.excl = excl


class Op:
    __slots__ = ("eng", "fn", "deps", "dma", "need_inc", "sem", "val")

    def __init__(self, eng, fn, dma):
        self.eng = eng
        self.fn = fn
        self.dma = dma
        self.deps = []
        self.need_inc = False
        self.sem = None
        self.val = 0


class Sched:
    EPOCH = 30000
    NDMA = 24

    def __init__(self, nc, es):
        self.nc = nc
        self.es = es
        self.ops = []
        self.engs = {"pe": nc.tensor, "act": nc.scalar, "dve": nc.vector,
                     "pool": nc.gpsimd, "sp": nc.sync}
        self.last = {k: None for k in self.engs}
        self.dmas_since_barrier = []
        self.bar_deps = {k: [] for k in self.engs}
        self.nsem = 0

    def add(self, eng, fn, reads=(), writes=(), dma=False):
        op = Op(eng, fn, dma)
        deps = []
        for u in reads:
            if u.w is not None:
                deps.append(u.w)
            deps.extend(u.wd)
            if u.excl:
                deps.extend(o for o in u.r if o.eng != eng)
        for u in writes:
            if u.w is not None:
                deps.append(u.w)
            deps.extend(u.wd)
            deps.extend(u.r)
        if self.bar_deps[eng]:
            deps.extend(self.bar_deps[eng])
            self.bar_deps[eng] = []
        seen = set()
        for d in deps:
            if d is op or id(d) in seen:
                continue
            if d.eng == "pe" and eng == "pe" and not d.dma and not dma:
                continue
            seen.add(id(d))
            d.need_inc = True
            op.deps.append(d)
        for u in reads:
            if not dma:
                u.r = [o for o in u.r if o.dma or o.eng != eng]
            u.r.append(op)
        for u in writes:
            u.w = op
            u.r = []
            if dma:
                u.wd.append(op)
                if len(u.wd) > 48:
                    u.wd = u.wd[-48:]
            else:
                u.wd = []
        if dma:
            op.need_inc = True
            self.dmas_since_barrier.append(op)
        self.ops.append(op)
        self.last[eng] = op
        return op

    def barrier(self):
        deps = [o for o in self.last.values() if o is not None] + self.dmas_since_barrier
        self.dmas_since_barrier = []
        for k in self.engs:
            self.bar_deps[k] = list(deps)

    def _newsem(self):
        self.nsem += 1
        return self.es.enter_context(self.nc.semaphore("s%d" % self.nsem))

    def emit(self):
        self.barrier()
        self.add("sp", lambda e: None)
        esem, ecount = {}, {}
        dsem = [self._newsem() for _ in range(self.NDMA)]
        dcount = [0] * self.NDMA
        nd = 0
        seen = {k: {} for k in self.engs}
        for op in self.ops:
            e = self.engs[op.eng]
            waits = []
            if op.dma:
                j = nd % self.NDMA
                nd += 1
                if dcount[j]:
                    waits.append((dsem[j], dcount[j]))
                if dcount[j] >= 30000:
                    dsem[j] = self._newsem()
                    dcount[j] = 0
                dcount[j] += 16
                op.sem, op.val = dsem[j], dcount[j]
            elif op.need_inc:
                if op.eng not in esem or ecount[op.eng] >= self.EPOCH:
                    esem[op.eng] = self._newsem()
                    ecount[op.eng] = 0
                ecount[op.eng] += 1
                op.sem, op.val = esem[op.eng], ecount[op.eng]
            for d in op.deps:
                waits.append((d.sem, d.val))
            sn = seen[op.eng]
            for (s, v) in waits:
                if sn.get(id(s), 0) >= v:
                    continue
                sn[id(s)] = v
                e.wait_ge(s, v)
            ins = op.fn(e)
            if op.sem is not None and ins is not None:
                ins.then_inc(op.sem, 16 if op.dma else 1)
        return len(self.ops)


_UID = [0]


def uname(name):
    _UID[0] += 1
    return "%s_%d" % (name, _UID[0])


class TPool:
    def __init__(self, nc, es, name, shape, dtype, n, psum=False):
        self.t = []
        for i in range(n):
            mk = nc.psum_tensor if psum else nc.sbuf_tensor
            self.t.append((es.enter_context(mk(uname(name), shape, dtype)), Unit(excl=psum)))
        self.i = 0

    def get(self):
        r = self.t[self.i % len(self.t)]
        self.i += 1
        return r


def _pk(v):
    return np.ascontiguousarray(v.reshape(-1, 128).T)


VEC_FIELDS = [("g1", 8), ("g2", 8), ("bmod", 48), ("dncw", 96), ("hycw", 72), ("hycb", 24),
              ("lrucw", 32), ("lrucb", 8), ("lba", 16), ("lbx", 16), ("llam", 16), ("hybias", 8),
              ("ffncw", 396), ("fng", 8), ("dnng", 1), ("alog", 1), ("dtb", 1),
              ("hyb1", 1), ("hyf1", 1), ("hyb2", 1), ("hyf2", 1), ("hyb32", 32)]
VOFF = {}
_o = 0
for _n, _w in VEC_FIELDS:
    VOFF[_n] = (_o, _w)
    _o += _w
NV = _o


def build_vec(inp, l):
    v = np.zeros((128, NV), np.float32)

    def put(name, arr):
        o, w = VOFF[name]
        v[:arr.shape[0], o:o + w] = arr.reshape(arr.shape[0], w)
    put("g1", _pk(inp["norm1_g"][l]))
    put("g2", _pk(inp["norm2_g"][l]))
    put("bmod", _pk(inp["b_mod"][l]))
    put("dncw", inp["dn_conv_w"][l].reshape(4, 24, 128).transpose(2, 1, 0))
    put("hycw", inp["hy_conv_w"][l].reshape(3, 24, 128).transpose(2, 1, 0))
    put("hycb", _pk(inp["hy_conv_b"][l]))
    put("lrucw", inp["lru_conv_w"][l].reshape(4, 8, 128).transpose(2, 1, 0))
    put("lrucb", _pk(inp["lru_conv_b"][l]))
    put("lba", inp["lru_b_a"][l].reshape(2, 8, 128).transpose(2, 0, 1))
    put("lbx", inp["lru_b_x"][l].reshape(2, 8, 128).transpose(2, 0, 1))
    put("llam", inp["lru_lambda"][l].reshape(2, 8, 128).transpose(2, 0, 1))
    put("hybias", _pk(inp["hy_bias"][l]))
    put("ffncw", inp["ffn_conv_w"][l].reshape(9, 44, 128).transpose(2, 1, 0))
    put("fng", _pk(inp["final_norm_g"]))
    put("dnng", inp["dn_norm_g"][l].reshape(128, 1))
    al = np.zeros((40, 1), np.float32)
    db = np.zeros((40, 1), np.float32)
    for d in range(2):
        al[d * 32:d * 32 + 8, 0] = inp["dn_a_log"][l][d]
        db[d * 32:d * 32 + 8, 0] = inp["dn_dt_bias"][l][d]
    put("alog", al)
    put("dtb", db)
    put("hyb1", inp["hy_b1"][l].reshape(64, 1))
    put("hyf1", inp["hy_f1"][l].reshape(64, 1))
    put("hyb2", inp["hy_b2"][l].reshape(64, 1))
    put("hyf2", inp["hy_f2"][l].reshape(64, 1))
    put("hyb32", np.ascontiguousarray(inp["hy_bias"][l].reshape(32, 32).T))
    return v


CST_FIELDS = [("ident", 128), ("lowi", 128), ("lows", 128), ("uppi", 128), ("upps", 128),
              ("deltas", 8)]
COFF = {}
_o = 0
for _n, _w in CST_FIELDS:
    COFF[_n] = (_o, _w)
    _o += _w
NCST = _o


def build_cst():
    c = np.zeros((128, NCST), np.float32)
    i = np.arange(128)[:, None]
    j = np.arange(128)[None, :]
    same = (i // 64) == (j // 64)

    def put(name, arr):
        o, w = COFF[name]
        c[:, o:o + w] = arr
    put("ident", (i == j).astype(np.float32))
    put("lowi", ((i >= j) & same).astype(np.float32))
    put("lows", ((i > j) & same).astype(np.float32))
    put("uppi", ((i <= j) & same).astype(np.float32))
    put("upps", ((i < j) & same).astype(np.float32))
    lt = math.log(1e-2)
    deltas = np.abs(np.linspace(lt / 1.5, lt / 0.3, 1024, dtype=np.float32))
    put("deltas", _pk(deltas))
    return c


def build_rmask():
    m = np.ones((2, U), np.float32)
    starts = list(range(0, 256, 64)) + list(range(LT0, U, 64))
    for s in starts:
        m[0, s] = 0.0
        m[1, s + 63] = 0.0
    out = np.zeros((40, U), np.float32)
    out[0:8] = m[0]
    out[32:40] = m[1]
    return out


class Builder:
    def __init__(self, debug=False, stop=None, only=None, feed=()):
        self.debug = debug
        self.stop = stop
        self.only = only
        self.feed = set(feed)
        self.nc = bass.Bass("TRN2", target_bir_lowering=False)
        self.dbg_names = []

    def din(self, name, shape, dt=F32):
        return self.nc.dram_tensor(name, list(shape), dt, kind="ExternalInput").ap()

    def dscr(self, name, shape, dt=F32):
        kind = "ExternalOutput" if self.debug else "Internal"
        if name in self.feed:
            kind = "ExternalInput"
        elif self.debug:
            self.dbg_names.append(name)
        return self.nc.dram_tensor(name, list(shape), dt, kind=kind).ap()

    def vs(self, name, k=None):
        o, w = VOFF[name]
        if k is None:
            return self.vec[:, o:o + w]
        return self.vec[:, o + k:o + k + 1]

    def cs(self, name):
        o, w = COFF[name]
        return self.cst[:, o:o + w]

    def dma(self, out, in_, reads=(), writes=(), eng="sp"):
        return self.S.add(eng, lambda e: e.dma_start(out=out, in_=in_), reads=reads, writes=writes, dma=True)

    def build(self):
        nc = self.nc
        I = {}
        I["xT0"] = self.din("xT0", [D, U])
        I["cc"] = self.din("cc", [128, 16])
        I["vec"] = self.din("vec", [DEPTH, 128, NV])
        I["cst"] = self.din("cst", [128, NCST])
        I["rmask"] = self.din("rmask", [40, U])
        I["hy_w1"] = self.din("hy_w1", [DEPTH, 33, 64])
        I["hy_w2"] = self.din("hy_w2", [DEPTH, 64, 64])
        I["hy_w3"] = self.din("hy_w3", [DEPTH, 64, 2048])
        if self.only == "mg":
            I["w_mod"] = self.din("w_mod", [DEPTH, D, 6 * D])
            for n in ("w_proj_dn", "w_proj_hy", "w_proj_lru", "w_out"):
                I[n] = self.din(n, [DEPTH, D, D])
            I["ffn_up"] = self.din("ffn_up", [DEPTH, D, 2 * FH])
            I["ffn_down"] = self.din("ffn_down", [DEPTH, FH, D])
        if self.only:
            self.I = I
            return self.build2()
        I["w_mod"] = self.din("w_mod", [DEPTH, D, 6 * D])
        I["w_in"] = self.din("w_in", [DEPTH, D, NIN])
        I["lru_w_a"] = self.din("lru_w_a", [DEPTH, 2, 8, 128, 128])
        I["lru_w_x"] = self.din("lru_w_x", [DEPTH, 2, 8, 128, 128])
        for n in ("w_proj_dn", "w_proj_hy", "w_proj_lru", "w_out"):
            I[n] = self.din(n, [DEPTH, D, D])
        I["ffn_up"] = self.din("ffn_up", [DEPTH, D, 2 * FH])
        I["ffn_down"] = self.din("ffn_down", [DEPTH, FH, D])
        self.I = I
        return self.build2()

    def build2(self):
        nc, I = self.nc, self.I
        self.outT = nc.dram_tensor("outT", [D, L], F32, kind="ExternalOutput").ap()
        Sx = {}
        Sx["qT"] = self.dscr("qT", [8, 128, U])
        Sx["kT"] = self.dscr("kT", [8, 128, U])
        Sx["vT"] = self.dscr("vT", [8, 128, U])
        Sx["szT"] = self.dscr("szT", [D, U])
        Sx["gab"] = self.dscr("gab", [3, 40, U])
        Sx["zzT"] = self.dscr("zzT", [D, U])
        Sx["x0T"] = self.dscr("x0T", [D, U])
        Sx["olru"] = self.dscr("olru", [D, U], BF16)
        Sx["odn"] = self.dscr("odn", [D, U], BF16)
        Sx["ohy"] = self.dscr("ohy", [D, U], BF16)
        Sx["gT"] = self.dscr("gT", [3 * D, U], BF16)
        Sx["xA"] = self.dscr("xA", [D, U])
        Sx["xB"] = self.dscr("xB", [D, U])
        Sx["actT"] = self.dscr("actT", [FH, U], BF16)
        Sx["hp_lat"] = self.dscr("hp_lat", [2, D, L])
        Sx["hp_ctx"] = self.dscr("hp_ctx", [2, D, LC])
        for seg, n_ in (("lat", L), ("ctx", LC)):
            N_, NS1_, NF1_, NFH_ = hy_dims(n_)
            I["hy_zT_" + seg] = self.din("hy_zT_" + seg, [33, n_])
            I["hy_negt_" + seg] = self.din("hy_negt_" + seg, [128, n_])
            I["hy_F1_" + seg] = self.din("hy_F1_" + seg, [NS1_, 2 * NFH_])
            I["hy_G_" + seg] = self.din("hy_G_" + seg, [128, NFH_, 2, 128])
            I["hy_Tout_" + seg] = self.din("hy_Tout_" + seg, [NFH_, 128, 2, NS1_])
        I["hy_Winv"] = self.din("hy_Winv", [128, 2, 256])
        self.Sx = Sx
        self.Ux = {k: Unit() for k in Sx}

        with ExitStack() as es:
            self.es = es
            self.S = Sched(nc, es)
            S = self.S
            self.cst = es.enter_context(nc.sbuf_tensor("cst_sb", [128, NCST], F32))
            self.vec = es.enter_context(nc.sbuf_tensor("vec_sb", [128, NV], F32))
            self.ones = es.enter_context(nc.sbuf_tensor("ones", [128, 128], F32R))
            self.sc = es.enter_context(nc.sbuf_tensor("sc", [128, 16], F32))
            self.modv = es.enter_context(nc.sbuf_tensor("modv", [128, 48, 2], F32))
            self.der = es.enter_context(nc.sbuf_tensor("der", [128, 2, 2, 8], F32))
            self.u_cst, self.u_vec, self.u_ones, self.u_sc = Unit(), Unit(), Unit(), Unit()
            self.u_modv, self.u_der = Unit(), Unit()
            self.PS = TPool(nc, es, "ps", [128, 512], F32, 8, psum=True)
            self.dma(self.cst[:], I["cst"], writes=[self.u_cst])
            self.eps6 = es.enter_context(nc.sbuf_tensor("eps6", [128, 1], F32))
            S.add("dve", lambda e: e.memset(self.eps6[:], 1e-6), writes=[self.u_cst])
            self.one1 = es.enter_context(nc.sbuf_tensor("one1", [128, 1], F32))
            S.add("dve", lambda e: e.memset(self.one1[:], 1.0), writes=[self.u_cst])
            ones32 = es.enter_context(nc.sbuf_tensor("ones32", [128, 128], F32))
            u_o32 = Unit()
            S.add("dve", lambda e: e.memset(ones32[:], 1.0), writes=[u_o32])
            S.add("act", lambda e: e.activation(out=self.ones[:], in_=ones32[:], func=AF.Copy), reads=[u_o32], writes=[self.u_ones])
            self.ones_bf = es.enter_context(nc.sbuf_tensor("ones_bf", [128, 128], BF16))
            S.add("act", lambda e: e.activation(out=self.ones_bf[:], in_=ones32[:], func=AF.Copy), reads=[u_o32], writes=[self.u_ones])
            self.dma(self.sc[:], I["cc"], writes=[self.u_sc])
            S.add("act", lambda e: e.activation(out=self.sc[:], in_=self.sc[:], func=AF.Silu),
                  reads=[self.u_sc], writes=[self.u_sc])
            xin = I["xT0"]
            u_xin = Unit()
            for l in range(DEPTH):
                self.l = l
                self.dma(self.vec[:], I["vec"][l], writes=[self.u_vec])
                if self.only == "dn":
                    with ExitStack() as es2:
                        dn_phase(self, l, es2)
                        S.barrier()
                    break
                if self.only == "mg":
                    self.phase_mod(l)
                    with ExitStack() as es2:
                        merge_phase(self, l, es2, xin, u_xin, TT)
                        S.barrier()
                    with ExitStack() as es2:
                        ffn_phase(self, l, es2, TT)
                        S.barrier()
                    with ExitStack() as es2:
                        final_phase(self, es2)
                        S.barrier()
                    break
                if self.only == "hy":
                    for seg in (("lat", "ctx") if "ctx" in self.stop else ("lat",)):
                        with ExitStack() as es2:
                            hyena_phase(self, l, es2, seg)
                            S.barrier()
                    break
                self.phase_mod(l)
                if self.stop == "mod":
                    break
                with ExitStack() as es2:
                    self.phase_mixer_pre(l, es2, xin, u_xin)
                    S.barrier()
                if self.stop and self.stop.startswith("pre"):
                    break
                with ExitStack() as es2:
                    dn_phase(self, l, es2)
                    S.barrier()
                if self.stop and self.stop.startswith("dn"):
                    break
                for seg in (("lat", "ctx") if l < DEPTH - 1 else ("lat",)):
                    with ExitStack() as es2:
                        hyena_phase(self, l, es2, seg)
                        S.barrier()
                if self.stop and self.stop.startswith("hy"):
                    break
                tiles = TT if l < DEPTH - 1 else TT[1:]
                with ExitStack() as es2:
                    merge_phase(self, l, es2, xin, u_xin, tiles)
                    S.barrier()
                if self.stop and self.stop.startswith("mg"):
                    break
                with ExitStack() as es2:
                    ffn_phase(self, l, es2, tiles)
                    S.barrier()
                xin, u_xin = Sx["xB"], self.Ux["xB"]
                if self.stop and self.stop.startswith("ffn"):
                    break
                if l == DEPTH - 1:
                    with ExitStack() as es2:
                        final_phase(self, es2)
                        S.barrier()
            n = S.emit()
        self.n_ops = n
        return nc

    def phase_mod(self, l):
        nc, S = self.nc, self.S
        with ExitStack() as es:
            wm_pool = TPool(nc, es, "wm", [128, 8, 512], F32, 2)
            ps, ups = self.PS.get()
            for pn in range(12):
                wm, uwm = wm_pool.get()
                self.dma(wm[:], self.I["w_mod"][l][:, pn * 512:(pn + 1) * 512].rearrange("(k p) n -> p k n", p=128),
                         writes=[uwm])
                for cc in range(4):
                    c = pn * 4 + cc
                    for k in range(8):
                        S.add("pe", (lambda e, c=c, cc=cc, k=k, wm=wm: e.matmul(
                            ps[:, 2 * c:2 * c + 2], wm[:, k, cc * 128:(cc + 1) * 128], self.sc[:, 2 * k:2 * k + 2],
                            start=(k == 0), stop=(k == 7))), reads=[uwm, self.u_sc], writes=[ups])
            bm = self.vs("bmod")
            for s in range(2):
                S.add("dve", (lambda e, s=s: e.tensor_tensor(out=self.modv[:, :, s], in0=ps[:, s:96:2], in1=bm, op=ALU.add)),
                      reads=[ups, self.u_vec], writes=[self.u_modv])
            for w, (gname, j) in enumerate((("g1", 1), ("g2", 4))):
                for s in range(2):
                    S.add("dve", (lambda e, w=w, s=s, j=j, gname=gname: e.scalar_tensor_tensor(
                        out=self.der[:, w, s, :], in0=self.modv[:, j * 8:(j + 1) * 8, s], scalar=1.0, in1=self.vs(gname),
                        op0=ALU.add, op1=ALU.mult)), reads=[self.u_modv, self.u_vec], writes=[self.u_der])
            S.barrier()

    def modcol(self, j, s, k):
        return self.modv[:, j * 8 + k, s:s + 1]

    def norm_to_hT(self, xsrc, u_xsrc, which, hT, u_hT, tiles, es, tile_ids=None):
        nc, S = self.nc, self.S
        xp = TPool(nc, es, "nx", [128, 8, 512], F32, 2)
        sqp = TPool(nc, es, "nsq", [128, 8, 512], F32R, 1)
        rsp = TPool(nc, es, "nrs", [128, 512], F32, 2)
        shj = 0 if which == 0 else 3
        for ti_, (u0, n) in enumerate(tiles):
            ti = tile_ids[ti_] if tile_ids is not None else ti_
            seg = 1 if u0 == 0 else 0
            x, ux = xp.get()
            self.dma(x[:, :, :n], xsrc.rearrange("(k p) t -> p k t", p=128)[:, :, u0:u0 + n], reads=[u_xsrc], writes=[ux])
            sq, usq = sqp.get()
            S.add("act", (lambda e, x=x, sq=sq, n=n: e.activation(out=sq[:, :, :n], in_=x[:, :, :n], func=AF.Square)),
                  reads=[ux], writes=[usq])
            ps, ups = self.PS.get()
            for k in range(8):
                S.add("pe", (lambda e, k=k, sq=sq, ps=ps, n=n: e.matmul(ps[:, :n], self.ones[:], sq[:, k, :n],
                                                                         start=(k == 0), stop=(k == 7))),
                      reads=[usq, self.u_ones], writes=[ups])
            rs, urs = rsp.get()
            S.add("act", (lambda e, rs=rs, ps=ps, n=n: e.activation(out=rs[:, :n], in_=ps[:, :n], func=AF.Sqrt, scale=1.0 / D, bias=self.eps6[:])),
                  reads=[ups], writes=[urs])
            S.add("dve", (lambda e, rs=rs, n=n: e.reciprocal(out=rs[:, :n], in_=rs[:, :n])), reads=[urs], writes=[urs])
            S.add("dve", (lambda e, x=x, rs=rs, n=n: e.tensor_tensor(out=x[:, :, :n], in0=x[:, :, :n],
                                                                      in1=rs[:, :n].unsqueeze(1).to_broadcast([128, 8, n]), op=ALU.mult)),
                  reads=[urs, ux], writes=[ux])
            for k in range(8):
                S.add("act", (lambda e, k=k, x=x, n=n, u0=u0, seg=seg: e.activation(
                    out=hT[:, k, u0:u0 + n], in_=x[:, k, :n], func=AF.Identity,
                    scale=self.der[:, which, seg, k:k + 1], bias=self.modcol(shj, seg, k))),
                    reads=[ux, self.u_der, self.u_modv], writes=[u_hT[ti]])

    def load_w_bf16(self, wsrc, M, wst_pool, wbf_pool, eng="pool"):
        S = self.S
        wst, uws = wst_pool.get()
        self.dma(wst[:, :, :M], wsrc.rearrange("(k p) m -> p k m", p=128), writes=[uws])
        wb, uwb = wbf_pool.get()
        S.add(eng, (lambda e, wb=wb, wst=wst, M=M: e.tensor_copy(out=wb[:, :, :M], in_=wst[:, :, :M])),
              reads=[uws], writes=[uwb])
        return wb, uwb

    def mm_tile(self, wb, uwb, M, hT, u_hT, ti):
        S = self.S
        u0, n = TT[ti]
        ps, ups = self.PS.get()
        for k in range(8):
            S.add("pe", (lambda e, k=k, ps=ps, wb=wb: e.matmul(ps[:M, :n], wb[:, k, :M], hT[:, k, u0:u0 + n],
                                                                start=(k == 0), stop=(k == 7))),
                  reads=[uwb, u_hT[ti]], writes=[ups])
        return ps, ups

    def conv(self, pp, upp, cw, ntap, bias, out_ap, uout):
        S = self.S
        acc, uacc = self._acc, self._uacc
        S.add("act", (lambda e: e.activation(out=acc[:, :U], in_=pp[:, 0:U], func=AF.Identity,
                                             scale=cw(0), bias=(bias if bias is not None else 0.0))),
              reads=[upp, self.u_vec], writes=[uacc])
        for j in range(1, ntap):
            last = j == ntap - 1
            o = out_ap if last else acc[:, :U]
            uo = uout if last else uacc
            S.add("dve", (lambda e, j=j, o=o: e.scalar_tensor_tensor(out=o, in0=pp[:, j:j + U], scalar=cw(j),
                                                                    in1=acc[:, :U], op0=ALU.mult, op1=ALU.add)),
                  reads=[upp, uacc, self.u_vec], writes=[uo])

    def phase_mixer_pre(self, l, es, xin, u_xin):
        nc, S, I, Sx, Ux = self.nc, self.S, self.I, self.Sx, self.Ux
        hT = es.enter_context(nc.sbuf_tensor(uname("hT"), [128, 8, U], BF16))
        u_hT = [Unit() for _ in TT]
        with ExitStack() as es3:
            self.norm_to_hT(xin, u_xin, 0, hT, u_hT, TT, es3)
            S.barrier()
        if self.stop == "norm":
            dbg = self.nc.dram_tensor(uname("dbg_hT"), [128, 8, U], BF16, kind="ExternalOutput").ap()
            self.dma(dbg, hT[:], reads=u_hT)
            return
        WK = TPool(nc, es, "wk", [128, UP], F32, 4)
        PP = TPool(nc, es, "pp", [128, UP], F32, 1)
        wst_pool = TPool(nc, es, "wst", [128, 8, 128], F32, 2)
        wbf_pool = TPool(nc, es, "wbf", [128, 8, 128], BF16, 2)
        rsp = TPool(nc, es, "rs", [128, 512], F32, 2)
        gtp = TPool(nc, es, "gt", [128, 512], BF16, 4)
        ztp = TPool(nc, es, "zt", [128, 512], F32, 3)
        lwp = TPool(nc, es, "lw", [128, 128], F32, 2)
        lwr = TPool(nc, es, "lwr", [128, 128], BF16, 2)
        sm = es.enter_context(nc.sbuf_tensor(uname("sm"), [128, 40], F32))
        u_sm = Unit()
        pp, upp = PP.get()
        S.add("pool", lambda e: e.memset(pp[:], 0.0), writes=[upp])
        sqb = es.enter_context(nc.sbuf_tensor(uname("sqb"), [128, UP], BF16))
        usqb = Unit()
        self._cnt = 0

        def evac_copy(ps, ups, out_ap, uo, M=128, n=512):
            self._cnt += 1
            if self._cnt % 2:
                S.add("act", (lambda e: e.activation(out=out_ap, in_=ps[:M, :n], func=AF.Copy)), reads=[ups], writes=[uo])
            else:
                S.add("dve", (lambda e: e.tensor_copy(out=out_ap, in_=ps[:M, :n])), reads=[ups], writes=[uo])

        def proj_to_pp(col0):
            wb, uwb = self.load_w_bf16(I["w_in"][l][:, col0:col0 + 128], 128, wst_pool, wbf_pool)
            for ti, (u0, n) in enumerate(TT):
                ps, ups = self.mm_tile(wb, uwb, 128, hT, u_hT, ti)
                evac_copy(ps, ups, pp[:, u0 + 1:u0 + 1 + n], upp, 128, n)

        def proj_act(col0, func, out_tile_fn, bias=None):
            wb, uwb = self.load_w_bf16(I["w_in"][l][:, col0:col0 + 128], 128, wst_pool, wbf_pool)
            for ti, (u0, n) in enumerate(TT):
                ps, ups = self.mm_tile(wb, uwb, 128, hT, u_hT, ti)
                o, uo = out_tile_fn(ti, u0, n)
                S.add("act", (lambda e, ps=ps, o=o, n=n: e.activation(out=o, in_=ps[:, :n], func=func)),
                      reads=[ups], writes=[uo])

        S.add("act", lambda e: e.activation(out=sm[:40, 0:1], in_=self.vs("alog")[:40], func=AF.Exp),
              reads=[self.u_vec], writes=[u_sm])
        S.add("dve", lambda e: e.tensor_scalar(out=sm[:40, 0:1], in0=sm[:40, 0:1], scalar1=-1.0, scalar2=None, op0=ALU.mult),
              reads=[u_sm], writes=[u_sm])
        S.add("act", lambda e: e.activation(out=sm[:, 8:24], in_=self.vs("llam"), func=AF.Exp, scale=-1.0),
              reads=[self.u_vec, u_sm], writes=[u_sm])
        S.add("act", lambda e: e.activation(out=sm[:, 8:24], in_=sm[:, 8:24], func=AF.Ln, bias=1.0),
              reads=[u_sm], writes=[u_sm])
        S.add("dve", lambda e: e.tensor_scalar(out=sm[:, 8:24], in0=sm[:, 8:24], scalar1=-8.0, scalar2=None, op0=ALU.mult),
              reads=[u_sm], writes=[u_sm])
        S.add("dve", lambda e: e.tensor_scalar(out=sm[:, 24:40], in0=sm[:, 8:24], scalar1=2.0, scalar2=None, op0=ALU.mult),
              reads=[u_sm], writes=[u_sm])

        def z_chunk(c):
            wb, uwb = self.load_w_bf16(I["w_in"][l][:, OFF_Z + c * 128:OFF_Z + (c + 1) * 128], 128, wst_pool, wbf_pool)
            for ti, (u0, n) in enumerate(TT):
                ps, ups = self.mm_tile(wb, uwb, 128, hT, u_hT, ti)
                t, ut = ztp.get()
                S.add("act", (lambda e, ps=ps, t=t, n=n: e.activation(out=t[:, :n], in_=ps[:, :n], func=AF.Silu)),
                      reads=[ups], writes=[ut])
                self.dma(Sx["szT"][c * 128:(c + 1) * 128, u0:u0 + n], t[:, :n], reads=[ut], writes=[Ux["szT"]])

        def gate_chunk(c):
            wb, uwb = self.load_w_bf16(I["w_in"][l][:, OFF_GATE + c * 128:OFF_GATE + (c + 1) * 128], 128, wst_pool, wbf_pool)
            for ti, (u0, n) in enumerate(TT):
                ps, ups = self.mm_tile(wb, uwb, 128, hT, u_hT, ti)
                t, ut = gtp.get()
                S.add("act", (lambda e, ps=ps, t=t, n=n: e.activation(out=t[:, :n], in_=ps[:, :n], func=AF.Sigmoid)),
                      reads=[ups], writes=[ut])
                self.dma(Sx["gT"][c * 128:(c + 1) * 128, u0:u0 + n], t[:, :n], reads=[ut], writes=[Ux["gT"]])
        fillers = [(z_chunk, c) for c in range(8)] + [(gate_chunk, c) for c in range(24)]

        def fill():
            if fillers:
                fn, c = fillers.pop(0)
                fn(c)

        def dn_front(c):
            proj_to_pp(OFF_QKV + c * 128)
            (q, uq) = WK.get()
            self._acc, self._uacc = q, uq
            self.conv(pp, upp, (lambda j, c=c: self.vs("dncw", c * 4 + j)), 4, None, q[:, :U], uq)
            return q, uq

        def dn_tail(c, q, uq):
            w3, h = c // 8, c % 8
            S.add("act", (lambda e, q=q: e.activation(out=q[:, :U], in_=q[:, :U], func=AF.Silu)), reads=[uq], writes=[uq])
            if w3 < 2:
                (sq, usq) = (sqb, usqb)
                S.add("act", (lambda e, q=q, sq=sq: e.activation(out=sq[:, :U], in_=q[:, :U], func=AF.Square)),
                      reads=[uq], writes=[usq])
                for (u0, n) in TT:
                    ps, ups = self.PS.get()
                    S.add("pe", (lambda e, ps=ps, sq=sq, u0=u0, n=n: e.matmul(ps[:, :n], self.ones_bf[:], sq[:, u0:u0 + n],
                                                                               start=True, stop=True)),
                          reads=[usq, self.u_ones], writes=[ups])
                    rs, urs = rsp.get()
                    S.add("act", (lambda e, rs=rs, ps=ps, n=n: e.activation(out=rs[:, :n], in_=ps[:, :n], func=AF.Sqrt, bias=self.eps6[:])),
                          reads=[ups], writes=[urs])
                    S.add("dve", (lambda e, rs=rs, n=n: e.reciprocal(out=rs[:, :n], in_=rs[:, :n])), reads=[urs], writes=[urs])
                    S.add("dve", (lambda e, q=q, rs=rs, u0=u0, n=n: e.tensor_tensor(out=q[:, u0:u0 + n], in0=q[:, u0:u0 + n],
                                                                                    in1=rs[:, :n], op=ALU.mult)),
                          reads=[urs, uq], writes=[uq])
            dst = (Sx["qT"], Sx["kT"], Sx["vT"])[w3]
            ud = (Ux["qT"], Ux["kT"], Ux["vT"])[w3]
            self.dma(dst[h], q[:, :U], reads=[uq], writes=[ud])
        prev = None
        for c in range(24):
            cur = dn_front(c)
            if prev is not None:
                dn_tail(c - 1, *prev)
                fill()
            prev = cur
        dn_tail(23, *prev)
        fill()
        if self.stop == "pre1":
            return
        if self.stop == "pre2":
            return
        rm, urm = WK.get()
        self.dma(rm[:40, :U], I["rmask"], writes=[urm])
        G, uG = WK.get()
        LNB, uLNB = WK.get()
        for which, (dst, udst) in enumerate(((G, uG), (LNB, uLNB))):
            wst, uws = wst_pool.get()
            S.add("pool", (lambda e, wst=wst: e.memset(wst[:], 0.0)), writes=[uws])
            for d in range(2):
                c0 = OFF_AB + d * 16 + which * 8
                self.dma(wst[:, :, d * 32:d * 32 + 8], I["w_in"][l][:, c0:c0 + 8].rearrange("(k p) m -> p k m", p=128),
                         reads=[uws], writes=[uws])
            wb, uwb = wbf_pool.get()
            S.add("pool", (lambda e, wb=wb, wst=wst: e.tensor_copy(out=wb[:, :, :40], in_=wst[:, :, :40])), reads=[uws], writes=[uwb])
            for ti, (u0, n) in enumerate(TT):
                ps, ups = self.mm_tile(wb, uwb, 40, hT, u_hT, ti)
                evac_copy(ps, ups, dst[:40, u0:u0 + n], udst, 40, n)
        for (t_, ut_) in ((G, uG), (LNB, uLNB)):
            S.add("pool", (lambda e, t_=t_: e.memset(t_[:40, 256:259], 0.0)), reads=[ut_], writes=[ut_])
        S.add("act", lambda e: e.activation(out=G[:40, :U], in_=G[:40, :U], func=AF.Exp, bias=self.vs("dtb")[:40]),
              reads=[uG, self.u_vec], writes=[uG])
        S.add("act", lambda e: e.activation(out=G[:40, :U], in_=G[:40, :U], func=AF.Ln, bias=1.0), reads=[uG], writes=[uG])
        S.add("dve", lambda e: e.tensor_scalar(out=G[:40, :U], in0=G[:40, :U], scalar1=sm[:40, 0:1], scalar2=None, op0=ALU.mult),
              reads=[uG, u_sm], writes=[uG])
        S.add("act", lambda e: e.activation(out=LNB[:40, :U], in_=LNB[:40, :U], func=AF.Exp, scale=-1.0), reads=[uLNB], writes=[uLNB])
        S.add("act", lambda e: e.activation(out=LNB[:40, :U], in_=LNB[:40, :U], func=AF.Ln, bias=1.0), reads=[uLNB], writes=[uLNB])
        S.add("dve", lambda e: e.tensor_scalar(out=LNB[:40, :U], in0=LNB[:40, :U], scalar1=-1.0, scalar2=None, op0=ALU.mult),
              reads=[uLNB], writes=[uLNB])
        GC, uGC = WK.get()
        S.add("pool", lambda e: e.memset(GC[:40, :U], 0.0), writes=[uGC])
        S.add("dve", lambda e: e.tensor_tensor_scan(out=GC[0:8, :U], data0=rm[0:8, :U], data1=G[0:8, :U], initial=0.0,
                                                    op0=ALU.mult, op1=ALU.add), reads=[urm, uG, uGC], writes=[uGC])
        S.add("dve", lambda e: e.tensor_tensor_scan(out=GC[32:40, U - 1::-1], data0=rm[32:40, U - 1::-1], data1=G[32:40, U - 1::-1],
                                                    initial=0.0, op0=ALU.mult, op1=ALU.add), reads=[urm, uG, uGC], writes=[uGC])
        self.dma(Sx["gab"][0], GC[:40, :U], reads=[uGC], writes=[Ux["gab"]])
        self.dma(Sx["gab"][2], LNB[:40, :U], reads=[uLNB], writes=[Ux["gab"]])
        S.add("dve", lambda e: e.tensor_tensor(out=G[:40, :U], in0=GC[:40, :U], in1=LNB[:40, :U], op=ALU.add),
              reads=[uGC, uLNB, uG], writes=[uG])
        self.dma(Sx["gab"][1], G[:40, :U], reads=[uG], writes=[Ux["gab"]])
        if self.stop == "pre3":
            return
        for c in range(8):
            tl = []
            for part in (1, 2, 0):
                cidx = part * 8 + c
                proj_to_pp(OFF_HY + cidx * 128)
                t, ut = WK.get()
                self._acc, self._uacc = t, ut
                self.conv(pp, upp, (lambda j, cidx=cidx: self.vs("hycw", cidx * 3 + j)), 3, self.vs("hycb", cidx), t[:, :U], ut)
                tl.append((t, ut))
                fill()
            (a1, ua1), (a2, ua2), (a0, ua0) = tl
            S.add("dve", (lambda e, a1=a1, a2=a2: e.tensor_tensor(out=a1[:, :U], in0=a1[:, :U], in1=a2[:, :U], op=ALU.mult)),
                  reads=[ua1, ua2], writes=[ua1])
            self.dma(Sx["zzT"][c * 128:(c + 1) * 128], a1[:, :U], reads=[ua1], writes=[Ux["zzT"]])
            self.dma(Sx["x0T"][c * 128:(c + 1) * 128], a0[:, :U], reads=[ua0], writes=[Ux["x0T"]])
        if self.stop == "pre4":
            return
        while fillers:
            fill()
        for g in range(8):
            proj_to_pp(OFF_LX + g * 128)
            (xs, uxs), (H, uH), (A, uA), (Bt, uB) = WK.t
            self._acc, self._uacc = H, uH
            self.conv(pp, upp, (lambda j, g=g: self.vs("lrucw", g * 4 + j)), 4, self.vs("lrucb", g), xs[:, :U], uxs)
            S.add("act", (lambda e: e.activation(out=sqb[:, :U], in_=xs[:, :U], func=AF.Copy)), reads=[uxs], writes=[usqb])
            for d in range(2):
                tt_, utt = (H, uH) if d == 0 else (pp, upp)
                for (wname, bname, dst, udst) in (("lru_w_a", "lba", A, uA), ("lru_w_x", "lbx", Bt, uB)):
                    lw, ulw = lwp.get()
                    self.dma(lw[:], I[wname][l, d, g], writes=[ulw])
                    lr, ulr = lwr.get()
                    S.add("act", (lambda e, lw=lw, lr=lr: e.activation(out=lr[:], in_=lw[:], func=AF.Copy)), reads=[ulw], writes=[ulr])
                    for (u0, n) in TT:
                        ps, ups = self.PS.get()
                        S.add("pe", (lambda e, ps=ps, lr=lr, u0=u0, n=n: e.matmul(ps[:, :n], lr[:], sqb[:, u0:u0 + n],
                                                                                   start=True, stop=True)),
                              reads=[ulr, usqb], writes=[ups])
                        S.add("act", (lambda e, ps=ps, dst=dst, u0=u0, n=n, bname=bname, d=d, g=g: e.activation(
                            out=dst[:, u0:u0 + n], in_=ps[:, :n], func=AF.Sigmoid, bias=self.vs(bname, d * 8 + g))),
                            reads=[ups, self.u_vec], writes=[udst])
                if self.stop == "pre5":
                    return
                S.add("pool", (lambda e, A=A: e.memset(A[:, 256:259], 0.0)), reads=[uA], writes=[uA])
                S.add("act", (lambda e, A=A, t=tt_, d=d, g=g: e.activation(out=t[:, :U], in_=A[:, :U], func=AF.Exp,
                                                                           scale=sm[:, 24 + d * 8 + g:25 + d * 8 + g])),
                      reads=[uA, u_sm, utt], writes=[utt])
                S.add("act", (lambda e, A=A, d=d, g=g: e.activation(out=A[:, :U], in_=A[:, :U], func=AF.Exp,
                                                                     scale=sm[:, 8 + d * 8 + g:9 + d * 8 + g])),
                      reads=[uA, u_sm], writes=[uA])
                S.add("act", (lambda e, t=tt_: e.activation(out=t[:, :U], in_=t[:, :U], func=AF.Sqrt, scale=-1.0, bias=self.one1[:])),
                      reads=[utt], writes=[utt])
                S.add("dve", (lambda e, Bt=Bt, t=tt_: e.tensor_tensor(out=Bt[:, :U], in0=Bt[:, :U], in1=t[:, :U], op=ALU.mult)),
                      reads=[uB, utt], writes=[uB])
                S.add("dve", (lambda e, Bt=Bt: e.tensor_tensor(out=Bt[:, :U], in0=Bt[:, :U], in1=xs[:, :U], op=ALU.mult)),
                      reads=[uB, uxs], writes=[uB])
                S.add("pool", (lambda e, Bt=Bt: e.memset(Bt[:, 256:259], 0.0)), reads=[uB], writes=[uB])
                if self.stop == "pre6":
                    return
                if d == 0:
                    S.add("dve", (lambda e, A=A, Bt=Bt, t=tt_: e.tensor_tensor_scan(out=t[:, 0:U], data0=A[:, 0:U], data1=Bt[:, 0:U],
                                                                                    initial=0.0, op0=ALU.mult, op1=ALU.add)),
                          reads=[uA, uB, utt], writes=[utt])
                else:
                    S.add("dve", (lambda e, A=A, Bt=Bt, t=tt_: e.tensor_tensor_scan(out=t[:, 255::-1], data0=A[:, 255::-1], data1=Bt[:, 255::-1],
                                                                                    initial=0.0, op0=ALU.mult, op1=ALU.add)),
                          reads=[uA, uB, utt], writes=[utt])
                    S.add("dve", (lambda e, A=A, Bt=Bt, t=tt_: e.tensor_tensor_scan(out=t[:, U - 1:LT0 - 1:-1], data0=A[:, U - 1:LT0 - 1:-1],
                                                                                    data1=Bt[:, U - 1:LT0 - 1:-1], initial=t[:, 0:1],
                                                                                    op0=ALU.mult, op1=ALU.add)),
                          reads=[uA, uB, utt], writes=[utt])
                    S.add("dve", (lambda e, t=tt_: e.tensor_tensor(out=H[:, :U], in0=H[:, :U], in1=t[:, :U], op=ALU.add)),
                          reads=[utt, uH], writes=[uH])
            if self.stop == "pre7":
                return
            S.add("pool", lambda e: e.memset(pp[:, 0:1], 0.0), reads=[upp], writes=[upp])
            S.add("pool", lambda e: e.memset(pp[:, 257:260], 0.0), reads=[upp], writes=[upp])
            S.add("pool", lambda e: e.memset(pp[:, 4356:UP], 0.0), reads=[upp], writes=[upp])
            (Y, uY), (T2, uT2) = WK.t[2], WK.t[3]
            wb, uwb = self.load_w_bf16(I["w_in"][l][:, OFF_LY + g * 128:OFF_LY + (g + 1) * 128], 128, wst_pool, wbf_pool)
            for ti, (u0, n) in enumerate(TT):
                ps, ups = self.mm_tile(wb, uwb, 128, hT, u_hT, ti)
                evac_copy(ps, ups, Y[:, u0:u0 + n], uY, 128, n)
            S.add("pool", (lambda e, Y=Y: e.memset(Y[:, 256:259], 0.0)), reads=[uY], writes=[uY])
            S.add("act", (lambda e, Y=Y, T2=T2: e.activation(out=T2[:, :U], in_=Y[:, :U], func=AF.Square)), reads=[uY], writes=[uT2])
            S.add("dve", (lambda e, T2=T2: e.tensor_scalar(out=T2[:, :U], in0=T2[:, :U], scalar1=0.044715, scalar2=1.0,
                                                           op0=ALU.mult, op1=ALU.add)), reads=[uT2], writes=[uT2])
            S.add("dve", (lambda e, Y=Y, T2=T2: e.tensor_tensor(out=T2[:, :U], in0=T2[:, :U], in1=Y[:, :U], op=ALU.mult)),
                  reads=[uT2, uY], writes=[uT2])
            S.add("act", (lambda e, T2=T2: e.activation(out=T2[:, :U], in_=T2[:, :U], func=AF.Sigmoid, scale=1.5957691216057308)),
                  reads=[uT2], writes=[uT2])
            S.add("dve", (lambda e, Y=Y, T2=T2: e.tensor_tensor(out=Y[:, :U], in0=Y[:, :U], in1=T2[:, :U], op=ALU.mult)),
                  reads=[uT2, uY], writes=[uY])
            S.add("dve", (lambda e, Y=Y: e.tensor_tensor(out=T2[:, :U].bitcast(BF16)[:, :U], in0=Y[:, :U], in1=H[:, :U], op=ALU.mult)),
                  reads=[uY, uH, uT2], writes=[uT2])
            self.dma(Sx["olru"][g * 128:(g + 1) * 128], T2[:, :U].bitcast(BF16)[:, :U], reads=[uT2], writes=[Ux["olru"]])
            if self.stop == "pre8":
                return


def prep_inputs(inp, b):
    m = {}
    xT = np.zeros((D, U), np.float32)
    xT[:, 0:LC] = inp["ctx"][b].T
    xT[:, LT0:] = inp["x"][b].T
    m["xT0"] = xT
    cc = np.zeros((128, 8, 2), np.float32)
    cc[:, :, 0] = _pk(inp["c"][b])
    cc[:, :, 1] = _pk(inp["c_ctx"])
    m["cc"] = cc.reshape(128, 16)
    m["vec"] = np.stack([build_vec(inp, l) for l in range(DEPTH)])
    m["cst"] = build_cst()
    m["rmask"] = build_rmask()
    for seg, n_ in (("lat", L), ("ctx", LC)):
        tbs = hyena_tables(n_)
        for k in ("zT", "negt", "F1", "G", "Tout"):
            m["hy_%s_%s" % (k, seg)] = tbs[k]
        m["hy_Winv"] = tbs["Winv"]
    for n in ("w_mod", "w_in", "lru_w_a", "lru_w_x", "w_proj_dn", "w_proj_hy", "w_proj_lru", "w_out",
              "ffn_up", "ffn_down", "hy_w1", "hy_w2", "hy_w3"):
        m[n] = np.ascontiguousarray(inp[n], dtype=np.float32)
    return m


DN_BLOCKS = [0, 128] + [LT0 + 128 * i for i in range(32)]


class T128:
    def __init__(self, nc, es, names, dtype=F32):
        self.t = {}
        for n in names:
            self.t[n] = (es.enter_context(nc.sbuf_tensor(uname(n), [128, 128], dtype)), Unit())

    def __getitem__(self, n):
        return self.t[n]


def dn_phase(B, l, es):
    nc, S, Sx, Ux = B.nc, B.S, B.Sx, B.Ux
    ident = B.cs("ident")
    NB = len(DN_BLOCKS)
    GR = es.enter_context(nc.sbuf_tensor(uname("GR"), [40, U], F32))
    uGR = Unit()
    B.dma(GR[:], Sx["gab"][0], reads=[Ux["gab"]], writes=[uGR])
    TG = es.enter_context(nc.sbuf_tensor(uname("TG"), [128, NB, 3, 40], F32))
    TE = es.enter_context(nc.sbuf_tensor(uname("TE"), [128, NB, 2, 40], F32))
    uTG = Unit()
    gl_pool = TPool(nc, es, "gl", [40, 3, 128], F32, 2)
    for bi, ub in enumerate(DN_BLOCKS):
        gl, ugl = gl_pool.get()
        B.dma(gl[:], Sx["gab"][:, :, ub:ub + 128].rearrange("k r u -> r k u"), reads=[Ux["gab"]], writes=[ugl])
        ps, ups = B.PS.get()
        for k in range(3):
            S.add("pe", (lambda e, ps=ps, k=k, gl=gl: e.transpose(ps[:, k * 40:(k + 1) * 40], gl[:, k, :], ident[:40, :40])),
                  reads=[ugl, B.u_cst], writes=[ups])
        S.add("dve", (lambda e, ps=ps, bi=bi: e.tensor_copy(out=TG[:, bi].rearrange("p k r -> p (k r)"), in_=ps[:, 0:120])),
              reads=[ups], writes=[uTG])
        S.add("act", (lambda e, ps=ps, bi=bi: e.activation(out=TE[:, bi].rearrange("p k r -> p (k r)"), in_=ps[:, 40:120], func=AF.Exp)),
              reads=[ups], writes=[uTG])
    SELN = es.enter_context(nc.sbuf_tensor(uname("SELN"), [40, 16, 128], F32))
    uSEL = Unit()
    for q in range(16):
        r = (q // 8) * 32 + (q % 8)
        S.add("dve", (lambda e, q=q, r=r: e.tensor_scalar(out=SELN[:, q, :], in0=ident[:40, r:r + 1].to_broadcast([40, 128]),
                                                          scalar1=-1.0, scalar2=None, op0=ALU.mult)),
              reads=[B.u_cst], writes=[uSEL])
    zero = es.enter_context(nc.sbuf_tensor(uname("zero"), [128, 128], F32))
    uzero = Unit()
    S.add("pool", lambda e: e.memset(zero[:], 0.0), writes=[uzero])

    if B.stop == "dn0":
        dbg = nc.dram_tensor(uname("dbg_TG"), [128, NB, 3, 40], F32, kind="ExternalOutput").ap()
        B.dma(dbg, TG[:], reads=[uTG])
        return
    NCH = 4
    chains_res = []
    for ci in range(NCH):
        res = {}
        res["f32"] = T128(nc, es, ["qt", "kt", "vt", "Dm", "A", "E1", "M", "AT", "MT", "Y", "P0", "P1", "Q0", "Q1", "Us", "EG"])
        res["f32b"] = T128(nc, es, ["qt", "kt", "vt"])
        res["r"] = T128(nc, es, ["ktr", "qtr", "attnT", "Kd0", "Kd1", "Ktb", "Vb", "QdT", "YR", "WTs", "VN", "S"], F32R)
        res["cd"] = (es.enter_context(nc.sbuf_tensor(uname("cd"), [128, 2], F32)), Unit())
        if ci % 2 == 0:
            res["O"] = (es.enter_context(nc.sbuf_tensor(uname("O"), [128, NB, 128], F32)), [Unit() for _ in range(NB)])
        else:
            res["O"] = chains_res[ci - 1]["O"]
        res["ps"] = [B.PS.t[2 * ci], B.PS.t[2 * ci + 1]]
        chains_res.append(res)
    OD = es.enter_context(nc.sbuf_tensor(uname("OD"), [128, U], BF16))
    uOD = Unit()
    S.add("pool", lambda e: e.memset(OD[:, 256:259], 0.0), writes=[uOD])
    pp_t = TPool(nc, es, "dnpost", [128, 128], F32, 3)
    pp_s = TPool(nc, es, "dnsm", [128, 2], F32, 3)
    DK = 128.0 ** -0.5

    def chain(h, d, res):
        q = d * 8 + h
        r = d * 32 + h
        f, fb, rr = res["f32"], res["f32b"], res["r"]
        cd, ucd = res["cd"]
        O, uO = res["O"]
        psl = res["ps"]
        slot_i = [0]

        def slot():
            k = slot_i[0] % 8
            slot_i[0] += 1
            t, u = psl[k // 4]
            qd = k % 4
            return t[:, qd * 128:(qd + 1) * 128], u
        incl = B.cs("lowi") if d == 0 else B.cs("uppi")
        strict = B.cs("lows") if d == 0 else B.cs("upps")
        iend = (lambda c: c * 64 + 63) if d == 0 else (lambda c: c * 64)
        (Sst, uS), (VN, uVN) = rr["S"], rr["VN"]
        S.add("dve", (lambda e: e.tensor_copy(out=Sst[:], in_=zero[:])), reads=[uzero], writes=[uS])
        S.add("dve", (lambda e: e.tensor_copy(out=VN[:], in_=zero[:])), reads=[uzero], writes=[uVN])
        order = list(range(NB)) if d == 0 else [1, 0] + list(range(NB - 1, 1, -1))
        for oi, bi in enumerate(order):
            ub = DN_BLOCKS[bi]
            ld = f if oi % 2 == 0 else fb
            (qt, uqt), (kt, ukt), (vt, uvt) = ld["qt"], ld["kt"], ld["vt"]
            B.dma(qt[:], Sx["qT"][h][:, ub:ub + 128], reads=[Ux["qT"]], writes=[uqt])
            B.dma(kt[:], Sx["kT"][h][:, ub:ub + 128], reads=[Ux["kT"]], writes=[ukt])
            B.dma(vt[:], Sx["vT"][h][:, ub:ub + 128], reads=[Ux["vT"]], writes=[uvt])
            (ktr, uktr), (qtr, uqtr) = rr["ktr"], rr["qtr"]
            S.add("act", (lambda e, kt=kt: e.activation(out=ktr[:], in_=kt[:], func=AF.Copy)), reads=[ukt], writes=[uktr])
            S.add("act", (lambda e, qt=qt: e.activation(out=qtr[:], in_=qt[:], func=AF.Copy)), reads=[uqt], writes=[uqtr])
            yield
            if B.stop.endswith(":A"):
                return
            pKK, uKK = slot()
            S.add("pe", (lambda e, o=pKK: e.matmul(o, ktr[:], ktr[:], start=True, stop=True)), reads=[uktr], writes=[uKK])
            pQK, uQK = slot()
            S.add("pe", (lambda e, o=pQK: e.matmul(o, ktr[:], qtr[:], start=True, stop=True)), reads=[uktr, uqtr], writes=[uQK])
            pbc, ubc = slot()
            S.add("pe", (lambda e, o=pbc, ub=ub: e.matmul(o, SELN[:, q, :], GR[:, ub:ub + 128], start=True, stop=True)),
                  reads=[uSEL, uGR], writes=[ubc])
            pvt, uvtk = slot()
            S.add("pe", (lambda e, o=pvt, vt=vt: e.transpose(o, vt[:], ident)), reads=[uvt, B.u_cst], writes=[uvtk])
            pkt, uktk = slot()
            S.add("pe", (lambda e, o=pkt, kt=kt: e.transpose(o, kt[:], ident)), reads=[ukt, B.u_cst], writes=[uktk])
            yield
            if B.stop.endswith(":B"):
                return
            (Dm, uDm), (A, uA), (E1, uE1), (M, uM), (EG, uEG) = f["Dm"], f["A"], f["E1"], f["M"], f["EG"]
            gcol = TG[:, bi, 0, r:r + 1]
            climit = int(B.stop.split(":K")[1]) if ":K" in B.stop else 99
            if 0 < climit:
                S.add("dve", (lambda e, o=pbc, gcol=gcol: e.tensor_scalar(out=Dm[:], in0=o, scalar1=gcol, scalar2=0.0, op0=ALU.add, op1=ALU.min)),
                      reads=[ubc, uTG], writes=[uDm])
            if 2 < climit:
                S.add("act", (lambda e, o=pbc: e.activation(out=EG[:], in_=o, func=AF.Exp, scale=-1.0)), reads=[ubc], writes=[uEG])
            if 3 < climit:
                S.add("act", (lambda e: e.activation(out=A[:], in_=Dm[:], func=AF.Exp)), reads=[uDm], writes=[uA])
            if 4 < climit:
                S.add("dve", (lambda e: e.tensor_tensor(out=A[:], in0=A[:], in1=incl, op=ALU.mult)), reads=[uA, B.u_cst], writes=[uA])
            bcol = TE[:, bi, 1, r:r + 1]
            if 5 < climit:
                S.add("dve", (lambda e, bcol=bcol: e.scalar_tensor_tensor(out=E1[:], in0=A[:], scalar=bcol, in1=strict, op0=ALU.mult, op1=ALU.mult)),
                      reads=[uA, uTG, B.u_cst], writes=[uE1])
            if 6 < climit:
                S.add("dve", (lambda e, o=pKK: e.tensor_tensor(out=M[:], in0=o, in1=E1[:], op=ALU.mult)), reads=[uKK, uE1], writes=[uM])
            (QdT, uQdT) = rr["QdT"]
            if 7 < climit:
                S.add("dve", (lambda e, qt=qt: e.scalar_tensor_tensor(out=QdT[:], in0=qt[:], scalar=DK, in1=EG[:], op0=ALU.mult, op1=ALU.mult)),
                      reads=[uqt, uEG], writes=[uQdT])
            yield
            if B.stop.endswith(":C") or ":K" in B.stop:
                return
            pAT, uAT_ = slot()
            S.add("pe", (lambda e, o=pAT: e.transpose(o, A[:], ident)), reads=[uA, B.u_cst], writes=[uAT_])
            pMT, uMT_ = slot()
            S.add("pe", (lambda e, o=pMT: e.transpose(o, M[:], ident)), reads=[uM, B.u_cst], writes=[uMT_])
            yield
            (AT, uAT), (MT, uMT), (Y, uY) = f["AT"], f["MT"], f["Y"]
            S.add("act", (lambda e, o=pAT: e.activation(out=AT[:], in_=o, func=AF.Copy)), reads=[uAT_], writes=[uAT])
            S.add("act", (lambda e, o=pMT: e.activation(out=MT[:], in_=o, func=AF.Copy)), reads=[uMT_], writes=[uMT])
            S.add("dve", (lambda e, o=pMT: e.tensor_tensor(out=Y[:], in0=ident, in1=o, op=ALU.subtract)), reads=[uMT_, B.u_cst], writes=[uY])
            (attnT, uattn), (Kd0, uKd0), (Kd1, uKd1), (Ktb, uKtb), (Vb, uVb) = rr["attnT"], rr["Kd0"], rr["Kd1"], rr["Ktb"], rr["Vb"]
            S.add("dve", (lambda e, o=pQK: e.scalar_tensor_tensor(out=attnT[:], in0=o, scalar=DK, in1=AT[:], op0=ALU.mult, op1=ALU.mult)),
                  reads=[uQK, uAT], writes=[uattn])
            for c, (Kd, uKd) in enumerate(((Kd0, uKd0), (Kd1, uKd1))):
                S.add("act", (lambda e, o=pkt, Kd=Kd, c=c: e.activation(out=Kd[:], in_=o, func=AF.Copy, scale=AT[:, iend(c):iend(c) + 1])),
                      reads=[uktk, uAT], writes=[uKd])
            wcol = TE[:, bi, 0, r:r + 1]
            S.add("dve", (lambda e, o=pkt, wcol=wcol: e.tensor_scalar(out=Ktb[:], in0=o, scalar1=wcol, scalar2=None, op0=ALU.mult)),
                  reads=[uktk, uTG], writes=[uKtb])
            S.add("dve", (lambda e, o=pvt, bcol=bcol: e.tensor_scalar(out=Vb[:], in0=o, scalar1=bcol, scalar2=None, op0=ALU.mult)),
                  reads=[uvtk, uTG], writes=[uVb])
            yield
            if B.stop.endswith(":E"):
                return
            P, uP = M, uM
            PT, uPT = MT, uMT
            bufs = [(f["P0"], f["Q0"]), (f["P1"], f["Q1"])]
            (YR, uYR) = rr["YR"]
            pend = None
            for lev in range(1, 7):
                cur = None
                if lev <= 5:
                    (Pn, uPn), (PnT, uPnT) = bufs[lev % 2]
                    pP, upP = slot()
                    S.add("pe", (lambda e, o=pP, PT=PT, P=P: e.matmul(o, PT[:], P[:], start=True, stop=True)), reads=[uPT, uP], writes=[upP])
                    pPT = None
                    if lev < 5:
                        pPT, upPT = slot()
                        S.add("pe", (lambda e, o=pPT, PT=PT, P=P: e.matmul(o, P[:], PT[:], start=True, stop=True)), reads=[uPT, uP], writes=[upPT])
                    cur = (Pn, uPn, PnT, uPnT, pP, upP, pPT, upPT if lev < 5 else None)
                pY = None
                if pend is not None:
                    pY, upY = slot()
                    S.add("pe", (lambda e, o=pY, Pq=pend[0]: e.matmul(o, Pq[:], Y[:], start=True, stop=True)), reads=[pend[1], uY], writes=[upY])
                yield
                if pY is not None:
                    if lev <= 5:
                        S.add("dve", (lambda e, o=pY: e.tensor_tensor(out=Y[:], in0=Y[:], in1=o, op=ALU.add)), reads=[upY, uY], writes=[uY])
                    else:
                        S.add("dve", (lambda e, o=pY: e.tensor_tensor(out=YR[:], in0=Y[:], in1=o, op=ALU.add)), reads=[upY, uY], writes=[uYR])
                if cur is not None:
                    (Pn, uPn, PnT, uPnT, pP, upP, pPT, upPT) = cur
                    S.add("act", (lambda e, o=pP, Pn=Pn: e.activation(out=Pn[:], in_=o, func=AF.Copy)), reads=[upP], writes=[uPn])
                    if pPT is not None:
                        S.add("act", (lambda e, o=pPT, PnT=PnT: e.activation(out=PnT[:], in_=o, func=AF.Copy)), reads=[upPT], writes=[uPnT])
                    pend = (Pn, uPn)
                    P, uP, PT, uPT = Pn, uPn, PnT, uPnT
                yield
            if B.stop.endswith(":G"):
                return
            (Us, uUs), (WTs, uWTs) = f["Us"], rr["WTs"]
            pU, upU = slot()
            S.add("pe", (lambda e, o=pU: e.matmul(o, YR[:], Vb[:], start=True, stop=True)), reads=[uYR, uVb], writes=[upU])
            pW, upW = slot()
            S.add("pe", (lambda e, o=pW: e.matmul(o, Ktb[:], YR[:], start=True, stop=True)), reads=[uYR, uKtb], writes=[upW])
            yield
            S.add("act", (lambda e, o=pU: e.activation(out=Us[:], in_=o, func=AF.Copy)), reads=[upU], writes=[uUs])
            S.add("dve", (lambda e, o=pW: e.tensor_copy(out=WTs[:], in_=o)), reads=[upW], writes=[uWTs])
            yield
            if B.stop.endswith(":H"):
                return
            for c in ((0, 1) if d == 0 else (1, 0)):
                rows = slice(c * 64, (c + 1) * 64)
                Kd, uKd = (Kd0, uKd0) if c == 0 else (Kd1, uKd1)
                p1, up1 = slot()
                S.add("pe", (lambda e, o=p1: e.matmul(o, WTs[:], Sst[:], start=True, stop=True)), reads=[uWTs, uS], writes=[up1])
                yield
                S.add("dve", (lambda e, o=p1, rows=rows: e.tensor_tensor(out=VN[rows, :], in0=Us[rows, :], in1=o[rows, :], op=ALU.subtract)),
                      reads=[up1, uUs, uVN], writes=[uVN])
                yield
                p2, up2 = slot()
                S.add("pe", (lambda e, o=p2: e.matmul(o, QdT[:], Sst[:], start=True, stop=False)), reads=[uQdT, uS], writes=[up2])
                S.add("pe", (lambda e, o=p2: e.matmul(o, attnT[:], VN[:], start=False, stop=True)), reads=[uattn, uVN], writes=[up2])
                p3, up3 = slot()
                S.add("pe", (lambda e, o=p3, Kd=Kd: e.matmul(o, Kd[:], VN[:], start=True, stop=True)), reads=[uKd, uVN], writes=[up3])
                yield
                oi_other = (bi if d == 1 else ([1, 0] + list(range(NB - 1, 1, -1))).index(bi))
                first = (oi < oi_other) or (oi == oi_other and d == 0)
                if first:
                    S.add("act", (lambda e, o=p2, rows=rows, bi=bi: e.activation(out=O[rows, bi, :], in_=o[rows, :], func=AF.Copy)),
                          reads=[up2], writes=[uO[bi]])
                else:
                    S.add("dve", (lambda e, o=p2, rows=rows, bi=bi: e.tensor_tensor(out=O[rows, bi, :], in0=O[rows, bi, :], in1=o[rows, :], op=ALU.add)),
                          reads=[up2, uO[bi]], writes=[uO[bi]])
                S.add("dve", (lambda e, o=p3, c=c: e.scalar_tensor_tensor(out=Sst[:], in0=Sst[:], scalar=EG[:, iend(c):iend(c) + 1], in1=o,
                                                                         op0=ALU.mult, op1=ALU.add)), reads=[up3, uS, uEG], writes=[uS])
                yield

    def post(h, resf, resb):
        (Of, uOf) = resf["O"]
        for bi, ub in enumerate(DN_BLOCKS):
            t, ut = pp_t.get()
            sm_, usm = pp_s.get()
            S.add("pool", (lambda e, t=t, bi=bi: e.tensor_copy(out=t[:], in_=Of[:, bi, :])),
                  reads=[uOf[bi]], writes=[ut])
            t2, ut2 = pp_t.get()
            S.add("act", (lambda e, t=t, t2=t2, sm_=sm_: e.activation(out=t2[:], in_=t[:], func=AF.Square, accum_out=sm_[:, 0:1])),
                  reads=[ut], writes=[ut2, usm])
            S.add("act", (lambda e, sm_=sm_: e.activation(out=sm_[:, 1:2], in_=sm_[:, 0:1], func=AF.Sqrt, scale=1.0 / 128, bias=B.eps6[:])),
                  reads=[usm], writes=[usm])
            S.add("dve", (lambda e, sm_=sm_: e.reciprocal(out=sm_[:, 1:2], in_=sm_[:, 1:2])), reads=[usm], writes=[usm])
            S.add("dve", (lambda e, t=t, sm_=sm_: e.tensor_scalar(out=t[:], in0=t[:], scalar1=sm_[:, 1:2], scalar2=None, op0=ALU.mult)),
                  reads=[usm, ut], writes=[ut])
            ps, ups = B.PS.get()
            S.add("pe", (lambda e, ps=ps, t=t: e.transpose(ps[:, 0:128], t[:], ident)), reads=[ut, B.u_cst], writes=[ups])
            sz, usz = pp_t.get()
            B.dma(sz[:], Sx["szT"][h * 128:(h + 1) * 128, ub:ub + 128], reads=[Ux["szT"]], writes=[usz])
            S.add("dve", (lambda e, ps=ps, sz=sz, ub=ub: e.scalar_tensor_tensor(out=OD[:, ub:ub + 128], in0=ps[:, 0:128], scalar=B.vs("dnng"),
                                                                              in1=sz[:], op0=ALU.mult, op1=ALU.mult)),
                  reads=[ups, usz, B.u_vec, uOD], writes=[uOD])
        B.dma(Sx["odn"][h * 128:(h + 1) * 128], OD[:], reads=[uOD], writes=[Ux["odn"]])

    nheads = 8 if not (B.stop or "").startswith("dn1") else 2
    if B.stop is None:
        B.stop = ""
    for h0 in range(0, nheads, 2):
        gens = []
        for j in range(2):
            for d in range(2):
                gens.append(chain(h0 + j, d, chains_res[j * 2 + d]))
        active = list(gens)
        while active:
            nxt = []
            for g in active:
                try:
                    next(g)
                    nxt.append(g)
                except StopIteration:
                    pass
            active = nxt
        if ":" in B.stop:
            return
        for j in range(2):
            post(h0 + j, chains_res[j * 2], chains_res[j * 2 + 1])


def hy_dims(n):
    N = 2 * n
    NS1 = n // 128
    NF1 = N // 128
    NFH = NF1 // 2 + 1
    return N, NS1, NF1, NFH


def hyena_tables(n):
    N, NS1, NF1, NFH = hy_dims(n)
    f64 = np.float64
    s1 = np.arange(NS1, dtype=f64)[:, None]
    f1 = np.arange(NFH, dtype=f64)[None, :]
    ang = 2 * np.pi * f1 * s1 / NF1
    F1 = np.concatenate([np.cos(ang), -np.sin(ang)], axis=1)
    s2 = np.arange(128, dtype=f64)[:, None, None]
    f1b = np.arange(NFH, dtype=f64)[None, :, None]
    f2 = np.arange(128, dtype=f64)[None, None, :]
    ang = 2 * np.pi * (f1b + NF1 * f2) * s2 / N
    G = np.stack([np.cos(ang), -np.sin(ang)], axis=2)
    f2c = np.arange(128, dtype=f64)[:, None]
    t2 = np.arange(128, dtype=f64)[None, :]
    th = 2 * np.pi * f2c * t2 / 128
    Winv = np.stack([np.concatenate([np.cos(th), np.sin(th)], 1), np.concatenate([-np.sin(th), np.cos(th)], 1)], axis=1)
    NT1 = NS1
    f1c = np.arange(NFH, dtype=f64)[:, None, None]
    t2c = np.arange(128, dtype=f64)[None, :, None]
    t1c = np.arange(NT1, dtype=f64)[None, None, :]
    ph = 2 * np.pi * f1c * (128 * t1c + t2c) / N
    w = np.full((NFH, 1, 1), 2.0)
    w[0] = 1.0
    w[NFH - 1] = 1.0
    Tout = np.stack([w / N * np.cos(ph), -w / N * np.sin(ph)], axis=2)
    t = np.linspace(0.0, 1.0, n, dtype=np.float32)[:, None]
    omega = (2.0 * math.pi * np.arange(n, dtype=np.float32)[:, None] / n).astype(np.float32)
    bands = np.linspace(1e-4, 15, 16, dtype=np.float32)[None, :]
    z = np.concatenate([t, np.cos(bands * omega), -np.sin(bands * omega)], axis=-1).astype(np.float32)
    negt = np.broadcast_to(-t[:, 0][None, :], (128, n))
    return dict(F1=F1.astype(np.float32), G=G.astype(np.float32), Winv=Winv.astype(np.float32),
                Tout=Tout.astype(np.float32), zT=np.ascontiguousarray(z.T), negt=np.ascontiguousarray(negt, dtype=np.float32))


HY_C = 32


def hyena_phase(B, l, es, seg):
    nc, S, Sx, Ux, I = B.nc, B.S, B.Sx, B.Ux, B.I
    n, base = (L, LT0) if seg == "lat" else (LC, 0)
    N, NS1, NF1, NFH = hy_dims(n)
    NT1 = NS1
    W2 = 2 * NFH
    C = HY_C
    tb = lambda k: I["hy_%s_%s" % (k, seg)]
    hp, uhp = Sx["hp_" + seg], Ux["hp_" + seg]
    NTL = [(i * 512, min(512, n - i * 512)) for i in range((n + 511) // 512)]
    MAGIC = 12582912.0

    with ExitStack() as e1:
        zT = e1.enter_context(nc.sbuf_tensor(uname("zT"), [33, n], F32))
        h1 = e1.enter_context(nc.sbuf_tensor(uname("h1"), [64, n], F32))
        h2 = e1.enter_context(nc.sbuf_tensor(uname("h2"), [64, n], F32))
        negt = e1.enter_context(nc.sbuf_tensor(uname("negt"), [128, n], F32))
        dec = e1.enter_context(nc.sbuf_tensor(uname("dec"), [128, n], F32))
        w1 = e1.enter_context(nc.sbuf_tensor(uname("w1"), [33, 64], F32))
        w2 = e1.enter_context(nc.sbuf_tensor(uname("w2"), [64, 64], F32))
        w3 = e1.enter_context(nc.sbuf_tensor(uname("w3"), [64, 2048], F32))
        fb = e1.enter_context(nc.sbuf_tensor(uname("fb"), [64, 2], F32))
        uz, uh1, uh2, unegt, udec, uw, ufb = Unit(), Unit(), Unit(), Unit(), Unit(), Unit(), Unit()
        tmp_p = TPool(nc, e1, "hyt", [128, 512], F32, 3)
        out_p = TPool(nc, e1, "hyo", [128, 2, 512], F32, 2)
        B.dma(zT[:], tb("zT"), writes=[uz])
        B.dma(negt[:], tb("negt"), writes=[unegt])
        B.dma(w1[:], I["hy_w1"][l], writes=[uw])
        B.dma(w2[:], I["hy_w2"][l], writes=[uw])
        B.dma(w3[:], I["hy_w3"][l], writes=[uw])
        S.add("dve", lambda e: e.tensor_tensor(out=fb[:, 0:1], in0=B.vs("hyf1")[:64], in1=B.vs("hyb1")[:64], op=ALU.mult),
              reads=[B.u_vec], writes=[ufb])
        S.add("dve", lambda e: e.tensor_tensor(out=fb[:, 1:2], in0=B.vs("hyf2")[:64], in1=B.vs("hyb2")[:64], op=ALU.mult),
              reads=[B.u_vec, ufb], writes=[ufb])
        for li, (wt, K, src, usrc, dst, udst, fname) in enumerate(((w1, 33, zT, uz, h1, uh1, "hyf1"), (w2, 64, h1, uh1, h2, uh2, "hyf2"))):
            for (t0, tn) in NTL:
                ps, ups = B.PS.get()
                S.add("pe", (lambda e, ps=ps, wt=wt, K=K, src=src, t0=t0, tn=tn: e.matmul(ps[:64, :tn], wt[:K, :], src[:K, t0:t0 + tn],
                                                                                          start=True, stop=True)),
                      reads=[uw, usrc], writes=[ups])
                x, ux = tmp_p.get()
                k2, uk2 = tmp_p.get()
                S.add("dve", (lambda e, ps=ps, x=x, tn=tn, li=li, fname=fname: e.tensor_scalar(
                    out=x[:64, :tn], in0=ps[:64, :tn], scalar1=B.vs(fname)[:64], scalar2=fb[:, li:li + 1], op0=ALU.mult, op1=ALU.add)),
                    reads=[ups, B.u_vec, ufb], writes=[ux])
                S.add("dve", (lambda e, x=x, k2=k2, tn=tn: e.tensor_scalar(out=k2[:64, :tn], in0=x[:64, :tn], scalar1=1.0 / (2 * math.pi),
                                                                           scalar2=MAGIC, op0=ALU.mult, op1=ALU.add)), reads=[ux], writes=[uk2])
                S.add("dve", (lambda e, k2=k2, tn=tn: e.tensor_scalar(out=k2[:64, :tn], in0=k2[:64, :tn], scalar1=-MAGIC, scalar2=-2 * math.pi,
                                                                      op0=ALU.add, op1=ALU.mult)), reads=[uk2], writes=[uk2])
                S.add("dve", (lambda e, x=x, k2=k2, tn=tn: e.tensor_tensor(out=x[:64, :tn], in0=x[:64, :tn], in1=k2[:64, :tn], op=ALU.add)),
                      reads=[ux, uk2], writes=[ux])
                S.add("act", (lambda e, x=x, dst=dst, t0=t0, tn=tn: e.activation(out=dst[:, t0:t0 + tn], in_=x[:64, :tn], func=AF.Sin)),
                      reads=[ux], writes=[udst])
        for cch in range(8):
            S.add("act", (lambda e, cch=cch: e.activation(out=dec[:], in_=negt[:], func=AF.Exp, scale=B.cs("deltas")[:, cch:cch + 1])),
                  reads=[unegt, B.u_cst], writes=[udec])
            for (t0, tn) in NTL:
                psf, upsf = B.PS.get()
                psb, upsb = B.PS.get()
                for (ps_, ups_, dr) in ((psf, upsf, 0), (psb, upsb, 1)):
                    S.add("pe", (lambda e, ps_=ps_, dr=dr, cch=cch, t0=t0, tn=tn: e.matmul(
                        ps_[:, :tn], w3[:, dr * 1024 + cch * 128:dr * 1024 + (cch + 1) * 128], h2[:, t0:t0 + tn], start=True, stop=True)),
                        reads=[uw, uh2], writes=[ups_])
                a1, ua1 = tmp_p.get()
                S.add("act", (lambda e, a1=a1, psf=psf, tn=tn: e.activation(out=a1[:, :tn], in_=psf[:, :tn], func=AF.Copy)), reads=[upsf], writes=[ua1])
                o, uo = out_p.get()
                for k_, op_ in ((0, ALU.add), (1, ALU.subtract)):
                    S.add("dve", (lambda e, o=o, a1=a1, psb=psb, tn=tn, k_=k_, op_=op_: e.tensor_tensor(out=o[:, k_, :tn], in0=a1[:, :tn], in1=psb[:, :tn], op=op_)),
                          reads=[ua1, upsb, uo], writes=[uo])
                    S.add("pool", (lambda e, o=o, tn=tn, t0=t0, k_=k_: e.tensor_tensor(out=o[:, k_, :tn], in0=o[:, k_, :tn], in1=dec[:, t0:t0 + tn], op=ALU.mult)),
                          reads=[uo, udec], writes=[uo])
                B.dma(hp[:, cch * 128:(cch + 1) * 128, t0:t0 + tn].rearrange("k c t -> c k t"), o[:, :, :tn], reads=[uo], writes=[uhp])
        S.barrier()
    if B.stop and B.stop.endswith("hyf"):
        return

    with ExitStack() as e2:
        cvt_pool = TPool(nc, e2, "cvtst", [128, 1024], F32, 2)

        def cvt(name, shape, src_ap):
            t = e2.enter_context(nc.sbuf_tensor(uname(name), shape, F32R))
            ut = Unit()
            flat = 1
            for s_ in shape[1:]:
                flat *= s_
            step = 1024
            st_pool = cvt_pool
            tf = t[:].rearrange(" ".join(["p"] + ["a%d" % i for i in range(len(shape) - 1)]) + " -> p (" + " ".join("a%d" % i for i in range(len(shape) - 1)) + ")") if len(shape) > 2 else t[:]
            sf = src_ap.rearrange(" ".join(["p"] + ["a%d" % i for i in range(len(shape) - 1)]) + " -> p (" + " ".join("a%d" % i for i in range(len(shape) - 1)) + ")") if len(shape) > 2 else src_ap
            P_ = shape[0]
            for o_ in range(0, flat, step):
                w_ = min(step, flat - o_)
                st, ust = st_pool.get()
                B.dma(st[:P_, :w_], sf[:, o_:o_ + w_], writes=[ust])
                S.add("act", (lambda e, st=st, o_=o_, w_=w_: e.activation(out=tf[:, o_:o_ + w_], in_=st[:P_, :w_], func=AF.Copy)),
                      reads=[ust], writes=[ut])
            return t, ut
        F1 = e2.enter_context(nc.sbuf_tensor(uname("F1"), [NS1, W2], F32))
        uF1 = Unit()
        B.dma(F1[:], tb("F1"), writes=[uF1])
        G, uG = cvt("G", [128, NFH, 2, 128], tb("G"))
        Wv, uWv = cvt("Wv", [128, 2, 256], I["hy_Winv"])
        To, uTo = cvt("To", [NFH, 128, 2, NT1], tb("Tout"))
        xin_p = TPool(nc, e2, "xin", [NS1, C, 128], F32, 1)
        Yb = e2.enter_context(nc.sbuf_tensor(uname("Yb"), [128, NFH, 3, C], F32R))
        Kb = e2.enter_context(nc.sbuf_tensor(uname("Kb"), [128, NFH, 2, C], F32))
        XP = e2.enter_context(nc.sbuf_tensor(uname("XP"), [128, NFH, 2, C], F32R))
        Vb = e2.enter_context(nc.sbuf_tensor(uname("Vb"), [NFH, 128, 2, C], F32R))
        yb = e2.enter_context(nc.sbuf_tensor(uname("yb"), [C, n], F32))
        uYb, uKb, uXP, uVb, uyb = Unit(), Unit(), Unit(), Unit(), Unit()
        tp = TPool(nc, e2, "hytp", [128, 8, C], F32, 4)
        xz_p = TPool(nc, e2, "hyxz", [C, 2, 1024], F32, 1)
        ob_p = TPool(nc, e2, "hyob", [C, 1024], BF16, 2)
        NPB = 512 // W2
        XP2 = e2.enter_context(nc.sbuf_tensor(uname("XP2"), [128, NFH, 2, C], F32R))
        uXP2 = Unit()
        XPS = [(XP, uXP), (XP2, uXP2)]

        def fwd(g):
            XP, uXP = XPS[g % 2]
            c0 = g * C
            for sig in ("p", "m", "zz"):
                xin, uxin = xin_p.get()
                if sig == "zz":
                    src, usrc = Sx["zzT"][c0:c0 + C, base:base + n], Ux["zzT"]
                else:
                    src, usrc = hp[0 if sig == "p" else 1, c0:c0 + C, :], uhp
                B.dma(xin[:], src.rearrange("c (a b) -> a c b", b=128), reads=[usrc], writes=[uxin])
                for cb in range(0, C, NPB):
                    nb = min(NPB, C - cb)
                    ps, ups = B.PS.get()
                    for ci in range(nb):
                        S.add("pe", (lambda e, ps=ps, xin=xin, ci=ci, cb=cb: e.matmul(ps[:, ci * W2:(ci + 1) * W2], xin[:, cb + ci, :], F1[:, :],
                                                                                      start=True, stop=True)),
                              reads=[uxin, uF1], writes=[ups])
                    pv = ps[:, :nb * W2].rearrange("p (c r f) -> p c r f", c=nb, r=2)
                    for r_ in range(2):
                        S.add("act" if r_ == 0 else "dve",
                              (lambda e, pv=pv, r_=r_, cb=cb, nb=nb: (e.activation(out=Yb[:, :, r_, cb:cb + nb], in_=pv[:, :, r_, :].rearrange("p c f -> p f c"), func=AF.Copy)
                                                                       if r_ == 0 else
                                                                       e.tensor_copy(out=Yb[:, :, r_, cb:cb + nb], in_=pv[:, :, r_, :].rearrange("p c f -> p f c")))),
                              reads=[ups], writes=[uYb])
                    S.add("act", (lambda e, pv=pv, cb=cb, nb=nb: e.activation(out=Yb[:, :, 2, cb:cb + nb], in_=pv[:, :, 1, :].rearrange("p c f -> p f c"),
                                                                              func=AF.Copy, scale=-1.0)), reads=[ups], writes=[uYb])
                    yield
                for fq in range(0, NFH, 8):
                    nf = min(8, NFH - fq)
                    ps, ups = B.PS.get()
                    for fi in range(nf):
                        f1_ = fq + fi
                        if sig != "m":
                            S.add("pe", (lambda e, ps=ps, fi=fi, f1_=f1_: e.matmul(ps[:, (fi * 2) * C:(fi * 2 + 1) * C], G[:, f1_, 0, :], Yb[:, f1_, 0, :], start=True, stop=False)),
                                  reads=[uG, uYb], writes=[ups])
                            S.add("pe", (lambda e, ps=ps, fi=fi, f1_=f1_: e.matmul(ps[:, (fi * 2) * C:(fi * 2 + 1) * C], G[:, f1_, 1, :], Yb[:, f1_, 2, :], start=False, stop=True)),
                                  reads=[uG, uYb], writes=[ups])
                        if sig != "p":
                            S.add("pe", (lambda e, ps=ps, fi=fi, f1_=f1_: e.matmul(ps[:, (fi * 2 + 1) * C:(fi * 2 + 2) * C], G[:, f1_, 1, :], Yb[:, f1_, 0, :], start=True, stop=False)),
                                  reads=[uG, uYb], writes=[ups])
                            S.add("pe", (lambda e, ps=ps, fi=fi, f1_=f1_: e.matmul(ps[:, (fi * 2 + 1) * C:(fi * 2 + 2) * C], G[:, f1_, 0, :], Yb[:, f1_, 1, :], start=False, stop=True)),
                                  reads=[uG, uYb], writes=[ups])
                    pv = ps[:, :nf * 2 * C].rearrange("p (f r c) -> p f r c", f=nf, r=2)
                    if sig == "p":
                        S.add("act", (lambda e, pv=pv, fq=fq, nf=nf: e.activation(out=Kb[:, fq:fq + nf, 0, :], in_=pv[:, :, 0, :], func=AF.Copy)), reads=[ups], writes=[uKb])
                    elif sig == "m":
                        S.add("act", (lambda e, pv=pv, fq=fq, nf=nf: e.activation(out=Kb[:, fq:fq + nf, 1, :], in_=pv[:, :, 1, :], func=AF.Copy)), reads=[ups], writes=[uKb])
                    else:
                        (t1, u1), (t2, u2), (t3, u3), (t4, u4) = tp.get(), tp.get(), tp.get(), tp.get()
                        kr, ki = Kb[:, fq:fq + nf, 0, :], Kb[:, fq:fq + nf, 1, :]
                        S.add("dve", (lambda e, pv=pv, t1=t1, nf=nf, kr=kr: e.tensor_tensor(out=t1[:, :nf, :], in0=pv[:, :, 0, :], in1=kr, op=ALU.mult)), reads=[ups, uKb], writes=[u1])
                        S.add("dve", (lambda e, pv=pv, t2=t2, nf=nf, ki=ki: e.tensor_tensor(out=t2[:, :nf, :], in0=pv[:, :, 1, :], in1=ki, op=ALU.mult)), reads=[ups, uKb], writes=[u2])
                        S.add("dve", (lambda e, pv=pv, t3=t3, nf=nf, ki=ki: e.tensor_tensor(out=t3[:, :nf, :], in0=pv[:, :, 0, :], in1=ki, op=ALU.mult)), reads=[ups, uKb], writes=[u3])
                        S.add("dve", (lambda e, pv=pv, t4=t4, nf=nf, kr=kr: e.tensor_tensor(out=t4[:, :nf, :], in0=pv[:, :, 1, :], in1=kr, op=ALU.mult)), reads=[ups, uKb], writes=[u4])
                        S.add("dve", (lambda e, t1=t1, t2=t2, fq=fq, nf=nf: e.tensor_tensor(out=XP[:, fq:fq + nf, 0, :], in0=t1[:, :nf, :], in1=t2[:, :nf, :], op=ALU.subtract)),
                              reads=[u1, u2], writes=[uXP])
                        S.add("dve", (lambda e, t3=t3, t4=t4, fq=fq, nf=nf: e.tensor_tensor(out=XP[:, fq:fq + nf, 1, :], in0=t3[:, :nf, :], in1=t4[:, :nf, :], op=ALU.add)),
                              reads=[u3, u4], writes=[uXP])
                    yield

        def inv(g):
            XP, uXP = XPS[g % 2]
            c0 = g * C
            for cb in range(0, C, 2):
                ps, ups = B.PS.get()
                for ci in range(2):
                    c_ = cb + ci
                    S.add("pe", (lambda e, ps=ps, ci=ci, c_=c_: e.matmul(ps[:NFH, ci * 256:(ci + 1) * 256], XP[:, :, 0, c_], Wv[:, 0, :], start=True, stop=False)),
                          reads=[uXP, uWv], writes=[ups])
                    S.add("pe", (lambda e, ps=ps, ci=ci, c_=c_: e.matmul(ps[:NFH, ci * 256:(ci + 1) * 256], XP[:, :, 1, c_], Wv[:, 1, :], start=False, stop=True)),
                          reads=[uXP, uWv], writes=[ups])
                pv = ps[:NFH, :].rearrange("p (c r t) -> p c r t", c=2, r=2)
                S.add("act" if (cb // 2) % 2 == 0 else "dve",
                      (lambda e, pv=pv, cb=cb: (e.activation(out=Vb[:, :, :, cb:cb + 2], in_=pv.rearrange("p c r t -> p t r c"), func=AF.Copy)
                                                if (cb // 2) % 2 == 0 else e.tensor_copy(out=Vb[:, :, :, cb:cb + 2], in_=pv.rearrange("p c r t -> p t r c")))),
                      reads=[ups], writes=[uVb])
                yield
            TB = 512 // NT1 if NT1 * 16 > 512 else 16
            for tq in range(0, 128, TB):
                ps, ups = B.PS.get()
                for ti in range(TB):
                    t2_ = tq + ti
                    S.add("pe", (lambda e, ps=ps, ti=ti, t2_=t2_: e.matmul(ps[:C, ti * NT1:(ti + 1) * NT1], Vb[:, t2_, 0, :], To[:, t2_, 0, :], start=True, stop=False)),
                          reads=[uVb, uTo], writes=[ups])
                    S.add("pe", (lambda e, ps=ps, ti=ti, t2_=t2_: e.matmul(ps[:C, ti * NT1:(ti + 1) * NT1], Vb[:, t2_, 1, :], To[:, t2_, 1, :], start=False, stop=True)),
                          reads=[uVb, uTo], writes=[ups])
                S.add("act", (lambda e, ps=ps, tq=tq: e.activation(out=yb[:, :].rearrange("c (a b) -> c a b", b=128)[:, :, tq:tq + TB],
                                                                   in_=ps[:C, :TB * NT1].rearrange("c (b a) -> c a b", a=NT1), func=AF.Copy)),
                      reads=[ups], writes=[uyb])
                yield
            for q0 in range(0, n, 1024):
                qn = min(1024, n - q0)
                xz, uxz = xz_p.get()
                B.dma(xz[:, 0, :qn], Sx["x0T"][c0:c0 + C, base + q0:base + q0 + qn], reads=[Ux["x0T"]], writes=[uxz])
                B.dma(xz[:, 1, :qn], Sx["zzT"][c0:c0 + C, base + q0:base + q0 + qn], reads=[Ux["zzT"]], writes=[uxz])
                S.add("dve", (lambda e, xz=xz, q0=q0, qn=qn, c0=c0: e.scalar_tensor_tensor(out=xz[:, 1, :qn], in0=xz[:, 1, :qn], scalar=B.vs("hyb32")[:C, c0 // C:c0 // C + 1],
                                                                                      in1=yb[:, q0:q0 + qn], op0=ALU.mult, op1=ALU.add)),
                      reads=[uxz, uyb, B.u_vec], writes=[uxz])
                ob, uob = ob_p.get()
                S.add("pool", (lambda e, xz=xz, ob=ob, qn=qn: e.tensor_tensor(out=ob[:, :qn], in0=xz[:, 0, :qn], in1=xz[:, 1, :qn], op=ALU.mult)),
                      reads=[uxz], writes=[uob])
                B.dma(Sx["ohy"][c0:c0 + C, base + q0:base + q0 + qn], ob[:, :qn], reads=[uob], writes=[Ux["ohy"]])
                yield

        NG = D // C
        if B.stop and B.stop.endswith("hy1"):
            NG = 2
        for g in range(NG + 1):
            gens = []
            if g < NG:
                gens.append(fwd(g))
            if g >= 1:
                gens.append(inv(g - 1))
            while gens:
                nxt = []
                for gen in gens:
                    try:
                        next(gen)
                        nxt.append(gen)
                    except StopIteration:
                        pass
                gens = nxt
        S.barrier()


def load_wres(B, es, name, src, kparts, ncols, st_pool, eng="pool"):
    nc, S = B.nc, B.S
    w = es.enter_context(nc.sbuf_tensor(uname(name), [128, kparts, ncols], BF16))
    uw = Unit()
    srcv = src.rearrange("(k p) n -> p k n", p=128)
    for k0 in range(0, kparts, 4):
        kn = min(4, kparts - k0)
        for c0 in range(0, ncols, 512):
            st, ust = st_pool.get()
            B.dma(st[:, :kn, :], srcv[:, k0:k0 + kn, c0:c0 + 512], writes=[ust])
            S.add(eng, (lambda e, st=st, k0=k0, kn=kn, c0=c0: e.tensor_copy(out=w[:, k0:k0 + kn, c0:c0 + 512], in_=st[:, :kn, :])),
                  reads=[ust], writes=[uw])
    return w, uw


def merge_phase(B, l, es, xin, u_xin, tiles):
    nc, S, Sx, Ux, I = B.nc, B.S, B.Sx, B.Ux, B.I
    st_pool = TPool(nc, es, "mst", [128, 4, 512], F32, 2)
    Ws = []
    for nm in ("w_proj_dn", "w_proj_hy", "w_proj_lru", "w_out"):
        Ws.append(load_wres(B, es, nm + "_sb", I[nm][l], 8, D, st_pool))
    ot_p = TPool(nc, es, "mot", [128, 3, 8, 512], BF16, 1)
    gt_p = TPool(nc, es, "mgt", [128, 24, 512], BF16, 1)
    x_p = TPool(nc, es, "mx", [128, 8, 512], F32, 2)
    mg_p = TPool(nc, es, "mmg", [128, 8, 512], BF16, 1)
    t_p = TPool(nc, es, "mt", [128, 512], F32, 4)
    for (u0, n) in tiles:
        seg = 1 if u0 == 0 else 0
        ot, uot = ot_p.get()
        for bi, nm in enumerate(("odn", "ohy", "olru")):
            B.dma(ot[:, bi, :, :n], Sx[nm].rearrange("(k p) u -> p k u", p=128)[:, :, u0:u0 + n], reads=[Ux[nm]], writes=[uot])
        gt, ugt = gt_p.get()
        B.dma(gt[:, :, :n], Sx["gT"].rearrange("(k p) u -> p k u", p=128)[:, :, u0:u0 + n], reads=[Ux["gT"]], writes=[ugt])
        x, ux = x_p.get()
        B.dma(x[:, :, :n], xin.rearrange("(k p) u -> p k u", p=128)[:, :, u0:u0 + n], reads=[u_xin], writes=[ux])
        mg, umg = mg_p.get()
        for m in range(8):
            pss = []
            for bi in range(3):
                ps, ups = B.PS.get()
                w, uw = Ws[bi]
                for k in range(8):
                    S.add("pe", (lambda e, ps=ps, w=w, k=k, m=m, bi=bi, ot=ot, n=n: e.matmul(ps[:, :n], w[:, k, m * 128:(m + 1) * 128], ot[:, bi, k, :n],
                                                                                           start=(k == 0), stop=(k == 7))),
                          reads=[uw, uot], writes=[ups])
                pss.append((ps, ups))
            ta, uta = t_p.get()
            tb_, utb = t_p.get()
            S.add("dve", (lambda e, ta=ta, ps=pss[0][0], gt=gt, m=m, n=n: e.tensor_tensor(out=ta[:, :n], in0=ps[:, :n], in1=gt[:, m, :n], op=ALU.mult)),
                  reads=[pss[0][1], ugt], writes=[uta])
            S.add("dve", (lambda e, tb_=tb_, ps=pss[1][0], gt=gt, m=m, n=n: e.tensor_tensor(out=tb_[:, :n], in0=ps[:, :n], in1=gt[:, 8 + m, :n], op=ALU.mult)),
                  reads=[pss[1][1], ugt], writes=[utb])
            S.add("pool", (lambda e, ta=ta, tb_=tb_, n=n: e.tensor_tensor(out=ta[:, :n], in0=ta[:, :n], in1=tb_[:, :n], op=ALU.add)),
                  reads=[uta, utb], writes=[uta])
            tc_, utc = t_p.get()
            S.add("dve", (lambda e, tc_=tc_, ps=pss[2][0], gt=gt, m=m, n=n: e.tensor_tensor(out=tc_[:, :n], in0=ps[:, :n], in1=gt[:, 16 + m, :n], op=ALU.mult)),
                  reads=[pss[2][1], ugt], writes=[utc])
            S.add("pool", (lambda e, ta=ta, tc_=tc_, mg=mg, m=m, n=n: e.tensor_tensor(out=mg[:, m, :n], in0=ta[:, :n], in1=tc_[:, :n], op=ALU.add)),
                  reads=[uta, utc, umg], writes=[umg])
        w, uw = Ws[3]
        for nn in range(8):
            ps, ups = B.PS.get()
            for m in range(8):
                S.add("pe", (lambda e, ps=ps, m=m, nn=nn, mg=mg, n=n: e.matmul(ps[:, :n], w[:, m, nn * 128:(nn + 1) * 128], mg[:, m, :n],
                                                                              start=(m == 0), stop=(m == 7))),
                      reads=[uw, umg], writes=[ups])
            S.add("dve", (lambda e, ps=ps, x=x, nn=nn, n=n, seg=seg: e.scalar_tensor_tensor(out=x[:, nn, :n], in0=ps[:, :n], scalar=B.modcol(2, seg, nn),
                                                                                       in1=x[:, nn, :n], op0=ALU.mult, op1=ALU.add)),
                  reads=[ups, ux, B.u_modv], writes=[ux])
        B.dma(Sx["xA"].rearrange("(k p) u -> p k u", p=128)[:, :, u0:u0 + n], x[:, :, :n], reads=[ux], writes=[Ux["xA"]])


def ffn_phase(B, l, es, tiles):
    nc, S, Sx, Ux, I = B.nc, B.S, B.Sx, B.Ux, B.I
    with_ctx = tiles[0][0] == 0
    esu = ExitStack()
    hT = esu.enter_context(nc.sbuf_tensor(uname("hT2"), [128, 8, U], BF16))
    u_hT = [Unit() for _ in TT]
    with ExitStack() as es3:
        B.norm_to_hT(Sx["xA"], Ux["xA"], 1, hT, u_hT, tiles, es3, tile_ids=[TT.index(t) for t in tiles])
        S.barrier()
    with ExitStack() as es4:
        wst_pool = TPool(nc, es4, "fwst", [128, 8, 128], F32, 2)
        wbf_pool = TPool(nc, es4, "fwbf", [128, 8, 128], BF16, 2)
        up_p = TPool(nc, es4, "fup", [128, 66, 66], F32, 2)
        cp_p = TPool(nc, es4, "fcp", [128, 260], F32, 2)
        acc_p = TPool(nc, es4, "facc", [128, U], F32, 2)
        ab_p = TPool(nc, es4, "fab", [128, U], BF16, 2)
        for (t, ut) in up_p.t:
            S.add("pool", (lambda e, t=t: e.memset(t[:], 0.0)), writes=[ut])
        for (t, ut) in cp_p.t:
            S.add("pool", (lambda e, t=t: e.memset(t[:], 0.0)), writes=[ut])
        for j in range(FH // 128):
            accs = []
            for part in range(2):
                cidx = part * (FH // 128) + j
                wb, uwb = B.load_w_bf16(I["ffn_up"][l][:, cidx * 128:(cidx + 1) * 128], 128, wst_pool, wbf_pool)
                up, uup = up_p.get()
                cp, ucp = cp_p.get()
                for (u0, n) in tiles:
                    ti = TT.index((u0, n))
                    ps, ups = B.mm_tile(wb, uwb, 128, hT, u_hT, ti)
                    if u0 == 0:
                        S.add("act", (lambda e, ps=ps, cp=cp, n=n: e.activation(out=cp[:, 1:1 + n], in_=ps[:, :n], func=AF.Copy)), reads=[ups], writes=[ucp])
                    else:
                        r0 = (u0 - LT0) // 64
                        S.add("act", (lambda e, ps=ps, up=up, r0=r0: e.activation(out=up[:, 1 + r0:9 + r0, 1:65], in_=ps[:, :512].rearrange("p (r c) -> p r c", c=64),
                                                                                  func=AF.Copy)), reads=[ups], writes=[uup])
                acc, uacc = acc_p.get()
                cw = lambda tap, cidx=cidx: B.vs("ffncw", cidx * 9 + tap)
                av = acc[:, LT0:U].rearrange("p (r c) -> p r c", c=64)
                S.add("act", (lambda e, up=up, av=av, cw=cw: e.activation(out=av, in_=up[:, 0:64, 0:64], func=AF.Identity, scale=cw(0))),
                      reads=[uup, B.u_vec], writes=[uacc])
                for tap in range(1, 9):
                    di, dj = tap // 3, tap % 3
                    S.add("dve", (lambda e, up=up, av=av, cw=cw, tap=tap, di=di, dj=dj: e.scalar_tensor_tensor(
                        out=av, in0=up[:, di:di + 64, dj:dj + 64], scalar=cw(tap), in1=av, op0=ALU.mult, op1=ALU.add)),
                        reads=[uup, uacc, B.u_vec], writes=[uacc])
                if with_ctx:
                    S.add("act", (lambda e, cp=cp, acc=acc, cw=cw: e.activation(out=acc[:, 0:256], in_=cp[:, 0:256], func=AF.Identity, scale=cw(3))),
                          reads=[ucp, B.u_vec, uacc], writes=[uacc])
                    for tap in (4, 5):
                        S.add("dve", (lambda e, cp=cp, acc=acc, cw=cw, tap=tap: e.scalar_tensor_tensor(
                            out=acc[:, 0:256], in0=cp[:, tap - 3:tap - 3 + 256], scalar=cw(tap), in1=acc[:, 0:256], op0=ALU.mult, op1=ALU.add)),
                            reads=[ucp, uacc, B.u_vec], writes=[uacc])
                accs.append((acc, uacc))
            (ag, uag), (av_, uav) = accs
            lo = 0 if with_ctx else LT0
            S.add("act", (lambda e, ag=ag, lo=lo: e.activation(out=ag[:, lo:U], in_=ag[:, lo:U], func=AF.Silu)), reads=[uag], writes=[uag])
            ab, uab = ab_p.get()
            S.add("pool", (lambda e, ab=ab: e.memset(ab[:, 0:LT0], 0.0)), writes=[uab])
            S.add("dve", (lambda e, ag=ag, av_=av_, ab=ab, lo=lo: e.tensor_tensor(out=ab[:, lo:U], in0=ag[:, lo:U], in1=av_[:, lo:U], op=ALU.mult)),
                  reads=[uag, uav, uab], writes=[uab])
            B.dma(Sx["actT"][j * 128:(j + 1) * 128], ab[:], reads=[uab], writes=[Ux["actT"]])
        S.barrier()
    esu.close()
    st_pool = TPool(nc, es, "dst", [128, 4, 512], F32, 2)
    wd, uwd = load_wres(B, es, "wdown", I["ffn_down"][l], FH // 128, D, st_pool)
    at_p = TPool(nc, es, "dat", [128, FH // 128, 512], BF16, 2)
    x_p = TPool(nc, es, "dx", [128, 8, 512], F32, 2)
    for (u0, n) in tiles:
        seg = 1 if u0 == 0 else 0
        at, uat = at_p.get()
        B.dma(at[:, :, :n], Sx["actT"].rearrange("(k p) u -> p k u", p=128)[:, :, u0:u0 + n], reads=[Ux["actT"]], writes=[uat])
        x, ux = x_p.get()
        B.dma(x[:, :, :n], Sx["xA"].rearrange("(k p) u -> p k u", p=128)[:, :, u0:u0 + n], reads=[Ux["xA"]], writes=[ux])
        for nn in range(8):
            ps, ups = B.PS.get()
            for j in range(FH // 128):
                S.add("pe", (lambda e, ps=ps, j=j, nn=nn, at=at, n=n: e.matmul(ps[:, :n], wd[:, j, nn * 128:(nn + 1) * 128], at[:, j, :n],
                                                                              start=(j == 0), stop=(j == FH // 128 - 1))),
                      reads=[uwd, uat], writes=[ups])
            S.add("dve", (lambda e, ps=ps, x=x, nn=nn, n=n, seg=seg: e.scalar_tensor_tensor(out=x[:, nn, :n], in0=ps[:, :n], scalar=B.modcol(5, seg, nn),
                                                                                       in1=x[:, nn, :n], op0=ALU.mult, op1=ALU.add)),
                  reads=[ups, ux, B.u_modv], writes=[ux])
        B.dma(Sx["xB"].rearrange("(k p) u -> p k u", p=128)[:, :, u0:u0 + n], x[:, :, :n], reads=[ux], writes=[Ux["xB"]])


def final_phase(B, es):
    nc, S, Sx, Ux = B.nc, B.S, B.Sx, B.Ux
    xp = TPool(nc, es, "fx", [128, 8, 512], F32, 2)
    sqp = TPool(nc, es, "fsq", [128, 8, 512], F32R, 1)
    rsp = TPool(nc, es, "frs", [128, 512], F32, 2)
    for (u0, n) in TT[1:]:
        x, ux = xp.get()
        B.dma(x[:], Sx["xB"].rearrange("(k p) u -> p k u", p=128)[:, :, u0:u0 + n], reads=[Ux["xB"]], writes=[ux])
        sq, usq = sqp.get()
        S.add("act", (lambda e, x=x, sq=sq: e.activation(out=sq[:], in_=x[:], func=AF.Square)), reads=[ux], writes=[usq])
        ps, ups = B.PS.get()
        for k in range(8):
            S.add("pe", (lambda e, k=k, sq=sq, ps=ps: e.matmul(ps[:], B.ones[:], sq[:, k, :], start=(k == 0), stop=(k == 7))),
                  reads=[usq, B.u_ones], writes=[ups])
        rs, urs = rsp.get()
        S.add("act", (lambda e, rs=rs, ps=ps: e.activation(out=rs[:], in_=ps[:], func=AF.Sqrt, scale=1.0 / D, bias=B.eps6[:])), reads=[ups], writes=[urs])
        S.add("dve", (lambda e, rs=rs: e.reciprocal(out=rs[:], in_=rs[:])), reads=[urs], writes=[urs])
        S.add("dve", (lambda e, x=x, rs=rs: e.tensor_tensor(out=x[:], in0=x[:], in1=rs[:].unsqueeze(1).to_broadcast([128, 8, 512]), op=ALU.mult)),
              reads=[urs, ux], writes=[ux])
        S.add("dve", (lambda e, x=x: e.tensor_tensor(out=x[:], in0=x[:], in1=B.vs("fng").unsqueeze(2).to_broadcast([128, 8, 512]), op=ALU.mult)),
              reads=[ux, B.u_vec], writes=[ux])
        t0 = u0 - LT0
        B.dma(B.outT.rearrange("(k p) t -> p k t", p=128)[:, :, t0:t0 + n], x[:], reads=[ux])


_CACHE = {}


def kernel(**inputs):
    inp = {k: np.asarray(v) for k, v in inputs.items()}
    if "nc" not in _CACHE:
        b = Builder(debug=False)
        _CACHE["nc"] = b.build()
    nc = _CACHE["nc"]
    bsz = inp["x"].shape[0]
    in_maps = [prep_inputs(inp, b_) for b_ in range(bsz)]
    res = run_bass_kernel_spmd(nc, in_maps, core_ids=list(range(bsz)))
    out = np.stack([np.ascontiguousarray(np.asarray(r["outT"]).T) for r in res.results])
    return out.astype(np.float32)
```

```python
import math
from contextlib import ExitStack

import numpy as np
import ml_dtypes
import concourse.bass as bass
import concourse.mybir as mybir
from concourse.bass_utils import run_bass_kernel_spmd

F32 = mybir.dt.float32
F32R = mybir.dt.float32r
BF16 = mybir.dt.bfloat16
ALU = mybir.AluOpType
AF = mybir.ActivationFunctionType
AX = mybir.AxisListType

D = 1024
L = 4096
LC = 256
U = 4355
LT0 = 259
UP = 4360
DEPTH = 2
NIN = 12320
OFF_QKV, OFF_Z, OFF_AB, OFF_HY, OFF_LX, OFF_LY, OFF_GATE = 0, 3072, 4096, 4128, 7200, 8224, 9248
FH = 2816
TT = [(0, 256)] + [(LT0 + 512 * i, 512) for i in range(8)]
NCORES = 4


class Unit:
    __slots__ = ("w", "r", "excl", "wd")

    def __init__(self, excl=False):
        self.w = None
        self.r = []
        self.wd = []
        self.excl = excl


class Op:
    __slots__ = ("eng", "fn", "deps", "dma", "need_inc", "sem", "val")

    def __init__(self, eng, fn, dma):
        self.eng = eng
        self.fn = fn
        self.dma = dma
        self.deps = []
        self.need_inc = False
        self.sem = None
        self.val = 0


class Sched:
    EPOCH = 30000
    NDMA = 24

    def __init__(self, nc, es):
        self.nc = nc
        self.es = es
        self.ops = []
        self.engs = {"pe": nc.tensor, "act": nc.scalar, "dve": nc.vector,
                     "pool": nc.gpsimd, "sp": nc.sync}
        self.last = {k: None for k in self.engs}
        self.dmas_since_barrier = []
        self.bar_deps = {k: [] for k in self.engs}
        self.nsem = 0

    def add(self, eng, fn, reads=(), writes=(), dma=False):
        op = Op(eng, fn, dma)
        deps = []
        for u in reads:
            if u.w is not None:
                deps.append(u.w)
            deps.extend(u.wd)
            if u.excl:
                deps.extend(o for o in u.r if o.eng != eng)
        for u in writes:
            if u.w is not None:
                deps.append(u.w)
            deps.extend(u.wd)
            deps.extend(u.r)
        if self.bar_deps[eng]:
            deps.extend(self.bar_deps[eng])
            self.bar_deps[eng] = []
        seen = set()
        for d in deps:
            if d is op or id(d) in seen:
                continue
            if d.eng == "pe" and eng == "pe" and not d.dma and not dma:
                continue
            seen.add(id(d))
            d.need_inc = True
            op.deps.append(d)
        for u in reads:
            if not dma:
                u.r = [o for o in u.r if o.dma or o.eng != eng]
            u.r.append(op)
        for u in writes:
            u.w = op
            u.r = []
            if dma:
                u.wd.append(op)
                if len(u.wd) > 48:
                    u.wd = u.wd[-48:]
            else:
                u.wd = []
        if dma:
            op.need_inc = True
            self.dmas_since_barrier.append(op)
        self.ops.append(op)
        self.last[eng] = op
        return op

    def barrier(self):
        deps = [o for o in self.last.values() if o is not None] + self.dmas_since_barrier
        self.dmas_since_barrier = []
        for k in self.engs:
            self.bar_deps[k] = list(deps)

    def _newsem(self):
        self.nsem += 1
        return self.es.enter_context(self.nc.semaphore("s%d" % self.nsem))

    def emit(self):
        self.barrier()
        self.add("sp", lambda e: None)
        esem, ecount = {}, {}
        dsem = [self._newsem() for _ in range(self.NDMA)]
        dcount = [0] * self.NDMA
        nd = 0
        seen = {k: {} for k in self.engs}
        for op in self.ops:
            e = self.engs[op.eng]
            waits = []
            if op.dma:
                j = nd % self.NDMA
                nd += 1
                if dcount[j]:
                    waits.append((dsem[j], dcount[j]))
                if dcount[j] >= 30000:
                    dsem[j] = self._newsem()
                    dcount[j] = 0
                dcount[j] += 16
                op.sem, op.val = dsem[j], dcount[j]
            elif op.need_inc:
                if op.eng not in esem or ecount[op.eng] >= self.EPOCH:
                    esem[op.eng] = self._newsem()
                    ecount[op.eng] = 0
                ecount[op.eng] += 1
                op.sem, op.val = esem[op.eng], ecount[op.eng]
            for d in op.deps:
                waits.append((d.sem, d.val))
            sn = seen[op.eng]
            for (s, v) in waits:
                if sn.get(id(s), 0) >= v:
                    continue
                sn[id(s)] = v
                e.wait_ge(s, v)
            ins = op.fn(e)
            if op.sem is not None and ins is not None:
                ins.then_inc(op.sem, 16 if op.dma else 1)
        return len(self.ops)


_UID = [0]


def uname(name):
    _UID[0] += 1
    return "%s_%d" % (name, _UID[0])


class TPool:
    def __init__(self, nc, es, name, shape, dtype, n, psum=False):
        self.t = []
        for i in range(n):
            mk = nc.psum_tensor if psum else nc.sbuf_tensor
            self.t.append((es.enter_context(mk(uname(name), shape, dtype)), Unit(excl=psum)))
        self.i = 0

    def get(self):
        r = self.t[self.i % len(self.t)]
        self.i += 1
        return r


def _pk(v):
    return np.ascontiguousarray(v.reshape(-1, 128).T)


VEC_FIELDS = [("g1", 8), ("g2", 8), ("bmod", 48), ("dncw", 96), ("hycw", 72), ("hycb", 24),
              ("lrucw", 32), ("lrucb", 8), ("lba", 16), ("lbx", 16), ("llam", 16), ("hybias", 8),
              ("ffncw", 396), ("fng", 8), ("dnng", 1), ("alog", 1), ("dtb", 1),
              ("hyb1", 1), ("hyf1", 1), ("hyb2", 1), ("hyf2", 1), ("hyb32", 32)]
VOFF = {}
_o = 0
for _n, _w in VEC_FIELDS:
    VOFF[_n] = (_o, _w)
    _o += _w
NV = _o


def build_vec(inp, l):
    v = np.zeros((128, NV), np.float32)

    def put(name, arr):
        o, w = VOFF[name]
        v[:arr.shape[0], o:o + w] = arr.reshape(arr.shape[0], w)
    put("g1", _pk(inp["norm1_g"][l]))
    put("g2", _pk(inp["norm2_g"][l]))
    put("bmod", _pk(inp["b_mod"][l]))
    put("dncw", inp["dn_conv_w"][l].reshape(4, 24, 128).transpose(2, 1, 0))
    put("hycw", inp["hy_conv_w"][l].reshape(3, 24, 128).transpose(2, 1, 0))
    put("hycb", _pk(inp["hy_conv_b"][l]))
    put("lrucw", inp["lru_conv_w"][l].reshape(4, 8, 128).transpose(2, 1, 0))
    put("lrucb", _pk(inp["lru_conv_b"][l]))
    put("lba", inp["lru_b_a"][l].reshape(2, 8, 128).transpose(2, 0, 1))
    put("lbx", inp["lru_b_x"][l].reshape(2, 8, 128).transpose(2, 0, 1))
    put("llam", inp["lru_lambda"][l].reshape(2, 8, 128).transpose(2, 0, 1))
    put("hybias", _pk(inp["hy_bias"][l]))
    put("ffncw", inp["ffn_conv_w"][l].reshape(9, 44, 128).transpose(2, 1, 0))
    put("fng", _pk(inp["final_norm_g"]))
    put("dnng", inp["dn_norm_g"][l].reshape(128, 1))
    al = np.zeros((40, 1), np.float32)
    db = np.zeros((40, 1), np.float32)
    for d in range(2):
        al[d * 32:d * 32 + 8, 0] = inp["dn_a_log"][l][d]
        db[d * 32:d * 32 + 8, 0] = inp["dn_dt_bias"][l][d]
    put("alog", al)
    put("dtb", db)
    put("hyb1", inp["hy_b1"][l].reshape(64, 1))
    put("hyf1", inp["hy_f1"][l].reshape(64, 1))
    put("hyb2", inp["hy_b2"][l].reshape(64, 1))
    put("hyf2", inp["hy_f2"][l].reshape(64, 1))
    put("hyb32", np.ascontiguousarray(inp["hy_bias"][l].reshape(32, 32).T))
    return v


CST_FIELDS = [("ident", 128), ("lowi", 128), ("lows", 128), ("uppi", 128), ("upps", 128),
              ("deltas", 8)]
COFF = {}
_o = 0
for _n, _w in CST_FIELDS:
    COFF[_n] = (_o, _w)
    _o += _w
NCST = _o


def build_cst():
    c = np.zeros((128, NCST), np.float32)
    i = np.arange(128)[:, None]
    j = np.arange(128)[None, :]
    same = (i // 64) == (j // 64)

    def put(name, arr):
        o, w = COFF[name]
        c[:, o:o + w] = arr
    put("ident", (i == j).astype(np.float32))
    put("lowi", ((i >= j) & same).astype(np.float32))
    put("lows", ((i > j) & same).astype(np.float32))
    put("uppi", ((i <= j) & same).astype(np.float32))
    put("upps", ((i < j) & same).astype(np.float32))
    lt = math.log(1e-2)
    deltas = np.abs(np.linspace(lt / 1.5, lt / 0.3, 1024, dtype=np.float32))
    put("deltas", _pk(deltas))
    return c


def build_rmask():
    m = np.ones((2, U), np.float32)
    starts = list(range(0, 256, 64)) + list(range(LT0, U, 64))
    for s in starts:
        m[0, s] = 0.0
        m[1, s + 63] = 0.0
    out = np.zeros((40, U), np.float32)
    out[0:8] = m[0]
    out[32:40] = m[1]
    return out


class Builder:
    def __init__(self, debug=False, stop=None, only=None, feed=()):
        self.debug = debug
        self.stop = stop
        self.only = only
        self.feed = set(feed)
        self.nc = bass.Bass("TRN2", target_bir_lowering=False)
        self.dbg_names = []

    def din(self, name, shape, dt=F32):
        return self.nc.dram_tensor(name, list(shape), dt, kind="ExternalInput").ap()

    def dscr(self, name, shape, dt=F32):
        kind = "ExternalOutput" if self.debug else "Internal"
        if name in self.feed:
            kind = "ExternalInput"
        elif self.debug:
            self.dbg_names.append(name)
        return self.nc.dram_tensor(name, list(shape), dt, kind=kind).ap()

    def vs(self, name, k=None):
        o, w = VOFF[name]
        if k is None:
            return self.vec[:, o:o + w]
        return self.vec[:, o + k:o + k + 1]

    def cs(self, name):
        o, w = COFF[name]
        return self.cst[:, o:o + w]

    def dma(self, out, in_, reads=(), writes=(), eng="sp"):
        return self.S.add(eng, lambda e: e.dma_start(out=out, in_=in_), reads=reads, writes=writes, dma=True)

    def build(self):
        nc = self.nc
        I = {}
        I["xT0"] = self.din("xT0", [D, U])
        I["cc"] = self.din("cc", [128, 16])
        I["vec"] = self.din("vec", [DEPTH, 128, NV])
        I["cst"] = self.din("cst", [128, NCST])
        I["rmask"] = self.din("rmask", [40, U])
        I["hy_w1"] = self.din("hy_w1", [DEPTH, 33, 64])
        I["hy_w2"] = self.din("hy_w2", [DEPTH, 64, 64])
        I["hy_w3"] = self.din("hy_w3", [DEPTH, 64, 2048])
        if self.only == "mg":
            I["w_mod"] = self.din("w_mod", [DEPTH, D, 6 * D])
            for n in ("w_proj_dn", "w_proj_hy", "w_proj_lru", "w_out"):
                I[n] = self.din(n, [DEPTH, D, D])
            I["ffn_up"] = self.din("ffn_up", [DEPTH, D, 2 * FH])
            I["ffn_down"] = self.din("ffn_down", [DEPTH, FH, D])
        if self.only:
            self.I = I
            return self.build2()
        I["w_mod"] = self.din("w_mod", [DEPTH, D, 6 * D])
        I["w_in"] = self.din("w_in", [DEPTH, D, NIN])
        I["lru_w_a"] = self.din("lru_w_a", [DEPTH, 2, 8, 128, 128])
        I["lru_w_x"] = self.din("lru_w_x", [DEPTH, 2, 8, 128, 128])
        for n in ("w_proj_dn", "w_proj_hy", "w_proj_lru", "w_out"):
            I[n] = self.din(n, [DEPTH, D, D])
        I["ffn_up"] = self.din("ffn_up", [DEPTH, D, 2 * FH])
        I["ffn_down"] = self.din("ffn_down", [DEPTH, FH, D])
        self.I = I
        return self.build2()

    def build2(self):
        nc, I = self.nc, self.I
        self.outT = nc.dram_tensor("outT", [D, L], F32, kind="ExternalOutput").ap()
        Sx = {}
        Sx["qT"] = self.dscr("qT", [8, 128, U])
        Sx["kT"] = self.dscr("kT", [8, 128, U])
        Sx["vT"] = self.dscr("vT", [8, 128, U])
        Sx["szT"] = self.dscr("szT", [D, U])
        Sx["gab"] = self.dscr("gab", [3, 40, U])
        Sx["zzT"] = self.dscr("zzT", [D, U])
        Sx["x0T"] = self.dscr("x0T", [D, U])
        Sx["olru"] = self.dscr("olru", [D, U], BF16)
        Sx["odn"] = self.dscr("odn", [D, U], BF16)
        Sx["ohy"] = self.dscr("ohy", [D, U], BF16)
        Sx["gT"] = self.dscr("gT", [3 * D, U], BF16)
        Sx["xA"] = self.dscr("xA", [D, U])
        Sx["xB"] = self.dscr("xB", [D, U])
        Sx["actT"] = self.dscr("actT", [FH, U], BF16)
        Sx["hp_lat"] = self.dscr("hp_lat", [2, D, L])
        Sx["hp_ctx"] = self.dscr("hp_ctx", [2, D, LC])
        for seg, n_ in (("lat", L), ("ctx", LC)):
            N_, NS1_, NF1_, NFH_ = hy_dims(n_)
            I["hy_zT_" + seg] = self.din("hy_zT_" + seg, [33, n_])
            I["hy_negt_" + seg] = self.din("hy_negt_" + seg, [128, n_])
            I["hy_F1_" + seg] = self.din("hy_F1_" + seg, [NS1_, 2 * NFH_])
            I["hy_G_" + seg] = self.din("hy_G_" + seg, [128, NFH_, 2, 128])
            I["hy_Tout_" + seg] = self.din("hy_Tout_" + seg, [NFH_, 128, 2, NS1_])
        I["hy_Winv"] = self.din("hy_Winv", [128, 2, 256])
        self.Sx = Sx
        self.Ux = {k: Unit() for k in Sx}

        with ExitStack() as es:
            self.es = es
            self.S = Sched(nc, es)
            S = self.S
            self.cst = es.enter_context(nc.sbuf_tensor("cst_sb", [128, NCST], F32))
            self.vec = es.enter_context(nc.sbuf_tensor("vec_sb", [128, NV], F32))
            self.ones = es.enter_context(nc.sbuf_tensor("ones", [128, 128], F32R))
            self.sc = es.enter_context(nc.sbuf_tensor("sc", [128, 16], F32))
            self.modv = es.enter_context(nc.sbuf_tensor("modv", [128, 48, 2], F32))
            self.der = es.enter_context(nc.sbuf_tensor("der", [128, 2, 2, 8], F32))
            self.u_cst, self.u_vec, self.u_ones, self.u_sc = Unit(), Unit(), Unit(), Unit()
            self.u_modv, self.u_der = Unit(), Unit()
            self.PS = TPool(nc, es, "ps", [128, 512], F32, 8, psum=True)
            self.dma(self.cst[:], I["cst"], writes=[self.u_cst])
            self.eps6 = es.enter_context(nc.sbuf_tensor("eps6", [128, 1], F32))
            S.add("dve", lambda e: e.memset(self.eps6[:], 1e-6), writes=[self.u_cst])
            self.one1 = es.enter_context(nc.sbuf_tensor("one1", [128, 1], F32))
            S.add("dve", lambda e: e.memset(self.one1[:], 1.0), writes=[self.u_cst])
            ones32 = es.enter_context(nc.sbuf_tensor("ones32", [128, 128], F32))
            u_o32 = Unit()
            S.add("dve", lambda e: e.memset(ones32[:], 1.0), writes=[u_o32])
            S.add("act", lambda e: e.activation(out=self.ones[:], in_=ones32[:], func=AF.Copy), reads=[u_o32], writes=[self.u_ones])
            self.ones_bf = es.enter_context(nc.sbuf_tensor("ones_bf", [128, 128], BF16))
            S.add("act", lambda e: e.activation(out=self.ones_bf[:], in_=ones32[:], func=AF.Copy), reads=[u_o32], writes=[self.u_ones])
            self.dma(self.sc[:], I["cc"], writes=[self.u_sc])
            S.add("act", lambda e: e.activation(out=self.sc[:], in_=self.sc[:], func=AF.Silu),
                  reads=[self.u_sc], writes=[self.u_sc])
            xin = I["xT0"]
            u_xin = Unit()
            for l in range(DEPTH):
                self.l = l
                self.dma(self.vec[:], I["vec"][l], writes=[self.u_vec])
                if self.only == "dn":
                    with ExitStack() as es2:
                        dn_phase(self, l, es2)
                        S.barrier()
                    break
                if self.only == "mg":
                    self.phase_mod(l)
                    with ExitStack() as es2:
                        merge_phase(self, l, es2, xin, u_xin, TT)
                        S.barrier()
                    with ExitStack() as es2:
                        ffn_phase(self, l, es2, TT)
                        S.barrier()
                    with ExitStack() as es2:
                        final_phase(self, es2)
                        S.barrier()
                    break
                if self.only == "hy":
                    for seg in (("lat", "ctx") if "ctx" in self.stop else ("lat",)):
                        with ExitStack() as es2:
                            hyena_phase(self, l, es2, seg)
                            S.barrier()
                    break
                self.phase_mod(l)
                if self.stop == "mod":
                    break
                with ExitStack() as es2:
                    self.phase_mixer_pre(l, es2, xin, u_xin)
                    S.barrier()
                if self.stop and self.stop.startswith("pre"):
                    break
                with ExitStack() as es2:
                    dn_phase(self, l, es2)
                    S.barrier()
                if self.stop and self.stop.startswith("dn"):
                    break
                for seg in (("lat", "ctx") if l < DEPTH - 1 else ("lat",)):
                    with ExitStack() as es2:
                        hyena_phase(self, l, es2, seg)
                        S.barrier()
                if self.stop and self.stop.startswith("hy"):
                    break
                tiles = TT if l < DEPTH - 1 else TT[1:]
                with ExitStack() as es2:
                    merge_phase(self, l, es2, xin, u_xin, tiles)
                    S.barrier()
                if self.stop and self.stop.startswith("mg"):
                    break
                with ExitStack() as es2:
                    ffn_phase(self, l, es2, tiles)
                    S.barrier()
                xin, u_xin = Sx["xB"], self.Ux["xB"]
                if self.stop and self.stop.startswith("ffn"):
                    break
                if l == DEPTH - 1:
                    with ExitStack() as es2:
                        final_phase(self, es2)
                        S.barrier()
            n = S.emit()
        self.n_ops = n
        return nc

    def phase_mod(self, l):
        nc, S = self.nc, self.S
        with ExitStack() as es:
            wm_pool = TPool(nc, es, "wm", [128, 8, 512], F32, 2)
            ps, ups = self.PS.get()
            for pn in range(12):
                wm, uwm = wm_pool.get()
                self.dma(wm[:], self.I["w_mod"][l][:, pn * 512:(pn + 1) * 512].rearrange("(k p) n -> p k n", p=128),
                         writes=[uwm])
                for cc in range(4):
                    c = pn * 4 + cc
                    for k in range(8):
                        S.add("pe", (lambda e, c=c, cc=cc, k=k, wm=wm: e.matmul(
                            ps[:, 2 * c:2 * c + 2], wm[:, k, cc * 128:(cc + 1) * 128], self.sc[:, 2 * k:2 * k + 2],
                            start=(k == 0), stop=(k == 7))), reads=[uwm, self.u_sc], writes=[ups])
            bm = self.vs("bmod")
            for s in range(2):
                S.add("dve", (lambda e, s=s: e.tensor_tensor(out=self.modv[:, :, s], in0=ps[:, s:96:2], in1=bm, op=ALU.add)),
                      reads=[ups, self.u_vec], writes=[self.u_modv])
            for w, (gname, j) in enumerate((("g1", 1), ("g2", 4))):
                for s in range(2):
                    S.add("dve", (lambda e, w=w, s=s, j=j, gname=gname: e.scalar_tensor_tensor(
                        out=self.der[:, w, s, :], in0=self.modv[:, j * 8:(j + 1) * 8, s], scalar=1.0, in1=self.vs(gname),
                        op0=ALU.add, op1=ALU.mult)), reads=[self.u_modv, self.u_vec], writes=[self.u_der])
            S.barrier()

    def modcol(self, j, s, k):
        return self.modv[:, j * 8 + k, s:s + 1]

    def norm_to_hT(self, xsrc, u_xsrc, which, hT, u_hT, tiles, es, tile_ids=None):
        nc, S = self.nc, self.S
        xp = TPool(nc, es, "nx", [128, 8, 512], F32, 2)
        sqp = TPool(nc, es, "nsq", [128, 8, 512], F32R, 1)
        rsp = TPool(nc, es, "nrs", [128, 512], F32, 2)
        shj = 0 if which == 0 else 3
        for ti_, (u0, n) in enumerate(tiles):
            ti = tile_ids[ti_] if tile_ids is not None else ti_
            seg = 1 if u0 == 0 else 0
            x, ux = xp.get()
            self.dma(x[:, :, :n], xsrc.rearrange("(k p) t -> p k t", p=128)[:, :, u0:u0 + n], reads=[u_xsrc], writes=[ux])
            sq, usq = sqp.get()
            S.add("act", (lambda e, x=x, sq=sq, n=n: e.activation(out=sq[:, :, :n], in_=x[:, :, :n], func=AF.Square)),
                  reads=[ux], writes=[usq])
            ps, ups = self.PS.get()
            for k in range(8):
                S.add("pe", (lambda e, k=k, sq=sq, ps=ps, n=n: e.matmul(ps[:, :n], self.ones[:], sq[:, k, :n],
                                                                         start=(k == 0), stop=(k == 7))),
                      reads=[usq, self.u_ones], writes=[ups])
            rs, urs = rsp.get()
            S.add("act", (lambda e, rs=rs, ps=ps, n=n: e.activation(out=rs[:, :n], in_=ps[:, :n], func=AF.Sqrt, scale=1.0 / D, bias=self.eps6[:])),
                  reads=[ups], writes=[urs])
            S.add("dve", (lambda e, rs=rs, n=n: e.reciprocal(out=rs[:, :n], in_=rs[:, :n])), reads=[urs], writes=[urs])
            S.add("dve", (lambda e, x=x, rs=rs, n=n: e.tensor_tensor(out=x[:, :, :n], in0=x[:, :, :n],
                                                                      in1=rs[:, :n].unsqueeze(1).to_broadcast([128, 8, n]), op=ALU.mult)),
                  reads=[urs, ux], writes=[ux])
            for k in range(8):
                S.add("act", (lambda e, k=k, x=x, n=n, u0=u0, seg=seg: e.activation(
                    out=hT[:, k, u0:u0 + n], in_=x[:, k, :n], func=AF.Identity,
                    scale=self.der[:, which, seg, k:k + 1], bias=self.modcol(shj, seg, k))),
                    reads=[ux, self.u_der, self.u_modv], writes=[u_hT[ti]])

    def load_w_bf16(self, wsrc, M, wst_pool, wbf_pool, eng="pool"):
        S = self.S
        wst, uws = wst_pool.get()
        self.dma(wst[:, :, :M], wsrc.rearrange("(k p) m -> p k m", p=128), writes=[uws])
        wb, uwb = wbf_pool.get()
        S.add(eng, (lambda e, wb=wb, wst=wst, M=M: e.tensor_copy(out=wb[:, :, :M], in_=wst[:, :, :M])),
              reads=[uws], writes=[uwb])
        return wb, uwb

    def mm_tile(self, wb, uwb, M, hT, u_hT, ti):
        S = self.S
        u0, n = TT[ti]
        ps, ups = self.PS.get()
        for k in range(8):
            S.add("pe", (lambda e, k=k, ps=ps, wb=wb: e.matmul(ps[:M, :n], wb[:, k, :M], hT[:, k, u0:u0 + n],
                                                                start=(k == 0), stop=(k == 7))),
                  reads=[uwb, u_hT[ti]], writes=[ups])
        return ps, ups

    def conv(self, pp, upp, cw, ntap, bias, out_ap, uout):
        S = self.S
        acc, uacc = self._acc, self._uacc
        S.add("act", (lambda e: e.activation(out=acc[:, :U], in_=pp[:, 0:U], func=AF.Identity,
                                             scale=cw(0), bias=(bias if bias is not None else 0.0))),
              reads=[upp, self.u_vec], writes=[uacc])
        for j in range(1, ntap):
            last = j == ntap - 1
            o = out_ap if last else acc[:, :U]
            uo = uout if last else uacc
            S.add("dve", (lambda e, j=j, o=o: e.scalar_tensor_tensor(out=o, in0=pp[:, j:j + U], scalar=cw(j),
                                                                    in1=acc[:, :U], op0=ALU.mult, op1=ALU.add)),
                  reads=[upp, uacc, self.u_vec], writes=[uo])

    def phase_mixer_pre(self, l, es, xin, u_xin):
        nc, S, I, Sx, Ux = self.nc, self.S, self.I, self.Sx, self.Ux
        hT = es.enter_context(nc.sbuf_tensor(uname("hT"), [128, 8, U], BF16))
        u_hT = [Unit() for _ in TT]
        with ExitStack() as es3:
            self.norm_to_hT(xin, u_xin, 0, hT, u_hT, TT, es3)
            S.barrier()
        if self.stop == "norm":
            dbg = self.nc.dram_tensor(uname("dbg_hT"), [128, 8, U], BF16, kind="ExternalOutput").ap()
            self.dma(dbg, hT[:], reads=u_hT)
            return
        WK = TPool(nc, es, "wk", [128, UP], F32, 4)
        PP = TPool(nc, es, "pp", [128, UP], F32, 1)
        wst_pool = TPool(nc, es, "wst", [128, 8, 128], F32, 2)
        wbf_pool = TPool(nc, es, "wbf", [128, 8, 128], BF16, 2)
        rsp = TPool(nc, es, "rs", [128, 512], F32, 2)
        gtp = TPool(nc, es, "gt", [128, 512], BF16, 4)
        ztp = TPool(nc, es, "zt", [128, 512], F32, 3)
        lwp = TPool(nc, es, "lw", [128, 128], F32, 2)
        lwr = TPool(nc, es, "lwr", [128, 128], BF16, 2)
        sm = es.enter_context(nc.sbuf_tensor(uname("sm"), [128, 40], F32))
        u_sm = Unit()
        pp, upp = PP.get()
        S.add("pool", lambda e: e.memset(pp[:], 0.0), writes=[upp])
        sqb = es.enter_context(nc.sbuf_tensor(uname("sqb"), [128, UP], BF16))
        usqb = Unit()
        self._cnt = 0

        def evac_copy(ps, ups, out_ap, uo, M=128, n=512):
            self._cnt += 1
            if self._cnt % 2:
                S.add("act", (lambda e: e.activation(out=out_ap, in_=ps[:M, :n], func=AF.Copy)), reads=[ups], writes=[uo])
            else:
                S.add("dve", (lambda e: e.tensor_copy(out=out_ap, in_=ps[:M, :n])), reads=[ups], writes=[uo])

        def proj_to_pp(col0):
            wb, uwb = self.load_w_bf16(I["w_in"][l][:, col0:col0 + 128], 128, wst_pool, wbf_pool)
            for ti, (u0, n) in enumerate(TT):
                ps, ups = self.mm_tile(wb, uwb, 128, hT, u_hT, ti)
                evac_copy(ps, ups, pp[:, u0 + 1:u0 + 1 + n], upp, 128, n)

        def proj_act(col0, func, out_tile_fn, bias=None):
            wb, uwb = self.load_w_bf16(I["w_in"][l][:, col0:col0 + 128], 128, wst_pool, wbf_pool)
            for ti, (u0, n) in enumerate(TT):
                ps, ups = self.mm_tile(wb, uwb, 128, hT, u_hT, ti)
                o, uo = out_tile_fn(ti, u0, n)
                S.add("act", (lambda e, ps=ps, o=o, n=n: e.activation(out=o, in_=ps[:, :n], func=func)),
                      reads=[ups], writes=[uo])

        S.add("act", lambda e: e.activation(out=sm[:40, 0:1], in_=self.vs("alog")[:40], func=AF.Exp),
              reads=[self.u_vec], writes=[u_sm])
        S.add("dve", lambda e: e.tensor_scalar(out=sm[:40, 0:1], in0=sm[:40, 0:1], scalar1=-1.0, scalar2=None, op0=ALU.mult),
              reads=[u_sm], writes=[u_sm])
        S.add("act", lambda e: e.activation(out=sm[:, 8:24], in_=self.vs("llam"), func=AF.Exp, scale=-1.0),
              reads=[self.u_vec, u_sm], writes=[u_sm])
        S.add("act", lambda e: e.activation(out=sm[:, 8:24], in_=sm[:, 8:24], func=AF.Ln, bias=1.0),
              reads=[u_sm], writes=[u_sm])
        S.add("dve", lambda e: e.tensor_scalar(out=sm[:, 8:24], in0=sm[:, 8:24], scalar1=-8.0, scalar2=None, op0=ALU.mult),
              reads=[u_sm], writes=[u_sm])
        S.add("dve", lambda e: e.tensor_scalar(out=sm[:, 24:40], in0=sm[:, 8:24], scalar1=2.0, scalar2=None, op0=ALU.mult),
              reads=[u_sm], writes=[u_sm])

        def z_chunk(c):
            wb, uwb = self.load_w_bf16(I["w_in"][l][:, OFF_Z + c * 128:OFF_Z + (c + 1) * 128], 128, wst_pool, wbf_pool)
            for ti, (u0, n) in enumerate(TT):
                ps, ups = self.mm_tile(wb, uwb, 128, hT, u_hT, ti)
                t, ut = ztp.get()
                S.add("act", (lambda e, ps=ps, t=t, n=n: e.activation(out=t[:, :n], in_=ps[:, :n], func=AF.Silu)),
                      reads=[ups], writes=[ut])
                self.dma(Sx["szT"][c * 128:(c + 1) * 128, u0:u0 + n], t[:, :n], reads=[ut], writes=[Ux["szT"]])

        def gate_chunk(c):
            wb, uwb = self.load_w_bf16(I["w_in"][l][:, OFF_GATE + c * 128:OFF_GATE + (c + 1) * 128], 128, wst_pool, wbf_pool)
            for ti, (u0, n) in enumerate(TT):
                ps, ups = self.mm_tile(wb, uwb, 128, hT, u_hT, ti)
                t, ut = gtp.get()
                S.add("act", (lambda e, ps=ps, t=t, n=n: e.activation(out=t[:, :n], in_=ps[:, :n], func=AF.Sigmoid)),
                      reads=[ups], writes=[ut])
                self.dma(Sx["gT"][c * 128:(c + 1) * 128, u0:u0 + n], t[:, :n], reads=[ut], writes=[Ux["gT"]])
        fillers = [(z_chunk, c) for c in range(8)] + [(gate_chunk, c) for c in range(24)]

        def fill():
            if fillers:
                fn, c = fillers.pop(0)
                fn(c)

        def dn_front(c):
            proj_to_pp(OFF_QKV + c * 128)
            (q, uq) = WK.get()
            self._acc, self._uacc = q, uq
            self.conv(pp, upp, (lambda j, c=c: self.vs("dncw", c * 4 + j)), 4, None, q[:, :U], uq)
            return q, uq

        def dn_tail(c, q, uq):
            w3, h = c // 8, c % 8
            S.add("act", (lambda e, q=q: e.activation(out=q[:, :U], in_=q[:, :U], func=AF.Silu)), reads=[uq], writes=[uq])
            if w3 < 2:
                (sq, usq) = (sqb, usqb)
                S.add("act", (lambda e, q=q, sq=sq: e.activation(out=sq[:, :U], in_=q[:, :U], func=AF.Square)),
                      reads=[uq], writes=[usq])
                for (u0, n) in TT:
                    ps, ups = self.PS.get()
                    S.add("pe", (lambda e, ps=ps, sq=sq, u0=u0, n=n: e.matmul(ps[:, :n], self.ones_bf[:], sq[:, u0:u0 + n],
                                                                               start=True, stop=True)),
                          reads=[usq, self.u_ones], writes=[ups])
                    rs, urs = rsp.get()
                    S.add("act", (lambda e, rs=rs, ps=ps, n=n: e.activation(out=rs[:, :n], in_=ps[:, :n], func=AF.Sqrt, bias=self.eps6[:])),
                          reads=[ups], writes=[urs])
                    S.add("dve", (lambda e, rs=rs, n=n: e.reciprocal(out=rs[:, :n], in_=rs[:, :n])), reads=[urs], writes=[urs])
                    S.add("dve", (lambda e, q=q, rs=rs, u0=u0, n=n: e.tensor_tensor(out=q[:, u0:u0 + n], in0=q[:, u0:u0 + n],
                                                                                    in1=rs[:, :n], op=ALU.mult)),
                          reads=[urs, uq], writes=[uq])
            dst = (Sx["qT"], Sx["kT"], Sx["vT"])[w3]
            ud = (Ux["qT"], Ux["kT"], Ux["vT"])[w3]
            self.dma(dst[h], q[:, :U], reads=[uq], writes=[ud])
        prev = None
        for c in range(24):
            cur = dn_front(c)
            if prev is not None:
                dn_tail(c - 1, *prev)
                fill()
            prev = cur
        dn_tail(23, *prev)
        fill()
        if self.stop == "pre1":
            return
        if self.stop == "pre2":
            return
        rm, urm = WK.get()
        self.dma(rm[:40, :U], I["rmask"], writes=[urm])
        G, uG = WK.get()
        LNB, uLNB = WK.get()
        for which, (dst, udst) in enumerate(((G, uG), (LNB, uLNB))):
            wst, uws = wst_pool.get()
            S.add("pool", (lambda e, wst=wst: e.memset(wst[:], 0.0)), writes=[uws])
            for d in range(2):
                c0 = OFF_AB + d * 16 + which * 8
                self.dma(wst[:, :, d * 32:d * 32 + 8], I["w_in"][l][:, c0:c0 + 8].rearrange("(k p) m -> p k m", p=128),
                         reads=[uws], writes=[uws])
            wb, uwb = wbf_pool.get()
            S.add("pool", (lambda e, wb=wb, wst=wst: e.tensor_copy(out=wb[:, :, :40], in_=wst[:, :, :40])), reads=[uws], writes=[uwb])
            for ti, (u0, n) in enumerate(TT):
                ps, ups = self.mm_tile(wb, uwb, 40, hT, u_hT, ti)
                evac_copy(ps, ups, dst[:40, u0:u0 + n], udst, 40, n)
        for (t_, ut_) in ((G, uG), (LNB, uLNB)):
            S.add("pool", (lambda e, t_=t_: e.memset(t_[:40, 256:259], 0.0)), reads=[ut_], writes=[ut_])
        S.add("act", lambda e: e.activation(out=G[:40, :U], in_=G[:40, :U], func=AF.Exp, bias=self.vs("dtb")[:40]),
              reads=[uG, self.u_vec], writes=[uG])
        S.add("act", lambda e: e.activation(out=G[:40, :U], in_=G[:40, :U], func=AF.Ln, bias=1.0), reads=[uG], writes=[uG])
        S.add("dve", lambda e: e.tensor_scalar(out=G[:40, :U], in0=G[:40, :U], scalar1=sm[:40, 0:1], scalar2=None, op0=ALU.mult),
              reads=[uG, u_sm], writes=[uG])
        S.add("act", lambda e: e.activation(out=LNB[:40, :U], in_=LNB[:40, :U], func=AF.Exp, scale=-1.0), reads=[uLNB], writes=[uLNB])
        S.add("act", lambda e: e.activation(out=LNB[:40, :U], in_=LNB[:40, :U], func=AF.Ln, bias=1.0), reads=[uLNB], writes=[uLNB])
        S.add("dve", lambda e: e.tensor_scalar(out=LNB[:40, :U], in0=LNB[:40, :U], scalar1=-1.0, scalar2=None, op0=ALU.mult),
              reads=[uLNB], writes=[uLNB])
        GC, uGC = WK.get()
        S.add("pool", lambda e: e.memset(GC[:40, :U], 0.0), writes=[uGC])
        S.add("dve", lambda e: e.tensor_tensor_scan(out=GC[0:8, :U], data0=rm[0:8, :U], data1=G[0:8, :U], initial=0.0,
                                                    op0=ALU.mult, op1=ALU.add), reads=[urm, uG, uGC], writes=[uGC])
        S.add("dve", lambda e: e.tensor_tensor_scan(out=GC[32:40, U - 1::-1], data0=rm[32:40, U - 1::-1], data1=G[32:40, U - 1::-1],
                                                    initial=0.0, op0=ALU.mult, op1=ALU.add), reads=[urm, uG, uGC], writes=[uGC])
        self.dma(Sx["gab"][0], GC[:40, :U], reads=[uGC], writes=[Ux["gab"]])
        self.dma(Sx["gab"][2], LNB[:40, :U], reads=[uLNB], writes=[Ux["gab"]])
        S.add("dve", lambda e: e.tensor_tensor(out=G[:40, :U], in0=GC[:40, :U], in1=LNB[:40, :U], op=ALU.add),
              reads=[uGC, uLNB, uG], writes=[uG])
        self.dma(Sx["gab"][1], G[:40, :U], reads=[uG], writes=[Ux["gab"]])
        if self.stop == "pre3":
            return
        for c in range(8):
            tl = []
            for part in (1, 2, 0):
                cidx = part * 8 + c
                proj_to_pp(OFF_HY + cidx * 128)
                t, ut = WK.get()
                self._acc, self._uacc = t, ut
                self.conv(pp, upp, (lambda j, cidx=cidx: self.vs("hycw", cidx * 3 + j)), 3, self.vs("hycb", cidx), t[:, :U], ut)
                tl.append((t, ut))
                fill()
            (a1, ua1), (a2, ua2), (a0, ua0) = tl
            S.add("dve", (lambda e, a1=a1, a2=a2: e.tensor_tensor(out=a1[:, :U], in0=a1[:, :U], in1=a2[:, :U], op=ALU.mult)),
                  reads=[ua1, ua2], writes=[ua1])
            self.dma(Sx["zzT"][c * 128:(c + 1) * 128], a1[:, :U], reads=[ua1], writes=[Ux["zzT"]])
            self.dma(Sx["x0T"][c * 128:(c + 1) * 128], a0[:, :U], reads=[ua0], writes=[Ux["x0T"]])
        if self.stop == "pre4":
            return
        while fillers:
            fill()
        for g in range(8):
            proj_to_pp(OFF_LX + g * 128)
            (xs, uxs), (H, uH), (A, uA), (Bt, uB) = WK.t
            self._acc, self._uacc = H, uH
            self.conv(pp, upp, (lambda j, g=g: self.vs("lrucw", g * 4 + j)), 4, self.vs("lrucb", g), xs[:, :U], uxs)
            S.add("act", (lambda e: e.activation(out=sqb[:, :U], in_=xs[:, :U], func=AF.Copy)), reads=[uxs], writes=[usqb])
            for d in range(2):
                tt_, utt = (H, uH) if d == 0 else (pp, upp)
                for (wname, bname, dst, udst) in (("lru_w_a", "lba", A, uA), ("lru_w_x", "lbx", Bt, uB)):
                    lw, ulw = lwp.get()
                    self.dma(lw[:], I[wname][l, d, g], writes=[ulw])
                    lr, ulr = lwr.get()
                    S.add("act", (lambda e, lw=lw, lr=lr: e.activation(out=lr[:], in_=lw[:], func=AF.Copy)), reads=[ulw], writes=[ulr])
                    for (u0, n) in TT:
                        ps, ups = self.PS.get()
                        S.add("pe", (lambda e, ps=ps, lr=lr, u0=u0, n=n: e.matmul(ps[:, :n], lr[:], sqb[:, u0:u0 + n],
                                                                                   start=True, stop=True)),
                              reads=[ulr, usqb], writes=[ups])
                        S.add("act", (lambda e, ps=ps, dst=dst, u0=u0, n=n, bname=bname, d=d, g=g: e.activation(
                            out=dst[:, u0:u0 + n], in_=ps[:, :n], func=AF.Sigmoid, bias=self.vs(bname, d * 8 + g))),
                            reads=[ups, self.u_vec], writes=[udst])
                if self.stop == "pre5":
                    return
                S.add("pool", (lambda e, A=A: e.memset(A[:, 256:259], 0.0)), reads=[uA], writes=[uA])
                S.add("act", (lambda e, A=A, t=tt_, d=d, g=g: e.activation(out=t[:, :U], in_=A[:, :U], func=AF.Exp,
                                                                           scale=sm[:, 24 + d * 8 + g:25 + d * 8 + g])),
                      reads=[uA, u_sm, utt], writes=[utt])
                S.add("act", (lambda e, A=A, d=d, g=g: e.activation(out=A[:, :U], in_=A[:, :U], func=AF.Exp,
                                                                     scale=sm[:, 8 + d * 8 + g:9 + d * 8 + g])),
                      reads=[uA, u_sm], writes=[uA])
                S.add("act", (lambda e, t=tt_: e.activation(out=t[:, :U], in_=t[:, :U], func=AF.Sqrt, scale=-1.0, bias=self.one1[:])),
                      reads=[utt], writes=[utt])
                S.add("dve", (lambda e, Bt=Bt, t=tt_: e.tensor_tensor(out=Bt[:, :U], in0=Bt[:, :U], in1=t[:, :U], op=ALU.mult)),
                      reads=[uB, utt], writes=[uB])
                S.add("dve", (lambda e, Bt=Bt: e.tensor_tensor(out=Bt[:, :U], in0=Bt[:, :U], in1=xs[:, :U], op=ALU.mult)),
                      reads=[uB, uxs], writes=[uB])
                S.add("pool", (lambda e, Bt=Bt: e.memset(Bt[:, 256:259], 0.0)), reads=[uB], writes=[uB])
                if self.stop == "pre6":
                    return
                if d == 0:
                    S.add("dve", (lambda e, A=A, Bt=Bt, t=tt_: e.tensor_tensor_scan(out=t[:, 0:U], data0=A[:, 0:U], data1=Bt[:, 0:U],
                                                                                    initial=0.0, op0=ALU.mult, op1=ALU.add)),
                          reads=[uA, uB, utt], writes=[utt])
                else:
                    S.add("dve", (lambda e, A=A, Bt=Bt, t=tt_: e.tensor_tensor_scan(out=t[:, 255::-1], data0=A[:, 255::-1], data1=Bt[:, 255::-1],
                                                                                    initial=0.0, op0=ALU.mult, op1=ALU.add)),
                          reads=[uA, uB, utt], writes=[utt])
                    S.add("dve", (lambda e, A=A, Bt=Bt, t=tt_: e.tensor_tensor_scan(out=t[:, U - 1:LT0 - 1:-1], data0=A[:, U - 1:LT0 - 1:-1],
                                                                                    data1=Bt[:, U - 1:LT0 - 1:-1], initial=t[:, 0:1],
                                                                                    op0=ALU.mult, op1=ALU.add)),
                          reads=[uA, uB, utt], writes=[utt])
                    S.add("dve", (lambda e, t=tt_: e.tensor_tensor(out=H[:, :U], in0=H[:, :U], in1=t[:, :U], op=ALU.add)),
                          reads=[utt, uH], writes=[uH])
            if self.stop == "pre7":
                return
            S.add("pool", lambda e: e.memset(pp[:, 0:1], 0.0), reads=[upp], writes=[upp])
            S.add("pool", lambda e: e.memset(pp[:, 257:260], 0.0), reads=[upp], writes=[upp])
            S.add("pool", lambda e: e.memset(pp[:, 4356:UP], 0.0), reads=[upp], writes=[upp])
            (Y, uY), (T2, uT2) = WK.t[2], WK.t[3]
            wb, uwb = self.load_w_bf16(I["w_in"][l][:, OFF_LY + g * 128:OFF_LY + (g + 1) * 128], 128, wst_pool, wbf_pool)
            for ti, (u0, n) in enumerate(TT):
                ps, ups = self.mm_tile(wb, uwb, 128, hT, u_hT, ti)
                evac_copy(ps, ups, Y[:, u0:u0 + n], uY, 128, n)
            S.add("pool", (lambda e, Y=Y: e.memset(Y[:, 256:259], 0.0)), reads=[uY], writes=[uY])
            S.add("act", (lambda e, Y=Y, T2=T2: e.activation(out=T2[:, :U], in_=Y[:, :U], func=AF.Square)), reads=[uY], writes=[uT2])
            S.add("dve", (lambda e, T2=T2: e.tensor_scalar(out=T2[:, :U], in0=T2[:, :U], scalar1=0.044715, scalar2=1.0,
                                                           op0=ALU.mult, op1=ALU.add)), reads=[uT2], writes=[uT2])
            S.add("dve", (lambda e, Y=Y, T2=T2: e.tensor_tensor(out=T2[:, :U], in0=T2[:, :U], in1=Y[:, :U], op=ALU.mult)),
                  reads=[uT2, uY], writes=[uT2])
            S.add("act", (lambda e, T2=T2: e.activation(out=T2[:, :U], in_=T2[:, :U], func=AF.Sigmoid, scale=1.5957691216057308)),
                  reads=[uT2], writes=[uT2])
            S.add("dve", (lambda e, Y=Y, T2=T2: e.tensor_tensor(out=Y[:, :U], in0=Y[:, :U], in1=T2[:, :U], op=ALU.mult)),
                  reads=[uT2, uY], writes=[uY])
            S.add("dve", (lambda e, Y=Y: e.tensor_tensor(out=T2[:, :U].bitcast(BF16)[:, :U], in0=Y[:, :U], in1=H[:, :U], op=ALU.mult)),
                  reads=[uY, uH, uT2], writes=[uT2])
            self.dma(Sx["olru"][g * 128:(g + 1) * 128], T2[:, :U].bitcast(BF16)[:, :U], reads=[uT2], writes=[Ux["olru"]])
            if self.stop == "pre8":
                return


def prep_inputs(inp, b):
    m = {}
    xT = np.zeros((D, U), np.float32)
    xT[:, 0:LC] = inp["ctx"][b].T
    xT[:, LT0:] = inp["x"][b].T
    m["xT0"] = xT
    cc = np.zeros((128, 8, 2), np.float32)
    cc[:, :, 0] = _pk(inp["c"][b])
    cc[:, :, 1] = _pk(inp["c_ctx"])
    m["cc"] = cc.reshape(128, 16)
    m["vec"] = np.stack([build_vec(inp, l) for l in range(DEPTH)])
    m["cst"] = build_cst()
    m["rmask"] = build_rmask()
    for seg, n_ in (("lat", L), ("ctx", LC)):
        tbs = hyena_tables(n_)
        for k in ("zT", "negt", "F1", "G", "Tout"):
            m["hy_%s_%s" % (k, seg)] = tbs[k]
        m["hy_Winv"] = tbs["Winv"]
    for n in ("w_mod", "w_in", "lru_w_a", "lru_w_x", "w_proj_dn", "w_proj_hy", "w_proj_lru", "w_out",
              "ffn_up", "ffn_down", "hy_w1", "hy_w2", "hy_w3"):
        m[n] = np.ascontiguousarray(inp[n], dtype=np.float32)
    return m


DN_BLOCKS = [0, 128] + [LT0 + 128 * i for i in range(32)]


class T128:
    def __init__(self, nc, es, names, dtype=F32):
        self.t = {}
        for n in names:
            self.t[n] = (es.enter_context(nc.sbuf_tensor(uname(n), [128, 128], dtype)), Unit())

    def __getitem__(self, n):
        return self.t[n]


def dn_phase(B, l, es):
    nc, S, Sx, Ux = B.nc, B.S, B.Sx, B.Ux
    ident = B.cs("ident")
    NB = len(DN_BLOCKS)
    GR = es.enter_context(nc.sbuf_tensor(uname("GR"), [40, U], F32))
    uGR = Unit()
    B.dma(GR[:], Sx["gab"][0], reads=[Ux["gab"]], writes=[uGR])
    TG = es.enter_context(nc.sbuf_tensor(uname("TG"), [128, NB, 3, 40], F32))
    TE = es.enter_context(nc.sbuf_tensor(uname("TE"), [128, NB, 2, 40], F32))
    uTG = Unit()
    gl_pool = TPool(nc, es, "gl", [40, 3, 128], F32, 2)
    for bi, ub in enumerate(DN_BLOCKS):
        gl, ugl = gl_pool.get()
        B.dma(gl[:], Sx["gab"][:, :, ub:ub + 128].rearrange("k r u -> r k u"), reads=[Ux["gab"]], writes=[ugl])
        ps, ups = B.PS.get()
        for k in range(3):
            S.add("pe", (lambda e, ps=ps, k=k, gl=gl: e.transpose(ps[:, k * 40:(k + 1) * 40], gl[:, k, :], ident[:40, :40])),
                  reads=[ugl, B.u_cst], writes=[ups])
        S.add("dve", (lambda e, ps=ps, bi=bi: e.tensor_copy(out=TG[:, bi].rearrange("p k r -> p (k r)"), in_=ps[:, 0:120])),
              reads=[ups], writes=[uTG])
        S.add("act", (lambda e, ps=ps, bi=bi: e.activation(out=TE[:, bi].rearrange("p k r -> p (k r)"), in_=ps[:, 40:120], func=AF.Exp)),
              reads=[ups], writes=[uTG])
    SELN = es.enter_context(nc.sbuf_tensor(uname("SELN"), [40, 16, 128], F32))
    uSEL = Unit()
    for q in range(16):
        r = (q // 8) * 32 + (q % 8)
        S.add("dve", (lambda e, q=q, r=r: e.tensor_scalar(out=SELN[:, q, :], in0=ident[:40, r:r + 1].to_broadcast([40, 128]),
                                                          scalar1=-1.0, scalar2=None, op0=ALU.mult)),
              reads=[B.u_cst], writes=[uSEL])
    zero = es.enter_context(nc.sbuf_tensor(uname("zero"), [128, 128], F32))
    uzero = Unit()
    S.add("pool", lambda e: e.memset(zero[:], 0.0), writes=[uzero])

    if B.stop == "dn0":
        dbg = nc.dram_tensor(uname("dbg_TG"), [128, NB, 3, 40], F32, kind="ExternalOutput").ap()
        B.dma(dbg, TG[:], reads=[uTG])
        return
    NCH = 4
    chains_res = []
    for ci in range(NCH):
        res = {}
        res["f32"] = T128(nc, es, ["qt", "kt", "vt", "Dm", "A", "E1", "M", "AT", "MT", "Y", "P0", "P1", "Q0", "Q1", "Us", "EG"])
        res["f32b"] = T128(nc, es, ["qt", "kt", "vt"])
        res["r"] = T128(nc, es, ["ktr", "qtr", "attnT", "Kd0", "Kd1", "Ktb", "Vb", "QdT", "YR", "WTs", "VN", "S"], F32R)
        res["cd"] = (es.enter_context(nc.sbuf_tensor(uname("cd"), [128, 2], F32)), Unit())
        if ci % 2 == 0:
            res["O"] = (es.enter_context(nc.sbuf_tensor(uname("O"), [128, NB, 128], F32)), [Unit() for _ in range(NB)])
        else:
            res["O"] = chains_res[ci - 1]["O"]
        res["ps"] = [B.PS.t[2 * ci], B.PS.t[2 * ci + 1]]
        chains_res.append(res)
    OD = es.enter_context(nc.sbuf_tensor(uname("OD"), [128, U], BF16))
    uOD = Unit()
    S.add("pool", lambda e: e.memset(OD[:, 256:259], 0.0), writes=[uOD])
    pp_t = TPool(nc, es, "dnpost", [128, 128], F32, 3)
    pp_s = TPool(nc, es, "dnsm", [128, 2], F32, 3)
    DK = 128.0 ** -0.5

    def chain(h, d, res):
        q = d * 8 + h
        r = d * 32 + h
        f, fb, rr = res["f32"], res["f32b"], res["r"]
        cd, ucd = res["cd"]
        O, uO = res["O"]
        psl = res["ps"]
        slot_i = [0]

        def slot():
            k = slot_i[0] % 8
            slot_i[0] += 1
            t, u = psl[k // 4]
            qd = k % 4
            return t[:, qd * 128:(qd + 1) * 128], u
        incl = B.cs("lowi") if d == 0 else B.cs("uppi")
        strict = B.cs("lows") if d == 0 else B.cs("upps")
        iend = (lambda c: c * 64 + 63) if d == 0 else (lambda c: c * 64)
        (Sst, uS), (VN, uVN) = rr["S"], rr["VN"]
        S.add("dve", (lambda e: e.tensor_copy(out=Sst[:], in_=zero[:])), reads=[uzero], writes=[uS])
        S.add("dve", (lambda e: e.tensor_copy(out=VN[:], in_=zero[:])), reads=[uzero], writes=[uVN])
        order = list(range(NB)) if d == 0 else [1, 0] + list(range(NB - 1, 1, -1))
        for oi, bi in enumerate(order):
            ub = DN_BLOCKS[bi]
            ld = f if oi % 2 == 0 else fb
            (qt, uqt), (kt, ukt), (vt, uvt) = ld["qt"], ld["kt"], ld["vt"]
            B.dma(qt[:], Sx["qT"][h][:, ub:ub + 128], reads=[Ux["qT"]], writes=[uqt])
            B.dma(kt[:], Sx["kT"][h][:, ub:ub + 128], reads=[Ux["kT"]], writes=[ukt])
            B.dma(vt[:], Sx["vT"][h][:, ub:ub + 128], reads=[Ux["vT"]], writes=[uvt])
            pKK, uKK = slot()
            S.add("pe", (lambda e, o=pKK, kt=kt: e.matmul(o, kt[:], kt[:], start=True, stop=True)), reads=[ukt], writes=[uKK])
            pQK, uQK = slot()
            S.add("pe", (lambda e, o=pQK, kt=kt, qt=qt: e.matmul(o, kt[:], qt[:], start=True, stop=True)), reads=[ukt, uqt], writes=[uQK])
            pbc, ubc = slot()
            S.add("pe", (lambda e, o=pbc, ub=ub: e.matmul(o, SELN[:, q, :], GR[:, ub:ub + 128], start=True, stop=True)),
                  reads=[uSEL, uGR], writes=[ubc])
            pvt, uvtk = slot()
            S.add("pe", (lambda e, o=pvt, vt=vt: e.transpose(o, vt[:], ident)), reads=[uvt, B.u_cst], writes=[uvtk])
            pkt, uktk = slot()
            S.add("pe", (lambda e, o=pkt, kt=kt: e.transpose(o, kt[:], ident)), reads=[ukt, B.u_cst], writes=[uktk])
            yield
            if B.stop.endswith(":B"):
                return
            (Dm, uDm), (A, uA), (E1, uE1), (M, uM), (EG, uEG) = f["Dm"], f["A"], f["E1"], f["M"], f["EG"]
            gcol = TG[:, bi, 0, r:r + 1]
            climit = int(B.stop.split(":K")[1]) if ":K" in B.stop else 99
            if 0 < climit:
                S.add("dve", (lambda e, o=pbc, gcol=gcol: e.tensor_scalar(out=Dm[:], in0=o, scalar1=gcol, scalar2=0.0, op0=ALU.add, op1=ALU.min)),
                      reads=[ubc, uTG], writes=[uDm])
            if 2 < climit:
                S.add("act", (lambda e, o=pbc: e.activation(out=EG[:], in_=o, func=AF.Exp, scale=-1.0)), reads=[ubc], writes=[uEG])
            if 3 < climit:
                S.add("act", (lambda e: e.activation(out=A[:], in_=Dm[:], func=AF.Exp)), reads=[uDm], writes=[uA])
            if 4 < climit:
                S.add("dve", (lambda e: e.tensor_tensor(out=A[:], in0=A[:], in1=incl, op=ALU.mult)), reads=[uA, B.u_cst], writes=[uA])
            bcol = TE[:, bi, 1, r:r + 1]
            if 5 < climit:
                S.add("dve", (lambda e, bcol=bcol: e.scalar_tensor_tensor(out=E1[:], in0=A[:], scalar=bcol, in1=strict, op0=ALU.mult, op1=ALU.mult)),
                      reads=[uA, uTG, B.u_cst], writes=[uE1])
            if 6 < climit:
                S.add("dve", (lambda e, o=pKK: e.tensor_tensor(out=M[:], in0=o, in1=E1[:], op=ALU.mult)), reads=[uKK, uE1], writes=[uM])
            (QdT, uQdT) = rr["QdT"]
            if 7 < climit:
                S.add("dve", (lambda e, qt=qt: e.scalar_tensor_tensor(out=QdT[:], in0=qt[:], scalar=DK, in1=EG[:], op0=ALU.mult, op1=ALU.mult)),
                      reads=[uqt, uEG], writes=[uQdT])
            yield
            if B.stop.endswith(":C") or ":K" in B.stop:
                return
            pAT, uAT_ = slot()
            S.add("pe", (lambda e, o=pAT: e.transpose(o, A[:], ident)), reads=[uA, B.u_cst], writes=[uAT_])
            pMT, uMT_ = slot()
            S.add("pe", (lambda e, o=pMT: e.transpose(o, M[:], ident)), reads=[uM, B.u_cst], writes=[uMT_])
            yield
            (AT, uAT), (MT, uMT), (Y, uY) = f["AT"], f["MT"], f["Y"]
            S.add("act", (lambda e, o=pAT: e.activation(out=AT[:], in_=o, func=AF.Copy)), reads=[uAT_], writes=[uAT])
            S.add("act", (lambda e, o=pMT: e.activation(out=MT[:], in_=o, func=AF.Copy)), reads=[uMT_], writes=[uMT])
            S.add("dve", (lambda e, o=pMT: e.tensor_tensor(out=Y[:], in0=ident, in1=o, op=ALU.subtract)), reads=[uMT_, B.u_cst], writes=[uY])
            (attnT, uattn), (Kd0, uKd0), (Kd1, uKd1), (Ktb, uKtb), (Vb, uVb) = rr["attnT"], rr["Kd0"], rr["Kd1"], rr["Ktb"], rr["Vb"]
            S.add("dve", (lambda e, o=pQK: e.scalar_tensor_tensor(out=attnT[:], in0=o, scalar=DK, in1=AT[:], op0=ALU.mult, op1=ALU.mult)),
                  reads=[uQK, uAT], writes=[uattn])
            for c, (Kd, uKd) in enumerate(((Kd0, uKd0), (Kd1, uKd1))):
                S.add("act", (lambda e, o=pkt, Kd=Kd, c=c: e.activation(out=Kd[:], in_=o, func=AF.Copy, scale=AT[:, iend(c):iend(c) + 1])),
                      reads=[uktk, uAT], writes=[uKd])
            wcol = TE[:, bi, 0, r:r + 1]
            S.add("dve", (lambda e, o=pkt, wcol=wcol: e.tensor_scalar(out=Ktb[:], in0=o, scalar1=wcol, scalar2=None, op0=ALU.mult)),
                  reads=[uktk, uTG], writes=[uKtb])
            S.add("dve", (lambda e, o=pvt, bcol=bcol: e.tensor_scalar(out=Vb[:], in0=o, scalar1=bcol, scalar2=None, op0=ALU.mult)),
                  reads=[uvtk, uTG], writes=[uVb])
            yield
            if B.stop.endswith(":E"):
                return
            P, uP = M, uM
            PT, uPT = MT, uMT
            bufs = [(f["P0"], f["Q0"]), (f["P1"], f["Q1"])]
            (YR, uYR) = rr["YR"]
            pend = None
            for lev in range(1, 7):
                cur = None
                if lev <= 5:
                    (Pn, uPn), (PnT, uPnT) = bufs[lev % 2]
                    pP, upP = slot()
                    S.add("pe", (lambda e, o=pP, PT=PT, P=P: e.matmul(o, PT[:], P[:], start=True, stop=True)), reads=[uPT, uP], writes=[upP])
                    pPT = None
                    if lev < 5:
                        pPT, upPT = slot()
                        S.add("pe", (lambda e, o=pPT, PT=PT, P=P: e.matmul(o, P[:], PT[:], start=True, stop=True)), reads=[uPT, uP], writes=[upPT])
                    cur = (Pn, uPn, PnT, uPnT, pP, upP, pPT, upPT if lev < 5 else None)
                pY = None
                if pend is not None:
                    pY, upY = slot()
                    S.add("pe", (lambda e, o=pY, Pq=pend[0]: e.matmul(o, Pq[:], Y[:], start=True, stop=True)), reads=[pend[1], uY], writes=[upY])
                yield
                if pY is not None:
                    if lev <= 5:
                        S.add("dve", (lambda e, o=pY: e.tensor_tensor(out=Y[:], in0=Y[:], in1=o, op=ALU.add)), reads=[upY, uY], writes=[uY])
                    else:
                        S.add("dve", (lambda e, o=pY: e.tensor_tensor(out=YR[:], in0=Y[:], in1=o, op=ALU.add)), reads=[upY, uY], writes=[uYR])
                if cur is not None:
                    (Pn, uPn, PnT, uPnT, pP, upP, pPT, upPT) = cur
                    S.add("act", (lambda e, o=pP, Pn=Pn: e.activation(out=Pn[:], in_=o, func=AF.Copy)), reads=[upP], writes=[uPn])
                    if pPT is not None:
                        S.add("act", (lambda e, o=pPT, PnT=PnT: e.activation(out=PnT[:], in_=o, func=AF.Copy)), reads=[upPT], writes=[uPnT])
                    pend = (Pn, uPn)
                    P, uP, PT, uPT = Pn, uPn, PnT, uPnT
                yield
            if B.stop.endswith(":G"):
                return
            (Us, uUs), (WTs, uWTs) = f["Us"], rr["WTs"]
            pU, upU = slot()
            S.add("pe", (lambda e, o=pU: e.matmul(o, YR[:], Vb[:], start=True, stop=True)), reads=[uYR, uVb], writes=[upU])
            pW, upW = slot()
            S.add("pe", (lambda e, o=pW: e.matmul(o, Ktb[:], YR[:], start=True, stop=True)), reads=[uYR, uKtb], writes=[upW])
            yield
            S.add("act", (lambda e, o=pU: e.activation(out=Us[:], in_=o, func=AF.Copy)), reads=[upU], writes=[uUs])
            S.add("dve", (lambda e, o=pW: e.tensor_copy(out=WTs[:], in_=o)), reads=[upW], writes=[uWTs])
            yield
            if B.stop.endswith(":H"):
                return
            for c in ((0, 1) if d == 0 else (1, 0)):
                rows = slice(c * 64, (c + 1) * 64)
                Kd, uKd = (Kd0, uKd0) if c == 0 else (Kd1, uKd1)
                p1, up1 = slot()
                S.add("pe", (lambda e, o=p1: e.matmul(o, WTs[:], Sst[:], start=True, stop=True)), reads=[uWTs, uS], writes=[up1])
                yield
                S.add("dve", (lambda e, o=p1, rows=rows: e.tensor_tensor(out=VN[rows, :], in0=Us[rows, :], in1=o[rows, :], op=ALU.subtract)),
                      reads=[up1, uUs, uVN], writes=[uVN])
                yield
                p2, up2 = slot()
                S.add("pe", (lambda e, o=p2: e.matmul(o, QdT[:], Sst[:], start=True, stop=False)), reads=[uQdT, uS], writes=[up2])
                S.add("pe", (lambda e, o=p2: e.matmul(o, attnT[:], VN[:], start=False, stop=True)), reads=[uattn, uVN], writes=[up2])
                p3, up3 = slot()
                S.add("pe", (lambda e, o=p3, Kd=Kd: e.matmul(o, Kd[:], VN[:], start=True, stop=True)), reads=[uKd, uVN], writes=[up3])
                yield
                oi_other = (bi if d == 1 else ([1, 0] + list(range(NB - 1, 1, -1))).index(bi))
                first = (oi < oi_other) or (oi == oi_other and d == 0)
                if first:
                    S.add("act", (lambda e, o=p2, rows=rows, bi=bi: e.activation(out=O[rows, bi, :], in_=o[rows, :], func=AF.Copy)),
                          reads=[up2], writes=[uO[bi]])
                else:
                    S.add("dve", (lambda e, o=p2, rows=rows, bi=bi: e.tensor_tensor(out=O[rows, bi, :], in0=O[rows, bi, :], in1=o[rows, :], op=ALU.add)),
                          reads=[up2, uO[bi]], writes=[uO[bi]])
                S.add("dve", (lambda e, o=p3, c=c: e.scalar_tensor_tensor(out=Sst[:], in0=Sst[:], scalar=EG[:, iend(c):iend(c) + 1], in1=o,
                                                                         op0=ALU.mult, op1=ALU.add)), reads=[up3, uS, uEG], writes=[uS])
                yield

    def post(h, resf, resb):
        (Of, uOf) = resf["O"]
        for bi, ub in enumerate(DN_BLOCKS):
            t, ut = pp_t.get()
            sm_, usm = pp_s.get()
            S.add("pool", (lambda e, t=t, bi=bi: e.tensor_copy(out=t[:], in_=Of[:, bi, :])),
                  reads=[uOf[bi]], writes=[ut])
            t2, ut2 = pp_t.get()
            S.add("act", (lambda e, t=t, t2=t2, sm_=sm_: e.activation(out=t2[:], in_=t[:], func=AF.Square, accum_out=sm_[:, 0:1])),
                  reads=[ut], writes=[ut2, usm])
            S.add("act", (lambda e, sm_=sm_: e.activation(out=sm_[:, 1:2], in_=sm_[:, 0:1], func=AF.Sqrt, scale=1.0 / 128, bias=B.eps6[:])),
                  reads=[usm], writes=[usm])
            S.add("dve", (lambda e, sm_=sm_: e.reciprocal(out=sm_[:, 1:2], in_=sm_[:, 1:2])), reads=[usm], writes=[usm])
            S.add("dve", (lambda e, t=t, sm_=sm_: e.tensor_scalar(out=t[:], in0=t[:], scalar1=sm_[:, 1:2], scalar2=None, op0=ALU.mult)),
                  reads=[usm, ut], writes=[ut])
            ps, ups = B.PS.get()
            S.add("pe", (lambda e, ps=ps, t=t: e.transpose(ps[:, 0:128], t[:], ident)), reads=[ut, B.u_cst], writes=[ups])
            sz, usz = pp_t.get()
            B.dma(sz[:], Sx["szT"][h * 128:(h + 1) * 128, ub:ub + 128], reads=[Ux["szT"]], writes=[usz])
            S.add("dve", (lambda e, ps=ps, sz=sz, ub=ub: e.scalar_tensor_tensor(out=OD[:, ub:ub + 128], in0=ps[:, 0:128], scalar=B.vs("dnng"),
                                                                              in1=sz[:], op0=ALU.mult, op1=ALU.mult)),
                  reads=[ups, usz, B.u_vec, uOD], writes=[uOD])
        B.dma(Sx["odn"][h * 128:(h + 1) * 128], OD[:], reads=[uOD], writes=[Ux["odn"]])

    nheads = 8 if not (B.stop or "").startswith("dn1") else 2
    if B.stop is None:
        B.stop = ""
    for h0 in range(0, nheads, 2):
        gens = []
        for j in range(2):
            for d in range(2):
                gens.append(chain(h0 + j, d, chains_res[j * 2 + d]))
        active = list(gens)
        while active:
            nxt = []
            for g in active:
                try:
                    next(g)
                    nxt.append(g)
                except StopIteration:
                    pass
            active = nxt
        if ":" in B.stop:
            return
        for j in range(2):
            post(h0 + j, chains_res[j * 2], chains_res[j * 2 + 1])


def hy_dims(n):
    N = 2 * n
    NS1 = n // 128
    NF1 = N // 128
    NFH = NF1 // 2 + 1
    return N, NS1, NF1, NFH


def hyena_tables(n):
    N, NS1, NF1, NFH = hy_dims(n)
    f64 = np.float64
    s1 = np.arange(NS1, dtype=f64)[:, None]
    f1 = np.arange(NFH, dtype=f64)[None, :]
    ang = 2 * np.pi * f1 * s1 / NF1
    F1 = np.concatenate([np.cos(ang), -np.sin(ang)], axis=1)
    s2 = np.arange(128, dtype=f64)[:, None, None]
    f1b = np.arange(NFH, dtype=f64)[None, :, None]
    f2 = np.arange(128, dtype=f64)[None, None, :]
    ang = 2 * np.pi * (f1b + NF1 * f2) * s2 / N
    G = np.stack([np.cos(ang), -np.sin(ang)], axis=2)
    f2c = np.arange(128, dtype=f64)[:, None]
    t2 = np.arange(128, dtype=f64)[None, :]
    th = 2 * np.pi * f2c * t2 / 128
    Winv = np.stack([np.concatenate([np.cos(th), np.sin(th)], 1), np.concatenate([-np.sin(th), np.cos(th)], 1)], axis=1)
    NT1 = NS1
    f1c = np.arange(NFH, dtype=f64)[:, None, None]
    t2c = np.arange(128, dtype=f64)[None, :, None]
    t1c = np.arange(NT1, dtype=f64)[None, None, :]
    ph = 2 * np.pi * f1c * (128 * t1c + t2c) / N
    w = np.full((NFH, 1, 1), 2.0)
    w[0] = 1.0
    w[NFH - 1] = 1.0
    Tout = np.stack([w / N * np.cos(ph), -w / N * np.sin(ph)], axis=2)
    t = np.linspace(0.0, 1.0, n, dtype=np.float32)[:, None]
    omega = (2.0 * math.pi * np.arange(n, dtype=np.float32)[:, None] / n).astype(np.float32)
    bands = np.linspace(1e-4, 15, 16, dtype=np.float32)[None, :]
    z = np.concatenate([t, np.cos(bands * omega), -np.sin(bands * omega)], axis=-1).astype(np.float32)
    negt = np.broadcast_to(-t[:, 0][None, :], (128, n))
    return dict(F1=F1.astype(np.float32), G=G.astype(np.float32), Winv=Winv.astype(np.float32),
                Tout=Tout.astype(np.float32), zT=np.ascontiguousarray(z.T), negt=np.ascontiguousarray(negt, dtype=np.float32))


HY_C = 32


def hyena_phase(B, l, es, seg):
    nc, S, Sx, Ux, I = B.nc, B.S, B.Sx, B.Ux, B.I
    n, base = (L, LT0) if seg == "lat" else (LC, 0)
    N, NS1, NF1, NFH = hy_dims(n)
    NT1 = NS1
    W2 = 2 * NFH
    C = HY_C
    tb = lambda k: I["hy_%s_%s" % (k, seg)]
    hp, uhp = Sx["hp_" + seg], Ux["hp_" + seg]
    NTL = [(i * 512, min(512, n - i * 512)) for i in range((n + 511) // 512)]
    MAGIC = 12582912.0

    with ExitStack() as e1:
        zT = e1.enter_context(nc.sbuf_tensor(uname("zT"), [33, n], F32))
        h1 = e1.enter_context(nc.sbuf_tensor(uname("h1"), [64, n], F32))
        h2 = e1.enter_context(nc.sbuf_tensor(uname("h2"), [64, n], F32))
        negt = e1.enter_context(nc.sbuf_tensor(uname("negt"), [128, n], F32))
        dec = e1.enter_context(nc.sbuf_tensor(uname("dec"), [128, n], F32))
        w1 = e1.enter_context(nc.sbuf_tensor(uname("w1"), [33, 64], F32))
        w2 = e1.enter_context(nc.sbuf_tensor(uname("w2"), [64, 64], F32))
        w3 = e1.enter_context(nc.sbuf_tensor(uname("w3"), [64, 2048], F32))
        fb = e1.enter_context(nc.sbuf_tensor(uname("fb"), [64, 2], F32))
        uz, uh1, uh2, unegt, udec, uw, ufb = Unit(), Unit(), Unit(), Unit(), Unit(), Unit(), Unit()
        tmp_p = TPool(nc, e1, "hyt", [128, 512], F32, 3)
        out_p = TPool(nc, e1, "hyo", [128, 2, 512], F32, 2)
        B.dma(zT[:], tb("zT"), writes=[uz])
        B.dma(negt[:], tb("negt"), writes=[unegt])
        B.dma(w1[:], I["hy_w1"][l], writes=[uw])
        B.dma(w2[:], I["hy_w2"][l], writes=[uw])
        B.dma(w3[:], I["hy_w3"][l], writes=[uw])
        S.add("dve", lambda e: e.tensor_tensor(out=fb[:, 0:1], in0=B.vs("hyf1")[:64], in1=B.vs("hyb1")[:64], op=ALU.mult),
              reads=[B.u_vec], writes=[ufb])
        S.add("dve", lambda e: e.tensor_tensor(out=fb[:, 1:2], in0=B.vs("hyf2")[:64], in1=B.vs("hyb2")[:64], op=ALU.mult),
              reads=[B.u_vec, ufb], writes=[ufb])
        for li, (wt, K, src, usrc, dst, udst, fname) in enumerate(((w1, 33, zT, uz, h1, uh1, "hyf1"), (w2, 64, h1, uh1, h2, uh2, "hyf2"))):
            for (t0, tn) in NTL:
                ps, ups = B.PS.get()
                S.add("pe", (lambda e, ps=ps, wt=wt, K=K, src=src, t0=t0, tn=tn: e.matmul(ps[:64, :tn], wt[:K, :], src[:K, t0:t0 + tn],
                                                                                          start=True, stop=True)),
                      reads=[uw, usrc], writes=[ups])
                x, ux = tmp_p.get()
                k2, uk2 = tmp_p.get()
                S.add("dve", (lambda e, ps=ps, x=x, tn=tn, li=li, fname=fname: e.tensor_scalar(
                    out=x[:64, :tn], in0=ps[:64, :tn], scalar1=B.vs(fname)[:64], scalar2=fb[:, li:li + 1], op0=ALU.mult, op1=ALU.add)),
                    reads=[ups, B.u_vec, ufb], writes=[ux])
                S.add("dve", (lambda e, x=x, k2=k2, tn=tn: e.tensor_scalar(out=k2[:64, :tn], in0=x[:64, :tn], scalar1=1.0 / (2 * math.pi),
                                                                           scalar2=MAGIC, op0=ALU.mult, op1=ALU.add)), reads=[ux], writes=[uk2])
                S.add("dve", (lambda e, k2=k2, tn=tn: e.tensor_scalar(out=k2[:64, :tn], in0=k2[:64, :tn], scalar1=-MAGIC, scalar2=-2 * math.pi,
                                                                      op0=ALU.add, op1=ALU.mult)), reads=[uk2], writes=[uk2])
                S.add("dve", (lambda e, x=x, k2=k2, tn=tn: e.tensor_tensor(out=x[:64, :tn], in0=x[:64, :tn], in1=k2[:64, :tn], op=ALU.add)),
                      reads=[ux, uk2], writes=[ux])
                S.add("act", (lambda e, x=x, dst=dst, t0=t0, tn=tn: e.activation(out=dst[:, t0:t0 + tn], in_=x[:64, :tn], func=AF.Sin)),
                      reads=[ux], writes=[udst])
        for cch in range(8):
            S.add("act", (lambda e, cch=cch: e.activation(out=dec[:], in_=negt[:], func=AF.Exp, scale=B.cs("deltas")[:, cch:cch + 1])),
                  reads=[unegt, B.u_cst], writes=[udec])
            for (t0, tn) in NTL:
                psf, upsf = B.PS.get()
                psb, upsb = B.PS.get()
                for (ps_, ups_, dr) in ((psf, upsf, 0), (psb, upsb, 1)):
                    S.add("pe", (lambda e, ps_=ps_, dr=dr, cch=cch, t0=t0, tn=tn: e.matmul(
                        ps_[:, :tn], w3[:, dr * 1024 + cch * 128:dr * 1024 + (cch + 1) * 128], h2[:, t0:t0 + tn], start=True, stop=True)),
                        reads=[uw, uh2], writes=[ups_])
                a1, ua1 = tmp_p.get()
                S.add("act", (lambda e, a1=a1, psf=psf, tn=tn: e.activation(out=a1[:, :tn], in_=psf[:, :tn], func=AF.Copy)), reads=[upsf], writes=[ua1])
                o, uo = out_p.get()
                for k_, op_ in ((0, ALU.add), (1, ALU.subtract)):
                    S.add("dve", (lambda e, o=o, a1=a1, psb=psb, tn=tn, k_=k_, op_=op_: e.tensor_tensor(out=o[:, k_, :tn], in0=a1[:, :tn], in1=psb[:, :tn], op=op_)),
                          reads=[ua1, upsb, uo], writes=[uo])
                    S.add("pool", (lambda e, o=o, tn=tn, t0=t0, k_=k_: e.tensor_tensor(out=o[:, k_, :tn], in0=o[:, k_, :tn], in1=dec[:, t0:t0 + tn], op=ALU.mult)),
                          reads=[uo, udec], writes=[uo])
                B.dma(hp[:, cch * 128:(cch + 1) * 128, t0:t0 + tn].rearrange("k c t -> c k t"), o[:, :, :tn], reads=[uo], writes=[uhp])
        S.barrier()
    if B.stop and B.stop.endswith("hyf"):
        return

    with ExitStack() as e2:
        cvt_pool = TPool(nc, e2, "cvtst", [128, 1024], F32, 2)

        def cvt(name, shape, src_ap):
            t = e2.enter_context(nc.sbuf_tensor(uname(name), shape, F32R))
            ut = Unit()
            flat = 1
            for s_ in shape[1:]:
                flat *= s_
            step = 1024
            st_pool = cvt_pool
            tf = t[:].rearrange(" ".join(["p"] + ["a%d" % i for i in range(len(shape) - 1)]) + " -> p (" + " ".join("a%d" % i for i in range(len(shape) - 1)) + ")") if len(shape) > 2 else t[:]
            sf = src_ap.rearrange(" ".join(["p"] + ["a%d" % i for i in range(len(shape) - 1)]) + " -> p (" + " ".join("a%d" % i for i in range(len(shape) - 1)) + ")") if len(shape) > 2 else src_ap
            P_ = shape[0]
            for o_ in range(0, flat, step):
                w_ = min(step, flat - o_)
                st, ust = st_pool.get()
                B.dma(st[:P_, :w_], sf[:, o_:o_ + w_], writes=[ust])
                S.add("act", (lambda e, st=st, o_=o_, w_=w_: e.activation(out=tf[:, o_:o_ + w_], in_=st[:P_, :w_], func=AF.Copy)),
                      reads=[ust], writes=[ut])
            return t, ut
        F1 = e2.enter_context(nc.sbuf_tensor(uname("F1"), [NS1, W2], F32))
        uF1 = Unit()
        B.dma(F1[:], tb("F1"), writes=[uF1])
        G, uG = cvt("G", [128, NFH, 2, 128], tb("G"))
        Wv, uWv = cvt("Wv", [128, 2, 256], I["hy_Winv"])
        To, uTo = cvt("To", [NFH, 128, 2, NT1], tb("Tout"))
        xin_p = TPool(nc, e2, "xin", [NS1, C, 128], F32, 1)
        Yb = e2.enter_context(nc.sbuf_tensor(uname("Yb"), [128, NFH, 3, C], F32R))
        Kb = e2.enter_context(nc.sbuf_tensor(uname("Kb"), [128, NFH, 2, C], F32))
        XP = e2.enter_context(nc.sbuf_tensor(uname("XP"), [128, NFH, 2, C], F32R))
        Vb = e2.enter_context(nc.sbuf_tensor(uname("Vb"), [NFH, 128, 2, C], F32R))
        yb = e2.enter_context(nc.sbuf_tensor(uname("yb"), [C, n], F32))
        uYb, uKb, uXP, uVb, uyb = Unit(), Unit(), Unit(), Unit(), Unit()
        tp = TPool(nc, e2, "hytp", [128, 8, C], F32, 4)
        xz_p = TPool(nc, e2, "hyxz", [C, 2, 1024], F32, 1)
        ob_p = TPool(nc, e2, "hyob", [C, 1024], BF16, 2)
        NPB = 512 // W2
        XP2 = e2.enter_context(nc.sbuf_tensor(uname("XP2"), [128, NFH, 2, C], F32R))
        uXP2 = Unit()
        XPS = [(XP, uXP), (XP2, uXP2)]

        def fwd(g):
            XP, uXP = XPS[g % 2]
            c0 = g * C
            for sig in ("p", "m", "zz"):
                xin, uxin = xin_p.get()
                if sig == "zz":
                    src, usrc = Sx["zzT"][c0:c0 + C, base:base + n], Ux["zzT"]
                else:
                    src, usrc = hp[0 if sig == "p" else 1, c0:c0 + C, :], uhp
                B.dma(xin[:], src.rearrange("c (a b) -> a c b", b=128), reads=[usrc], writes=[uxin])
                for cb in range(0, C, NPB):
                    nb = min(NPB, C - cb)
                    ps, ups = B.PS.get()
                    for ci in range(nb):
                        S.add("pe", (lambda e, ps=ps, xin=xin, ci=ci, cb=cb: e.matmul(ps[:, ci * W2:(ci + 1) * W2], xin[:, cb + ci, :], F1[:, :],
                                                                                      start=True, stop=True)),
                              reads=[uxin, uF1], writes=[ups])
                    pv = ps[:, :nb * W2].rearrange("p (c r f) -> p c r f", c=nb, r=2)
                    for r_ in range(2):
                        S.add("act" if r_ == 0 else "dve",
                              (lambda e, pv=pv, r_=r_, cb=cb, nb=nb: (e.activation(out=Yb[:, :, r_, cb:cb + nb], in_=pv[:, :, r_, :].rearrange("p c f -> p f c"), func=AF.Copy)
                                                                       if r_ == 0 else
                                                                       e.tensor_copy(out=Yb[:, :, r_, cb:cb + nb], in_=pv[:, :, r_, :].rearrange("p c f -> p f c")))),
                              reads=[ups], writes=[uYb])
                    S.add("act", (lambda e, pv=pv, cb=cb, nb=nb: e.activation(out=Yb[:, :, 2, cb:cb + nb], in_=pv[:, :, 1, :].rearrange("p c f -> p f c"),
                                                                              func=AF.Copy, scale=-1.0)), reads=[ups], writes=[uYb])
                    yield
                for fq in range(0, NFH, 8):
                    nf = min(8, NFH - fq)
                    ps, ups = B.PS.get()
                    for fi in range(nf):
                        f1_ = fq + fi
                        if sig != "m":
                            S.add("pe", (lambda e, ps=ps, fi=fi, f1_=f1_: e.matmul(ps[:, (fi * 2) * C:(fi * 2 + 1) * C], G[:, f1_, 0, :], Yb[:, f1_, 0, :], start=True, stop=False)),
                                  reads=[uG, uYb], writes=[ups])
                            S.add("pe", (lambda e, ps=ps, fi=fi, f1_=f1_: e.matmul(ps[:, (fi * 2) * C:(fi * 2 + 1) * C], G[:, f1_, 1, :], Yb[:, f1_, 2, :], start=False, stop=True)),
                                  reads=[uG, uYb], writes=[ups])
                        if sig != "p":
                            S.add("pe", (lambda e, ps=ps, fi=fi, f1_=f1_: e.matmul(ps[:, (fi * 2 + 1) * C:(fi * 2 + 2) * C], G[:, f1_, 1, :], Yb[:, f1_, 0, :], start=True, stop=False)),
                                  reads=[uG, uYb], writes=[ups])
                            S.add("pe", (lambda e, ps=ps, fi=fi, f1_=f1_: e.matmul(ps[:, (fi * 2 + 1) * C:(fi * 2 + 2) * C], G[:, f1_, 0, :], Yb[:, f1_, 1, :], start=False, stop=True)),
                                  reads=[uG, uYb], writes=[ups])
                    pv = ps[:, :nf * 2 * C].rearrange("p (f r c) -> p f r c", f=nf, r=2)
                    if sig == "p":
                        S.add("act", (lambda e, pv=pv, fq=fq, nf=nf: e.activation(out=Kb[:, fq:fq + nf, 0, :], in_=pv[:, :, 0, :], func=AF.Copy)), reads=[ups], writes=[uKb])
                    elif sig == "m":
                        S.add("act", (lambda e, pv=pv, fq=fq, nf=nf: e.activation(out=Kb[:, fq:fq + nf, 1, :], in_=pv[:, :, 1, :], func=AF.Copy)), reads=[ups], writes=[uKb])
                    else:
                        (t1, u1), (t2, u2), (t3, u3), (t4, u4) = tp.get(), tp.get(), tp.get(), tp.get()
                        kr, ki = Kb[:, fq:fq + nf, 0, :], Kb[:, fq:fq + nf, 1, :]
                        S.add("dve", (lambda e, pv=pv, t1=t1, nf=nf, kr=kr: e.tensor_tensor(out=t1[:, :nf, :], in0=pv[:, :, 0, :], in1=kr, op=ALU.mult)), reads=[ups, uKb], writes=[u1])
                        S.add("dve", (lambda e, pv=pv, t2=t2, nf=nf, ki=ki: e.tensor_tensor(out=t2[:, :nf, :], in0=pv[:, :, 1, :], in1=ki, op=ALU.mult)), reads=[ups, uKb], writes=[u2])
                        S.add("dve", (lambda e, pv=pv, t3=t3, nf=nf, ki=ki: e.tensor_tensor(out=t3[:, :nf, :], in0=pv[:, :, 0, :], in1=ki, op=ALU.mult)), reads=[ups, uKb], writes=[u3])
                        S.add("dve", (lambda e, pv=pv, t4=t4, nf=nf, kr=kr: e.tensor_tensor(out=t4[:, :nf, :], in0=pv[:, :, 1, :], in1=kr, op=ALU.mult)), reads=[ups, uKb], writes=[u4])
                        S.add("dve", (lambda e, t1=t1, t2=t2, fq=fq, nf=nf: e.tensor_tensor(out=XP[:, fq:fq + nf, 0, :], in0=t1[:, :nf, :], in1=t2[:, :nf, :], op=ALU.subtract)),
                              reads=[u1, u2], writes=[uXP])
                        S.add("dve", (lambda e, t3=t3, t4=t4, fq=fq, nf=nf: e.tensor_tensor(out=XP[:, fq:fq + nf, 1, :], in0=t3[:, :nf, :], in1=t4[:, :nf, :], op=ALU.add)),
                              reads=[u3, u4], writes=[uXP])
                    yield

        def inv(g):
            XP, uXP = XPS[g % 2]
            c0 = g * C
            for cb in range(0, C, 2):
                ps, ups = B.PS.get()
                for ci in range(2):
                    c_ = cb + ci
                    S.add("pe", (lambda e, ps=ps, ci=ci, c_=c_: e.matmul(ps[:NFH, ci * 256:(ci + 1) * 256], XP[:, :, 0, c_], Wv[:, 0, :], start=True, stop=False)),
                          reads=[uXP, uWv], writes=[ups])
                    S.add("pe", (lambda e, ps=ps, ci=ci, c_=c_: e.matmul(ps[:NFH, ci * 256:(ci + 1) * 256], XP[:, :, 1, c_], Wv[:, 1, :], start=False, stop=True)),
                          reads=[uXP, uWv], writes=[ups])
                pv = ps[:NFH, :].rearrange("p (c r t) -> p c r t", c=2, r=2)
                S.add("act" if (cb // 2) % 2 == 0 else "dve",
                      (lambda e, pv=pv, cb=cb: (e.activation(out=Vb[:, :, :, cb:cb + 2], in_=pv.rearrange("p c r t -> p t r c"), func=AF.Copy)
                                                if (cb // 2) % 2 == 0 else e.tensor_copy(out=Vb[:, :, :, cb:cb + 2], in_=pv.rearrange("p c r t -> p t r c")))),
                      reads=[ups], writes=[uVb])
                yield
            TB = 512 // NT1 if NT1 * 16 > 512 else 16
            for tq in range(0, 128, TB):
                ps, ups = B.PS.get()
                for ti in range(TB):
                    t2_ = tq + ti
                    S.add("pe", (lambda e, ps=ps, ti=ti, t2_=t2_: e.matmul(ps[:C, ti * NT1:(ti + 1) * NT1], Vb[:, t2_, 0, :], To[:, t2_, 0, :], start=True, stop=False)),
                          reads=[uVb, uTo], writes=[ups])
                    S.add("pe", (lambda e, ps=ps, ti=ti, t2_=t2_: e.matmul(ps[:C, ti * NT1:(ti + 1) * NT1], Vb[:, t2_, 1, :], To[:, t2_, 1, :], start=False, stop=True)),
                          reads=[uVb, uTo], writes=[ups])
                S.add("act", (lambda e, ps=ps, tq=tq: e.activation(out=yb[:, :].rearrange("c (a b) -> c a b", b=128)[:, :, tq:tq + TB],
                                                                   in_=ps[:C, :TB * NT1].rearrange("c (b a) -> c a b", a=NT1), func=AF.Copy)),
                      reads=[ups], writes=[uyb])
                yield
            for q0 in range(0, n, 1024):
                qn = min(1024, n - q0)
                xz, uxz = xz_p.get()
                B.dma(xz[:, 0, :qn], Sx["x0T"][c0:c0 + C, base + q0:base + q0 + qn], reads=[Ux["x0T"]], writes=[uxz])
                B.dma(xz[:, 1, :qn], Sx["zzT"][c0:c0 + C, base + q0:base + q0 + qn], reads=[Ux["zzT"]], writes=[uxz])
                S.add("dve", (lambda e, xz=xz, q0=q0, qn=qn, c0=c0: e.scalar_tensor_tensor(out=xz[:, 1, :qn], in0=xz[:, 1, :qn], scalar=B.vs("hyb32")[:C, c0 // C:c0 // C + 1],
                                                                                      in1=yb[:, q0:q0 + qn], op0=ALU.mult, op1=ALU.add)),
                      reads=[uxz, uyb, B.u_vec], writes=[uxz])
                ob, uob = ob_p.get()
                S.add("dve", (lambda e, xz=xz, ob=ob, qn=qn: e.tensor_tensor(out=ob[:, :qn], in0=xz[:, 0, :qn], in1=xz[:, 1, :qn], op=ALU.mult)),
                      reads=[uxz], writes=[uob])
                B.dma(Sx["ohy"][c0:c0 + C, base + q0:base + q0 + qn], ob[:, :qn], reads=[uob], writes=[Ux["ohy"]])
                yield

        NG = D // C
        if B.stop and B.stop.endswith("hy1"):
            NG = 2
        for g in range(NG + 1):
            gens = []
            if g < NG:
                gens.append(fwd(g))
            if g >= 1:
                gens.append(inv(g - 1))
            while gens:
                nxt = []
                for gen in gens:
                    try:
                        next(gen)
                        nxt.append(gen)
                    except StopIteration:
                        pass
                gens = nxt
        S.barrier()


def load_wres(B, es, name, src, kparts, ncols, st_pool, eng="pool"):
    nc, S = B.nc, B.S
    w = es.enter_context(nc.sbuf_tensor(uname(name), [128, kparts, ncols], BF16))
    uw = Unit()
    srcv = src.rearrange("(k p) n -> p k n", p=128)
    for k0 in range(0, kparts, 4):
        kn = min(4, kparts - k0)
        for c0 in range(0, ncols, 512):
            st, ust = st_pool.get()
            B.dma(st[:, :kn, :], srcv[:, k0:k0 + kn, c0:c0 + 512], writes=[ust])
            S.add(eng, (lambda e, st=st, k0=k0, kn=kn, c0=c0: e.tensor_copy(out=w[:, k0:k0 + kn, c0:c0 + 512], in_=st[:, :kn, :])),
                  reads=[ust], writes=[uw])
    return w, uw


def merge_phase(B, l, es, xin, u_xin, tiles):
    nc, S, Sx, Ux, I = B.nc, B.S, B.Sx, B.Ux, B.I
    st_pool = TPool(nc, es, "mst", [128, 4, 512], F32, 2)
    Ws = []
    for nm in ("w_proj_dn", "w_proj_hy", "w_proj_lru", "w_out"):
        Ws.append(load_wres(B, es, nm + "_sb", I[nm][l], 8, D, st_pool))
    ot_p = TPool(nc, es, "mot", [128, 3, 8, 512], BF16, 1)
    gt_p = TPool(nc, es, "mgt", [128, 24, 512], BF16, 1)
    x_p = TPool(nc, es, "mx", [128, 8, 512], F32, 2)
    mg_p = TPool(nc, es, "mmg", [128, 8, 512], BF16, 1)
    t_p = TPool(nc, es, "mt", [128, 512], F32, 4)
    for (u0, n) in tiles:
        seg = 1 if u0 == 0 else 0
        ot, uot = ot_p.get()
        for bi, nm in enumerate(("odn", "ohy", "olru")):
            B.dma(ot[:, bi, :, :n], Sx[nm].rearrange("(k p) u -> p k u", p=128)[:, :, u0:u0 + n], reads=[Ux[nm]], writes=[uot])
        gt, ugt = gt_p.get()
        B.dma(gt[:, :, :n], Sx["gT"].rearrange("(k p) u -> p k u", p=128)[:, :, u0:u0 + n], reads=[Ux["gT"]], writes=[ugt])
        x, ux = x_p.get()
        B.dma(x[:, :, :n], xin.rearrange("(k p) u -> p k u", p=128)[:, :, u0:u0 + n], reads=[u_xin], writes=[ux])
        mg, umg = mg_p.get()
        for m in range(8):
            pss = []
            for bi in range(3):
                ps, ups = B.PS.get()
                w, uw = Ws[bi]
                for k in range(8):
                    S.add("pe", (lambda e, ps=ps, w=w, k=k, m=m, bi=bi, ot=ot, n=n: e.matmul(ps[:, :n], w[:, k, m * 128:(m + 1) * 128], ot[:, bi, k, :n],
                                                                                           start=(k == 0), stop=(k == 7))),
                          reads=[uw, uot], writes=[ups])
                pss.append((ps, ups))
            ta, uta = t_p.get()
            tb_, utb = t_p.get()
            S.add("dve", (lambda e, ta=ta, ps=pss[0][0], gt=gt, m=m, n=n: e.tensor_tensor(out=ta[:, :n], in0=ps[:, :n], in1=gt[:, m, :n], op=ALU.mult)),
                  reads=[pss[0][1], ugt], writes=[uta])
            S.add("dve", (lambda e, tb_=tb_, ps=pss[1][0], gt=gt, m=m, n=n: e.tensor_tensor(out=tb_[:, :n], in0=ps[:, :n], in1=gt[:, 8 + m, :n], op=ALU.mult)),
                  reads=[pss[1][1], ugt], writes=[utb])
            S.add("pool", (lambda e, ta=ta, tb_=tb_, n=n: e.tensor_tensor(out=ta[:, :n], in0=ta[:, :n], in1=tb_[:, :n], op=ALU.add)),
                  reads=[uta, utb], writes=[uta])
            tc_, utc = t_p.get()
            S.add("dve", (lambda e, tc_=tc_, ps=pss[2][0], gt=gt, m=m, n=n: e.tensor_tensor(out=tc_[:, :n], in0=ps[:, :n], in1=gt[:, 16 + m, :n], op=ALU.mult)),
                  reads=[pss[2][1], ugt], writes=[utc])
            S.add("pool", (lambda e, ta=ta, tc_=tc_, mg=mg, m=m, n=n: e.tensor_tensor(out=mg[:, m, :n], in0=ta[:, :n], in1=tc_[:, :n], op=ALU.add)),
                  reads=[uta, utc, umg], writes=[umg])
        w, uw = Ws[3]
        for nn in range(8):
            ps, ups = B.PS.get()
            for m in range(8):
                S.add("pe", (lambda e, ps=ps, m=m, nn=nn, mg=mg, n=n: e.matmul(ps[:, :n], w[:, m, nn * 128:(nn + 1) * 128], mg[:, m, :n],
                                                                              start=(m == 0), stop=(m == 7))),
                      reads=[uw, umg], writes=[ups])
            S.add("dve", (lambda e, ps=ps, x=x, nn=nn, n=n, seg=seg: e.scalar_tensor_tensor(out=x[:, nn, :n], in0=ps[:, :n], scalar=B.modcol(2, seg, nn),
                                                                                       in1=x[:, nn, :n], op0=ALU.mult, op1=ALU.add)),
                  reads=[ups, ux, B.u_modv], writes=[ux])
        B.dma(Sx["xA"].rearrange("(k p) u -> p k u", p=128)[:, :, u0:u0 + n], x[:, :, :n], reads=[ux], writes=[Ux["xA"]])


def ffn_phase(B, l, es, tiles):
    nc, S, Sx, Ux, I = B.nc, B.S, B.Sx, B.Ux, B.I
    with_ctx = tiles[0][0] == 0
    esu = ExitStack()
    hT = esu.enter_context(nc.sbuf_tensor(uname("hT2"), [128, 8, U], BF16))
    u_hT = [Unit() for _ in TT]
    with ExitStack() as es3:
        B.norm_to_hT(Sx["xA"], Ux["xA"], 1, hT, u_hT, tiles, es3, tile_ids=[TT.index(t) for t in tiles])
        S.barrier()
    with ExitStack() as es4:
        wst_pool = TPool(nc, es4, "fwst", [128, 8, 128], F32, 2)
        wbf_pool = TPool(nc, es4, "fwbf", [128, 8, 128], BF16, 2)
        up_p = TPool(nc, es4, "fup", [128, 66, 66], F32, 2)
        cp_p = TPool(nc, es4, "fcp", [128, 260], F32, 2)
        acc_p = TPool(nc, es4, "facc", [128, U], F32, 2)
        ab_p = TPool(nc, es4, "fab", [128, U], BF16, 2)
        for (t, ut) in up_p.t:
            S.add("pool", (lambda e, t=t: e.memset(t[:], 0.0)), writes=[ut])
        for (t, ut) in cp_p.t:
            S.add("pool", (lambda e, t=t: e.memset(t[:], 0.0)), writes=[ut])
        for j in range(FH // 128):
            accs = []
            for part in range(2):
                cidx = part * (FH // 128) + j
                wb, uwb = B.load_w_bf16(I["ffn_up"][l][:, cidx * 128:(cidx + 1) * 128], 128, wst_pool, wbf_pool)
                up, uup = up_p.get()
                cp, ucp = cp_p.get()
                for (u0, n) in tiles:
                    ti = TT.index((u0, n))
                    ps, ups = B.mm_tile(wb, uwb, 128, hT, u_hT, ti)
                    if u0 == 0:
                        S.add("act", (lambda e, ps=ps, cp=cp, n=n: e.activation(out=cp[:, 1:1 + n], in_=ps[:, :n], func=AF.Copy)), reads=[ups], writes=[ucp])
                    else:
                        r0 = (u0 - LT0) // 64
                        S.add("act", (lambda e, ps=ps, up=up, r0=r0: e.activation(out=up[:, 1 + r0:9 + r0, 1:65], in_=ps[:, :512].rearrange("p (r c) -> p r c", c=64),
                                                                                  func=AF.Copy)), reads=[ups], writes=[uup])
                acc, uacc = acc_p.get()
                cw = lambda tap, cidx=cidx: B.vs("ffncw", cidx * 9 + tap)
                av = acc[:, LT0:U].rearrange("p (r c) -> p r c", c=64)
                S.add("act", (lambda e, up=up, av=av, cw=cw: e.activation(out=av, in_=up[:, 0:64, 0:64], func=AF.Identity, scale=cw(0))),
                      reads=[uup, B.u_vec], writes=[uacc])
                for tap in range(1, 9):
                    di, dj = tap // 3, tap % 3
                    S.add("dve", (lambda e, up=up, av=av, cw=cw, tap=tap, di=di, dj=dj: e.scalar_tensor_tensor(
                        out=av, in0=up[:, di:di + 64, dj:dj + 64], scalar=cw(tap), in1=av, op0=ALU.mult, op1=ALU.add)),
                        reads=[uup, uacc, B.u_vec], writes=[uacc])
                if with_ctx:
                    S.add("act", (lambda e, cp=cp, acc=acc, cw=cw: e.activation(out=acc[:, 0:256], in_=cp[:, 0:256], func=AF.Identity, scale=cw(3))),
                          reads=[ucp, B.u_vec, uacc], writes=[uacc])
                    for tap in (4, 5):
                        S.add("dve", (lambda e, cp=cp, acc=acc, cw=cw, tap=tap: e.scalar_tensor_tensor(
                            out=acc[:, 0:256], in0=cp[:, tap - 3:tap - 3 + 256], scalar=cw(tap), in1=acc[:, 0:256], op0=ALU.mult, op1=ALU.add)),
                            reads=[ucp, uacc, B.u_vec], writes=[uacc])
                accs.append((acc, uacc))
            (ag, uag), (av_, uav) = accs
            lo = 0 if with_ctx else LT0
            S.add("act", (lambda e, ag=ag, lo=lo: e.activation(out=ag[:, lo:U], in_=ag[:, lo:U], func=AF.Silu)), reads=[uag], writes=[uag])
            ab, uab = ab_p.get()
            S.add("pool", (lambda e, ab=ab: e.memset(ab[:, 0:LT0], 0.0)), writes=[uab])
            S.add("dve", (lambda e, ag=ag, av_=av_, ab=ab, lo=lo: e.tensor_tensor(out=ab[:, lo:U], in0=ag[:, lo:U], in1=av_[:, lo:U], op=ALU.mult)),
                  reads=[uag, uav, uab], writes=[uab])
            B.dma(Sx["actT"][j * 128:(j + 1) * 128], ab[:], reads=[uab], writes=[Ux["actT"]])
        S.barrier()
    esu.close()
    st_pool = TPool(nc, es, "dst", [128, 4, 512], F32, 2)
    wd, uwd = load_wres(B, es, "wdown", I["ffn_down"][l], FH // 128, D, st_pool)
    at_p = TPool(nc, es, "dat", [128, FH // 128, 512], BF16, 2)
    x_p = TPool(nc, es, "dx", [128, 8, 512], F32, 2)
    for (u0, n) in tiles:
        seg = 1 if u0 == 0 else 0
        at, uat = at_p.get()
        B.dma(at[:, :, :n], Sx["actT"].rearrange("(k p) u -> p k u", p=128)[:, :, u0:u0 + n], reads=[Ux["actT"]], writes=[uat])
        x, ux = x_p.get()
        B.dma(x[:, :, :n], Sx["xA"].rearrange("(k p) u -> p k u", p=128)[:, :, u0:u0 + n], reads=[Ux["xA"]], writes=[ux])
        for nn in range(8):
            ps, ups = B.PS.get()
            for j in range(FH // 128):
                S.add("pe", (lambda e, ps=ps, j=j, nn=nn, at=at, n=n: e.matmul(ps[:, :n], wd[:, j, nn * 128:(nn + 1) * 128], at[:, j, :n],
                                                                              start=(j == 0), stop=(j == FH // 128 - 1))),
                      reads=[uwd, uat], writes=[ups])
            S.add("dve", (lambda e, ps=ps, x=x, nn=nn, n=n, seg=seg: e.scalar_tensor_tensor(out=x[:, nn, :n], in0=ps[:, :n], scalar=B.modcol(5, seg, nn),
                                                                                       in1=x[:, nn, :n], op0=ALU.mult, op1=ALU.add)),
                  reads=[ups, ux, B.u_modv], writes=[ux])
        B.dma(Sx["xB"].rearrange("(k p) u -> p k u", p=128)[:, :, u0:u0 + n], x[:, :, :n], reads=[ux], writes=[Ux["xB"]])


def final_phase(B, es):
    nc, S, Sx, Ux = B.nc, B.S, B.Sx, B.Ux
    xp = TPool(nc, es, "fx", [128, 8, 512], F32, 2)
    sqp = TPool(nc, es, "fsq", [128, 8, 512], F32R, 1)
    rsp = TPool(nc, es, "frs", [128, 512], F32, 2)
    for (u0, n) in TT[1:]:
        x, ux = xp.get()
        B.dma(x[:], Sx["xB"].rearrange("(k p) u -> p k u", p=128)[:, :, u0:u0 + n], reads=[Ux["xB"]], writes=[ux])
        sq, usq = sqp.get()
        S.add("act", (lambda e, x=x, sq=sq: e.activation(out=sq[:], in_=x[:], func=AF.Square)), reads=[ux], writes=[usq])
        ps, ups = B.PS.get()
        for k in range(8):
            S.add("pe", (lambda e, k=k, sq=sq, ps=ps: e.matmul(ps[:], B.ones[:], sq[:, k, :], start=(k == 0), stop=(k == 7))),
                  reads=[usq, B.u_ones], writes=[ups])
        rs, urs = rsp.get()
        S.add("act", (lambda e, rs=rs, ps=ps: e.activation(out=rs[:], in_=ps[:], func=AF.Sqrt, scale=1.0 / D, bias=B.eps6[:])), reads=[ups], writes=[urs])
        S.add("dve", (lambda e, rs=rs: e.reciprocal(out=rs[:], in_=rs[:])), reads=[urs], writes=[urs])
        S.add("dve", (lambda e, x=x, rs=rs: e.tensor_tensor(out=x[:], in0=x[:], in1=rs[:].unsqueeze(1).to_broadcast([128, 8, 512]), op=ALU.mult)),
              reads=[urs, ux], writes=[ux])
        S.add("dve", (lambda e, x=x: e.tensor_tensor(out=x[:], in0=x[:], in1=B.vs("fng").unsqueeze(2).to_broadcast([128, 8, 512]), op=ALU.mult)),
              reads=[ux, B.u_vec], writes=[ux])
        t0 = u0 - LT0
        B.dma(B.outT.rearrange("(k p) t -> p k t", p=128)[:, :, t0:t0 + n], x[:], reads=[ux])


_CACHE = {}


def kernel(**inputs):
    inp = {k: np.asarray(v) for k, v in inputs.items()}
    if "nc" not in _CACHE:
        b = Builder(debug=False)
        _CACHE["nc"] = b.build()
    nc = _CACHE["nc"]
    bsz = inp["x"].shape[0]
    in_maps = [prep_inputs(inp, b_) for b_ in range(bsz)]
    res = run_bass_kernel_spmd(nc, in_maps, core_ids=list(range(bsz)))
    out = np.stack([np.ascontiguousarray(np.asarray(r["outT"]).T) for r in res.results])
    return out.astype(np.float32)
```

```python
import math
from contextlib import ExitStack

import numpy as np
import ml_dtypes
import concourse.bass as bass
import concourse.mybir as mybir
from concourse.bass_utils import run_bass_kernel_spmd

F32 = mybir.dt.float32
F32R = mybir.dt.float32r
BF16 = mybir.dt.bfloat16
ALU = mybir.AluOpType
AF = mybir.ActivationFunctionType
AX = mybir.AxisListType

D = 1024
L = 4096
LC = 256
U = 4355
LT0 = 259
UP = 4360
DEPTH = 2
NIN = 12320
OFF_QKV, OFF_Z, OFF_AB, OFF_HY, OFF_LX, OFF_LY, OFF_GATE = 0, 3072, 4096, 4128, 7200, 8224, 9248
FH = 2816
TT = [(0, 256)] + [(LT0 + 512 * i, 512) for i in range(8)]
NCORES = 4


class Unit:
    __slots__ = ("w", "r", "excl", "wd")

    def __init__(self, excl=False):
        self.w = None
        self.r = []
        self.wd = []
        self.excl = excl


class Op:
    __slots__ = ("eng", "fn", "deps", "dma", "need_inc", "sem", "val")

    def __init__(self, eng, fn, dma):
        self.eng = eng
        self.fn = fn
        self.dma = dma
        self.deps = []
        self.need_inc = False
        self.sem = None
        self.val = 0


class Sched:
    EPOCH = 30000
    NDMA = 24

    def __init__(self, nc, es):
        self.nc = nc
        self.es = es
        self.ops = []
        self.engs = {"pe": nc.tensor, "act": nc.scalar, "dve": nc.vector,
                     "pool": nc.gpsimd, "sp": nc.sync}
        self.last = {k: None for k in self.engs}
        self.dmas_since_barrier = []
        self.bar_deps = {k: [] for k in self.engs}
        self.nsem = 0

    def add(self, eng, fn, reads=(), writes=(), dma=False):
        op = Op(eng, fn, dma)
        deps = []
        for u in reads:
            if u.w is not None:
                deps.append(u.w)
            deps.extend(u.wd)
            if u.excl:
                deps.extend(o for o in u.r if o.eng != eng)
        for u in writes:
            if u.w is not None:
                deps.append(u.w)
            deps.extend(u.wd)
            deps.extend(u.r)
        if self.bar_deps[eng]:
            deps.extend(self.bar_deps[eng])
            self.bar_deps[eng] = []
        seen = set()
        for d in deps:
            if d is op or id(d) in seen:
                continue
            if d.eng == "pe" and eng == "pe" and not d.dma and not dma:
                continue
            seen.add(id(d))
            d.need_inc = True
            op.deps.append(d)
        for u in reads:
            if not dma:
                u.r = [o for o in u.r if o.dma or o.eng != eng]
            u.r.append(op)
        for u in writes:
            u.w = op
            u.r = []
            if dma:
                u.wd.append(op)
                if len(u.wd) > 48:
                    u.wd = u.wd[-48:]
            else:
                u.wd = []
        if dma:
            op.need_inc = True
            self.dmas_since_barrier.append(op)
        self.ops.append(op)
        self.last[eng] = op
        return op

    def barrier(self):
        deps = [o for o in self.last.values() if o is not None] + self.dmas_since_barrier
        self.dmas_since_barrier = []
        for k in self.engs:
            self.bar_deps[k] = list(deps)

    def _newsem(self):
        self.nsem += 1
        return self.es.enter_context(self.nc.semaphore("s%d" % self.nsem))

    def emit(self):
        self.barrier()
        self.add("sp", lambda e: None)
        esem, ecount = {}, {}
        dsem = [self._newsem() for _ in range(self.NDMA)]
        dcount = [0] * self.NDMA
        nd = 0
        seen = {k: {} for k in self.engs}
        for op in self.ops:
            e = self.engs[op.eng]
            waits = []
            if op.dma:
                j = nd % self.NDMA
                nd += 1
                if dcount[j]:
                    waits.append((dsem[j], dcount[j]))
                if dcount[j] >= 30000:
                    dsem[j] = self._newsem()
                    dcount[j] = 0
                dcount[j] += 16
                op.sem, op.val = dsem[j], dcount[j]
            elif op.need_inc:
                if op.eng not in esem or ecount[op.eng] >= self.EPOCH:
                    esem[op.eng] = self._newsem()
                    ecount[op.eng] = 0
                ecount[op.eng] += 1
                op.sem, op.val = esem[op.eng], ecount[op.eng]
            for d in op.deps:
                waits.append((d.sem, d.val))
            sn = seen[op.eng]
            for (s, v) in waits:
                if sn.get(id(s), 0) >= v:
                    continue
                sn[id(s)] = v
                e.wait_ge(s, v)
            ins = op.fn(e)
            if op.sem is not None and ins is not None:
                ins.then_inc(op.sem, 16 if op.dma else 1)
        return len(self.ops)


_UID = [0]


def uname(name):
    _UID[0] += 1
    return "%s_%d" % (name, _UID[0])


class TPool:
    def __init__(self, nc, es, name, shape, dtype, n, psum=False):
        self.t = []
        for i in range(n):
            mk = nc.psum_tensor if psum else nc.sbuf_tensor
            self.t.append((es.enter_context(mk(uname(name), shape, dtype)), Unit(excl=psum)))
        self.i = 0

    def get(self):
        r = self.t[self.i % len(self.t)]
        self.i += 1
        return r


def _pk(v):
    return np.ascontiguousarray(v.reshape(-1, 128).T)


VEC_FIELDS = [("g1", 8), ("g2", 8), ("bmod", 48), ("dncw", 96), ("hycw", 72), ("hycb", 24),
              ("lrucw", 32), ("lrucb", 8), ("lba", 16), ("lbx", 16), ("llam", 16), ("hybias", 8),
              ("ffncw", 396), ("fng", 8), ("dnng", 1), ("alog", 1), ("dtb", 1),
              ("hyb1", 1), ("hyf1", 1), ("hyb2", 1), ("hyf2", 1), ("hyb32", 32)]
VOFF = {}
_o = 0
for _n, _w in VEC_FIELDS:
    VOFF[_n] = (_o, _w)
    _o += _w
NV = _o


def build_vec(inp, l):
    v = np.zeros((128, NV), np.float32)

    def put(name, arr):
        o, w = VOFF[name]
        v[:arr.shape[0], o:o + w] = arr.reshape(arr.shape[0], w)
    put("g1", _pk(inp["norm1_g"][l]))
    put("g2", _pk(inp["norm2_g"][l]))
    put("bmod", _pk(inp["b_mod"][l]))
    put("dncw", inp["dn_conv_w"][l].reshape(4, 24, 128).transpose(2, 1, 0))
    put("hycw", inp["hy_conv_w"][l].reshape(3, 24, 128).transpose(2, 1, 0))
    put("hycb", _pk(inp["hy_conv_b"][l]))
    put("lrucw", inp["lru_conv_w"][l].reshape(4, 8, 128).transpose(2, 1, 0))
    put("lrucb", _pk(inp["lru_conv_b"][l]))
    put("lba", inp["lru_b_a"][l].reshape(2, 8, 128).transpose(2, 0, 1))
    put("lbx", inp["lru_b_x"][l].reshape(2, 8, 128).transpose(2, 0, 1))
    put("llam", inp["lru_lambda"][l].reshape(2, 8, 128).transpose(2, 0, 1))
    put("hybias", _pk(inp["hy_bias"][l]))
    put("ffncw", inp["ffn_conv_w"][l].reshape(9, 44, 128).transpose(2, 1, 0))
    put("fng", _pk(inp["final_norm_g"]))
    put("dnng", inp["dn_norm_g"][l].reshape(128, 1))
    al = np.zeros((40, 1), np.float32)
    db = np.zeros((40, 1), np.float32)
    for d in range(2):
        al[d * 32:d * 32 + 8, 0] = inp["dn_a_log"][l][d]
        db[d * 32:d * 32 + 8, 0] = inp["dn_dt_bias"][l][d]
    put("alog", al)
    put("dtb", db)
    put("hyb1", inp["hy_b1"][l].reshape(64, 1))
    put("hyf1", inp["hy_f1"][l].reshape(64, 1))
    put("hyb2", inp["hy_b2"][l].reshape(64, 1))
    put("hyf2", inp["hy_f2"][l].reshape(64, 1))
    put("hyb32", np.ascontiguousarray(inp["hy_bias"][l].reshape(32, 32).T))
    return v


CST_FIELDS = [("ident", 128), ("lowi", 128), ("lows", 128), ("uppi", 128), ("upps", 128),
              ("deltas", 8)]
COFF = {}
_o = 0
for _n, _w in CST_FIELDS:
    COFF[_n] = (_o, _w)
    _o += _w
NCST = _o


def build_cst():
    c = np.zeros((128, NCST), np.float32)
    i = np.arange(128)[:, None]
    j = np.arange(128)[None, :]
    same = (i // 64) == (j // 64)

    def put(name, arr):
        o, w = COFF[name]
        c[:, o:o + w] = arr
    put("ident", (i == j).astype(np.float32))
    put("lowi", ((i >= j) & same).astype(np.float32))
    put("lows", ((i > j) & same).astype(np.float32))
    put("uppi", ((i <= j) & same).astype(np.float32))
    put("upps", ((i < j) & same).astype(np.float32))
    lt = math.log(1e-2)
    deltas = np.abs(np.linspace(lt / 1.5, lt / 0.3, 1024, dtype=np.float32))
    put("deltas", _pk(deltas))
    return c


def build_rmask():
    m = np.ones((2, U), np.float32)
    starts = list(range(0, 256, 64)) + list(range(LT0, U, 64))
    for s in starts:
        m[0, s] = 0.0
        m[1, s + 63] = 0.0
    out = np.zeros((40, U), np.float32)
    out[0:8] = m[0]
    out[32:40] = m[1]
    return out


class Builder:
    def __init__(self, debug=False, stop=None, only=None, feed=()):
        self.debug = debug
        self.stop = stop
        self.only = only
        self.feed = set(feed)
        self.nc = bass.Bass("TRN2", target_bir_lowering=False)
        self.dbg_names = []

    def din(self, name, shape, dt=F32):
        return self.nc.dram_tensor(name, list(shape), dt, kind="ExternalInput").ap()

    def dscr(self, name, shape, dt=F32):
        kind = "ExternalOutput" if self.debug else "Internal"
        if name in self.feed:
            kind = "ExternalInput"
        elif self.debug:
            self.dbg_names.append(name)
        return self.nc.dram_tensor(name, list(shape), dt, kind=kind).ap()

    def vs(self, name, k=None):
        o, w = VOFF[name]
        if k is None:
            return self.vec[:, o:o + w]
        return self.vec[:, o + k:o + k + 1]

    def cs(self, name):
        o, w = COFF[name]
        return self.cst[:, o:o + w]

    def dma(self, out, in_, reads=(), writes=(), eng="sp"):
        return self.S.add(eng, lambda e: e.dma_start(out=out, in_=in_), reads=reads, writes=writes, dma=True)

    def build(self):
        nc = self.nc
        I = {}
        I["xT0"] = self.din("xT0", [D, U])
        I["cc"] = self.din("cc", [128, 16])
        I["vec"] = self.din("vec", [DEPTH, 128, NV])
        I["cst"] = self.din("cst", [128, NCST])
        I["rmask"] = self.din("rmask", [40, U])
        I["hy_w1"] = self.din("hy_w1", [DEPTH, 33, 64])
        I["hy_w2"] = self.din("hy_w2", [DEPTH, 64, 64])
        I["hy_w3"] = self.din("hy_w3", [DEPTH, 64, 2048])
        if self.only == "mg":
            I["w_mod"] = self.din("w_mod", [DEPTH, D, 6 * D])
            for n in ("w_proj_dn", "w_proj_hy", "w_proj_lru", "w_out"):
                I[n] = self.din(n, [DEPTH, D, D])
            I["ffn_up"] = self.din("ffn_up", [DEPTH, D, 2 * FH])
            I["ffn_down"] = self.din("ffn_down", [DEPTH, FH, D])
        if self.only:
            self.I = I
            return self.build2()
        I["w_mod"] = self.din("w_mod", [DEPTH, D, 6 * D])
        I["w_in"] = self.din("w_in", [DEPTH, D, NIN])
        I["lru_w_a"] = self.din("lru_w_a", [DEPTH, 2, 8, 128, 128])
        I["lru_w_x"] = self.din("lru_w_x", [DEPTH, 2, 8, 128, 128])
        for n in ("w_proj_dn", "w_proj_hy", "w_proj_lru", "w_out"):
            I[n] = self.din(n, [DEPTH, D, D])
        I["ffn_up"] = self.din("ffn_up", [DEPTH, D, 2 * FH])
        I["ffn_down"] = self.din("ffn_down", [DEPTH, FH, D])
        self.I = I
        return self.build2()

    def build2(self):
        nc, I = self.nc, self.I
        self.outT = nc.dram_tensor("outT", [D, L], F32, kind="ExternalOutput").ap()
        Sx = {}
        Sx["qT"] = self.dscr("qT", [8, 128, U])
        Sx["kT"] = self.dscr("kT", [8, 128, U])
        Sx["vT"] = self.dscr("vT", [8, 128, U])
        Sx["szT"] = self.dscr("szT", [D, U])
        Sx["gab"] = self.dscr("gab", [3, 40, U])
        Sx["zzT"] = self.dscr("zzT", [D, U])
        Sx["x0T"] = self.dscr("x0T", [D, U])
        Sx["olru"] = self.dscr("olru", [D, U], BF16)
        Sx["odn"] = self.dscr("odn", [D, U], BF16)
        Sx["ohy"] = self.dscr("ohy", [D, U], BF16)
        Sx["gT"] = self.dscr("gT", [3 * D, U], BF16)
        Sx["xA"] = self.dscr("xA", [D, U])
        Sx["xB"] = self.dscr("xB", [D, U])
        Sx["actT"] = self.dscr("actT", [FH, U], BF16)
        Sx["hp_lat"] = self.dscr("hp_lat", [2, D, L])
        Sx["hp_ctx"] = self.dscr("hp_ctx", [2, D, LC])
        for seg, n_ in (("lat", L), ("ctx", LC)):
            N_, NS1_, NF1_, NFH_ = hy_dims(n_)
            I["hy_zT_" + seg] = self.din("hy_zT_" + seg, [33, n_])
            I["hy_negt_" + seg] = self.din("hy_negt_" + seg, [128, n_])
            I["hy_F1_" + seg] = self.din("hy_F1_" + seg, [NS1_, 2 * NFH_])
            I["hy_G_" + seg] = self.din("hy_G_" + seg, [128, NFH_, 2, 128])
            I["hy_Tout_" + seg] = self.din("hy_Tout_" + seg, [NFH_, 128, 2, NS1_])
        I["hy_Winv"] = self.din("hy_Winv", [128, 2, 256])
        self.Sx = Sx
        self.Ux = {k: Unit() for k in Sx}

        with ExitStack() as es:
            self.es = es
            self.S = Sched(nc, es)
            S = self.S
            self.cst = es.enter_context(nc.sbuf_tensor("cst_sb", [128, NCST], F32))
            self.vec = es.enter_context(nc.sbuf_tensor("vec_sb", [128, NV], F32))
            self.ones = es.enter_context(nc.sbuf_tensor("ones", [128, 128], F32R))
            self.sc = es.enter_context(nc.sbuf_tensor("sc", [128, 16], F32))
            self.modv = es.enter_context(nc.sbuf_tensor("modv", [128, 48, 2], F32))
            self.der = es.enter_context(nc.sbuf_tensor("der", [128, 2, 2, 8], F32))
            self.u_cst, self.u_vec, self.u_ones, self.u_sc = Unit(), Unit(), Unit(), Unit()
            self.u_modv, self.u_der = Unit(), Unit()
            self.PS = TPool(nc, es, "ps", [128, 512], F32, 8, psum=True)
            self.dma(self.cst[:], I["cst"], writes=[self.u_cst])
            self.eps6 = es.enter_context(nc.sbuf_tensor("eps6", [128, 1], F32))
            S.add("dve", lambda e: e.memset(self.eps6[:], 1e-6), writes=[self.u_cst])
            self.one1 = es.enter_context(nc.sbuf_tensor("one1", [128, 1], F32))
            S.add("dve", lambda e: e.memset(self.one1[:], 1.0), writes=[self.u_cst])
            ones32 = es.enter_context(nc.sbuf_tensor("ones32", [128, 128], F32))
            u_o32 = Unit()
            S.add("dve", lambda e: e.memset(ones32[:], 1.0), writes=[u_o32])
            S.add("act", lambda e: e.activation(out=self.ones[:], in_=ones32[:], func=AF.Copy), reads=[u_o32], writes=[self.u_ones])
            self.ones_bf = es.enter_context(nc.sbuf_tensor("ones_bf", [128, 128], BF16))
            S.add("act", lambda e: e.activation(out=self.ones_bf[:], in_=ones32[:], func=AF.Copy), reads=[u_o32], writes=[self.u_ones])
            self.dma(self.sc[:], I["cc"], writes=[self.u_sc])
            S.add("act", lambda e: e.activation(out=self.sc[:], in_=self.sc[:], func=AF.Silu),
                  reads=[self.u_sc], writes=[self.u_sc])
            xin = I["xT0"]
            u_xin = Unit()
            for l in range(DEPTH):
                self.l = l
                self.dma(self.vec[:], I["vec"][l], writes=[self.u_vec])
                if self.only == "dn":
                    with ExitStack() as es2:
                        dn_phase(self, l, es2)
                        S.barrier()
                    break
                if self.only == "mg":
                    self.phase_mod(l)
                    with ExitStack() as es2:
                        merge_phase(self, l, es2, xin, u_xin, TT)
                        S.barrier()
                    with ExitStack() as es2:
                        ffn_phase(self, l, es2, TT)
                        S.barrier()
                    with ExitStack() as es2:
                        final_phase(self, es2)
                        S.barrier()
                    break
                if self.only == "hy":
                    for seg in (("lat", "ctx") if "ctx" in self.stop else ("lat",)):
                        with ExitStack() as es2:
                            hyena_phase(self, l, es2, seg)
                            S.barrier()
                    break
                self.phase_mod(l)
                if self.stop == "mod":
                    break
                with ExitStack() as es2:
                    self.phase_mixer_pre(l, es2, xin, u_xin)
                    S.barrier()
                if self.stop and self.stop.startswith("pre"):
                    break
                with ExitStack() as es2:
                    dn_phase(self, l, es2)
                    S.barrier()
                if self.stop and self.stop.startswith("dn"):
                    break
                for seg in (("lat", "ctx") if l < DEPTH - 1 else ("lat",)):
                    with ExitStack() as es2:
                        hyena_phase(self, l, es2, seg)
                        S.barrier()
                if self.stop and self.stop.startswith("hy"):
                    break
                tiles = TT if l < DEPTH - 1 else TT[1:]
                with ExitStack() as es2:
                    merge_phase(self, l, es2, xin, u_xin, tiles)
                    S.barrier()
                if self.stop and self.stop.startswith("mg"):
                    break
                with ExitStack() as es2:
                    ffn_phase(self, l, es2, tiles)
                    S.barrier()
                xin, u_xin = Sx["xB"], self.Ux["xB"]
                if self.stop and self.stop.startswith("ffn"):
                    break
                if l == DEPTH - 1:
                    with ExitStack() as es2:
                        final_phase(self, es2)
                        S.barrier()
            n = S.emit()
        self.n_ops = n
        return nc

    def phase_mod(self, l):
        nc, S = self.nc, self.S
        with ExitStack() as es:
            wm_pool = TPool(nc, es, "wm", [128, 8, 512], F32, 2)
            ps, ups = self.PS.get()
            for pn in range(12):
                wm, uwm = wm_pool.get()
                self.dma(wm[:], self.I["w_mod"][l][:, pn * 512:(pn + 1) * 512].rearrange("(k p) n -> p k n", p=128),
                         writes=[uwm])
                for cc in range(4):
                    c = pn * 4 + cc
                    for k in range(8):
                        S.add("pe", (lambda e, c=c, cc=cc, k=k, wm=wm: e.matmul(
                            ps[:, 2 * c:2 * c + 2], wm[:, k, cc * 128:(cc + 1) * 128], self.sc[:, 2 * k:2 * k + 2],
                            start=(k == 0), stop=(k == 7))), reads=[uwm, self.u_sc], writes=[ups])
            bm = self.vs("bmod")
            for s in range(2):
                S.add("dve", (lambda e, s=s: e.tensor_tensor(out=self.modv[:, :, s], in0=ps[:, s:96:2], in1=bm, op=ALU.add)),
                      reads=[ups, self.u_vec], writes=[self.u_modv])
            for w, (gname, j) in enumerate((("g1", 1), ("g2", 4))):
                for s in range(2):
                    S.add("dve", (lambda e, w=w, s=s, j=j, gname=gname: e.scalar_tensor_tensor(
                        out=self.der[:, w, s, :], in0=self.modv[:, j * 8:(j + 1) * 8, s], scalar=1.0, in1=self.vs(gname),
                        op0=ALU.add, op1=ALU.mult)), reads=[self.u_modv, self.u_vec], writes=[self.u_der])
            S.barrier()

    def modcol(self, j, s, k):
        return self.modv[:, j * 8 + k, s:s + 1]

    def norm_to_hT(self, xsrc, u_xsrc, which, hT, u_hT, tiles, es, tile_ids=None):
        nc, S = self.nc, self.S
        xp = TPool(nc, es, "nx", [128, 8, 512], F32, 2)
        sqp = TPool(nc, es, "nsq", [128, 8, 512], F32R, 1)
        rsp = TPool(nc, es, "nrs", [128, 512], F32, 2)
        shj = 0 if which == 0 else 3
        for ti_, (u0, n) in enumerate(tiles):
            ti = tile_ids[ti_] if tile_ids is not None else ti_
            seg = 1 if u0 == 0 else 0
            x, ux = xp.get()
            self.dma(x[:, :, :n], xsrc.rearrange("(k p) t -> p k t", p=128)[:, :, u0:u0 + n], reads=[u_xsrc], writes=[ux])
            sq, usq = sqp.get()
            S.add("act", (lambda e, x=x, sq=sq, n=n: e.activation(out=sq[:, :, :n], in_=x[:, :, :n], func=AF.Square)),
                  reads=[ux], writes=[usq])
            ps, ups = self.PS.get()
            for k in range(8):
                S.add("pe", (lambda e, k=k, sq=sq, ps=ps, n=n: e.matmul(ps[:, :n], self.ones[:], sq[:, k, :n],
                                                                         start=(k == 0), stop=(k == 7))),
                      reads=[usq, self.u_ones], writes=[ups])
            rs, urs = rsp.get()
            S.add("act", (lambda e, rs=rs, ps=ps, n=n: e.activation(out=rs[:, :n], in_=ps[:, :n], func=AF.Sqrt, scale=1.0 / D, bias=self.eps6[:])),
                  reads=[ups], writes=[urs])
            S.add("dve", (lambda e, rs=rs, n=n: e.reciprocal(out=rs[:, :n], in_=rs[:, :n])), reads=[urs], writes=[urs])
            S.add("dve", (lambda e, x=x, rs=rs, n=n: e.tensor_tensor(out=x[:, :, :n], in0=x[:, :, :n],
                                                                      in1=rs[:, :n].unsqueeze(1).to_broadcast([128, 8, n]), op=ALU.mult)),
                  reads=[urs, ux], writes=[ux])
            for k in range(8):
                S.add("act", (lambda e, k=k, x=x, n=n, u0=u0, seg=seg: e.activation(
                    out=hT[:, k, u0:u0 + n], in_=x[:, k, :n], func=AF.Identity,
                    scale=self.der[:, which, seg, k:k + 1], bias=self.modcol(shj, seg, k))),
                    reads=[ux, self.u_der, self.u_modv], writes=[u_hT[ti]])

    def load_w_bf16(self, wsrc, M, wst_pool, wbf_pool, eng="pool"):
        S = self.S
        wst, uws = wst_pool.get()
        self.dma(wst[:, :, :M], wsrc.rearrange("(k p) m -> p k m", p=128), writes=[uws])
        wb, uwb = wbf_pool.get()
        S.add(eng, (lambda e, wb=wb, wst=wst, M=M: e.tensor_copy(out=wb[:, :, :M], in_=wst[:, :, :M])),
              reads=[uws], writes=[uwb])
        return wb, uwb

    def mm_tile(self, wb, uwb, M, hT, u_hT, ti):
        S = self.S
        u0, n = TT[ti]
        ps, ups = self.PS.get()
        for k in range(8):
            S.add("pe", (lambda e, k=k, ps=ps, wb=wb: e.matmul(ps[:M, :n], wb[:, k, :M], hT[:, k, u0:u0 + n],
                                                                start=(k == 0), stop=(k == 7))),
                  reads=[uwb, u_hT[ti]], writes=[ups])
        return ps, ups

    def conv(self, pp, upp, cw, ntap, bias, out_ap, uout):
        S = self.S
        acc, uacc = self._acc, self._uacc
        S.add("act", (lambda e: e.activation(out=acc[:, :U], in_=pp[:, 0:U], func=AF.Identity,
                                             scale=cw(0), bias=(bias if bias is not None else 0.0))),
              reads=[upp, self.u_vec], writes=[uacc])
        for j in range(1, ntap):
            last = j == ntap - 1
            o = out_ap if last else acc[:, :U]
            uo = uout if last else uacc
            S.add("dve", (lambda e, j=j, o=o: e.scalar_tensor_tensor(out=o, in0=pp[:, j:j + U], scalar=cw(j),
                                                                    in1=acc[:, :U], op0=ALU.mult, op1=ALU.add)),
                  reads=[upp, uacc, self.u_vec], writes=[uo])

    def phase_mixer_pre(self, l, es, xin, u_xin):
        nc, S, I, Sx, Ux = self.nc, self.S, self.I, self.Sx, self.Ux
        hT = es.enter_context(nc.sbuf_tensor(uname("hT"), [128, 8, U], BF16))
        u_hT = [Unit() for _ in TT]
        with ExitStack() as es3:
            self.norm_to_hT(xin, u_xin, 0, hT, u_hT, TT, es3)
            S.barrier()
        if self.stop == "norm":
            dbg = self.nc.dram_tensor(uname("dbg_hT"), [128, 8, U], BF16, kind="ExternalOutput").ap()
            self.dma(dbg, hT[:], reads=u_hT)
            return
        WK = TPool(nc, es, "wk", [128, UP], F32, 4)
        PP = TPool(nc, es, "pp", [128, UP], F32, 1)
        wst_pool = TPool(nc, es, "wst", [128, 8, 128], F32, 2)
        wbf_pool = TPool(nc, es, "wbf", [128, 8, 128], BF16, 2)
        rsp = TPool(nc, es, "rs", [128, 512], F32, 2)
        gtp = TPool(nc, es, "gt", [128, 512], BF16, 4)
        ztp = TPool(nc, es, "zt", [128, 512], F32, 2)
        lwp = TPool(nc, es, "lw", [128, 128], F32, 2)
        lwr = TPool(nc, es, "lwr", [128, 128], BF16, 2)
        sm = es.enter_context(nc.sbuf_tensor(uname("sm"), [128, 40], F32))
        u_sm = Unit()
        pp, upp = PP.get()
        S.add("pool", lambda e: e.memset(pp[:], 0.0), writes=[upp])
        sqb = es.enter_context(nc.sbuf_tensor(uname("sqb"), [128, UP], BF16))
        usqb = Unit()
        self._cnt = 0

        def evac_copy(ps, ups, out_ap, uo, M=128, n=512):
            self._cnt += 1
            if self._cnt % 2:
                S.add("act", (lambda e: e.activation(out=out_ap, in_=ps[:M, :n], func=AF.Copy)), reads=[ups], writes=[uo])
            else:
                S.add("dve", (lambda e: e.tensor_copy(out=out_ap, in_=ps[:M, :n])), reads=[ups], writes=[uo])

        pws_pool = TPool(nc, es, "pws", [128, 8, 128], F32, 2)
        pwb_pool = TPool(nc, es, "pwb", [128, 8, 128], BF16, 2)

        def make_prefetch(cols):
            st = {"next": None}

            def get(i):
                cur = st["next"] if st["next"] is not None else self.load_w_bf16(I["w_in"][l][:, cols[i]:cols[i] + 128], 128, pws_pool, pwb_pool)
                st["next"] = (self.load_w_bf16(I["w_in"][l][:, cols[i + 1]:cols[i + 1] + 128], 128, pws_pool, pwb_pool)
                              if i + 1 < len(cols) else None)
                return cur
            return get

        def proj_to_pp(col0, pre=None):
            wb, uwb = pre if pre is not None else self.load_w_bf16(I["w_in"][l][:, col0:col0 + 128], 128, wst_pool, wbf_pool)
            for ti, (u0, n) in enumerate(TT):
                ps, ups = self.mm_tile(wb, uwb, 128, hT, u_hT, ti)
                evac_copy(ps, ups, pp[:, u0 + 1:u0 + 1 + n], upp, 128, n)

        def proj_act(col0, func, out_tile_fn, bias=None):
            wb, uwb = self.load_w_bf16(I["w_in"][l][:, col0:col0 + 128], 128, wst_pool, wbf_pool)
            for ti, (u0, n) in enumerate(TT):
                ps, ups = self.mm_tile(wb, uwb, 128, hT, u_hT, ti)
                o, uo = out_tile_fn(ti, u0, n)
                S.add("act", (lambda e, ps=ps, o=o, n=n: e.activation(out=o, in_=ps[:, :n], func=func)),
                      reads=[ups], writes=[uo])

        S.add("act", lambda e: e.activation(out=sm[:40, 0:1], in_=self.vs("alog")[:40], func=AF.Exp),
              reads=[self.u_vec], writes=[u_sm])
        S.add("dve", lambda e: e.tensor_scalar(out=sm[:40, 0:1], in0=sm[:40, 0:1], scalar1=-1.0, scalar2=None, op0=ALU.mult),
              reads=[u_sm], writes=[u_sm])
        S.add("act", lambda e: e.activation(out=sm[:, 8:24], in_=self.vs("llam"), func=AF.Exp, scale=-1.0),
              reads=[self.u_vec, u_sm], writes=[u_sm])
        S.add("act", lambda e: e.activation(out=sm[:, 8:24], in_=sm[:, 8:24], func=AF.Ln, bias=1.0),
              reads=[u_sm], writes=[u_sm])
        S.add("dve", lambda e: e.tensor_scalar(out=sm[:, 8:24], in0=sm[:, 8:24], scalar1=-8.0, scalar2=None, op0=ALU.mult),
              reads=[u_sm], writes=[u_sm])
        S.add("dve", lambda e: e.tensor_scalar(out=sm[:, 24:40], in0=sm[:, 8:24], scalar1=2.0, scalar2=None, op0=ALU.mult),
              reads=[u_sm], writes=[u_sm])

        def z_chunk(c):
            wb, uwb = self.load_w_bf16(I["w_in"][l][:, OFF_Z + c * 128:OFF_Z + (c + 1) * 128], 128, wst_pool, wbf_pool)
            for ti, (u0, n) in enumerate(TT):
                ps, ups = self.mm_tile(wb, uwb, 128, hT, u_hT, ti)
                t, ut = ztp.get()
                S.add("act", (lambda e, ps=ps, t=t, n=n: e.activation(out=t[:, :n], in_=ps[:, :n], func=AF.Silu)),
                      reads=[ups], writes=[ut])
                self.dma(Sx["szT"][c * 128:(c + 1) * 128, u0:u0 + n], t[:, :n], reads=[ut], writes=[Ux["szT"]])

        def gate_chunk(c):
            wb, uwb = self.load_w_bf16(I["w_in"][l][:, OFF_GATE + c * 128:OFF_GATE + (c + 1) * 128], 128, wst_pool, wbf_pool)
            for ti, (u0, n) in enumerate(TT):
                ps, ups = self.mm_tile(wb, uwb, 128, hT, u_hT, ti)
                t, ut = gtp.get()
                S.add("act", (lambda e, ps=ps, t=t, n=n: e.activation(out=t[:, :n], in_=ps[:, :n], func=AF.Sigmoid)),
                      reads=[ups], writes=[ut])
                self.dma(Sx["gT"][c * 128:(c + 1) * 128, u0:u0 + n], t[:, :n], reads=[ut], writes=[Ux["gT"]])
        fillers = [(z_chunk, c) for c in range(8)] + [(gate_chunk, c) for c in range(24)]

        def fill():
            if fillers:
                fn, c = fillers.pop(0)
                fn(c)

        dn_w = make_prefetch([OFF_QKV + c * 128 for c in range(24)])

        def dn_front(c):
            proj_to_pp(OFF_QKV + c * 128, dn_w(c))
            (q, uq) = WK.get()
            self._acc, self._uacc = q, uq
            self.conv(pp, upp, (lambda j, c=c: self.vs("dncw", c * 4 + j)), 4, None, q[:, :U], uq)
            return q, uq

        def dn_tail(c, q, uq):
            w3, h = c // 8, c % 8
            S.add("act", (lambda e, q=q: e.activation(out=q[:, :U], in_=q[:, :U], func=AF.Silu)), reads=[uq], writes=[uq])
            if w3 < 2:
                (sq, usq) = (sqb, usqb)
                S.add("act", (lambda e, q=q, sq=sq: e.activation(out=sq[:, :U], in_=q[:, :U], func=AF.Square)),
                      reads=[uq], writes=[usq])
                for (u0, n) in TT:
                    ps, ups = self.PS.get()
                    S.add("pe", (lambda e, ps=ps, sq=sq, u0=u0, n=n: e.matmul(ps[:, :n], self.ones_bf[:], sq[:, u0:u0 + n],
                                                                               start=True, stop=True)),
                          reads=[usq, self.u_ones], writes=[ups])
                    rs, urs = rsp.get()
                    S.add("act", (lambda e, rs=rs, ps=ps, n=n: e.activation(out=rs[:, :n], in_=ps[:, :n], func=AF.Sqrt, bias=self.eps6[:])),
                          reads=[ups], writes=[urs])
                    S.add("dve", (lambda e, rs=rs, n=n: e.reciprocal(out=rs[:, :n], in_=rs[:, :n])), reads=[urs], writes=[urs])
                    S.add("dve", (lambda e, q=q, rs=rs, u0=u0, n=n: e.tensor_tensor(out=q[:, u0:u0 + n], in0=q[:, u0:u0 + n],
                                                                                    in1=rs[:, :n], op=ALU.mult)),
                          reads=[urs, uq], writes=[uq])
            dst = (Sx["qT"], Sx["kT"], Sx["vT"])[w3]
            ud = (Ux["qT"], Ux["kT"], Ux["vT"])[w3]
            self.dma(dst[h], q[:, :U], reads=[uq], writes=[ud])
        prev = None
        for c in range(24):
            cur = dn_front(c)
            if prev is not None:
                dn_tail(c - 1, *prev)
                fill()
            prev = cur
        dn_tail(23, *prev)
        fill()
        if self.stop == "pre1":
            return
        if self.stop == "pre2":
            return
        rm, urm = WK.get()
        self.dma(rm[:40, :U], I["rmask"], writes=[urm])
        G, uG = WK.get()
        LNB, uLNB = WK.get()
        for which, (dst, udst) in enumerate(((G, uG), (LNB, uLNB))):
            wst, uws = wst_pool.get()
            S.add("pool", (lambda e, wst=wst: e.memset(wst[:], 0.0)), writes=[uws])
            for d in range(2):
                c0 = OFF_AB + d * 16 + which * 8
                self.dma(wst[:, :, d * 32:d * 32 + 8], I["w_in"][l][:, c0:c0 + 8].rearrange("(k p) m -> p k m", p=128),
                         reads=[uws], writes=[uws])
            wb, uwb = wbf_pool.get()
            S.add("pool", (lambda e, wb=wb, wst=wst: e.tensor_copy(out=wb[:, :, :40], in_=wst[:, :, :40])), reads=[uws], writes=[uwb])
            for ti, (u0, n) in enumerate(TT):
                ps, ups = self.mm_tile(wb, uwb, 40, hT, u_hT, ti)
                evac_copy(ps, ups, dst[:40, u0:u0 + n], udst, 40, n)
        for (t_, ut_) in ((G, uG), (LNB, uLNB)):
            S.add("pool", (lambda e, t_=t_: e.memset(t_[:40, 256:259], 0.0)), reads=[ut_], writes=[ut_])
        S.add("act", lambda e: e.activation(out=G[:40, :U], in_=G[:40, :U], func=AF.Exp, bias=self.vs("dtb")[:40]),
              reads=[uG, self.u_vec], writes=[uG])
        S.add("act", lambda e: e.activation(out=G[:40, :U], in_=G[:40, :U], func=AF.Ln, bias=1.0), reads=[uG], writes=[uG])
        S.add("dve", lambda e: e.tensor_scalar(out=G[:40, :U], in0=G[:40, :U], scalar1=sm[:40, 0:1], scalar2=None, op0=ALU.mult),
              reads=[uG, u_sm], writes=[uG])
        S.add("act", lambda e: e.activation(out=LNB[:40, :U], in_=LNB[:40, :U], func=AF.Exp, scale=-1.0), reads=[uLNB], writes=[uLNB])
        S.add("act", lambda e: e.activation(out=LNB[:40, :U], in_=LNB[:40, :U], func=AF.Ln, bias=1.0), reads=[uLNB], writes=[uLNB])
        S.add("dve", lambda e: e.tensor_scalar(out=LNB[:40, :U], in0=LNB[:40, :U], scalar1=-1.0, scalar2=None, op0=ALU.mult),
              reads=[uLNB], writes=[uLNB])
        GC, uGC = WK.get()
        S.add("pool", lambda e: e.memset(GC[:40, :U], 0.0), writes=[uGC])
        S.add("dve", lambda e: e.tensor_tensor_scan(out=GC[0:8, :U], data0=rm[0:8, :U], data1=G[0:8, :U], initial=0.0,
                                                    op0=ALU.mult, op1=ALU.add), reads=[urm, uG, uGC], writes=[uGC])
        S.add("dve", lambda e: e.tensor_tensor_scan(out=GC[32:40, U - 1::-1], data0=rm[32:40, U - 1::-1], data1=G[32:40, U - 1::-1],
                                                    initial=0.0, op0=ALU.mult, op1=ALU.add), reads=[urm, uG, uGC], writes=[uGC])
        self.dma(Sx["gab"][0], GC[:40, :U], reads=[uGC], writes=[Ux["gab"]])
        self.dma(Sx["gab"][2], LNB[:40, :U], reads=[uLNB], writes=[Ux["gab"]])
        S.add("dve", lambda e: e.tensor_tensor(out=G[:40, :U], in0=GC[:40, :U], in1=LNB[:40, :U], op=ALU.add),
              reads=[uGC, uLNB, uG], writes=[uG])
        self.dma(Sx["gab"][1], G[:40, :U], reads=[uG], writes=[Ux["gab"]])
        if self.stop == "pre3":
            return
        hy_order = [part * 8 + c for c in range(8) for part in (1, 2, 0)]
        hy_w = make_prefetch([OFF_HY + cidx * 128 for cidx in hy_order])
        hy_i = 0
        for c in range(8):
            tl = []
            for part in (1, 2, 0):
                cidx = part * 8 + c
                proj_to_pp(OFF_HY + cidx * 128, hy_w(hy_i))
                hy_i += 1
                t, ut = WK.get()
                self._acc, self._uacc = t, ut
                self.conv(pp, upp, (lambda j, cidx=cidx: self.vs("hycw", cidx * 3 + j)), 3, self.vs("hycb", cidx), t[:, :U], ut)
                tl.append((t, ut))
                fill()
            (a1, ua1), (a2, ua2), (a0, ua0) = tl
            S.add("dve", (lambda e, a1=a1, a2=a2: e.tensor_tensor(out=a1[:, :U], in0=a1[:, :U], in1=a2[:, :U], op=ALU.mult)),
                  reads=[ua1, ua2], writes=[ua1])
            self.dma(Sx["zzT"][c * 128:(c + 1) * 128], a1[:, :U], reads=[ua1], writes=[Ux["zzT"]])
            self.dma(Sx["x0T"][c * 128:(c + 1) * 128], a0[:, :U], reads=[ua0], writes=[Ux["x0T"]])
        if self.stop == "pre4":
            return
        while fillers:
            fill()
        for g in range(8):
            proj_to_pp(OFF_LX + g * 128)
            (xs, uxs), (H, uH), (A, uA), (Bt, uB) = WK.t
            self._acc, self._uacc = H, uH
            self.conv(pp, upp, (lambda j, g=g: self.vs("lrucw", g * 4 + j)), 4, self.vs("lrucb", g), xs[:, :U], uxs)
            S.add("act", (lambda e: e.activation(out=sqb[:, :U], in_=xs[:, :U], func=AF.Copy)), reads=[uxs], writes=[usqb])
            for d in range(2):
                tt_, utt = (H, uH) if d == 0 else (pp, upp)
                for (wname, bname, dst, udst) in (("lru_w_a", "lba", A, uA), ("lru_w_x", "lbx", Bt, uB)):
                    lw, ulw = lwp.get()
                    self.dma(lw[:], I[wname][l, d, g], writes=[ulw])
                    lr, ulr = lwr.get()
                    S.add("act", (lambda e, lw=lw, lr=lr: e.activation(out=lr[:], in_=lw[:], func=AF.Copy)), reads=[ulw], writes=[ulr])
                    for (u0, n) in TT:
                        ps, ups = self.PS.get()
                        S.add("pe", (lambda e, ps=ps, lr=lr, u0=u0, n=n: e.matmul(ps[:, :n], lr[:], sqb[:, u0:u0 + n],
                                                                                   start=True, stop=True)),
                              reads=[ulr, usqb], writes=[ups])
                        S.add("act", (lambda e, ps=ps, dst=dst, u0=u0, n=n, bname=bname, d=d, g=g: e.activation(
                            out=dst[:, u0:u0 + n], in_=ps[:, :n], func=AF.Sigmoid, bias=self.vs(bname, d * 8 + g))),
                            reads=[ups, self.u_vec], writes=[udst])
                if self.stop == "pre5":
                    return
                S.add("pool", (lambda e, A=A: e.memset(A[:, 256:259], 0.0)), reads=[uA], writes=[uA])
                S.add("act", (lambda e, A=A, t=tt_, d=d, g=g: e.activation(out=t[:, :U], in_=A[:, :U], func=AF.Exp,
                                                                           scale=sm[:, 24 + d * 8 + g:25 + d * 8 + g])),
                      reads=[uA, u_sm, utt], writes=[utt])
                S.add("act", (lambda e, A=A, d=d, g=g: e.activation(out=A[:, :U], in_=A[:, :U], func=AF.Exp,
                                                                     scale=sm[:, 8 + d * 8 + g:9 + d * 8 + g])),
                      reads=[uA, u_sm], writes=[uA])
                S.add("act", (lambda e, t=tt_: e.activation(out=t[:, :U], in_=t[:, :U], func=AF.Sqrt, scale=-1.0, bias=self.one1[:])),
                      reads=[utt], writes=[utt])
                S.add("dve", (lambda e, Bt=Bt, t=tt_: e.tensor_tensor(out=Bt[:, :U], in0=Bt[:, :U], in1=t[:, :U], op=ALU.mult)),
                      reads=[uB, utt], writes=[uB])
                S.add("dve", (lambda e, Bt=Bt: e.tensor_tensor(out=Bt[:, :U], in0=Bt[:, :U], in1=xs[:, :U], op=ALU.mult)),
                      reads=[uB, uxs], writes=[uB])
                S.add("pool", (lambda e, Bt=Bt: e.memset(Bt[:, 256:259], 0.0)), reads=[uB], writes=[uB])
                if self.stop == "pre6":
                    return
                if d == 0:
                    S.add("dve", (lambda e, A=A, Bt=Bt, t=tt_: e.tensor_tensor_scan(out=t[:, 0:U], data0=A[:, 0:U], data1=Bt[:, 0:U],
                                                                                    initial=0.0, op0=ALU.mult, op1=ALU.add)),
                          reads=[uA, uB, utt], writes=[utt])
                else:
                    S.add("dve", (lambda e, A=A, Bt=Bt, t=tt_: e.tensor_tensor_scan(out=t[:, 255::-1], data0=A[:, 255::-1], data1=Bt[:, 255::-1],
                                                                                    initial=0.0, op0=ALU.mult, op1=ALU.add)),
                          reads=[uA, uB, utt], writes=[utt])
                    S.add("dve", (lambda e, A=A, Bt=Bt, t=tt_: e.tensor_tensor_scan(out=t[:, U - 1:LT0 - 1:-1], data0=A[:, U - 1:LT0 - 1:-1],
                                                                                    data1=Bt[:, U - 1:LT0 - 1:-1], initial=t[:, 0:1],
                                                                                    op0=ALU.mult, op1=ALU.add)),
                          reads=[uA, uB, utt], writes=[utt])
                    S.add("dve", (lambda e, t=tt_: e.tensor_tensor(out=H[:, :U], in0=H[:, :U], in1=t[:, :U], op=ALU.add)),
                          reads=[utt, uH], writes=[uH])
            if self.stop == "pre7":
                return
            S.add("pool", lambda e: e.memset(pp[:, 0:1], 0.0), reads=[upp], writes=[upp])
            S.add("pool", lambda e: e.memset(pp[:, 257:260], 0.0), reads=[upp], writes=[upp])
            S.add("pool", lambda e: e.memset(pp[:, 4356:UP], 0.0), reads=[upp], writes=[upp])
            (Y, uY), (T2, uT2) = WK.t[2], WK.t[3]
            wb, uwb = self.load_w_bf16(I["w_in"][l][:, OFF_LY + g * 128:OFF_LY + (g + 1) * 128], 128, wst_pool, wbf_pool)
            for ti, (u0, n) in enumerate(TT):
                ps, ups = self.mm_tile(wb, uwb, 128, hT, u_hT, ti)
                evac_copy(ps, ups, Y[:, u0:u0 + n], uY, 128, n)
            S.add("pool", (lambda e, Y=Y: e.memset(Y[:, 256:259], 0.0)), reads=[uY], writes=[uY])
            S.add("act", (lambda e, Y=Y, T2=T2: e.activation(out=T2[:, :U], in_=Y[:, :U], func=AF.Square)), reads=[uY], writes=[uT2])
            S.add("dve", (lambda e, T2=T2: e.tensor_scalar(out=T2[:, :U], in0=T2[:, :U], scalar1=0.044715, scalar2=1.0,
                                                           op0=ALU.mult, op1=ALU.add)), reads=[uT2], writes=[uT2])
            S.add("dve", (lambda e, Y=Y, T2=T2: e.tensor_tensor(out=T2[:, :U], in0=T2[:, :U], in1=Y[:, :U], op=ALU.mult)),
                  reads=[uT2, uY], writes=[uT2])
            S.add("act", (lambda e, T2=T2: e.activation(out=T2[:, :U], in_=T2[:, :U], func=AF.Sigmoid, scale=1.5957691216057308)),
                  reads=[uT2], writes=[uT2])
            S.add("dve", (lambda e, Y=Y, T2=T2: e.tensor_tensor(out=Y[:, :U], in0=Y[:, :U], in1=T2[:, :U], op=ALU.mult)),
                  reads=[uT2, uY], writes=[uY])
            S.add("dve", (lambda e, Y=Y: e.tensor_tensor(out=T2[:, :U].bitcast(BF16)[:, :U], in0=Y[:, :U], in1=H[:, :U], op=ALU.mult)),
                  reads=[uY, uH, uT2], writes=[uT2])
            self.dma(Sx["olru"][g * 128:(g + 1) * 128], T2[:, :U].bitcast(BF16)[:, :U], reads=[uT2], writes=[Ux["olru"]])
            if self.stop == "pre8":
                return


def prep_inputs(inp, b):
    m = {}
    xT = np.zeros((D, U), np.float32)
    xT[:, 0:LC] = inp["ctx"][b].T
    xT[:, LT0:] = inp["x"][b].T
    m["xT0"] = xT
    cc = np.zeros((128, 8, 2), np.float32)
    cc[:, :, 0] = _pk(inp["c"][b])
    cc[:, :, 1] = _pk(inp["c_ctx"])
    m["cc"] = cc.reshape(128, 16)
    m["vec"] = np.stack([build_vec(inp, l) for l in range(DEPTH)])
    m["cst"] = build_cst()
    m["rmask"] = build_rmask()
    for seg, n_ in (("lat", L), ("ctx", LC)):
        tbs = hyena_tables(n_)
        for k in ("zT", "negt", "F1", "G", "Tout"):
            m["hy_%s_%s" % (k, seg)] = tbs[k]
        m["hy_Winv"] = tbs["Winv"]
    for n in ("w_mod", "w_in", "lru_w_a", "lru_w_x", "w_proj_dn", "w_proj_hy", "w_proj_lru", "w_out",
              "ffn_up", "ffn_down", "hy_w1", "hy_w2", "hy_w3"):
        m[n] = np.ascontiguousarray(inp[n], dtype=np.float32)
    return m


DN_BLOCKS = [0, 128] + [LT0 + 128 * i for i in range(32)]


class T128:
    def __init__(self, nc, es, names, dtype=F32):
        self.t = {}
        for n in names:
            self.t[n] = (es.enter_context(nc.sbuf_tensor(uname(n), [128, 128], dtype)), Unit())

    def __getitem__(self, n):
        return self.t[n]


def dn_phase(B, l, es):
    nc, S, Sx, Ux = B.nc, B.S, B.Sx, B.Ux
    ident = B.cs("ident")
    NB = len(DN_BLOCKS)
    GR = es.enter_context(nc.sbuf_tensor(uname("GR"), [40, U], F32))
    uGR = Unit()
    B.dma(GR[:], Sx["gab"][0], reads=[Ux["gab"]], writes=[uGR])
    TG = es.enter_context(nc.sbuf_tensor(uname("TG"), [128, NB, 3, 40], F32))
    TE = es.enter_context(nc.sbuf_tensor(uname("TE"), [128, NB, 2, 40], F32))
    uTG = Unit()
    gl_pool = TPool(nc, es, "gl", [40, 3, 128], F32, 2)
    for bi, ub in enumerate(DN_BLOCKS):
        gl, ugl = gl_pool.get()
        B.dma(gl[:], Sx["gab"][:, :, ub:ub + 128].rearrange("k r u -> r k u"), reads=[Ux["gab"]], writes=[ugl])
        ps, ups = B.PS.get()
        for k in range(3):
            S.add("pe", (lambda e, ps=ps, k=k, gl=gl: e.transpose(ps[:, k * 40:(k + 1) * 40], gl[:, k, :], ident[:40, :40])),
                  reads=[ugl, B.u_cst], writes=[ups])
        S.add("dve", (lambda e, ps=ps, bi=bi: e.tensor_copy(out=TG[:, bi].rearrange("p k r -> p (k r)"), in_=ps[:, 0:120])),
              reads=[ups], writes=[uTG])
        S.add("act", (lambda e, ps=ps, bi=bi: e.activation(out=TE[:, bi].rearrange("p k r -> p (k r)"), in_=ps[:, 40:120], func=AF.Exp)),
              reads=[ups], writes=[uTG])
    SELN = es.enter_context(nc.sbuf_tensor(uname("SELN"), [40, 16, 128], F32))
    uSEL = Unit()
    for q in range(16):
        r = (q // 8) * 32 + (q % 8)
        S.add("dve", (lambda e, q=q, r=r: e.tensor_scalar(out=SELN[:, q, :], in0=ident[:40, r:r + 1].to_broadcast([40, 128]),
                                                          scalar1=-1.0, scalar2=None, op0=ALU.mult)),
              reads=[B.u_cst], writes=[uSEL])
    zero = es.enter_context(nc.sbuf_tensor(uname("zero"), [128, 128], F32))
    uzero = Unit()
    S.add("pool", lambda e: e.memset(zero[:], 0.0), writes=[uzero])

    if B.stop == "dn0":
        dbg = nc.dram_tensor(uname("dbg_TG"), [128, NB, 3, 40], F32, kind="ExternalOutput").ap()
        B.dma(dbg, TG[:], reads=[uTG])
        return
    NCH = 4
    chains_res = []
    for ci in range(NCH):
        res = {}
        res["f32"] = T128(nc, es, ["qt", "kt", "vt", "Dm", "A", "E1", "M", "AT", "MT", "Y", "P0", "P1", "Q0", "Q1", "Us", "EG"])
        res["f32b"] = T128(nc, es, ["qt", "kt", "vt"])
        res["r"] = T128(nc, es, ["ktr", "qtr", "attnT", "Kd0", "Kd1", "Ktb", "Vb", "QdT", "YR", "WTs", "VN", "S"], F32R)
        res["cd"] = (es.enter_context(nc.sbuf_tensor(uname("cd"), [128, 2], F32)), Unit())
        if ci % 2 == 0:
            res["O"] = (es.enter_context(nc.sbuf_tensor(uname("O"), [128, NB, 128], F32)), [Unit() for _ in range(NB)])
        else:
            res["O"] = chains_res[ci - 1]["O"]
        res["ps"] = [B.PS.t[2 * ci], B.PS.t[2 * ci + 1]]
        chains_res.append(res)
    OD = es.enter_context(nc.sbuf_tensor(uname("OD"), [128, U], BF16))
    uOD = Unit()
    S.add("pool", lambda e: e.memset(OD[:, 256:259], 0.0), writes=[uOD])
    pp_t = TPool(nc, es, "dnpost", [128, 128], F32, 3)
    pp_s = TPool(nc, es, "dnsm", [128, 2], F32, 3)
    DK = 128.0 ** -0.5

    def chain(h, d, res):
        q = d * 8 + h
        r = d * 32 + h
        f, fb, rr = res["f32"], res["f32b"], res["r"]
        cd, ucd = res["cd"]
        O, uO = res["O"]
        psl = res["ps"]
        slot_i = [0]

        def slot():
            k = slot_i[0] % 8
            slot_i[0] += 1
            t, u = psl[k // 4]
            qd = k % 4
            return t[:, qd * 128:(qd + 1) * 128], u
        incl = B.cs("lowi") if d == 0 else B.cs("uppi")
        strict = B.cs("lows") if d == 0 else B.cs("upps")
        iend = (lambda c: c * 64 + 63) if d == 0 else (lambda c: c * 64)
        (Sst, uS), (VN, uVN) = rr["S"], rr["VN"]
        S.add("dve", (lambda e: e.tensor_copy(out=Sst[:], in_=zero[:])), reads=[uzero], writes=[uS])
        S.add("dve", (lambda e: e.tensor_copy(out=VN[:], in_=zero[:])), reads=[uzero], writes=[uVN])
        order = list(range(NB)) if d == 0 else [1, 0] + list(range(NB - 1, 1, -1))
        for oi, bi in enumerate(order):
            ub = DN_BLOCKS[bi]
            ld = f if oi % 2 == 0 else fb
            (qt, uqt), (kt, ukt), (vt, uvt) = ld["qt"], ld["kt"], ld["vt"]
            B.dma(qt[:], Sx["qT"][h][:, ub:ub + 128], reads=[Ux["qT"]], writes=[uqt])
            B.dma(kt[:], Sx["kT"][h][:, ub:ub + 128], reads=[Ux["kT"]], writes=[ukt])
            B.dma(vt[:], Sx["vT"][h][:, ub:ub + 128], reads=[Ux["vT"]], writes=[uvt])
            pKK, uKK = slot()
            S.add("pe", (lambda e, o=pKK, kt=kt: e.matmul(o, kt[:], kt[:], start=True, stop=True)), reads=[ukt], writes=[uKK])
            pQK, uQK = slot()
            S.add("pe", (lambda e, o=pQK, kt=kt, qt=qt: e.matmul(o, kt[:], qt[:], start=True, stop=True)), reads=[ukt, uqt], writes=[uQK])
            pbc, ubc = slot()
            S.add("pe", (lambda e, o=pbc, ub=ub: e.matmul(o, SELN[:, q, :], GR[:, ub:ub + 128], start=True, stop=True)),
                  reads=[uSEL, uGR], writes=[ubc])
            pvt, uvtk = slot()
            S.add("pe", (lambda e, o=pvt, vt=vt: e.transpose(o, vt[:], ident)), reads=[uvt, B.u_cst], writes=[uvtk])
            pkt, uktk = slot()
            S.add("pe", (lambda e, o=pkt, kt=kt: e.transpose(o, kt[:], ident)), reads=[ukt, B.u_cst], writes=[uktk])
            yield
            if B.stop.endswith(":B"):
                return
            (Dm, uDm), (A, uA), (E1, uE1), (M, uM), (EG, uEG) = f["Dm"], f["A"], f["E1"], f["M"], f["EG"]
            gcol = TG[:, bi, 0, r:r + 1]
            climit = int(B.stop.split(":K")[1]) if ":K" in B.stop else 99
            if 0 < climit:
                S.add("dve", (lambda e, o=pbc, gcol=gcol: e.tensor_scalar(out=Dm[:], in0=o, scalar1=gcol, scalar2=0.0, op0=ALU.add, op1=ALU.min)),
                      reads=[ubc, uTG], writes=[uDm])
            if 2 < climit:
                S.add("act", (lambda e, o=pbc: e.activation(out=EG[:], in_=o, func=AF.Exp, scale=-1.0)), reads=[ubc], writes=[uEG])
            if 3 < climit:
                S.add("act", (lambda e: e.activation(out=A[:], in_=Dm[:], func=AF.Exp)), reads=[uDm], writes=[uA])
            if 4 < climit:
                S.add("dve", (lambda e: e.tensor_tensor(out=A[:], in0=A[:], in1=incl, op=ALU.mult)), reads=[uA, B.u_cst], writes=[uA])
            bcol = TE[:, bi, 1, r:r + 1]
            if 5 < climit:
                S.add("dve", (lambda e, bcol=bcol: e.scalar_tensor_tensor(out=E1[:], in0=A[:], scalar=bcol, in1=strict, op0=ALU.mult, op1=ALU.mult)),
                      reads=[uA, uTG, B.u_cst], writes=[uE1])
            if 6 < climit:
                S.add("dve", (lambda e, o=pKK: e.tensor_tensor(out=M[:], in0=o, in1=E1[:], op=ALU.mult)), reads=[uKK, uE1], writes=[uM])
            (QdT, uQdT) = rr["QdT"]
            if 7 < climit:
                S.add("dve", (lambda e, qt=qt: e.scalar_tensor_tensor(out=QdT[:], in0=qt[:], scalar=DK, in1=EG[:], op0=ALU.mult, op1=ALU.mult)),
                      reads=[uqt, uEG], writes=[uQdT])
            yield
            if B.stop.endswith(":C") or ":K" in B.stop:
                return
            pAT, uAT_ = slot()
            S.add("pe", (lambda e, o=pAT: e.transpose(o, A[:], ident)), reads=[uA, B.u_cst], writes=[uAT_])
            pMT, uMT_ = slot()
            S.add("pe", (lambda e, o=pMT: e.transpose(o, M[:], ident)), reads=[uM, B.u_cst], writes=[uMT_])
            yield
            (AT, uAT), (MT, uMT), (Y, uY) = f["AT"], f["MT"], f["Y"]
            S.add("act", (lambda e, o=pAT: e.activation(out=AT[:], in_=o, func=AF.Copy)), reads=[uAT_], writes=[uAT])
            S.add("act", (lambda e, o=pMT: e.activation(out=MT[:], in_=o, func=AF.Copy)), reads=[uMT_], writes=[uMT])
            S.add("dve", (lambda e, o=pMT: e.tensor_tensor(out=Y[:], in0=ident, in1=o, op=ALU.subtract)), reads=[uMT_, B.u_cst], writes=[uY])
            (attnT, uattn), (Kd0, uKd0), (Kd1, uKd1), (Ktb, uKtb), (Vb, uVb) = rr["attnT"], rr["Kd0"], rr["Kd1"], rr["Ktb"], rr["Vb"]
            S.add("dve", (lambda e, o=pQK: e.scalar_tensor_tensor(out=attnT[:], in0=o, scalar=DK, in1=AT[:], op0=ALU.mult, op1=ALU.mult)),
                  reads=[uQK, uAT], writes=[uattn])
            for c, (Kd, uKd) in enumerate(((Kd0, uKd0), (Kd1, uKd1))):
                S.add("act", (lambda e, o=pkt, Kd=Kd, c=c: e.activation(out=Kd[:], in_=o, func=AF.Copy, scale=AT[:, iend(c):iend(c) + 1])),
                      reads=[uktk, uAT], writes=[uKd])
            wcol = TE[:, bi, 0, r:r + 1]
            S.add("dve", (lambda e, o=pkt, wcol=wcol: e.tensor_scalar(out=Ktb[:], in0=o, scalar1=wcol, scalar2=None, op0=ALU.mult)),
                  reads=[uktk, uTG], writes=[uKtb])
            S.add("dve", (lambda e, o=pvt, bcol=bcol: e.tensor_scalar(out=Vb[:], in0=o, scalar1=bcol, scalar2=None, op0=ALU.mult)),
                  reads=[uvtk, uTG], writes=[uVb])
            yield
            if B.stop.endswith(":E"):
                return
            P, uP = M, uM
            PT, uPT = MT, uMT
            bufs = [(f["P0"], f["Q0"]), (f["P1"], f["Q1"])]
            (YR, uYR) = rr["YR"]
            pend = None
            for lev in range(1, 7):
                cur = None
                if lev <= 5:
                    (Pn, uPn), (PnT, uPnT) = bufs[lev % 2]
                    pP, upP = slot()
                    S.add("pe", (lambda e, o=pP, PT=PT, P=P: e.matmul(o, PT[:], P[:], start=True, stop=True)), reads=[uPT, uP], writes=[upP])
                    pPT = None
                    if lev < 5:
                        pPT, upPT = slot()
                        S.add("pe", (lambda e, o=pPT, PT=PT, P=P: e.matmul(o, P[:], PT[:], start=True, stop=True)), reads=[uPT, uP], writes=[upPT])
                    cur = (Pn, uPn, PnT, uPnT, pP, upP, pPT, upPT if lev < 5 else None)
                pY = None
                if pend is not None:
                    pY, upY = slot()
                    S.add("pe", (lambda e, o=pY, Pq=pend[0]: e.matmul(o, Pq[:], Y[:], start=True, stop=True)), reads=[pend[1], uY], writes=[upY])
                yield
                if pY is not None:
                    if lev <= 5:
                        S.add("dve", (lambda e, o=pY: e.tensor_tensor(out=Y[:], in0=Y[:], in1=o, op=ALU.add)), reads=[upY, uY], writes=[uY])
                    else:
                        S.add("dve", (lambda e, o=pY: e.tensor_tensor(out=YR[:], in0=Y[:], in1=o, op=ALU.add)), reads=[upY, uY], writes=[uYR])
                if cur is not None:
                    (Pn, uPn, PnT, uPnT, pP, upP, pPT, upPT) = cur
                    S.add("act", (lambda e, o=pP, Pn=Pn: e.activation(out=Pn[:], in_=o, func=AF.Copy)), reads=[upP], writes=[uPn])
                    if pPT is not None:
                        S.add("act", (lambda e, o=pPT, PnT=PnT: e.activation(out=PnT[:], in_=o, func=AF.Copy)), reads=[upPT], writes=[uPnT])
                    pend = (Pn, uPn)
                    P, uP, PT, uPT = Pn, uPn, PnT, uPnT
                yield
            if B.stop.endswith(":G"):
                return
            (Us, uUs), (WTs, uWTs) = f["Us"], rr["WTs"]
            pU, upU = slot()
            S.add("pe", (lambda e, o=pU: e.matmul(o, YR[:], Vb[:], start=True, stop=True)), reads=[uYR, uVb], writes=[upU])
            pW, upW = slot()
            S.add("pe", (lambda e, o=pW: e.matmul(o, Ktb[:], YR[:], start=True, stop=True)), reads=[uYR, uKtb], writes=[upW])
            yield
            S.add("act", (lambda e, o=pU: e.activation(out=Us[:], in_=o, func=AF.Copy)), reads=[upU], writes=[uUs])
            S.add("dve", (lambda e, o=pW: e.tensor_copy(out=WTs[:], in_=o)), reads=[upW], writes=[uWTs])
            yield
            if B.stop.endswith(":H"):
                return
            for c in ((0, 1) if d == 0 else (1, 0)):
                rows = slice(c * 64, (c + 1) * 64)
                Kd, uKd = (Kd0, uKd0) if c == 0 else (Kd1, uKd1)
                p1, up1 = slot()
                S.add("pe", (lambda e, o=p1: e.matmul(o, WTs[:], Sst[:], start=True, stop=True)), reads=[uWTs, uS], writes=[up1])
                yield
                S.add("dve", (lambda e, o=p1, rows=rows: e.tensor_tensor(out=VN[rows, :], in0=Us[rows, :], in1=o[rows, :], op=ALU.subtract)),
                      reads=[up1, uUs, uVN], writes=[uVN])
                yield
                p2, up2 = slot()
                S.add("pe", (lambda e, o=p2: e.matmul(o, QdT[:], Sst[:], start=True, stop=False)), reads=[uQdT, uS], writes=[up2])
                S.add("pe", (lambda e, o=p2: e.matmul(o, attnT[:], VN[:], start=False, stop=True)), reads=[uattn, uVN], writes=[up2])
                p3, up3 = slot()
                S.add("pe", (lambda e, o=p3, Kd=Kd: e.matmul(o, Kd[:], VN[:], start=True, stop=True)), reads=[uKd, uVN], writes=[up3])
                yield
                oi_other = (bi if d == 1 else ([1, 0] + list(range(NB - 1, 1, -1))).index(bi))
                first = (oi < oi_other) or (oi == oi_other and d == 0)
                if first:
                    S.add("act", (lambda e, o=p2, rows=rows, bi=bi: e.activation(out=O[rows, bi, :], in_=o[rows, :], func=AF.Copy)),
                          reads=[up2], writes=[uO[bi]])
                else:
                    S.add("dve", (lambda e, o=p2, rows=rows, bi=bi: e.tensor_tensor(out=O[rows, bi, :], in0=O[rows, bi, :], in1=o[rows, :], op=ALU.add)),
                          reads=[up2, uO[bi]], writes=[uO[bi]])
                S.add("dve", (lambda e, o=p3, c=c: e.scalar_tensor_tensor(out=Sst[:], in0=Sst[:], scalar=EG[:, iend(c):iend(c) + 1], in1=o,
                                                                         op0=ALU.mult, op1=ALU.add)), reads=[up3, uS, uEG], writes=[uS])
                yield

    def post(h, resf, resb):
        (Of, uOf) = resf["O"]
        for bi, ub in enumerate(DN_BLOCKS):
            t, ut = pp_t.get()
            sm_, usm = pp_s.get()
            S.add("pool", (lambda e, t=t, bi=bi: e.tensor_copy(out=t[:], in_=Of[:, bi, :])),
                  reads=[uOf[bi]], writes=[ut])
            t2, ut2 = pp_t.get()
            S.add("act", (lambda e, t=t, t2=t2, sm_=sm_: e.activation(out=t2[:], in_=t[:], func=AF.Square, accum_out=sm_[:, 0:1])),
                  reads=[ut], writes=[ut2, usm])
            S.add("act", (lambda e, sm_=sm_: e.activation(out=sm_[:, 1:2], in_=sm_[:, 0:1], func=AF.Sqrt, scale=1.0 / 128, bias=B.eps6[:])),
                  reads=[usm], writes=[usm])
            S.add("dve", (lambda e, sm_=sm_: e.reciprocal(out=sm_[:, 1:2], in_=sm_[:, 1:2])), reads=[usm], writes=[usm])
            S.add("dve", (lambda e, t=t, sm_=sm_: e.tensor_scalar(out=t[:], in0=t[:], scalar1=sm_[:, 1:2], scalar2=None, op0=ALU.mult)),
                  reads=[usm, ut], writes=[ut])
            ps, ups = B.PS.get()
            S.add("pe", (lambda e, ps=ps, t=t: e.transpose(ps[:, 0:128], t[:], ident)), reads=[ut, B.u_cst], writes=[ups])
            sz, usz = pp_t.get()
            B.dma(sz[:], Sx["szT"][h * 128:(h + 1) * 128, ub:ub + 128], reads=[Ux["szT"]], writes=[usz])
            S.add("dve", (lambda e, ps=ps, sz=sz, ub=ub: e.scalar_tensor_tensor(out=OD[:, ub:ub + 128], in0=ps[:, 0:128], scalar=B.vs("dnng"),
                                                                              in1=sz[:], op0=ALU.mult, op1=ALU.mult)),
                  reads=[ups, usz, B.u_vec, uOD], writes=[uOD])
        B.dma(Sx["odn"][h * 128:(h + 1) * 128], OD[:], reads=[uOD], writes=[Ux["odn"]])

    nheads = 8 if not (B.stop or "").startswith("dn1") else 2
    if B.stop is None:
        B.stop = ""
    for h0 in range(0, nheads, 2):
        gens = []
        for j in range(2):
            for d in range(2):
                gens.append(chain(h0 + j, d, chains_res[j * 2 + d]))
        active = list(gens)
        while active:
            nxt = []
            for g in active:
                try:
                    next(g)
                    nxt.append(g)
                except StopIteration:
                    pass
            active = nxt
        if ":" in B.stop:
            return
        for j in range(2):
            post(h0 + j, chains_res[j * 2], chains_res[j * 2 + 1])


def hy_dims(n):
    N = 2 * n
    NS1 = n // 128
    NF1 = N // 128
    NFH = NF1 // 2 + 1
    return N, NS1, NF1, NFH


def hyena_tables(n):
    N, NS1, NF1, NFH = hy_dims(n)
    f64 = np.float64
    s1 = np.arange(NS1, dtype=f64)[:, None]
    f1 = np.arange(NFH, dtype=f64)[None, :]
    ang = 2 * np.pi * f1 * s1 / NF1
    F1 = np.concatenate([np.cos(ang), -np.sin(ang)], axis=1)
    s2 = np.arange(128, dtype=f64)[:, None, None]
    f1b = np.arange(NFH, dtype=f64)[None, :, None]
    f2 = np.arange(128, dtype=f64)[None, None, :]
    ang = 2 * np.pi * (f1b + NF1 * f2) * s2 / N
    G = np.stack([np.cos(ang), -np.sin(ang)], axis=2)
    f2c = np.arange(128, dtype=f64)[:, None]
    t2 = np.arange(128, dtype=f64)[None, :]
    th = 2 * np.pi * f2c * t2 / 128
    Winv = np.stack([np.concatenate([np.cos(th), np.sin(th)], 1), np.concatenate([-np.sin(th), np.cos(th)], 1)], axis=1)
    NT1 = NS1
    f1c = np.arange(NFH, dtype=f64)[:, None, None]
    t2c = np.arange(128, dtype=f64)[None, :, None]
    t1c = np.arange(NT1, dtype=f64)[None, None, :]
    ph = 2 * np.pi * f1c * (128 * t1c + t2c) / N
    w = np.full((NFH, 1, 1), 2.0)
    w[0] = 1.0
    w[NFH - 1] = 1.0
    Tout = np.stack([w / N * np.cos(ph), -w / N * np.sin(ph)], axis=2)
    t = np.linspace(0.0, 1.0, n, dtype=np.float32)[:, None]
    omega = (2.0 * math.pi * np.arange(n, dtype=np.float32)[:, None] / n).astype(np.float32)
    bands = np.linspace(1e-4, 15, 16, dtype=np.float32)[None, :]
    z = np.concatenate([t, np.cos(bands * omega), -np.sin(bands * omega)], axis=-1).astype(np.float32)
    negt = np.broadcast_to(-t[:, 0][None, :], (128, n))
    return dict(F1=F1.astype(np.float32), G=G.astype(np.float32), Winv=Winv.astype(np.float32),
                Tout=Tout.astype(np.float32), zT=np.ascontiguousarray(z.T), negt=np.ascontiguousarray(negt, dtype=np.float32))


HY_C = 32


def hyena_phase(B, l, es, seg):
    nc, S, Sx, Ux, I = B.nc, B.S, B.Sx, B.Ux, B.I
    n, base = (L, LT0) if seg == "lat" else (LC, 0)
    N, NS1, NF1, NFH = hy_dims(n)
    NT1 = NS1
    W2 = 2 * NFH
    C = HY_C
    tb = lambda k: I["hy_%s_%s" % (k, seg)]
    hp, uhp = Sx["hp_" + seg], Ux["hp_" + seg]
    NTL = [(i * 512, min(512, n - i * 512)) for i in range((n + 511) // 512)]
    MAGIC = 12582912.0

    with ExitStack() as e1:
        zT = e1.enter_context(nc.sbuf_tensor(uname("zT"), [33, n], F32))
        h1 = e1.enter_context(nc.sbuf_tensor(uname("h1"), [64, n], F32))
        h2 = e1.enter_context(nc.sbuf_tensor(uname("h2"), [64, n], F32))
        negt = e1.enter_context(nc.sbuf_tensor(uname("negt"), [128, n], F32))
        dec = e1.enter_context(nc.sbuf_tensor(uname("dec"), [128, n], F32))
        w1 = e1.enter_context(nc.sbuf_tensor(uname("w1"), [33, 64], F32))
        w2 = e1.enter_context(nc.sbuf_tensor(uname("w2"), [64, 64], F32))
        w3 = e1.enter_context(nc.sbuf_tensor(uname("w3"), [64, 2048], F32))
        fb = e1.enter_context(nc.sbuf_tensor(uname("fb"), [64, 2], F32))
        uz, uh1, uh2, unegt, udec, uw, ufb = Unit(), Unit(), Unit(), Unit(), Unit(), Unit(), Unit()
        tmp_p = TPool(nc, e1, "hyt", [128, 512], F32, 3)
        out_p = TPool(nc, e1, "hyo", [128, 2, 512], F32, 2)
        B.dma(zT[:], tb("zT"), writes=[uz])
        B.dma(negt[:], tb("negt"), writes=[unegt])
        B.dma(w1[:], I["hy_w1"][l], writes=[uw])
        B.dma(w2[:], I["hy_w2"][l], writes=[uw])
        B.dma(w3[:], I["hy_w3"][l], writes=[uw])
        S.add("dve", lambda e: e.tensor_tensor(out=fb[:, 0:1], in0=B.vs("hyf1")[:64], in1=B.vs("hyb1")[:64], op=ALU.mult),
              reads=[B.u_vec], writes=[ufb])
        S.add("dve", lambda e: e.tensor_tensor(out=fb[:, 1:2], in0=B.vs("hyf2")[:64], in1=B.vs("hyb2")[:64], op=ALU.mult),
              reads=[B.u_vec, ufb], writes=[ufb])
        for li, (wt, K, src, usrc, dst, udst, fname) in enumerate(((w1, 33, zT, uz, h1, uh1, "hyf1"), (w2, 64, h1, uh1, h2, uh2, "hyf2"))):
            for (t0, tn) in NTL:
                ps, ups = B.PS.get()
                S.add("pe", (lambda e, ps=ps, wt=wt, K=K, src=src, t0=t0, tn=tn: e.matmul(ps[:64, :tn], wt[:K, :], src[:K, t0:t0 + tn],
                                                                                          start=True, stop=True)),
                      reads=[uw, usrc], writes=[ups])
                x, ux = tmp_p.get()
                k2, uk2 = tmp_p.get()
                S.add("dve", (lambda e, ps=ps, x=x, tn=tn, li=li, fname=fname: e.tensor_scalar(
                    out=x[:64, :tn], in0=ps[:64, :tn], scalar1=B.vs(fname)[:64], scalar2=fb[:, li:li + 1], op0=ALU.mult, op1=ALU.add)),
                    reads=[ups, B.u_vec, ufb], writes=[ux])
                S.add("dve", (lambda e, x=x, k2=k2, tn=tn: e.tensor_scalar(out=k2[:64, :tn], in0=x[:64, :tn], scalar1=1.0 / (2 * math.pi),
                                                                           scalar2=MAGIC, op0=ALU.mult, op1=ALU.add)), reads=[ux], writes=[uk2])
                S.add("dve", (lambda e, k2=k2, tn=tn: e.tensor_scalar(out=k2[:64, :tn], in0=k2[:64, :tn], scalar1=-MAGIC, scalar2=-2 * math.pi,
                                                                      op0=ALU.add, op1=ALU.mult)), reads=[uk2], writes=[uk2])
                S.add("dve", (lambda e, x=x, k2=k2, tn=tn: e.tensor_tensor(out=x[:64, :tn], in0=x[:64, :tn], in1=k2[:64, :tn], op=ALU.add)),
                      reads=[ux, uk2], writes=[ux])
                S.add("act", (lambda e, x=x, dst=dst, t0=t0, tn=tn: e.activation(out=dst[:, t0:t0 + tn], in_=x[:64, :tn], func=AF.Sin)),
                      reads=[ux], writes=[udst])
        for cch in range(8):
            S.add("act", (lambda e, cch=cch: e.activation(out=dec[:], in_=negt[:], func=AF.Exp, scale=B.cs("deltas")[:, cch:cch + 1])),
                  reads=[unegt, B.u_cst], writes=[udec])
            for (t0, tn) in NTL:
                psf, upsf = B.PS.get()
                psb, upsb = B.PS.get()
                for (ps_, ups_, dr) in ((psf, upsf, 0), (psb, upsb, 1)):
                    S.add("pe", (lambda e, ps_=ps_, dr=dr, cch=cch, t0=t0, tn=tn: e.matmul(
                        ps_[:, :tn], w3[:, dr * 1024 + cch * 128:dr * 1024 + (cch + 1) * 128], h2[:, t0:t0 + tn], start=True, stop=True)),
                        reads=[uw, uh2], writes=[ups_])
                a1, ua1 = tmp_p.get()
                S.add("act", (lambda e, a1=a1, psf=psf, tn=tn: e.activation(out=a1[:, :tn], in_=psf[:, :tn], func=AF.Copy)), reads=[upsf], writes=[ua1])
                o, uo = out_p.get()
                for k_, op_ in ((0, ALU.add), (1, ALU.subtract)):
                    S.add("dve", (lambda e, o=o, a1=a1, psb=psb, tn=tn, k_=k_, op_=op_: e.tensor_tensor(out=o[:, k_, :tn], in0=a1[:, :tn], in1=psb[:, :tn], op=op_)),
                          reads=[ua1, upsb, uo], writes=[uo])
                    S.add("pool", (lambda e, o=o, tn=tn, t0=t0, k_=k_: e.tensor_tensor(out=o[:, k_, :tn], in0=o[:, k_, :tn], in1=dec[:, t0:t0 + tn], op=ALU.mult)),
                          reads=[uo, udec], writes=[uo])
                B.dma(hp[:, cch * 128:(cch + 1) * 128, t0:t0 + tn].rearrange("k c t -> c k t"), o[:, :, :tn], reads=[uo], writes=[uhp])
        S.barrier()
    if B.stop and B.stop.endswith("hyf"):
        return

    with ExitStack() as e2:
        cvt_pool = TPool(nc, e2, "cvtst", [128, 1024], F32, 2)

        def cvt(name, shape, src_ap):
            t = e2.enter_context(nc.sbuf_tensor(uname(name), shape, F32R))
            ut = Unit()
            flat = 1
            for s_ in shape[1:]:
                flat *= s_
            step = 1024
            st_pool = cvt_pool
            tf = t[:].rearrange(" ".join(["p"] + ["a%d" % i for i in range(len(shape) - 1)]) + " -> p (" + " ".join("a%d" % i for i in range(len(shape) - 1)) + ")") if len(shape) > 2 else t[:]
            sf = src_ap.rearrange(" ".join(["p"] + ["a%d" % i for i in range(len(shape) - 1)]) + " -> p (" + " ".join("a%d" % i for i in range(len(shape) - 1)) + ")") if len(shape) > 2 else src_ap
            P_ = shape[0]
            for o_ in range(0, flat, step):
                w_ = min(step, flat - o_)
                st, ust = st_pool.get()
                B.dma(st[:P_, :w_], sf[:, o_:o_ + w_], writes=[ust])
                S.add("act", (lambda e, st=st, o_=o_, w_=w_: e.activation(out=tf[:, o_:o_ + w_], in_=st[:P_, :w_], func=AF.Copy)),
                      reads=[ust], writes=[ut])
            return t, ut
        F1 = e2.enter_context(nc.sbuf_tensor(uname("F1"), [NS1, W2], F32))
        uF1 = Unit()
        B.dma(F1[:], tb("F1"), writes=[uF1])
        G, uG = cvt("G", [128, NFH, 2, 128], tb("G"))
        Wv, uWv = cvt("Wv", [128, 2, 256], I["hy_Winv"])
        To, uTo = cvt("To", [NFH, 128, 2, NT1], tb("Tout"))
        xin_p = TPool(nc, e2, "xin", [NS1, C, 128], F32, 1)
        Yb = e2.enter_context(nc.sbuf_tensor(uname("Yb"), [128, NFH, 3, C], F32R))
        Kb = e2.enter_context(nc.sbuf_tensor(uname("Kb"), [128, NFH, 2, C], F32))
        XP = e2.enter_context(nc.sbuf_tensor(uname("XP"), [128, NFH, 2, C], F32R))
        Vb = e2.enter_context(nc.sbuf_tensor(uname("Vb"), [NFH, 128, 2, C], F32R))
        yb = e2.enter_context(nc.sbuf_tensor(uname("yb"), [C, n], F32))
        uYb, uKb, uXP, uVb, uyb = Unit(), Unit(), Unit(), Unit(), Unit()
        tp = TPool(nc, e2, "hytp", [128, 8, C], F32, 4)
        xz_p = TPool(nc, e2, "hyxz", [C, 2, 1024], F32, 1)
        ob_p = TPool(nc, e2, "hyob", [C, 1024], BF16, 2)
        NPB = 512 // W2
        XP2 = e2.enter_context(nc.sbuf_tensor(uname("XP2"), [128, NFH, 2, C], F32R))
        uXP2 = Unit()
        XPS = [(XP, uXP), (XP2, uXP2)]

        def fwd(g):
            XP, uXP = XPS[g % 2]
            c0 = g * C
            for sig in ("p", "m", "zz"):
                xin, uxin = xin_p.get()
                if sig == "zz":
                    src, usrc = Sx["zzT"][c0:c0 + C, base:base + n], Ux["zzT"]
                else:
                    src, usrc = hp[0 if sig == "p" else 1, c0:c0 + C, :], uhp
                B.dma(xin[:], src.rearrange("c (a b) -> a c b", b=128), reads=[usrc], writes=[uxin])
                for cb in range(0, C, NPB):
                    nb = min(NPB, C - cb)
                    ps, ups = B.PS.get()
                    for ci in range(nb):
                        S.add("pe", (lambda e, ps=ps, xin=xin, ci=ci, cb=cb: e.matmul(ps[:, ci * W2:(ci + 1) * W2], xin[:, cb + ci, :], F1[:, :],
                                                                                      start=True, stop=True)),
                              reads=[uxin, uF1], writes=[ups])
                    pv = ps[:, :nb * W2].rearrange("p (c r f) -> p c r f", c=nb, r=2)
                    for r_ in range(2):
                        S.add("act" if r_ == 0 else "dve",
                              (lambda e, pv=pv, r_=r_, cb=cb, nb=nb: (e.activation(out=Yb[:, :, r_, cb:cb + nb], in_=pv[:, :, r_, :].rearrange("p c f -> p f c"), func=AF.Copy)
                                                                       if r_ == 0 else
                                                                       e.tensor_copy(out=Yb[:, :, r_, cb:cb + nb], in_=pv[:, :, r_, :].rearrange("p c f -> p f c")))),
                              reads=[ups], writes=[uYb])
                    S.add("act", (lambda e, pv=pv, cb=cb, nb=nb: e.activation(out=Yb[:, :, 2, cb:cb + nb], in_=pv[:, :, 1, :].rearrange("p c f -> p f c"),
                                                                              func=AF.Copy, scale=-1.0)), reads=[ups], writes=[uYb])
                    yield
                for fq in range(0, NFH, 8):
                    nf = min(8, NFH - fq)
                    ps, ups = B.PS.get()
                    for fi in range(nf):
                        f1_ = fq + fi
                        if sig != "m":
                            S.add("pe", (lambda e, ps=ps, fi=fi, f1_=f1_: e.matmul(ps[:, (fi * 2) * C:(fi * 2 + 1) * C], G[:, f1_, 0, :], Yb[:, f1_, 0, :], start=True, stop=False)),
                                  reads=[uG, uYb], writes=[ups])
                            S.add("pe", (lambda e, ps=ps, fi=fi, f1_=f1_: e.matmul(ps[:, (fi * 2) * C:(fi * 2 + 1) * C], G[:, f1_, 1, :], Yb[:, f1_, 2, :], start=False, stop=True)),
                                  reads=[uG, uYb], writes=[ups])
                        if sig != "p":
                            S.add("pe", (lambda e, ps=ps, fi=fi, f1_=f1_: e.matmul(ps[:, (fi * 2 + 1) * C:(fi * 2 + 2) * C], G[:, f1_, 1, :], Yb[:, f1_, 0, :], start=True, stop=False)),
                                  reads=[uG, uYb], writes=[ups])
                            S.add("pe", (lambda e, ps=ps, fi=fi, f1_=f1_: e.matmul(ps[:, (fi * 2 + 1) * C:(fi * 2 + 2) * C], G[:, f1_, 0, :], Yb[:, f1_, 1, :], start=False, stop=True)),
                                  reads=[uG, uYb], writes=[ups])
                    pv = ps[:, :nf * 2 * C].rearrange("p (f r c) -> p f r c", f=nf, r=2)
                    if sig == "p":
                        S.add("act", (lambda e, pv=pv, fq=fq, nf=nf: e.activation(out=Kb[:, fq:fq + nf, 0, :], in_=pv[:, :, 0, :], func=AF.Copy)), reads=[ups], writes=[uKb])
                    elif sig == "m":
                        S.add("act", (lambda e, pv=pv, fq=fq, nf=nf: e.activation(out=Kb[:, fq:fq + nf, 1, :], in_=pv[:, :, 1, :], func=AF.Copy)), reads=[ups], writes=[uKb])
                    else:
                        (t1, u1), (t2, u2), (t3, u3), (t4, u4) = tp.get(), tp.get(), tp.get(), tp.get()
                        kr, ki = Kb[:, fq:fq + nf, 0, :], Kb[:, fq:fq + nf, 1, :]
                        S.add("dve", (lambda e, pv=pv, t1=t1, nf=nf, kr=kr: e.tensor_tensor(out=t1[:, :nf, :], in0=pv[:, :, 0, :], in1=kr, op=ALU.mult)), reads=[ups, uKb], writes=[u1])
                        S.add("dve", (lambda e, pv=pv, t2=t2, nf=nf, ki=ki: e.tensor_tensor(out=t2[:, :nf, :], in0=pv[:, :, 1, :], in1=ki, op=ALU.mult)), reads=[ups, uKb], writes=[u2])
                        S.add("dve", (lambda e, pv=pv, t3=t3, nf=nf, ki=ki: e.tensor_tensor(out=t3[:, :nf, :], in0=pv[:, :, 0, :], in1=ki, op=ALU.mult)), reads=[ups, uKb], writes=[u3])
                        S.add("dve", (lambda e, pv=pv, t4=t4, nf=nf, kr=kr: e.tensor_tensor(out=t4[:, :nf, :], in0=pv[:, :, 1, :], in1=kr, op=ALU.mult)), reads=[ups, uKb], writes=[u4])
                        S.add("dve", (lambda e, t1=t1, t2=t2, fq=fq, nf=nf: e.tensor_tensor(out=XP[:, fq:fq + nf, 0, :], in0=t1[:, :nf, :], in1=t2[:, :nf, :], op=ALU.subtract)),
                              reads=[u1, u2], writes=[uXP])
                        S.add("dve", (lambda e, t3=t3, t4=t4, fq=fq, nf=nf: e.tensor_tensor(out=XP[:, fq:fq + nf, 1, :], in0=t3[:, :nf, :], in1=t4[:, :nf, :], op=ALU.add)),
                              reads=[u3, u4], writes=[uXP])
                    yield

        def inv(g):
            XP, uXP = XPS[g % 2]
            c0 = g * C
            for cb in range(0, C, 2):
                ps, ups = B.PS.get()
                for ci in range(2):
                    c_ = cb + ci
                    S.add("pe", (lambda e, ps=ps, ci=ci, c_=c_: e.matmul(ps[:NFH, ci * 256:(ci + 1) * 256], XP[:, :, 0, c_], Wv[:, 0, :], start=True, stop=False)),
                          reads=[uXP, uWv], writes=[ups])
                    S.add("pe", (lambda e, ps=ps, ci=ci, c_=c_: e.matmul(ps[:NFH, ci * 256:(ci + 1) * 256], XP[:, :, 1, c_], Wv[:, 1, :], start=False, stop=True)),
                          reads=[uXP, uWv], writes=[ups])
                pv = ps[:NFH, :].rearrange("p (c r t) -> p c r t", c=2, r=2)
                S.add("act" if (cb // 2) % 2 == 0 else "dve",
                      (lambda e, pv=pv, cb=cb: (e.activation(out=Vb[:, :, :, cb:cb + 2], in_=pv.rearrange("p c r t -> p t r c"), func=AF.Copy)
                                                if (cb // 2) % 2 == 0 else e.tensor_copy(out=Vb[:, :, :, cb:cb + 2], in_=pv.rearrange("p c r t -> p t r c")))),
                      reads=[ups], writes=[uVb])
                yield
            TB = 512 // NT1 if NT1 * 16 > 512 else 16
            for tq in range(0, 128, TB):
                ps, ups = B.PS.get()
                for ti in range(TB):
                    t2_ = tq + ti
                    S.add("pe", (lambda e, ps=ps, ti=ti, t2_=t2_: e.matmul(ps[:C, ti * NT1:(ti + 1) * NT1], Vb[:, t2_, 0, :], To[:, t2_, 0, :], start=True, stop=False)),
                          reads=[uVb, uTo], writes=[ups])
                    S.add("pe", (lambda e, ps=ps, ti=ti, t2_=t2_: e.matmul(ps[:C, ti * NT1:(ti + 1) * NT1], Vb[:, t2_, 1, :], To[:, t2_, 1, :], start=False, stop=True)),
                          reads=[uVb, uTo], writes=[ups])
                S.add("act", (lambda e, ps=ps, tq=tq: e.activation(out=yb[:, :].rearrange("c (a b) -> c a b", b=128)[:, :, tq:tq + TB],
                                                                   in_=ps[:C, :TB * NT1].rearrange("c (b a) -> c a b", a=NT1), func=AF.Copy)),
                      reads=[ups], writes=[uyb])
                yield
            for q0 in range(0, n, 1024):
                qn = min(1024, n - q0)
                xz, uxz = xz_p.get()
                B.dma(xz[:, 0, :qn], Sx["x0T"][c0:c0 + C, base + q0:base + q0 + qn], reads=[Ux["x0T"]], writes=[uxz])
                B.dma(xz[:, 1, :qn], Sx["zzT"][c0:c0 + C, base + q0:base + q0 + qn], reads=[Ux["zzT"]], writes=[uxz])
                S.add("dve", (lambda e, xz=xz, q0=q0, qn=qn, c0=c0: e.scalar_tensor_tensor(out=xz[:, 1, :qn], in0=xz[:, 1, :qn], scalar=B.vs("hyb32")[:C, c0 // C:c0 // C + 1],
                                                                                      in1=yb[:, q0:q0 + qn], op0=ALU.mult, op1=ALU.add)),
                      reads=[uxz, uyb, B.u_vec], writes=[uxz])
                ob, uob = ob_p.get()
                S.add("dve", (lambda e, xz=xz, ob=ob, qn=qn: e.tensor_tensor(out=ob[:, :qn], in0=xz[:, 0, :qn], in1=xz[:, 1, :qn], op=ALU.mult)),
                      reads=[uxz], writes=[uob])
                B.dma(Sx["ohy"][c0:c0 + C, base + q0:base + q0 + qn], ob[:, :qn], reads=[uob], writes=[Ux["ohy"]])
                yield

        NG = D // C
        if B.stop and B.stop.endswith("hy1"):
            NG = 2
        for g in range(NG + 1):
            gens = []
            if g < NG:
                gens.append(fwd(g))
            if g >= 1:
                gens.append(inv(g - 1))
            while gens:
                nxt = []
                for gen in gens:
                    try:
                        next(gen)
                        nxt.append(gen)
                    except StopIteration:
                        pass
                gens = nxt
        S.barrier()


def load_wres(B, es, name, src, kparts, ncols, st_pool, eng="pool"):
    nc, S = B.nc, B.S
    w = es.enter_context(nc.sbuf_tensor(uname(name), [128, kparts, ncols], BF16))
    uw = Unit()
    srcv = src.rearrange("(k p) n -> p k n", p=128)
    for k0 in range(0, kparts, 4):
        kn = min(4, kparts - k0)
        for c0 in range(0, ncols, 512):
            st, ust = st_pool.get()
            B.dma(st[:, :kn, :], srcv[:, k0:k0 + kn, c0:c0 + 512], writes=[ust])
            S.add(eng, (lambda e, st=st, k0=k0, kn=kn, c0=c0: e.tensor_copy(out=w[:, k0:k0 + kn, c0:c0 + 512], in_=st[:, :kn, :])),
                  reads=[ust], writes=[uw])
    return w, uw


def merge_phase(B, l, es, xin, u_xin, tiles):
    nc, S, Sx, Ux, I = B.nc, B.S, B.Sx, B.Ux, B.I
    st_pool = TPool(nc, es, "mst", [128, 4, 512], F32, 2)
    Ws = []
    for nm in ("w_proj_dn", "w_proj_hy", "w_proj_lru", "w_out"):
        Ws.append(load_wres(B, es, nm + "_sb", I[nm][l], 8, D, st_pool))
    ot_p = TPool(nc, es, "mot", [128, 3, 8, 512], BF16, 1)
    gt_p = TPool(nc, es, "mgt", [128, 24, 512], BF16, 1)
    x_p = TPool(nc, es, "mx", [128, 8, 512], F32, 2)
    mg_p = TPool(nc, es, "mmg", [128, 8, 512], BF16, 1)
    t_p = TPool(nc, es, "mt", [128, 512], F32, 4)
    for (u0, n) in tiles:
        seg = 1 if u0 == 0 else 0
        ot, uot = ot_p.get()
        for bi, nm in enumerate(("odn", "ohy", "olru")):
            B.dma(ot[:, bi, :, :n], Sx[nm].rearrange("(k p) u -> p k u", p=128)[:, :, u0:u0 + n], reads=[Ux[nm]], writes=[uot])
        gt, ugt = gt_p.get()
        B.dma(gt[:, :, :n], Sx["gT"].rearrange("(k p) u -> p k u", p=128)[:, :, u0:u0 + n], reads=[Ux["gT"]], writes=[ugt])
        x, ux = x_p.get()
        B.dma(x[:, :, :n], xin.rearrange("(k p) u -> p k u", p=128)[:, :, u0:u0 + n], reads=[u_xin], writes=[ux])
        mg, umg = mg_p.get()
        for m in range(8):
            pss = []
            for bi in range(3):
                ps, ups = B.PS.get()
                w, uw = Ws[bi]
                for k in range(8):
                    S.add("pe", (lambda e, ps=ps, w=w, k=k, m=m, bi=bi, ot=ot, n=n: e.matmul(ps[:, :n], w[:, k, m * 128:(m + 1) * 128], ot[:, bi, k, :n],
                                                                                           start=(k == 0), stop=(k == 7))),
                          reads=[uw, uot], writes=[ups])
                pss.append((ps, ups))
            ta, uta = t_p.get()
            tb_, utb = t_p.get()
            S.add("dve", (lambda e, ta=ta, ps=pss[0][0], gt=gt, m=m, n=n: e.tensor_tensor(out=ta[:, :n], in0=ps[:, :n], in1=gt[:, m, :n], op=ALU.mult)),
                  reads=[pss[0][1], ugt], writes=[uta])
            S.add("dve", (lambda e, tb_=tb_, ps=pss[1][0], gt=gt, m=m, n=n: e.tensor_tensor(out=tb_[:, :n], in0=ps[:, :n], in1=gt[:, 8 + m, :n], op=ALU.mult)),
                  reads=[pss[1][1], ugt], writes=[utb])
            S.add("pool", (lambda e, ta=ta, tb_=tb_, n=n: e.tensor_tensor(out=ta[:, :n], in0=ta[:, :n], in1=tb_[:, :n], op=ALU.add)),
                  reads=[uta, utb], writes=[uta])
            tc_, utc = t_p.get()
            S.add("dve", (lambda e, tc_=tc_, ps=pss[2][0], gt=gt, m=m, n=n: e.tensor_tensor(out=tc_[:, :n], in0=ps[:, :n], in1=gt[:, 16 + m, :n], op=ALU.mult)),
                  reads=[pss[2][1], ugt], writes=[utc])
            S.add("pool", (lambda e, ta=ta, tc_=tc_, mg=mg, m=m, n=n: e.tensor_tensor(out=mg[:, m, :n], in0=ta[:, :n], in1=tc_[:, :n], op=ALU.add)),
                  reads=[uta, utc, umg], writes=[umg])
        w, uw = Ws[3]
        for nn in range(8):
            ps, ups = B.PS.get()
            for m in range(8):
                S.add("pe", (lambda e, ps=ps, m=m, nn=nn, mg=mg, n=n: e.matmul(ps[:, :n], w[:, m, nn * 128:(nn + 1) * 128], mg[:, m, :n],
                                                                              start=(m == 0), stop=(m == 7))),
                      reads=[uw, umg], writes=[ups])
            S.add("dve", (lambda e, ps=ps, x=x, nn=nn, n=n, seg=seg: e.scalar_tensor_tensor(out=x[:, nn, :n], in0=ps[:, :n], scalar=B.modcol(2, seg, nn),
                                                                                       in1=x[:, nn, :n], op0=ALU.mult, op1=ALU.add)),
                  reads=[ups, ux, B.u_modv], writes=[ux])
        B.dma(Sx["xA"].rearrange("(k p) u -> p k u", p=128)[:, :, u0:u0 + n], x[:, :, :n], reads=[ux], writes=[Ux["xA"]])


def ffn_phase(B, l, es, tiles):
    nc, S, Sx, Ux, I = B.nc, B.S, B.Sx, B.Ux, B.I
    with_ctx = tiles[0][0] == 0
    esu = ExitStack()
    hT = esu.enter_context(nc.sbuf_tensor(uname("hT2"), [128, 8, U], BF16))
    u_hT = [Unit() for _ in TT]
    with ExitStack() as es3:
        B.norm_to_hT(Sx["xA"], Ux["xA"], 1, hT, u_hT, tiles, es3, tile_ids=[TT.index(t) for t in tiles])
        S.barrier()
    with ExitStack() as es4:
        wst_pool = TPool(nc, es4, "fwst", [128, 8, 128], F32, 2)
        wbf_pool = TPool(nc, es4, "fwbf", [128, 8, 128], BF16, 2)
        up_p = TPool(nc, es4, "fup", [128, 66, 66], F32, 2)
        cp_p = TPool(nc, es4, "fcp", [128, 260], F32, 2)
        acc_p = TPool(nc, es4, "facc", [128, U], F32, 2)
        ab_p = TPool(nc, es4, "fab", [128, U], BF16, 2)
        for (t, ut) in up_p.t:
            S.add("pool", (lambda e, t=t: e.memset(t[:], 0.0)), writes=[ut])
        for (t, ut) in cp_p.t:
            S.add("pool", (lambda e, t=t: e.memset(t[:], 0.0)), writes=[ut])
        for j in range(FH // 128):
            accs = []
            for part in range(2):
                cidx = part * (FH // 128) + j
                wb, uwb = B.load_w_bf16(I["ffn_up"][l][:, cidx * 128:(cidx + 1) * 128], 128, wst_pool, wbf_pool)
                up, uup = up_p.get()
                cp, ucp = cp_p.get()
                for (u0, n) in tiles:
                    ti = TT.index((u0, n))
                    ps, ups = B.mm_tile(wb, uwb, 128, hT, u_hT, ti)
                    if u0 == 0:
                        S.add("act", (lambda e, ps=ps, cp=cp, n=n: e.activation(out=cp[:, 1:1 + n], in_=ps[:, :n], func=AF.Copy)), reads=[ups], writes=[ucp])
                    else:
                        r0 = (u0 - LT0) // 64
                        S.add("act", (lambda e, ps=ps, up=up, r0=r0: e.activation(out=up[:, 1 + r0:9 + r0, 1:65], in_=ps[:, :512].rearrange("p (r c) -> p r c", c=64),
                                                                                  func=AF.Copy)), reads=[ups], writes=[uup])
                acc, uacc = acc_p.get()
                cw = lambda tap, cidx=cidx: B.vs("ffncw", cidx * 9 + tap)
                av = acc[:, LT0:U].rearrange("p (r c) -> p r c", c=64)
                S.add("act", (lambda e, up=up, av=av, cw=cw: e.activation(out=av, in_=up[:, 0:64, 0:64], func=AF.Identity, scale=cw(0))),
                      reads=[uup, B.u_vec], writes=[uacc])
                for tap in range(1, 9):
                    di, dj = tap // 3, tap % 3
                    S.add("dve", (lambda e, up=up, av=av, cw=cw, tap=tap, di=di, dj=dj: e.scalar_tensor_tensor(
                        out=av, in0=up[:, di:di + 64, dj:dj + 64], scalar=cw(tap), in1=av, op0=ALU.mult, op1=ALU.add)),
                        reads=[uup, uacc, B.u_vec], writes=[uacc])
                if with_ctx:
                    S.add("act", (lambda e, cp=cp, acc=acc, cw=cw: e.activation(out=acc[:, 0:256], in_=cp[:, 0:256], func=AF.Identity, scale=cw(3))),
                          reads=[ucp, B.u_vec, uacc], writes=[uacc])
                    for tap in (4, 5):
                        S.add("dve", (lambda e, cp=cp, acc=acc, cw=cw, tap=tap: e.scalar_tensor_tensor(
                            out=acc[:, 0:256], in0=cp[:, tap - 3:tap - 3 + 256], scalar=cw(tap), in1=acc[:, 0:256], op0=ALU.mult, op1=ALU.add)),
                            reads=[ucp, uacc, B.u_vec], writes=[uacc])
                accs.append((acc, uacc))
            (ag, uag), (av_, uav) = accs
            lo = 0 if with_ctx else LT0
            S.add("act", (lambda e, ag=ag, lo=lo: e.activation(out=ag[:, lo:U], in_=ag[:, lo:U], func=AF.Silu)), reads=[uag], writes=[uag])
            ab, uab = ab_p.get()
            S.add("pool", (lambda e, ab=ab: e.memset(ab[:, 0:LT0], 0.0)), writes=[uab])
            S.add("dve", (lambda e, ag=ag, av_=av_, ab=ab, lo=lo: e.tensor_tensor(out=ab[:, lo:U], in0=ag[:, lo:U], in1=av_[:, lo:U], op=ALU.mult)),
                  reads=[uag, uav, uab], writes=[uab])
            B.dma(Sx["actT"][j * 128:(j + 1) * 128], ab[:], reads=[uab], writes=[Ux["actT"]])
        S.barrier()
    esu.close()
    st_pool = TPool(nc, es, "dst", [128, 4, 512], F32, 2)
    wd, uwd = load_wres(B, es, "wdown", I["ffn_down"][l], FH // 128, D, st_pool)
    at_p = TPool(nc, es, "dat", [128, FH // 128, 512], BF16, 2)
    x_p = TPool(nc, es, "dx", [128, 8, 512], F32, 2)
    for (u0, n) in tiles:
        seg = 1 if u0 == 0 else 0
        at, uat = at_p.get()
        B.dma(at[:, :, :n], Sx["actT"].rearrange("(k p) u -> p k u", p=128)[:, :, u0:u0 + n], reads=[Ux["actT"]], writes=[uat])
        x, ux = x_p.get()
        B.dma(x[:, :, :n], Sx["xA"].rearrange("(k p) u -> p k u", p=128)[:, :, u0:u0 + n], reads=[Ux["xA"]], writes=[ux])
        for nn in range(8):
            ps, ups = B.PS.get()
            for j in range(FH // 128):
                S.add("pe", (lambda e, ps=ps, j=j, nn=nn, at=at, n=n: e.matmul(ps[:, :n], wd[:, j, nn * 128:(nn + 1) * 128], at[:, j, :n],
                                                                              start=(j == 0), stop=(j == FH // 128 - 1))),
                      reads=[uwd, uat], writes=[ups])
            S.add("dve", (lambda e, ps=ps, x=x, nn=nn, n=n, seg=seg: e.scalar_tensor_tensor(out=x[:, nn, :n], in0=ps[:, :n], scalar=B.modcol(5, seg, nn),
                                                                                       in1=x[:, nn, :n], op0=ALU.mult, op1=ALU.add)),
                  reads=[ups, ux, B.u_modv], writes=[ux])
        B.dma(Sx["xB"].rearrange("(k p) u -> p k u", p=128)[:, :, u0:u0 + n], x[:, :, :n], reads=[ux], writes=[Ux["xB"]])


def final_phase(B, es):
    nc, S, Sx, Ux = B.nc, B.S, B.Sx, B.Ux
    xp = TPool(nc, es, "fx", [128, 8, 512], F32, 2)
    sqp = TPool(nc, es, "fsq", [128, 8, 512], F32R, 1)
    rsp = TPool(nc, es, "frs", [128, 512], F32, 2)
    for (u0, n) in TT[1:]:
        x, ux = xp.get()
        B.dma(x[:], Sx["xB"].rearrange("(k p) u -> p k u", p=128)[:, :, u0:u0 + n], reads=[Ux["xB"]], writes=[ux])
        sq, usq = sqp.get()
        S.add("act", (lambda e, x=x, sq=sq: e.activation(out=sq[:], in_=x[:], func=AF.Square)), reads=[ux], writes=[usq])
        ps, ups = B.PS.get()
        for k in range(8):
            S.add("pe", (lambda e, k=k, sq=sq, ps=ps: e.matmul(ps[:], B.ones[:], sq[:, k, :], start=(k == 0), stop=(k == 7))),
                  reads=[usq, B.u_ones], writes=[ups])
        rs, urs = rsp.get()
        S.add("act", (lambda e, rs=rs, ps=ps: e.activation(out=rs[:], in_=ps[:], func=AF.Sqrt, scale=1.0 / D, bias=B.eps6[:])), reads=[ups], writes=[urs])
        S.add("dve", (lambda e, rs=rs: e.reciprocal(out=rs[:], in_=rs[:])), reads=[urs], writes=[urs])
        S.add("dve", (lambda e, x=x, rs=rs: e.tensor_tensor(out=x[:], in0=x[:], in1=rs[:].unsqueeze(1).to_broadcast([128, 8, 512]), op=ALU.mult)),
              reads=[urs, ux], writes=[ux])
        S.add("dve", (lambda e, x=x: e.tensor_tensor(out=x[:], in0=x[:], in1=B.vs("fng").unsqueeze(2).to_broadcast([128, 8, 512]), op=ALU.mult)),
              reads=[ux, B.u_vec], writes=[ux])
        t0 = u0 - LT0
        B.dma(B.outT.rearrange("(k p) t -> p k t", p=128)[:, :, t0:t0 + n], x[:], reads=[ux])


_CACHE = {}


def kernel(**inputs):
    inp = {k: np.asarray(v) for k, v in inputs.items()}
    if "nc" not in _CACHE:
        b = Builder(debug=False)
        _CACHE["nc"] = b.build()
    nc = _CACHE["nc"]
    bsz = inp["x"].shape[0]
    in_maps = [prep_inputs(inp, b_) for b_ in range(bsz)]
    res = run_bass_kernel_spmd(nc, in_maps, core_ids=list(range(bsz)))
    out = np.stack([np.ascontiguousarray(np.asarray(r["outT"]).T) for r in res.results])
    return out.astype(np.float32)
```

```python
import math
from contextlib import ExitStack

import numpy as np
import ml_dtypes
import concourse.bass as bass
import concourse.mybir as mybir
from concourse.bass_utils import run_bass_kernel_spmd

F32 = mybir.dt.float32
F32R = mybir.dt.float32r
BF16 = mybir.dt.bfloat16
ALU = mybir.AluOpType
AF = mybir.ActivationFunctionType
AX = mybir.AxisListType

D = 1024
L = 4096
LC = 256
U = 4355
LT0 = 259
UP = 4360
DEPTH = 2
NIN = 12320
OFF_QKV, OFF_Z, OFF_AB, OFF_HY, OFF_LX, OFF_LY, OFF_GATE = 0, 3072, 4096, 4128, 7200, 8224, 9248
FH = 2816
TT = [(0, 256)] + [(LT0 + 512 * i, 512) for i in range(8)]
NCORES = 4


class Unit:
    __slots__ = ("w", "r", "excl", "wd")

    def __init__(self, excl=False):
        self.w = None
        self.r = []
        self.wd = []
        self.excl = excl


class Op:
    __slots__ = ("eng", "fn", "deps", "dma", "need_inc", "sem", "val")

    def __init__(self, eng, fn, dma):
        self.eng = eng
        self.fn = fn
        self.dma = dma
        self.deps = []
        self.need_inc = False
        self.sem = None
        self.val = 0


class Sched:
    EPOCH = 30000
    NDMA = 24

    def __init__(self, nc, es):
        self.nc = nc
        self.es = es
        self.ops = []
        self.engs = {"pe": nc.tensor, "act": nc.scalar, "dve": nc.vector,
                     "pool": nc.gpsimd, "sp": nc.sync}
        self.last = {k: None for k in self.engs}
        self.dmas_since_barrier = []
        self.bar_deps = {k: [] for k in self.engs}
        self.nsem = 0

    def add(self, eng, fn, reads=(), writes=(), dma=False):
        op = Op(eng, fn, dma)
        deps = []
        for u in reads:
            if u.w is not None:
                deps.append(u.w)
            deps.extend(u.wd)
            if u.excl:
                deps.extend(o for o in u.r if o.eng != eng)
        for u in writes:
            if u.w is not None:
                deps.append(u.w)
            deps.extend(u.wd)
            deps.extend(u.r)
        if self.bar_deps[eng]:
            deps.extend(self.bar_deps[eng])
            self.bar_deps[eng] = []
        seen = set()
        for d in deps:
            if d is op or id(d) in seen:
                continue
            if d.eng == "pe" and eng == "pe" and not d.dma and not dma:
                continue
            seen.add(id(d))
            d.need_inc = True
            op.deps.append(d)
        for u in reads:
            if not dma:
                u.r = [o for o in u.r if o.dma or o.eng != eng]
            u.r.append(op)
        for u in writes:
            u.w = op
            u.r = []
            if dma:
                u.wd.append(op)
                if len(u.wd) > 48:
                    u.wd = u.wd[-48:]
            else:
                u.wd = []
        if dma:
            op.need_inc = True
            self.dmas_since_barrier.append(op)
        self.ops.append(op)
        self.last[eng] = op
        return op

    def barrier(self):
        deps = [o for o in self.last.values() if o is not None] + self.dmas_since_barrier
        self.dmas_since_barrier = []
        for k in self.engs:
            self.bar_deps[k] = list(deps)

    def _newsem(self):
        self.nsem += 1
        return self.es.enter_context(self.nc.semaphore("s%d" % self.nsem))

    def emit(self):
        self.barrier()
        self.add("sp", lambda e: None)
        esem, ecount = {}, {}
        dsem = [self._newsem() for _ in range(self.NDMA)]
        dcount = [0] * self.NDMA
        nd = 0
        seen = {k: {} for k in self.engs}
        for op in self.ops:
            e = self.engs[op.eng]
            waits = []
            if op.dma:
                j = nd % self.NDMA
                nd += 1
                if dcount[j]:
                    waits.append((dsem[j], dcount[j]))
                if dcount[j] >= 30000:
                    dsem[j] = self._newsem()
                    dcount[j] = 0
                dcount[j] += 16
                op.sem, op.val = dsem[j], dcount[j]
            elif op.need_inc:
                if op.eng not in esem or ecount[op.eng] >= self.EPOCH:
                    esem[op.eng] = self._newsem()
                    ecount[op.eng] = 0
                ecount[op.eng] += 1
                op.sem, op.val = esem[op.eng], ecount[op.eng]
            for d in op.deps:
                waits.append((d.sem, d.val))
            sn = seen[op.eng]
            for (s, v) in waits:
                if sn.get(id(s), 0) >= v:
                    continue
                sn[id(s)] = v
                e.wait_ge(s, v)
            ins = op.fn(e)
            if op.sem is not None and ins is not None:
                ins.then_inc(op.sem, 16 if op.dma else 1)
        return len(self.ops)


_UID = [0]


def uname(name):
    _UID[0] += 1
    return "%s_%d" % (name, _UID[0])


class TPool:
    def __init__(self, nc, es, name, shape, dtype, n, psum=False):
        self.t = []
        for i in range(n):
            mk = nc.psum_tensor if psum else nc.sbuf_tensor
            self.t.append((es.enter_context(mk(uname(name), shape, dtype)), Unit(excl=psum)))
        self.i = 0

    def get(self):
        r = self.t[self.i % len(self.t)]
        self.i += 1
        return r


def _pk(v):
    return np.ascontiguousarray(v.reshape(-1, 128).T)


VEC_FIELDS = [("g1", 8), ("g2", 8), ("bmod", 48), ("dncw", 96), ("hycw", 72), ("hycb", 24),
              ("lrucw", 32), ("lrucb", 8), ("lba", 16), ("lbx", 16), ("llam", 16), ("hybias", 8),
              ("ffncw", 396), ("fng", 8), ("dnng", 1), ("alog", 1), ("dtb", 1),
              ("hyb1", 1), ("hyf1", 1), ("hyb2", 1), ("hyf2", 1), ("hyb32", 32)]
VOFF = {}
_o = 0
for _n, _w in VEC_FIELDS:
    VOFF[_n] = (_o, _w)
    _o += _w
NV = _o


def build_vec(inp, l):
    v = np.zeros((128, NV), np.float32)

    def put(name, arr):
        o, w = VOFF[name]
        v[:arr.shape[0], o:o + w] = arr.reshape(arr.shape[0], w)
    put("g1", _pk(inp["norm1_g"][l]))
    put("g2", _pk(inp["norm2_g"][l]))
    put("bmod", _pk(inp["b_mod"][l]))
    put("dncw", inp["dn_conv_w"][l].reshape(4, 24, 128).transpose(2, 1, 0))
    put("hycw", inp["hy_conv_w"][l].reshape(3, 24, 128).transpose(2, 1, 0))
    put("hycb", _pk(inp["hy_conv_b"][l]))
    put("lrucw", inp["lru_conv_w"][l].reshape(4, 8, 128).transpose(2, 1, 0))
    put("lrucb", _pk(inp["lru_conv_b"][l]))
    put("lba", inp["lru_b_a"][l].reshape(2, 8, 128).transpose(2, 0, 1))
    put("lbx", inp["lru_b_x"][l].reshape(2, 8, 128).transpose(2, 0, 1))
    put("llam", inp["lru_lambda"][l].reshape(2, 8, 128).transpose(2, 0, 1))
    put("hybias", _pk(inp["hy_bias"][l]))
    put("ffncw", inp["ffn_conv_w"][l].reshape(9, 44, 128).transpose(2, 1, 0))
    put("fng", _pk(inp["final_norm_g"]))
    put("dnng", inp["dn_norm_g"][l].reshape(128, 1))
    al = np.zeros((40, 1), np.float32)
    db = np.zeros((40, 1), np.float32)
    for d in range(2):
        al[d * 32:d * 32 + 8, 0] = inp["dn_a_log"][l][d]
        db[d * 32:d * 32 + 8, 0] = inp["dn_dt_bias"][l][d]
    put("alog", al)
    put("dtb", db)
    put("hyb1", inp["hy_b1"][l].reshape(64, 1))
    put("hyf1", inp["hy_f1"][l].reshape(64, 1))
    put("hyb2", inp["hy_b2"][l].reshape(64, 1))
    put("hyf2", inp["hy_f2"][l].reshape(64, 1))
    put("hyb32", np.ascontiguousarray(inp["hy_bias"][l].reshape(32, 32).T))
    return v


CST_FIELDS = [("ident", 128), ("lowi", 128), ("lows", 128), ("uppi", 128), ("upps", 128),
              ("deltas", 8)]
COFF = {}
_o = 0
for _n, _w in CST_FIELDS:
    COFF[_n] = (_o, _w)
    _o += _w
NCST = _o


def build_cst():
    c = np.zeros((128, NCST), np.float32)
    i = np.arange(128)[:, None]
    j = np.arange(128)[None, :]
    same = (i // 64) == (j // 64)

    def put(name, arr):
        o, w = COFF[name]
        c[:, o:o + w] = arr
    put("ident", (i == j).astype(np.float32))
    put("lowi", ((i >= j) & same).astype(np.float32))
    put("lows", ((i > j) & same).astype(np.float32))
    put("uppi", ((i <= j) & same).astype(np.float32))
    put("upps", ((i < j) & same).astype(np.float32))
    lt = math.log(1e-2)
    deltas = np.abs(np.linspace(lt / 1.5, lt / 0.3, 1024, dtype=np.float32))
    put("deltas", _pk(deltas))
    return c


def build_rmask():
    m = np.ones((2, U), np.float32)
    starts = list(range(0, 256, 64)) + list(range(LT0, U, 64))
    for s in starts:
        m[0, s] = 0.0
        m[1, s + 63] = 0.0
    out = np.zeros((40, U), np.float32)
    out[0:8] = m[0]
    out[32:40] = m[1]
    return out


class Builder:
    def __init__(self, debug=False, stop=None, only=None, feed=()):
        self.debug = debug
        self.stop = stop
        self.only = only
        self.feed = set(feed)
        self.nc = bass.Bass("TRN2", target_bir_lowering=False)
        self.dbg_names = []

    def din(self, name, shape, dt=F32):
        return self.nc.dram_tensor(name, list(shape), dt, kind="ExternalInput").ap()

    def dscr(self, name, shape, dt=F32):
        kind = "ExternalOutput" if self.debug else "Internal"
        if name in self.feed:
            kind = "ExternalInput"
        elif self.debug:
            self.dbg_names.append(name)
        return self.nc.dram_tensor(name, list(shape), dt, kind=kind).ap()

    def vs(self, name, k=None):
        o, w = VOFF[name]
        if k is None:
            return self.vec[:, o:o + w]
        return self.vec[:, o + k:o + k + 1]

    def cs(self, name):
        o, w = COFF[name]
        return self.cst[:, o:o + w]

    def dma(self, out, in_, reads=(), writes=(), eng="sp"):
        return self.S.add(eng, lambda e: e.dma_start(out=out, in_=in_), reads=reads, writes=writes, dma=True)

    def build(self):
        nc = self.nc
        I = {}
        I["xT0"] = self.din("xT0", [D, U])
        I["cc"] = self.din("cc", [128, 16])
        I["vec"] = self.din("vec", [DEPTH, 128, NV])
        I["cst"] = self.din("cst", [128, NCST])
        I["rmask"] = self.din("rmask", [40, U])
        I["hy_w1"] = self.din("hy_w1", [DEPTH, 33, 64])
        I["hy_w2"] = self.din("hy_w2", [DEPTH, 64, 64])
        I["hy_w3"] = self.din("hy_w3", [DEPTH, 64, 2048])
        if self.only == "mg":
            I["w_mod"] = self.din("w_mod", [DEPTH, D, 6 * D])
            for n in ("w_proj_dn", "w_proj_hy", "w_proj_lru", "w_out"):
                I[n] = self.din(n, [DEPTH, D, D])
            I["ffn_up"] = self.din("ffn_up", [DEPTH, D, 2 * FH])
            I["ffn_down"] = self.din("ffn_down", [DEPTH, FH, D])
        if self.only:
            self.I = I
            return self.build2()
        I["w_mod"] = self.din("w_mod", [DEPTH, D, 6 * D])
        I["w_in"] = self.din("w_in", [DEPTH, D, NIN])
        I["lru_w_a"] = self.din("lru_w_a", [DEPTH, 2, 8, 128, 128])
        I["lru_w_x"] = self.din("lru_w_x", [DEPTH, 2, 8, 128, 128])
        for n in ("w_proj_dn", "w_proj_hy", "w_proj_lru", "w_out"):
            I[n] = self.din(n, [DEPTH, D, D])
        I["ffn_up"] = self.din("ffn_up", [DEPTH, D, 2 * FH])
        I["ffn_down"] = self.din("ffn_down", [DEPTH, FH, D])
        self.I = I
        return self.build2()

    def build2(self):
        nc, I = self.nc, self.I
        self.outT = nc.dram_tensor("outT", [D, L], F32, kind="ExternalOutput").ap()
        Sx = {}
        Sx["qT"] = self.dscr("qT", [8, 128, U])
        Sx["kT"] = self.dscr("kT", [8, 128, U])
        Sx["vT"] = self.dscr("vT", [8, 128, U])
        Sx["szT"] = self.dscr("szT", [D, U])
        Sx["gab"] = self.dscr("gab", [3, 40, U])
        Sx["zzT"] = self.dscr("zzT", [D, U])
        Sx["x0T"] = self.dscr("x0T", [D, U])
        Sx["olru"] = self.dscr("olru", [D, U], BF16)
        Sx["odn"] = self.dscr("odn", [D, U], BF16)
        Sx["ohy"] = self.dscr("ohy", [D, U], BF16)
        Sx["gT"] = self.dscr("gT", [3 * D, U], BF16)
        Sx["xA"] = self.dscr("xA", [D, U])
        Sx["xB"] = self.dscr("xB", [D, U])
        Sx["actT"] = self.dscr("actT", [FH, U], BF16)
        Sx["hp_lat"] = self.dscr("hp_lat", [2, D, L])
        Sx["hp_ctx"] = self.dscr("hp_ctx", [2, D, LC])
        for seg, n_ in (("lat", L), ("ctx", LC)):
            N_, NS1_, NF1_, NFH_ = hy_dims(n_)
            I["hy_zT_" + seg] = self.din("hy_zT_" + seg, [33, n_])
            I["hy_negt_" + seg] = self.din("hy_negt_" + seg, [128, n_])
            I["hy_F1_" + seg] = self.din("hy_F1_" + seg, [NS1_, 2 * NFH_])
            I["hy_G_" + seg] = self.din("hy_G_" + seg, [128, NFH_, 2, 128])
            I["hy_Tout_" + seg] = self.din("hy_Tout_" + seg, [NFH_, 128, 2, NS1_])
        I["hy_Winv"] = self.din("hy_Winv", [128, 2, 256])
        self.Sx = Sx
        self.Ux = {k: Unit() for k in Sx}

        with ExitStack() as es:
            self.es = es
            self.S = Sched(nc, es)
            S = self.S
            self.cst = es.enter_context(nc.sbuf_tensor("cst_sb", [128, NCST], F32))
            self.vec = es.enter_context(nc.sbuf_tensor("vec_sb", [128, NV], F32))
            self.ones = es.enter_context(nc.sbuf_tensor("ones", [128, 128], F32R))
            self.sc = es.enter_context(nc.sbuf_tensor("sc", [128, 16], F32))
            self.modv = es.enter_context(nc.sbuf_tensor("modv", [128, 48, 2], F32))
            self.der = es.enter_context(nc.sbuf_tensor("der", [128, 2, 2, 8], F32))
            self.u_cst, self.u_vec, self.u_ones, self.u_sc = Unit(), Unit(), Unit(), Unit()
            self.u_modv, self.u_der = Unit(), Unit()
            self.PS = TPool(nc, es, "ps", [128, 512], F32, 8, psum=True)
            self.dma(self.cst[:], I["cst"], writes=[self.u_cst])
            self.eps6 = es.enter_context(nc.sbuf_tensor("eps6", [128, 1], F32))
            S.add("dve", lambda e: e.memset(self.eps6[:], 1e-6), writes=[self.u_cst])
            self.one1 = es.enter_context(nc.sbuf_tensor("one1", [128, 1], F32))
            S.add("dve", lambda e: e.memset(self.one1[:], 1.0), writes=[self.u_cst])
            ones32 = es.enter_context(nc.sbuf_tensor("ones32", [128, 128], F32))
            u_o32 = Unit()
            S.add("dve", lambda e: e.memset(ones32[:], 1.0), writes=[u_o32])
            S.add("act", lambda e: e.activation(out=self.ones[:], in_=ones32[:], func=AF.Copy), reads=[u_o32], writes=[self.u_ones])
            self.ones_bf = es.enter_context(nc.sbuf_tensor("ones_bf", [128, 128], BF16))
            S.add("act", lambda e: e.activation(out=self.ones_bf[:], in_=ones32[:], func=AF.Copy), reads=[u_o32], writes=[self.u_ones])
            self.dma(self.sc[:], I["cc"], writes=[self.u_sc])
            S.add("act", lambda e: e.activation(out=self.sc[:], in_=self.sc[:], func=AF.Silu),
                  reads=[self.u_sc], writes=[self.u_sc])
            xin = I["xT0"]
            u_xin = Unit()
            for l in range(DEPTH):
                self.l = l
                self.dma(self.vec[:], I["vec"][l], writes=[self.u_vec])
                if self.only == "dn":
                    with ExitStack() as es2:
                        dn_phase(self, l, es2)
                        S.barrier()
                    break
                if self.only == "mg":
                    self.phase_mod(l)
                    with ExitStack() as es2:
                        merge_phase(self, l, es2, xin, u_xin, TT)
                        S.barrier()
                    with ExitStack() as es2:
                        ffn_phase(self, l, es2, TT)
                        S.barrier()
                    with ExitStack() as es2:
                        final_phase(self, es2)
                        S.barrier()
                    break
                if self.only == "hy":
                    for seg in (("lat", "ctx") if "ctx" in self.stop else ("lat",)):
                        with ExitStack() as es2:
                            hyena_phase(self, l, es2, seg)
                            S.barrier()
                    break
                self.phase_mod(l)
                if self.stop == "mod":
                    break
                with ExitStack() as es2:
                    self.phase_mixer_pre(l, es2, xin, u_xin)
                    S.barrier()
                if self.stop and self.stop.startswith("pre"):
                    break
                with ExitStack() as es2:
                    dn_phase(self, l, es2)
                    S.barrier()
                if self.stop and self.stop.startswith("dn"):
                    break
                for seg in (("lat", "ctx") if l < DEPTH - 1 else ("lat",)):
                    with ExitStack() as es2:
                        hyena_phase(self, l, es2, seg)
                        S.barrier()
                if self.stop and self.stop.startswith("hy"):
                    break
                tiles = TT if l < DEPTH - 1 else TT[1:]
                with ExitStack() as es2:
                    merge_phase(self, l, es2, xin, u_xin, tiles)
                    S.barrier()
                if self.stop and self.stop.startswith("mg"):
                    break
                with ExitStack() as es2:
                    ffn_phase(self, l, es2, tiles)
                    S.barrier()
                xin, u_xin = Sx["xB"], self.Ux["xB"]
                if self.stop and self.stop.startswith("ffn"):
                    break
                if l == DEPTH - 1:
                    with ExitStack() as es2:
                        final_phase(self, es2)
                        S.barrier()
            n = S.emit()
        self.n_ops = n
        return nc

    def phase_mod(self, l):
        nc, S = self.nc, self.S
        with ExitStack() as es:
            wm_pool = TPool(nc, es, "wm", [128, 8, 512], F32, 2)
            ps, ups = self.PS.get()
            for pn in range(12):
                wm, uwm = wm_pool.get()
                self.dma(wm[:], self.I["w_mod"][l][:, pn * 512:(pn + 1) * 512].rearrange("(k p) n -> p k n", p=128),
                         writes=[uwm])
                for cc in range(4):
                    c = pn * 4 + cc
                    for k in range(8):
                        S.add("pe", (lambda e, c=c, cc=cc, k=k, wm=wm: e.matmul(
                            ps[:, 2 * c:2 * c + 2], wm[:, k, cc * 128:(cc + 1) * 128], self.sc[:, 2 * k:2 * k + 2],
                            start=(k == 0), stop=(k == 7))), reads=[uwm, self.u_sc], writes=[ups])
            bm = self.vs("bmod")
            for s in range(2):
                S.add("dve", (lambda e, s=s: e.tensor_tensor(out=self.modv[:, :, s], in0=ps[:, s:96:2], in1=bm, op=ALU.add)),
                      reads=[ups, self.u_vec], writes=[self.u_modv])
            for w, (gname, j) in enumerate((("g1", 1), ("g2", 4))):
                for s in range(2):
                    S.add("dve", (lambda e, w=w, s=s, j=j, gname=gname: e.scalar_tensor_tensor(
                        out=self.der[:, w, s, :], in0=self.modv[:, j * 8:(j + 1) * 8, s], scalar=1.0, in1=self.vs(gname),
                        op0=ALU.add, op1=ALU.mult)), reads=[self.u_modv, self.u_vec], writes=[self.u_der])
            S.barrier()

    def modcol(self, j, s, k):
        return self.modv[:, j * 8 + k, s:s + 1]

    def norm_to_hT(self, xsrc, u_xsrc, which, hT, u_hT, tiles, es, tile_ids=None):
        nc, S = self.nc, self.S
        xp = TPool(nc, es, "nx", [128, 8, 512], F32, 2)
        sqp = TPool(nc, es, "nsq", [128, 8, 512], F32R, 1)
        rsp = TPool(nc, es, "nrs", [128, 512], F32, 2)
        shj = 0 if which == 0 else 3
        for ti_, (u0, n) in enumerate(tiles):
            ti = tile_ids[ti_] if tile_ids is not None else ti_
            seg = 1 if u0 == 0 else 0
            x, ux = xp.get()
            self.dma(x[:, :, :n], xsrc.rearrange("(k p) t -> p k t", p=128)[:, :, u0:u0 + n], reads=[u_xsrc], writes=[ux])
            sq, usq = sqp.get()
            S.add("act", (lambda e, x=x, sq=sq, n=n: e.activation(out=sq[:, :, :n], in_=x[:, :, :n], func=AF.Square)),
                  reads=[ux], writes=[usq])
            ps, ups = self.PS.get()
            for k in range(8):
                S.add("pe", (lambda e, k=k, sq=sq, ps=ps, n=n: e.matmul(ps[:, :n], self.ones[:], sq[:, k, :n],
                                                                         start=(k == 0), stop=(k == 7))),
                      reads=[usq, self.u_ones], writes=[ups])
            rs, urs = rsp.get()
            S.add("act", (lambda e, rs=rs, ps=ps, n=n: e.activation(out=rs[:, :n], in_=ps[:, :n], func=AF.Sqrt, scale=1.0 / D, bias=self.eps6[:])),
                  reads=[ups], writes=[urs])
            S.add("dve", (lambda e, rs=rs, n=n: e.reciprocal(out=rs[:, :n], in_=rs[:, :n])), reads=[urs], writes=[urs])
            S.add("dve", (lambda e, x=x, rs=rs, n=n: e.tensor_tensor(out=x[:, :, :n], in0=x[:, :, :n],
                                                                      in1=rs[:, :n].unsqueeze(1).to_broadcast([128, 8, n]), op=ALU.mult)),
                  reads=[urs, ux], writes=[ux])
            for k in range(8):
                S.add("act", (lambda e, k=k, x=x, n=n, u0=u0, seg=seg: e.activation(
                    out=hT[:, k, u0:u0 + n], in_=x[:, k, :n], func=AF.Identity,
                    scale=self.der[:, which, seg, k:k + 1], bias=self.modcol(shj, seg, k))),
                    reads=[ux, self.u_der, self.u_modv], writes=[u_hT[ti]])

    def load_w_bf16(self, wsrc, M, wst_pool, wbf_pool, eng="pool"):
        S = self.S
        wst, uws = wst_pool.get()
        self.dma(wst[:, :, :M], wsrc.rearrange("(k p) m -> p k m", p=128), writes=[uws])
        wb, uwb = wbf_pool.get()
        S.add(eng, (lambda e, wb=wb, wst=wst, M=M: e.tensor_copy(out=wb[:, :, :M], in_=wst[:, :, :M])),
              reads=[uws], writes=[uwb])
        return wb, uwb

    def mm_tile(self, wb, uwb, M, hT, u_hT, ti):
        S = self.S
        u0, n = TT[ti]
        ps, ups = self.PS.get()
        for k in range(8):
            S.add("pe", (lambda e, k=k, ps=ps, wb=wb: e.matmul(ps[:M, :n], wb[:, k, :M], hT[:, k, u0:u0 + n],
                                                                start=(k == 0), stop=(k == 7))),
                  reads=[uwb, u_hT[ti]], writes=[ups])
        return ps, ups

    def conv(self, pp, upp, cw, ntap, bias, out_ap, uout):
        S = self.S
        acc, uacc = self._acc, self._uacc
        S.add("act", (lambda e: e.activation(out=acc[:, :U], in_=pp[:, 0:U], func=AF.Identity,
                                             scale=cw(0), bias=(bias if bias is not None else 0.0))),
              reads=[upp, self.u_vec], writes=[uacc])
        for j in range(1, ntap):
            last = j == ntap - 1
            o = out_ap if last else acc[:, :U]
            uo = uout if last else uacc
            S.add("dve", (lambda e, j=j, o=o: e.scalar_tensor_tensor(out=o, in0=pp[:, j:j + U], scalar=cw(j),
                                                                    in1=acc[:, :U], op0=ALU.mult, op1=ALU.add)),
                  reads=[upp, uacc, self.u_vec], writes=[uo])

    def phase_mixer_pre(self, l, es, xin, u_xin):
        nc, S, I, Sx, Ux = self.nc, self.S, self.I, self.Sx, self.Ux
        hT = es.enter_context(nc.sbuf_tensor(uname("hT"), [128, 8, U], BF16))
        u_hT = [Unit() for _ in TT]
        with ExitStack() as es3:
            self.norm_to_hT(xin, u_xin, 0, hT, u_hT, TT, es3)
            S.barrier()
        if self.stop == "norm":
            dbg = self.nc.dram_tensor(uname("dbg_hT"), [128, 8, U], BF16, kind="ExternalOutput").ap()
            self.dma(dbg, hT[:], reads=u_hT)
            return
        WK = TPool(nc, es, "wk", [128, UP], F32, 4)
        PP = TPool(nc, es, "pp", [128, UP], F32, 1)
        wst_pool = TPool(nc, es, "wst", [128, 8, 128], F32, 2)
        wbf_pool = TPool(nc, es, "wbf", [128, 8, 128], BF16, 2)
        rsp = TPool(nc, es, "rs", [128, 512], F32, 2)
        gtp = TPool(nc, es, "gt", [128, 512], BF16, 4)
        ztp = TPool(nc, es, "zt", [128, 512], F32, 2)
        lwp = TPool(nc, es, "lw", [128, 128], F32, 2)
        lwr = TPool(nc, es, "lwr", [128, 128], BF16, 2)
        sm = es.enter_context(nc.sbuf_tensor(uname("sm"), [128, 40], F32))
        u_sm = Unit()
        pp, upp = PP.get()
        S.add("pool", lambda e: e.memset(pp[:], 0.0), writes=[upp])
        sqb = es.enter_context(nc.sbuf_tensor(uname("sqb"), [128, UP], BF16))
        usqb = Unit()
        self._cnt = 0

        def evac_copy(ps, ups, out_ap, uo, M=128, n=512):
            self._cnt += 1
            if self._cnt % 2:
                S.add("act", (lambda e: e.activation(out=out_ap, in_=ps[:M, :n], func=AF.Copy)), reads=[ups], writes=[uo])
            else:
                S.add("dve", (lambda e: e.tensor_copy(out=out_ap, in_=ps[:M, :n])), reads=[ups], writes=[uo])

        pws_pool = TPool(nc, es, "pws", [128, 8, 128], F32, 2)
        pwb_pool = TPool(nc, es, "pwb", [128, 8, 128], BF16, 2)

        def make_prefetch(cols):
            st = {"next": None}

            def get(i):
                cur = st["next"] if st["next"] is not None else self.load_w_bf16(I["w_in"][l][:, cols[i]:cols[i] + 128], 128, pws_pool, pwb_pool)
                st["next"] = (self.load_w_bf16(I["w_in"][l][:, cols[i + 1]:cols[i + 1] + 128], 128, pws_pool, pwb_pool)
                              if i + 1 < len(cols) else None)
                return cur
            return get

        def proj_to_pp(col0, pre=None):
            wb, uwb = pre if pre is not None else self.load_w_bf16(I["w_in"][l][:, col0:col0 + 128], 128, wst_pool, wbf_pool)
            for ti, (u0, n) in enumerate(TT):
                ps, ups = self.mm_tile(wb, uwb, 128, hT, u_hT, ti)
                evac_copy(ps, ups, pp[:, u0 + 1:u0 + 1 + n], upp, 128, n)

        def proj_act(col0, func, out_tile_fn, bias=None):
            wb, uwb = self.load_w_bf16(I["w_in"][l][:, col0:col0 + 128], 128, wst_pool, wbf_pool)
            for ti, (u0, n) in enumerate(TT):
                ps, ups = self.mm_tile(wb, uwb, 128, hT, u_hT, ti)
                o, uo = out_tile_fn(ti, u0, n)
                S.add("act", (lambda e, ps=ps, o=o, n=n: e.activation(out=o, in_=ps[:, :n], func=func)),
                      reads=[ups], writes=[uo])

        S.add("act", lambda e: e.activation(out=sm[:40, 0:1], in_=self.vs("alog")[:40], func=AF.Exp),
              reads=[self.u_vec], writes=[u_sm])
        S.add("dve", lambda e: e.tensor_scalar(out=sm[:40, 0:1], in0=sm[:40, 0:1], scalar1=-1.0, scalar2=None, op0=ALU.mult),
              reads=[u_sm], writes=[u_sm])
        S.add("act", lambda e: e.activation(out=sm[:, 8:24], in_=self.vs("llam"), func=AF.Exp, scale=-1.0),
              reads=[self.u_vec, u_sm], writes=[u_sm])
        S.add("act", lambda e: e.activation(out=sm[:, 8:24], in_=sm[:, 8:24], func=AF.Ln, bias=1.0),
              reads=[u_sm], writes=[u_sm])
        S.add("dve", lambda e: e.tensor_scalar(out=sm[:, 8:24], in0=sm[:, 8:24], scalar1=-8.0, scalar2=None, op0=ALU.mult),
              reads=[u_sm], writes=[u_sm])
        S.add("dve", lambda e: e.tensor_scalar(out=sm[:, 24:40], in0=sm[:, 8:24], scalar1=2.0, scalar2=None, op0=ALU.mult),
              reads=[u_sm], writes=[u_sm])

        def z_chunk(c):
            wb, uwb = self.load_w_bf16(I["w_in"][l][:, OFF_Z + c * 128:OFF_Z + (c + 1) * 128], 128, wst_pool, wbf_pool)
            for ti, (u0, n) in enumerate(TT):
                ps, ups = self.mm_tile(wb, uwb, 128, hT, u_hT, ti)
                t, ut = ztp.get()
                S.add("act", (lambda e, ps=ps, t=t, n=n: e.activation(out=t[:, :n], in_=ps[:, :n], func=AF.Silu)),
                      reads=[ups], writes=[ut])
                self.dma(Sx["szT"][c * 128:(c + 1) * 128, u0:u0 + n], t[:, :n], reads=[ut], writes=[Ux["szT"]])

        def gate_chunk(c):
            wb, uwb = self.load_w_bf16(I["w_in"][l][:, OFF_GATE + c * 128:OFF_GATE + (c + 1) * 128], 128, wst_pool, wbf_pool)
            for ti, (u0, n) in enumerate(TT):
                ps, ups = self.mm_tile(wb, uwb, 128, hT, u_hT, ti)
                t, ut = gtp.get()
                S.add("act", (lambda e, ps=ps, t=t, n=n: e.activation(out=t[:, :n], in_=ps[:, :n], func=AF.Sigmoid)),
                      reads=[ups], writes=[ut])
                self.dma(Sx["gT"][c * 128:(c + 1) * 128, u0:u0 + n], t[:, :n], reads=[ut], writes=[Ux["gT"]])
        fillers = [(z_chunk, c) for c in range(8)] + [(gate_chunk, c) for c in range(24)]

        def fill():
            if fillers:
                fn, c = fillers.pop(0)
                fn(c)

        dn_w = make_prefetch([OFF_QKV + c * 128 for c in range(24)])

        def dn_front(c):
            proj_to_pp(OFF_QKV + c * 128, dn_w(c))
            (q, uq) = WK.get()
            self._acc, self._uacc = q, uq
            self.conv(pp, upp, (lambda j, c=c: self.vs("dncw", c * 4 + j)), 4, None, q[:, :U], uq)
            return q, uq

        def dn_tail(c, q, uq):
            w3, h = c // 8, c % 8
            S.add("act", (lambda e, q=q: e.activation(out=q[:, :U], in_=q[:, :U], func=AF.Silu)), reads=[uq], writes=[uq])
            if w3 < 2:
                (sq, usq) = (sqb, usqb)
                S.add("act", (lambda e, q=q, sq=sq: e.activation(out=sq[:, :U], in_=q[:, :U], func=AF.Square)),
                      reads=[uq], writes=[usq])
                for (u0, n) in TT:
                    ps, ups = self.PS.get()
                    S.add("pe", (lambda e, ps=ps, sq=sq, u0=u0, n=n: e.matmul(ps[:, :n], self.ones_bf[:], sq[:, u0:u0 + n],
                                                                               start=True, stop=True)),
                          reads=[usq, self.u_ones], writes=[ups])
                    rs, urs = rsp.get()
                    S.add("act", (lambda e, rs=rs, ps=ps, n=n: e.activation(out=rs[:, :n], in_=ps[:, :n], func=AF.Sqrt, bias=self.eps6[:])),
                          reads=[ups], writes=[urs])
                    S.add("dve", (lambda e, rs=rs, n=n: e.reciprocal(out=rs[:, :n], in_=rs[:, :n])), reads=[urs], writes=[urs])
                    S.add("dve", (lambda e, q=q, rs=rs, u0=u0, n=n: e.tensor_tensor(out=q[:, u0:u0 + n], in0=q[:, u0:u0 + n],
                                                                                    in1=rs[:, :n], op=ALU.mult)),
                          reads=[urs, uq], writes=[uq])
            dst = (Sx["qT"], Sx["kT"], Sx["vT"])[w3]
            ud = (Ux["qT"], Ux["kT"], Ux["vT"])[w3]
            self.dma(dst[h], q[:, :U], reads=[uq], writes=[ud])
        prev = None
        for c in range(24):
            cur = dn_front(c)
            if prev is not None:
                dn_tail(c - 1, *prev)
                fill()
            prev = cur
        dn_tail(23, *prev)
        fill()
        if self.stop == "pre1":
            return
        if self.stop == "pre2":
            return
        rm, urm = WK.get()
        self.dma(rm[:40, :U], I["rmask"], writes=[urm])
        G, uG = WK.get()
        LNB, uLNB = WK.get()
        for which, (dst, udst) in enumerate(((G, uG), (LNB, uLNB))):
            wst, uws = wst_pool.get()
            S.add("pool", (lambda e, wst=wst: e.memset(wst[:], 0.0)), writes=[uws])
            for d in range(2):
                c0 = OFF_AB + d * 16 + which * 8
                self.dma(wst[:, :, d * 32:d * 32 + 8], I["w_in"][l][:, c0:c0 + 8].rearrange("(k p) m -> p k m", p=128),
                         reads=[uws], writes=[uws])
            wb, uwb = wbf_pool.get()
            S.add("pool", (lambda e, wb=wb, wst=wst: e.tensor_copy(out=wb[:, :, :40], in_=wst[:, :, :40])), reads=[uws], writes=[uwb])
            for ti, (u0, n) in enumerate(TT):
                ps, ups = self.mm_tile(wb, uwb, 40, hT, u_hT, ti)
                evac_copy(ps, ups, dst[:40, u0:u0 + n], udst, 40, n)
        for (t_, ut_) in ((G, uG), (LNB, uLNB)):
            S.add("pool", (lambda e, t_=t_: e.memset(t_[:40, 256:259], 0.0)), reads=[ut_], writes=[ut_])
        S.add("act", lambda e: e.activation(out=G[:40, :U], in_=G[:40, :U], func=AF.Exp, bias=self.vs("dtb")[:40]),
              reads=[uG, self.u_vec], writes=[uG])
        S.add("act", lambda e: e.activation(out=G[:40, :U], in_=G[:40, :U], func=AF.Ln, bias=1.0), reads=[uG], writes=[uG])
        S.add("dve", lambda e: e.tensor_scalar(out=G[:40, :U], in0=G[:40, :U], scalar1=sm[:40, 0:1], scalar2=None, op0=ALU.mult),
              reads=[uG, u_sm], writes=[uG])
        S.add("act", lambda e: e.activation(out=LNB[:40, :U], in_=LNB[:40, :U], func=AF.Exp, scale=-1.0), reads=[uLNB], writes=[uLNB])
        S.add("act", lambda e: e.activation(out=LNB[:40, :U], in_=LNB[:40, :U], func=AF.Ln, bias=1.0), reads=[uLNB], writes=[uLNB])
        S.add("dve", lambda e: e.tensor_scalar(out=LNB[:40, :U], in0=LNB[:40, :U], scalar1=-1.0, scalar2=None, op0=ALU.mult),
              reads=[uLNB], writes=[uLNB])
        GC, uGC = WK.get()
        S.add("pool", lambda e: e.memset(GC[:40, :U], 0.0), writes=[uGC])
        S.add("dve", lambda e: e.tensor_tensor_scan(out=GC[0:8, :U], data0=rm[0:8, :U], data1=G[0:8, :U], initial=0.0,
                                                    op0=ALU.mult, op1=ALU.add), reads=[urm, uG, uGC], writes=[uGC])
        S.add("dve", lambda e: e.tensor_tensor_scan(out=GC[32:40, U - 1::-1], data0=rm[32:40, U - 1::-1], data1=G[32:40, U - 1::-1],
                                                    initial=0.0, op0=ALU.mult, op1=ALU.add), reads=[urm, uG, uGC], writes=[uGC])
        self.dma(Sx["gab"][0], GC[:40, :U], reads=[uGC], writes=[Ux["gab"]])
        self.dma(Sx["gab"][2], LNB[:40, :U], reads=[uLNB], writes=[Ux["gab"]])
        S.add("dve", lambda e: e.tensor_tensor(out=G[:40, :U], in0=GC[:40, :U], in1=LNB[:40, :U], op=ALU.add),
              reads=[uGC, uLNB, uG], writes=[uG])
        self.dma(Sx["gab"][1], G[:40, :U], reads=[uG], writes=[Ux["gab"]])
        if self.stop == "pre3":
            return
        hy_order = [part * 8 + c for c in range(8) for part in (1, 2, 0)]
        hy_w = make_prefetch([OFF_HY + cidx * 128 for cidx in hy_order])
        hy_i = 0
        for c in range(8):
            tl = []
            for part in (1, 2, 0):
                cidx = part * 8 + c
                proj_to_pp(OFF_HY + cidx * 128, hy_w(hy_i))
                hy_i += 1
                t, ut = WK.get()
                self._acc, self._uacc = t, ut
                self.conv(pp, upp, (lambda j, cidx=cidx: self.vs("hycw", cidx * 3 + j)), 3, self.vs("hycb", cidx), t[:, :U], ut)
                tl.append((t, ut))
                fill()
            (a1, ua1), (a2, ua2), (a0, ua0) = tl
            S.add("dve", (lambda e, a1=a1, a2=a2: e.tensor_tensor(out=a1[:, :U], in0=a1[:, :U], in1=a2[:, :U], op=ALU.mult)),
                  reads=[ua1, ua2], writes=[ua1])
            self.dma(Sx["zzT"][c * 128:(c + 1) * 128], a1[:, :U], reads=[ua1], writes=[Ux["zzT"]])
            self.dma(Sx["x0T"][c * 128:(c + 1) * 128], a0[:, :U], reads=[ua0], writes=[Ux["x0T"]])
        if self.stop == "pre4":
            return
        while fillers:
            fill()
        for g in range(8):
            proj_to_pp(OFF_LX + g * 128)
            (xs, uxs), (H, uH), (A, uA), (Bt, uB) = WK.t
            self._acc, self._uacc = H, uH
            self.conv(pp, upp, (lambda j, g=g: self.vs("lrucw", g * 4 + j)), 4, self.vs("lrucb", g), xs[:, :U], uxs)
            S.add("act", (lambda e: e.activation(out=sqb[:, :U], in_=xs[:, :U], func=AF.Copy)), reads=[uxs], writes=[usqb])
            for d in range(2):
                tt_, utt = (H, uH) if d == 0 else (pp, upp)
                for (wname, bname, dst, udst) in (("lru_w_a", "lba", A, uA), ("lru_w_x", "lbx", Bt, uB)):
                    lw, ulw = lwp.get()
                    self.dma(lw[:], I[wname][l, d, g], writes=[ulw])
                    lr, ulr = lwr.get()
                    S.add("act", (lambda e, lw=lw, lr=lr: e.activation(out=lr[:], in_=lw[:], func=AF.Copy)), reads=[ulw], writes=[ulr])
                    for (u0, n) in TT:
                        ps, ups = self.PS.get()
                        S.add("pe", (lambda e, ps=ps, lr=lr, u0=u0, n=n: e.matmul(ps[:, :n], lr[:], sqb[:, u0:u0 + n],
                                                                                   start=True, stop=True)),
                              reads=[ulr, usqb], writes=[ups])
                        S.add("act", (lambda e, ps=ps, dst=dst, u0=u0, n=n, bname=bname, d=d, g=g: e.activation(
                            out=dst[:, u0:u0 + n], in_=ps[:, :n], func=AF.Sigmoid, bias=self.vs(bname, d * 8 + g))),
                            reads=[ups, self.u_vec], writes=[udst])
                if self.stop == "pre5":
                    return
                S.add("pool", (lambda e, A=A: e.memset(A[:, 256:259], 0.0)), reads=[uA], writes=[uA])
                S.add("act", (lambda e, A=A, t=tt_, d=d, g=g: e.activation(out=t[:, :U], in_=A[:, :U], func=AF.Exp,
                                                                           scale=sm[:, 24 + d * 8 + g:25 + d * 8 + g])),
                      reads=[uA, u_sm, utt], writes=[utt])
                S.add("act", (lambda e, A=A, d=d, g=g: e.activation(out=A[:, :U], in_=A[:, :U], func=AF.Exp,
                                                                     scale=sm[:, 8 + d * 8 + g:9 + d * 8 + g])),
                      reads=[uA, u_sm], writes=[uA])
                S.add("act", (lambda e, t=tt_: e.activation(out=t[:, :U], in_=t[:, :U], func=AF.Sqrt, scale=-1.0, bias=self.one1[:])),
                      reads=[utt], writes=[utt])
                S.add("dve", (lambda e, Bt=Bt, t=tt_: e.tensor_tensor(out=Bt[:, :U], in0=Bt[:, :U], in1=t[:, :U], op=ALU.mult)),
                      reads=[uB, utt], writes=[uB])
                S.add("dve", (lambda e, Bt=Bt: e.tensor_tensor(out=Bt[:, :U], in0=Bt[:, :U], in1=xs[:, :U], op=ALU.mult)),
                      reads=[uB, uxs], writes=[uB])
                S.add("pool", (lambda e, Bt=Bt: e.memset(Bt[:, 256:259], 0.0)), reads=[uB], writes=[uB])
                if self.stop == "pre6":
                    return
                if d == 0:
                    S.add("dve", (lambda e, A=A, Bt=Bt, t=tt_: e.tensor_tensor_scan(out=t[:, 0:U], data0=A[:, 0:U], data1=Bt[:, 0:U],
                                                                                    initial=0.0, op0=ALU.mult, op1=ALU.add)),
                          reads=[uA, uB, utt], writes=[utt])
                else:
                    S.add("dve", (lambda e, A=A, Bt=Bt, t=tt_: e.tensor_tensor_scan(out=t[:, 255::-1], data0=A[:, 255::-1], data1=Bt[:, 255::-1],
                                                                                    initial=0.0, op0=ALU.mult, op1=ALU.add)),
                          reads=[uA, uB, utt], writes=[utt])
                    S.add("dve", (lambda e, A=A, Bt=Bt, t=tt_: e.tensor_tensor_scan(out=t[:, U - 1:LT0 - 1:-1], data0=A[:, U - 1:LT0 - 1:-1],
                                                                                    data1=Bt[:, U - 1:LT0 - 1:-1], initial=t[:, 0:1],
                                                                                    op0=ALU.mult, op1=ALU.add)),
                          reads=[uA, uB, utt], writes=[utt])
                    S.add("dve", (lambda e, t=tt_: e.tensor_tensor(out=H[:, :U], in0=H[:, :U], in1=t[:, :U], op=ALU.add)),
                          reads=[utt, uH], writes=[uH])
            if self.stop == "pre7":
                return
            S.add("pool", lambda e: e.memset(pp[:, 0:1], 0.0), reads=[upp], writes=[upp])
            S.add("pool", lambda e: e.memset(pp[:, 257:260], 0.0), reads=[upp], writes=[upp])
            S.add("pool", lambda e: e.memset(pp[:, 4356:UP], 0.0), reads=[upp], writes=[upp])
            (Y, uY), (T2, uT2) = WK.t[2], WK.t[3]
            wb, uwb = self.load_w_bf16(I["w_in"][l][:, OFF_LY + g * 128:OFF_LY + (g + 1) * 128], 128, wst_pool, wbf_pool)
            for ti, (u0, n) in enumerate(TT):
                ps, ups = self.mm_tile(wb, uwb, 128, hT, u_hT, ti)
                evac_copy(ps, ups, Y[:, u0:u0 + n], uY, 128, n)
            S.add("pool", (lambda e, Y=Y: e.memset(Y[:, 256:259], 0.0)), reads=[uY], writes=[uY])
            S.add("act", (lambda e, Y=Y, T2=T2: e.activation(out=T2[:, :U], in_=Y[:, :U], func=AF.Square)), reads=[uY], writes=[uT2])
            S.add("dve", (lambda e, T2=T2: e.tensor_scalar(out=T2[:, :U], in0=T2[:, :U], scalar1=0.044715, scalar2=1.0,
                                                           op0=ALU.mult, op1=ALU.add)), reads=[uT2], writes=[uT2])
            S.add("dve", (lambda e, Y=Y, T2=T2: e.tensor_tensor(out=T2[:, :U], in0=T2[:, :U], in1=Y[:, :U], op=ALU.mult)),
                  reads=[uT2, uY], writes=[uT2])
            S.add("act", (lambda e, T2=T2: e.activation(out=T2[:, :U], in_=T2[:, :U], func=AF.Sigmoid, scale=1.5957691216057308)),
                  reads=[uT2], writes=[uT2])
            S.add("dve", (lambda e, Y=Y, T2=T2: e.tensor_tensor(out=Y[:, :U], in0=Y[:, :U], in1=T2[:, :U], op=ALU.mult)),
                  reads=[uT2, uY], writes=[uY])
            S.add("dve", (lambda e, Y=Y: e.tensor_tensor(out=T2[:, :U].bitcast(BF16)[:, :U], in0=Y[:, :U], in1=H[:, :U], op=ALU.mult)),
                  reads=[uY, uH, uT2], writes=[uT2])
            self.dma(Sx["olru"][g * 128:(g + 1) * 128], T2[:, :U].bitcast(BF16)[:, :U], reads=[uT2], writes=[Ux["olru"]])
            if self.stop == "pre8":
                return


def prep_inputs(inp, b):
    m = {}
    xT = np.zeros((D, U), np.float32)
    xT[:, 0:LC] = inp["ctx"][b].T
    xT[:, LT0:] = inp["x"][b].T
    m["xT0"] = xT
    cc = np.zeros((128, 8, 2), np.float32)
    cc[:, :, 0] = _pk(inp["c"][b])
    cc[:, :, 1] = _pk(inp["c_ctx"])
    m["cc"] = cc.reshape(128, 16)
    m["vec"] = np.stack([build_vec(inp, l) for l in range(DEPTH)])
    m["cst"] = build_cst()
    m["rmask"] = build_rmask()
    for seg, n_ in (("lat", L), ("ctx", LC)):
        tbs = hyena_tables(n_)
        for k in ("zT", "negt", "F1", "G", "Tout"):
            m["hy_%s_%s" % (k, seg)] = tbs[k]
        m["hy_Winv"] = tbs["Winv"]
    for n in ("w_mod", "w_in", "lru_w_a", "lru_w_x", "w_proj_dn", "w_proj_hy", "w_proj_lru", "w_out",
              "ffn_up", "ffn_down", "hy_w1", "hy_w2", "hy_w3"):
        m[n] = np.ascontiguousarray(inp[n], dtype=np.float32)
    return m


DN_BLOCKS = [0, 128] + [LT0 + 128 * i for i in range(32)]


class T128:
    def __init__(self, nc, es, names, dtype=F32):
        self.t = {}
        for n in names:
            self.t[n] = (es.enter_context(nc.sbuf_tensor(uname(n), [128, 128], dtype)), Unit())

    def __getitem__(self, n):
        return self.t[n]


def dn_phase(B, l, es):
    nc, S, Sx, Ux = B.nc, B.S, B.Sx, B.Ux
    ident = B.cs("ident")
    NB = len(DN_BLOCKS)
    GR = es.enter_context(nc.sbuf_tensor(uname("GR"), [40, U], F32))
    uGR = Unit()
    B.dma(GR[:], Sx["gab"][0], reads=[Ux["gab"]], writes=[uGR])
    TG = es.enter_context(nc.sbuf_tensor(uname("TG"), [128, NB, 3, 40], F32))
    TE = es.enter_context(nc.sbuf_tensor(uname("TE"), [128, NB, 2, 40], F32))
    uTG = Unit()
    gl_pool = TPool(nc, es, "gl", [40, 3, 128], F32, 2)
    for bi, ub in enumerate(DN_BLOCKS):
        gl, ugl = gl_pool.get()
        B.dma(gl[:], Sx["gab"][:, :, ub:ub + 128].rearrange("k r u -> r k u"), reads=[Ux["gab"]], writes=[ugl])
        ps, ups = B.PS.get()
        for k in range(3):
            S.add("pe", (lambda e, ps=ps, k=k, gl=gl: e.transpose(ps[:, k * 40:(k + 1) * 40], gl[:, k, :], ident[:40, :40])),
                  reads=[ugl, B.u_cst], writes=[ups])
        S.add("dve", (lambda e, ps=ps, bi=bi: e.tensor_copy(out=TG[:, bi].rearrange("p k r -> p (k r)"), in_=ps[:, 0:120])),
              reads=[ups], writes=[uTG])
        S.add("act", (lambda e, ps=ps, bi=bi: e.activation(out=TE[:, bi].rearrange("p k r -> p (k r)"), in_=ps[:, 40:120], func=AF.Exp)),
              reads=[ups], writes=[uTG])
    SELN = es.enter_context(nc.sbuf_tensor(uname("SELN"), [40, 16, 128], F32))
    uSEL = Unit()
    for q in range(16):
        r = (q // 8) * 32 + (q % 8)
        S.add("dve", (lambda e, q=q, r=r: e.tensor_scalar(out=SELN[:, q, :], in0=ident[:40, r:r + 1].to_broadcast([40, 128]),
                                                          scalar1=-1.0, scalar2=None, op0=ALU.mult)),
              reads=[B.u_cst], writes=[uSEL])
    zero = es.enter_context(nc.sbuf_tensor(uname("zero"), [128, 128], F32))
    uzero = Unit()
    S.add("pool", lambda e: e.memset(zero[:], 0.0), writes=[uzero])

    if B.stop == "dn0":
        dbg = nc.dram_tensor(uname("dbg_TG"), [128, NB, 3, 40], F32, kind="ExternalOutput").ap()
        B.dma(dbg, TG[:], reads=[uTG])
        return
    NCH = 4
    chains_res = []
    for ci in range(NCH):
        res = {}
        res["f32"] = T128(nc, es, ["qt", "kt", "vt", "Dm", "A", "E1", "M", "AT", "MT", "Y", "P0", "P1", "Q0", "Q1", "Us", "EG"])
        res["f32b"] = T128(nc, es, ["qt", "kt", "vt"])
        res["r"] = T128(nc, es, ["ktr", "qtr", "attnT", "Kd0", "Kd1", "Ktb", "Vb", "QdT", "YR", "WTs", "VN", "S"], F32R)
        res["cd"] = (es.enter_context(nc.sbuf_tensor(uname("cd"), [128, 2], F32)), Unit())
        if ci % 2 == 0:
            res["O"] = (es.enter_context(nc.sbuf_tensor(uname("O"), [128, NB, 128], F32)), [Unit() for _ in range(NB)])
        else:
            res["O"] = chains_res[ci - 1]["O"]
        res["ps"] = [B.PS.t[2 * ci], B.PS.t[2 * ci + 1]]
        chains_res.append(res)
    OD = es.enter_context(nc.sbuf_tensor(uname("OD"), [128, U], BF16))
    uOD = Unit()
    S.add("pool", lambda e: e.memset(OD[:, 256:259], 0.0), writes=[uOD])
    pp_t = TPool(nc, es, "dnpost", [128, 128], F32, 3)
    pp_s = TPool(nc, es, "dnsm", [128, 2], F32, 3)
    DK = 128.0 ** -0.5

    def chain(h, d, res):
        q = d * 8 + h
        r = d * 32 + h
        f, fb, rr = res["f32"], res["f32b"], res["r"]
        cd, ucd = res["cd"]
        O, uO = res["O"]
        psl = res["ps"]
        slot_i = [0]

        def slot():
            k = slot_i[0] % 8
            slot_i[0] += 1
            t, u = psl[k // 4]
            qd = k % 4
            return t[:, qd * 128:(qd + 1) * 128], u
        incl = B.cs("lowi") if d == 0 else B.cs("uppi")
        strict = B.cs("lows") if d == 0 else B.cs("upps")
        iend = (lambda c: c * 64 + 63) if d == 0 else (lambda c: c * 64)
        (Sst, uS), (VN, uVN) = rr["S"], rr["VN"]
        S.add("dve", (lambda e: e.tensor_copy(out=Sst[:], in_=zero[:])), reads=[uzero], writes=[uS])
        S.add("dve", (lambda e: e.tensor_copy(out=VN[:], in_=zero[:])), reads=[uzero], writes=[uVN])
        order = list(range(NB)) if d == 0 else [1, 0] + list(range(NB - 1, 1, -1))
        for oi, bi in enumerate(order):
            ub = DN_BLOCKS[bi]
            ld = f if oi % 2 == 0 else fb
            (qt, uqt), (kt, ukt), (vt, uvt) = ld["qt"], ld["kt"], ld["vt"]
            B.dma(qt[:], Sx["qT"][h][:, ub:ub + 128], reads=[Ux["qT"]], writes=[uqt])
            B.dma(kt[:], Sx["kT"][h][:, ub:ub + 128], reads=[Ux["kT"]], writes=[ukt])
            B.dma(vt[:], Sx["vT"][h][:, ub:ub + 128], reads=[Ux["vT"]], writes=[uvt])
            pKK, uKK = slot()
            S.add("pe", (lambda e, o=pKK, kt=kt: e.matmul(o, kt[:], kt[:], start=True, stop=True)), reads=[ukt], writes=[uKK])
            pQK, uQK = slot()
            S.add("pe", (lambda e, o=pQK, kt=kt, qt=qt: e.matmul(o, kt[:], qt[:], start=True, stop=True)), reads=[ukt, uqt], writes=[uQK])
            pbc, ubc = slot()
            S.add("pe", (lambda e, o=pbc, ub=ub: e.matmul(o, SELN[:, q, :], GR[:, ub:ub + 128], start=True, stop=True)),
                  reads=[uSEL, uGR], writes=[ubc])
            pvt, uvtk = slot()
            S.add("pe", (lambda e, o=pvt, vt=vt: e.transpose(o, vt[:], ident)), reads=[uvt, B.u_cst], writes=[uvtk])
            pkt, uktk = slot()
            S.add("pe", (lambda e, o=pkt, kt=kt: e.transpose(o, kt[:], ident)), reads=[ukt, B.u_cst], writes=[uktk])
            yield
            if B.stop.endswith(":B"):
                return
            (Dm, uDm), (A, uA), (E1, uE1), (M, uM), (EG, uEG) = f["Dm"], f["A"], f["E1"], f["M"], f["EG"]
            gcol = TG[:, bi, 0, r:r + 1]
            climit = int(B.stop.split(":K")[1]) if ":K" in B.stop else 99
            if 0 < climit:
                S.add("dve", (lambda e, o=pbc, gcol=gcol: e.tensor_scalar(out=Dm[:], in0=o, scalar1=gcol, scalar2=0.0, op0=ALU.add, op1=ALU.min)),
                      reads=[ubc, uTG], writes=[uDm])
            if 2 < climit:
                S.add("act", (lambda e, o=pbc: e.activation(out=EG[:], in_=o, func=AF.Exp, scale=-1.0)), reads=[ubc], writes=[uEG])
            if 3 < climit:
                S.add("act", (lambda e: e.activation(out=A[:], in_=Dm[:], func=AF.Exp)), reads=[uDm], writes=[uA])
            if 4 < climit:
                S.add("dve", (lambda e: e.tensor_tensor(out=A[:], in0=A[:], in1=incl, op=ALU.mult)), reads=[uA, B.u_cst], writes=[uA])
            bcol = TE[:, bi, 1, r:r + 1]
            if 5 < climit:
                S.add("dve", (lambda e, bcol=bcol: e.scalar_tensor_tensor(out=E1[:], in0=A[:], scalar=bcol, in1=strict, op0=ALU.mult, op1=ALU.mult)),
                      reads=[uA, uTG, B.u_cst], writes=[uE1])
            if 6 < climit:
                S.add("dve", (lambda e, o=pKK: e.tensor_tensor(out=M[:], in0=o, in1=E1[:], op=ALU.mult)), reads=[uKK, uE1], writes=[uM])
            (QdT, uQdT) = rr["QdT"]
            if 7 < climit:
                S.add("dve", (lambda e, qt=qt: e.scalar_tensor_tensor(out=QdT[:], in0=qt[:], scalar=DK, in1=EG[:], op0=ALU.mult, op1=ALU.mult)),
                      reads=[uqt, uEG], writes=[uQdT])
            yield
            if B.stop.endswith(":C") or ":K" in B.stop:
                return
            pAT, uAT_ = slot()
            S.add("pe", (lambda e, o=pAT: e.transpose(o, A[:], ident)), reads=[uA, B.u_cst], writes=[uAT_])
            pMT, uMT_ = slot()
            S.add("pe", (lambda e, o=pMT: e.transpose(o, M[:], ident)), reads=[uM, B.u_cst], writes=[uMT_])
            yield
            (AT, uAT), (MT, uMT), (Y, uY) = f["AT"], f["MT"], f["Y"]
            S.add("act", (lambda e, o=pAT: e.activation(out=AT[:], in_=o, func=AF.Copy)), reads=[uAT_], writes=[uAT])
            S.add("act", (lambda e, o=pMT: e.activation(out=MT[:], in_=o, func=AF.Copy)), reads=[uMT_], writes=[uMT])
            S.add("dve", (lambda e, o=pMT: e.tensor_tensor(out=Y[:], in0=ident, in1=o, op=ALU.subtract)), reads=[uMT_, B.u_cst], writes=[uY])
            (attnT, uattn), (Kd0, uKd0), (Kd1, uKd1), (Ktb, uKtb), (Vb, uVb) = rr["attnT"], rr["Kd0"], rr["Kd1"], rr["Ktb"], rr["Vb"]
            S.add("dve", (lambda e, o=pQK: e.scalar_tensor_tensor(out=attnT[:], in0=o, scalar=DK, in1=AT[:], op0=ALU.mult, op1=ALU.mult)),
                  reads=[uQK, uAT], writes=[uattn])
            for c, (Kd, uKd) in enumerate(((Kd0, uKd0), (Kd1, uKd1))):
                S.add("act", (lambda e, o=pkt, Kd=Kd, c=c: e.activation(out=Kd[:], in_=o, func=AF.Copy, scale=AT[:, iend(c):iend(c) + 1])),
                      reads=[uktk, uAT], writes=[uKd])
            wcol = TE[:, bi, 0, r:r + 1]
            S.add("dve", (lambda e, o=pkt, wcol=wcol: e.tensor_scalar(out=Ktb[:], in0=o, scalar1=wcol, scalar2=None, op0=ALU.mult)),
                  reads=[uktk, uTG], writes=[uKtb])
            S.add("dve", (lambda e, o=pvt, bcol=bcol: e.tensor_scalar(out=Vb[:], in0=o, scalar1=bcol, scalar2=None, op0=ALU.mult)),
                  reads=[uvtk, uTG], writes=[uVb])
            yield
            if B.stop.endswith(":E"):
                return
            P, uP = M, uM
            PT, uPT = MT, uMT
            bufs = [(f["P0"], f["Q0"]), (f["P1"], f["Q1"])]
            (YR, uYR) = rr["YR"]
            pend = None
            for lev in range(1, 7):
                cur = None
                if lev <= 5:
                    (Pn, uPn), (PnT, uPnT) = bufs[lev % 2]
                    pP, upP = slot()
                    S.add("pe", (lambda e, o=pP, PT=PT, P=P: e.matmul(o, PT[:], P[:], start=True, stop=True)), reads=[uPT, uP], writes=[upP])
                    pPT = None
                    if lev < 5:
                        pPT, upPT = slot()
                        S.add("pe", (lambda e, o=pPT, PT=PT, P=P: e.matmul(o, P[:], PT[:], start=True, stop=True)), reads=[uPT, uP], writes=[upPT])
                    cur = (Pn, uPn, PnT, uPnT, pP, upP, pPT, upPT if lev < 5 else None)
                pY = None
                if pend is not None:
                    pY, upY = slot()
                    S.add("pe", (lambda e, o=pY, Pq=pend[0]: e.matmul(o, Pq[:], Y[:], start=True, stop=True)), reads=[pend[1], uY], writes=[upY])
                yield
                if pY is not None:
                    if lev <= 5:
                        S.add("dve", (lambda e, o=pY: e.tensor_tensor(out=Y[:], in0=Y[:], in1=o, op=ALU.add)), reads=[upY, uY], writes=[uY])
                    else:
                        S.add("dve", (lambda e, o=pY: e.tensor_tensor(out=YR[:], in0=Y[:], in1=o, op=ALU.add)), reads=[upY, uY], writes=[uYR])
                if cur is not None:
                    (Pn, uPn, PnT, uPnT, pP, upP, pPT, upPT) = cur
                    S.add("act", (lambda e, o=pP, Pn=Pn: e.activation(out=Pn[:], in_=o, func=AF.Copy)), reads=[upP], writes=[uPn])
                    if pPT is not None:
                        S.add("act", (lambda e, o=pPT, PnT=PnT: e.activation(out=PnT[:], in_=o, func=AF.Copy)), reads=[upPT], writes=[uPnT])
                    pend = (Pn, uPn)
                    P, uP, PT, uPT = Pn, uPn, PnT, uPnT
                yield
            if B.stop.endswith(":G"):
                return
            (Us, uUs), (WTs, uWTs) = f["Us"], rr["WTs"]
            pU, upU = slot()
            S.add("pe", (lambda e, o=pU: e.matmul(o, YR[:], Vb[:], start=True, stop=True)), reads=[uYR, uVb], writes=[upU])
            pW, upW = slot()
            S.add("pe", (lambda e, o=pW: e.matmul(o, Ktb[:], YR[:], start=True, stop=True)), reads=[uYR, uKtb], writes=[upW])
            yield
            S.add("act", (lambda e, o=pU: e.activation(out=Us[:], in_=o, func=AF.Copy)), reads=[upU], writes=[uUs])
            S.add("dve", (lambda e, o=pW: e.tensor_copy(out=WTs[:], in_=o)), reads=[upW], writes=[uWTs])
            yield
            if B.stop.endswith(":H"):
                return
            for c in ((0, 1) if d == 0 else (1, 0)):
                rows = slice(c * 64, (c + 1) * 64)
                Kd, uKd = (Kd0, uKd0) if c == 0 else (Kd1, uKd1)
                p1, up1 = slot()
                S.add("pe", (lambda e, o=p1: e.matmul(o, WTs[:], Sst[:], start=True, stop=True)), reads=[uWTs, uS], writes=[up1])
                yield
                S.add("dve", (lambda e, o=p1, rows=rows: e.tensor_tensor(out=VN[rows, :], in0=Us[rows, :], in1=o[rows, :], op=ALU.subtract)),
                      reads=[up1, uUs, uVN], writes=[uVN])
                yield
                p2, up2 = slot()
                S.add("pe", (lambda e, o=p2: e.matmul(o, QdT[:], Sst[:], start=True, stop=False)), reads=[uQdT, uS], writes=[up2])
                S.add("pe", (lambda e, o=p2: e.matmul(o, attnT[:], VN[:], start=False, stop=True)), reads=[uattn, uVN], writes=[up2])
                p3, up3 = slot()
                S.add("pe", (lambda e, o=p3, Kd=Kd: e.matmul(o, Kd[:], VN[:], start=True, stop=True)), reads=[uKd, uVN], writes=[up3])
                yield
                oi_other = (bi if d == 1 else ([1, 0] + list(range(NB - 1, 1, -1))).index(bi))
                first = (oi < oi_other) or (oi == oi_other and d == 0)
                if first:
                    S.add("act", (lambda e, o=p2, rows=rows, bi=bi: e.activation(out=O[rows, bi, :], in_=o[rows, :], func=AF.Copy)),
                          reads=[up2], writes=[uO[bi]])
                else:
                    S.add("dve", (lambda e, o=p2, rows=rows, bi=bi: e.tensor_tensor(out=O[rows, bi, :], in0=O[rows, bi, :], in1=o[rows, :], op=ALU.add)),
                          reads=[up2, uO[bi]], writes=[uO[bi]])
                S.add("dve", (lambda e, o=p3, c=c: e.scalar_tensor_tensor(out=Sst[:], in0=Sst[:], scalar=EG[:, iend(c):iend(c) + 1], in1=o,
                                                                         op0=ALU.mult, op1=ALU.add)), reads=[up3, uS, uEG], writes=[uS])
                yield

    def post(h, resf, resb):
        (Of, uOf) = resf["O"]
        for bi, ub in enumerate(DN_BLOCKS):
            t, ut = pp_t.get()
            sm_, usm = pp_s.get()
            S.add("pool", (lambda e, t=t, bi=bi: e.tensor_copy(out=t[:], in_=Of[:, bi, :])),
                  reads=[uOf[bi]], writes=[ut])
            t2, ut2 = pp_t.get()
            S.add("act", (lambda e, t=t, t2=t2, sm_=sm_: e.activation(out=t2[:], in_=t[:], func=AF.Square, accum_out=sm_[:, 0:1])),
                  reads=[ut], writes=[ut2, usm])
            S.add("act", (lambda e, sm_=sm_: e.activation(out=sm_[:, 1:2], in_=sm_[:, 0:1], func=AF.Sqrt, scale=1.0 / 128, bias=B.eps6[:])),
                  reads=[usm], writes=[usm])
            S.add("dve", (lambda e, sm_=sm_: e.reciprocal(out=sm_[:, 1:2], in_=sm_[:, 1:2])), reads=[usm], writes=[usm])
            S.add("dve", (lambda e, t=t, sm_=sm_: e.tensor_scalar(out=t[:], in0=t[:], scalar1=sm_[:, 1:2], scalar2=None, op0=ALU.mult)),
                  reads=[usm, ut], writes=[ut])
            ps, ups = B.PS.get()
            S.add("pe", (lambda e, ps=ps, t=t: e.transpose(ps[:, 0:128], t[:], ident)), reads=[ut, B.u_cst], writes=[ups])
            sz, usz = pp_t.get()
            B.dma(sz[:], Sx["szT"][h * 128:(h + 1) * 128, ub:ub + 128], reads=[Ux["szT"]], writes=[usz])
            S.add("dve", (lambda e, ps=ps, sz=sz, ub=ub: e.scalar_tensor_tensor(out=OD[:, ub:ub + 128], in0=ps[:, 0:128], scalar=B.vs("dnng"),
                                                                              in1=sz[:], op0=ALU.mult, op1=ALU.mult)),
                  reads=[ups, usz, B.u_vec, uOD], writes=[uOD])
        B.dma(Sx["odn"][h * 128:(h + 1) * 128], OD[:], reads=[uOD], writes=[Ux["odn"]])

    nheads = 8 if not (B.stop or "").startswith("dn1") else 2
    if B.stop is None:
        B.stop = ""
    for h0 in range(0, nheads, 2):
        gens = []
        for j in range(2):
            for d in range(2):
                gens.append(chain(h0 + j, d, chains_res[j * 2 + d]))
        active = list(gens)
        while active:
            nxt = []
            for g in active:
                try:
                    next(g)
                    nxt.append(g)
                except StopIteration:
                    pass
            active = nxt
        if ":" in B.stop:
            return
        for j in range(2):
            post(h0 + j, chains_res[j * 2], chains_res[j * 2 + 1])


def hy_dims(n):
    N = 2 * n
    NS1 = n // 128
    NF1 = N // 128
    NFH = NF1 // 2 + 1
    return N, NS1, NF1, NFH


def hyena_tables(n):
    N, NS1, NF1, NFH = hy_dims(n)
    f64 = np.float64
    s1 = np.arange(NS1, dtype=f64)[:, None]
    f1 = np.arange(NFH, dtype=f64)[None, :]
    ang = 2 * np.pi * f1 * s1 / NF1
    F1 = np.concatenate([np.cos(ang), -np.sin(ang)], axis=1)
    s2 = np.arange(128, dtype=f64)[:, None, None]
    f1b = np.arange(NFH, dtype=f64)[None, :, None]
    f2 = np.arange(128, dtype=f64)[None, None, :]
    ang = 2 * np.pi * (f1b + NF1 * f2) * s2 / N
    G = np.stack([np.cos(ang), -np.sin(ang)], axis=2)
    f2c = np.arange(128, dtype=f64)[:, None]
    t2 = np.arange(128, dtype=f64)[None, :]
    th = 2 * np.pi * f2c * t2 / 128
    Winv = np.stack([np.concatenate([np.cos(th), np.sin(th)], 1), np.concatenate([-np.sin(th), np.cos(th)], 1)], axis=1)
    NT1 = NS1
    f1c = np.arange(NFH, dtype=f64)[:, None, None]
    t2c = np.arange(128, dtype=f64)[None, :, None]
    t1c = np.arange(NT1, dtype=f64)[None, None, :]
    ph = 2 * np.pi * f1c * (128 * t1c + t2c) / N
    w = np.full((NFH, 1, 1), 2.0)
    w[0] = 1.0
    w[NFH - 1] = 1.0
    Tout = np.stack([w / N * np.cos(ph), -w / N * np.sin(ph)], axis=2)
    t = np.linspace(0.0, 1.0, n, dtype=np.float32)[:, None]
    omega = (2.0 * math.pi * np.arange(n, dtype=np.float32)[:, None] / n).astype(np.float32)
    bands = np.linspace(1e-4, 15, 16, dtype=np.float32)[None, :]
    z = np.concatenate([t, np.cos(bands * omega), -np.sin(bands * omega)], axis=-1).astype(np.float32)
    negt = np.broadcast_to(-t[:, 0][None, :], (128, n))
    return dict(F1=F1.astype(np.float32), G=G.astype(np.float32), Winv=Winv.astype(np.float32),
                Tout=Tout.astype(np.float32), zT=np.ascontiguousarray(z.T), negt=np.ascontiguousarray(negt, dtype=np.float32))


HY_C = 32


def hyena_phase(B, l, es, seg):
    nc, S, Sx, Ux, I = B.nc, B.S, B.Sx, B.Ux, B.I
    n, base = (L, LT0) if seg == "lat" else (LC, 0)
    N, NS1, NF1, NFH = hy_dims(n)
    NT1 = NS1
    W2 = 2 * NFH
    C = HY_C
    tb = lambda k: I["hy_%s_%s" % (k, seg)]
    hp, uhp = Sx["hp_" + seg], Ux["hp_" + seg]
    NTL = [(i * 512, min(512, n - i * 512)) for i in range((n + 511) // 512)]
    MAGIC = 12582912.0

    with ExitStack() as e1:
        zT = e1.enter_context(nc.sbuf_tensor(uname("zT"), [33, n], F32))
        h1 = e1.enter_context(nc.sbuf_tensor(uname("h1"), [64, n], F32))
        h2 = e1.enter_context(nc.sbuf_tensor(uname("h2"), [64, n], F32))
        negt = e1.enter_context(nc.sbuf_tensor(uname("negt"), [128, n], F32))
        dec = e1.enter_context(nc.sbuf_tensor(uname("dec"), [128, n], F32))
        w1 = e1.enter_context(nc.sbuf_tensor(uname("w1"), [33, 64], F32))
        w2 = e1.enter_context(nc.sbuf_tensor(uname("w2"), [64, 64], F32))
        w3 = e1.enter_context(nc.sbuf_tensor(uname("w3"), [64, 2048], F32))
        fb = e1.enter_context(nc.sbuf_tensor(uname("fb"), [64, 2], F32))
        uz, uh1, uh2, unegt, udec, uw, ufb = Unit(), Unit(), Unit(), Unit(), Unit(), Unit(), Unit()
        tmp_p = TPool(nc, e1, "hyt", [128, 512], F32, 3)
        out_p = TPool(nc, e1, "hyo", [128, 2, 512], F32, 2)
        B.dma(zT[:], tb("zT"), writes=[uz])
        B.dma(negt[:], tb("negt"), writes=[unegt])
        B.dma(w1[:], I["hy_w1"][l], writes=[uw])
        B.dma(w2[:], I["hy_w2"][l], writes=[uw])
        B.dma(w3[:], I["hy_w3"][l], writes=[uw])
        S.add("dve", lambda e: e.tensor_tensor(out=fb[:, 0:1], in0=B.vs("hyf1")[:64], in1=B.vs("hyb1")[:64], op=ALU.mult),
              reads=[B.u_vec], writes=[ufb])
        S.add("dve", lambda e: e.tensor_tensor(out=fb[:, 1:2], in0=B.vs("hyf2")[:64], in1=B.vs("hyb2")[:64], op=ALU.mult),
              reads=[B.u_vec, ufb], writes=[ufb])
        for li, (wt, K, src, usrc, dst, udst, fname) in enumerate(((w1, 33, zT, uz, h1, uh1, "hyf1"), (w2, 64, h1, uh1, h2, uh2, "hyf2"))):
            for (t0, tn) in NTL:
                ps, ups = B.PS.get()
                S.add("pe", (lambda e, ps=ps, wt=wt, K=K, src=src, t0=t0, tn=tn: e.matmul(ps[:64, :tn], wt[:K, :], src[:K, t0:t0 + tn],
                                                                                          start=True, stop=True)),
                      reads=[uw, usrc], writes=[ups])
                x, ux = tmp_p.get()
                k2, uk2 = tmp_p.get()
                S.add("dve", (lambda e, ps=ps, x=x, tn=tn, li=li, fname=fname: e.tensor_scalar(
                    out=x[:64, :tn], in0=ps[:64, :tn], scalar1=B.vs(fname)[:64], scalar2=fb[:, li:li + 1], op0=ALU.mult, op1=ALU.add)),
                    reads=[ups, B.u_vec, ufb], writes=[ux])
                S.add("dve", (lambda e, x=x, k2=k2, tn=tn: e.tensor_scalar(out=k2[:64, :tn], in0=x[:64, :tn], scalar1=1.0 / (2 * math.pi),
                                                                           scalar2=MAGIC, op0=ALU.mult, op1=ALU.add)), reads=[ux], writes=[uk2])
                S.add("dve", (lambda e, k2=k2, tn=tn: e.tensor_scalar(out=k2[:64, :tn], in0=k2[:64, :tn], scalar1=-MAGIC, scalar2=-2 * math.pi,
                                                                      op0=ALU.add, op1=ALU.mult)), reads=[uk2], writes=[uk2])
                S.add("dve", (lambda e, x=x, k2=k2, tn=tn: e.tensor_tensor(out=x[:64, :tn], in0=x[:64, :tn], in1=k2[:64, :tn], op=ALU.add)),
                      reads=[ux, uk2], writes=[ux])
                S.add("act", (lambda e, x=x, dst=dst, t0=t0, tn=tn: e.activation(out=dst[:, t0:t0 + tn], in_=x[:64, :tn], func=AF.Sin)),
                      reads=[ux], writes=[udst])
        for cch in range(8):
            S.add("act", (lambda e, cch=cch: e.activation(out=dec[:], in_=negt[:], func=AF.Exp, scale=B.cs("deltas")[:, cch:cch + 1])),
                  reads=[unegt, B.u_cst], writes=[udec])
            for (t0, tn) in NTL:
                psf, upsf = B.PS.get()
                psb, upsb = B.PS.get()
                for (ps_, ups_, dr) in ((psf, upsf, 0), (psb, upsb, 1)):
                    S.add("pe", (lambda e, ps_=ps_, dr=dr, cch=cch, t0=t0, tn=tn: e.matmul(
                        ps_[:, :tn], w3[:, dr * 1024 + cch * 128:dr * 1024 + (cch + 1) * 128], h2[:, t0:t0 + tn], start=True, stop=True)),
                        reads=[uw, uh2], writes=[ups_])
                a1, ua1 = tmp_p.get()
                S.add("act", (lambda e, a1=a1, psf=psf, tn=tn: e.activation(out=a1[:, :tn], in_=psf[:, :tn], func=AF.Copy)), reads=[upsf], writes=[ua1])
                o, uo = out_p.get()
                for k_, op_ in ((0, ALU.add), (1, ALU.subtract)):
                    S.add("dve", (lambda e, o=o, a1=a1, psb=psb, tn=tn, k_=k_, op_=op_: e.tensor_tensor(out=o[:, k_, :tn], in0=a1[:, :tn], in1=psb[:, :tn], op=op_)),
                          reads=[ua1, upsb, uo], writes=[uo])
                    S.add("pool", (lambda e, o=o, tn=tn, t0=t0, k_=k_: e.tensor_tensor(out=o[:, k_, :tn], in0=o[:, k_, :tn], in1=dec[:, t0:t0 + tn], op=ALU.mult)),
                          reads=[uo, udec], writes=[uo])
                B.dma(hp[:, cch * 128:(cch + 1) * 128, t0:t0 + tn].rearrange("k c t -> c k t"), o[:, :, :tn], reads=[uo], writes=[uhp])
        S.barrier()
    if B.stop and B.stop.endswith("hyf"):
        return

    with ExitStack() as e2:
        cvt_pool = TPool(nc, e2, "cvtst", [128, 1024], F32, 2)

        def cvt(name, shape, src_ap):
            t = e2.enter_context(nc.sbuf_tensor(uname(name), shape, F32R))
            ut = Unit()
            flat = 1
            for s_ in shape[1:]:
                flat *= s_
            step = 1024
            st_pool = cvt_pool
            tf = t[:].rearrange(" ".join(["p"] + ["a%d" % i for i in range(len(shape) - 1)]) + " -> p (" + " ".join("a%d" % i for i in range(len(shape) - 1)) + ")") if len(shape) > 2 else t[:]
            sf = src_ap.rearrange(" ".join(["p"] + ["a%d" % i for i in range(len(shape) - 1)]) + " -> p (" + " ".join("a%d" % i for i in range(len(shape) - 1)) + ")") if len(shape) > 2 else src_ap
            P_ = shape[0]
            for o_ in range(0, flat, step):
                w_ = min(step, flat - o_)
                st, ust = st_pool.get()
                B.dma(st[:P_, :w_], sf[:, o_:o_ + w_], writes=[ust])
                S.add("act", (lambda e, st=st, o_=o_, w_=w_: e.activation(out=tf[:, o_:o_ + w_], in_=st[:P_, :w_], func=AF.Copy)),
                      reads=[ust], writes=[ut])
            return t, ut
        F1 = e2.enter_context(nc.sbuf_tensor(uname("F1"), [NS1, W2], F32))
        uF1 = Unit()
        B.dma(F1[:], tb("F1"), writes=[uF1])
        G, uG = cvt("G", [128, NFH, 2, 128], tb("G"))
        Wv, uWv = cvt("Wv", [128, 2, 256], I["hy_Winv"])
        To, uTo = cvt("To", [NFH, 128, 2, NT1], tb("Tout"))
        xin_p = TPool(nc, e2, "xin", [NS1, C, 128], F32, 1)
        Yb = e2.enter_context(nc.sbuf_tensor(uname("Yb"), [128, NFH, 3, C], F32R))
        Kb = e2.enter_context(nc.sbuf_tensor(uname("Kb"), [128, NFH, 2, C], F32))
        XP = e2.enter_context(nc.sbuf_tensor(uname("XP"), [128, NFH, 2, C], F32R))
        Vb = e2.enter_context(nc.sbuf_tensor(uname("Vb"), [NFH, 128, 2, C], F32R))
        yb = e2.enter_context(nc.sbuf_tensor(uname("yb"), [C, n], F32))
        uYb, uKb, uXP, uVb, uyb = Unit(), Unit(), Unit(), Unit(), Unit()
        tp = TPool(nc, e2, "hytp", [128, 8, C], F32, 4)
        xz_p = TPool(nc, e2, "hyxz", [C, 2, 1024], F32, 1)
        ob_p = TPool(nc, e2, "hyob", [C, 1024], BF16, 2)
        NPB = 512 // W2
        XP2 = e2.enter_context(nc.sbuf_tensor(uname("XP2"), [128, NFH, 2, C], F32R))
        uXP2 = Unit()
        XPS = [(XP, uXP), (XP2, uXP2)]

        def fwd(g):
            XP, uXP = XPS[g % 2]
            c0 = g * C
            for sig in ("p", "m", "zz"):
                xin, uxin = xin_p.get()
                if sig == "zz":
                    src, usrc = Sx["zzT"][c0:c0 + C, base:base + n], Ux["zzT"]
                else:
                    src, usrc = hp[0 if sig == "p" else 1, c0:c0 + C, :], uhp
                B.dma(xin[:], src.rearrange("c (a b) -> a c b", b=128), reads=[usrc], writes=[uxin])
                for cb in range(0, C, NPB):
                    nb = min(NPB, C - cb)
                    ps, ups = B.PS.get()
                    for ci in range(nb):
                        S.add("pe", (lambda e, ps=ps, xin=xin, ci=ci, cb=cb: e.matmul(ps[:, ci * W2:(ci + 1) * W2], xin[:, cb + ci, :], F1[:, :],
                                                                                      start=True, stop=True)),
                              reads=[uxin, uF1], writes=[ups])
                    pv = ps[:, :nb * W2].rearrange("p (c r f) -> p c r f", c=nb, r=2)
                    for r_ in range(2):
                        S.add("act" if r_ == 0 else "dve",
                              (lambda e, pv=pv, r_=r_, cb=cb, nb=nb: (e.activation(out=Yb[:, :, r_, cb:cb + nb], in_=pv[:, :, r_, :].rearrange("p c f -> p f c"), func=AF.Copy)
                                                                       if r_ == 0 else
                                                                       e.tensor_copy(out=Yb[:, :, r_, cb:cb + nb], in_=pv[:, :, r_, :].rearrange("p c f -> p f c")))),
                              reads=[ups], writes=[uYb])
                    S.add("act", (lambda e, pv=pv, cb=cb, nb=nb: e.activation(out=Yb[:, :, 2, cb:cb + nb], in_=pv[:, :, 1, :].rearrange("p c f -> p f c"),
                                                                              func=AF.Copy, scale=-1.0)), reads=[ups], writes=[uYb])
                    yield
                for fq in range(0, NFH, 8):
                    nf = min(8, NFH - fq)
                    ps, ups = B.PS.get()
                    for fi in range(nf):
                        f1_ = fq + fi
                        if sig != "m":
                            S.add("pe", (lambda e, ps=ps, fi=fi, f1_=f1_: e.matmul(ps[:, (fi * 2) * C:(fi * 2 + 1) * C], G[:, f1_, 0, :], Yb[:, f1_, 0, :], start=True, stop=False)),
                                  reads=[uG, uYb], writes=[ups])
                            S.add("pe", (lambda e, ps=ps, fi=fi, f1_=f1_: e.matmul(ps[:, (fi * 2) * C:(fi * 2 + 1) * C], G[:, f1_, 1, :], Yb[:, f1_, 2, :], start=False, stop=True)),
                                  reads=[uG, uYb], writes=[ups])
                        if sig != "p":
                            S.add("pe", (lambda e, ps=ps, fi=fi, f1_=f1_: e.matmul(ps[:, (fi * 2 + 1) * C:(fi * 2 + 2) * C], G[:, f1_, 1, :], Yb[:, f1_, 0, :], start=True, stop=False)),
                                  reads=[uG, uYb], writes=[ups])
                            S.add("pe", (lambda e, ps=ps, fi=fi, f1_=f1_: e.matmul(ps[:, (fi * 2 + 1) * C:(fi * 2 + 2) * C], G[:, f1_, 0, :], Yb[:, f1_, 1, :], start=False, stop=True)),
                                  reads=[uG, uYb], writes=[ups])
                    pv = ps[:, :nf * 2 * C].rearrange("p (f r c) -> p f r c", f=nf, r=2)
                    if sig == "p":
                        S.add("act", (lambda e, pv=pv, fq=fq, nf=nf: e.activation(out=Kb[:, fq:fq + nf, 0, :], in_=pv[:, :, 0, :], func=AF.Copy)), reads=[ups], writes=[uKb])
                    elif sig == "m":
                        S.add("act", (lambda e, pv=pv, fq=fq, nf=nf: e.activation(out=Kb[:, fq:fq + nf, 1, :], in_=pv[:, :, 1, :], func=AF.Copy)), reads=[ups], writes=[uKb])
                    else:
                        (t1, u1), (t2, u2), (t3, u3), (t4, u4) = tp.get(), tp.get(), tp.get(), tp.get()
                        kr, ki = Kb[:, fq:fq + nf, 0, :], Kb[:, fq:fq + nf, 1, :]
                        S.add("dve", (lambda e, pv=pv, t1=t1, nf=nf, kr=kr: e.tensor_tensor(out=t1[:, :nf, :], in0=pv[:, :, 0, :], in1=kr, op=ALU.mult)), reads=[ups, uKb], writes=[u1])
                        S.add("dve", (lambda e, pv=pv, t2=t2, nf=nf, ki=ki: e.tensor_tensor(out=t2[:, :nf, :], in0=pv[:, :, 1, :], in1=ki, op=ALU.mult)), reads=[ups, uKb], writes=[u2])
                        S.add("dve", (lambda e, pv=pv, t3=t3, nf=nf, ki=ki: e.tensor_tensor(out=t3[:, :nf, :], in0=pv[:, :, 0, :], in1=ki, op=ALU.mult)), reads=[ups, uKb], writes=[u3])
                        S.add("dve", (lambda e, pv=pv, t4=t4, nf=nf, kr=kr: e.tensor_tensor(out=t4[:, :nf, :], in0=pv[:, :, 1, :], in1=kr, op=ALU.mult)), reads=[ups, uKb], writes=[u4])
                        S.add("dve", (lambda e, t1=t1, t2=t2, fq=fq, nf=nf: e.tensor_tensor(out=XP[:, fq:fq + nf, 0, :], in0=t1[:, :nf, :], in1=t2[:, :nf, :], op=ALU.subtract)),
                              reads=[u1, u2], writes=[uXP])
                        S.add("dve", (lambda e, t3=t3, t4=t4, fq=fq, nf=nf: e.tensor_tensor(out=XP[:, fq:fq + nf, 1, :], in0=t3[:, :nf, :], in1=t4[:, :nf, :], op=ALU.add)),
                              reads=[u3, u4], writes=[uXP])
                    yield

        def inv(g):
            XP, uXP = XPS[g % 2]
            c0 = g * C
            for cb in range(0, C, 2):
                ps, ups = B.PS.get()
                for ci in range(2):
                    c_ = cb + ci
                    S.add("pe", (lambda e, ps=ps, ci=ci, c_=c_: e.matmul(ps[:NFH, ci * 256:(ci + 1) * 256], XP[:, :, 0, c_], Wv[:, 0, :], start=True, stop=False)),
                          reads=[uXP, uWv], writes=[ups])
                    S.add("pe", (lambda e, ps=ps, ci=ci, c_=c_: e.matmul(ps[:NFH, ci * 256:(ci + 1) * 256], XP[:, :, 1, c_], Wv[:, 1, :], start=False, stop=True)),
                          reads=[uXP, uWv], writes=[ups])
                pv = ps[:NFH, :].rearrange("p (c r t) -> p c r t", c=2, r=2)
                S.add("act" if (cb // 2) % 2 == 0 else "dve",
                      (lambda e, pv=pv, cb=cb: (e.activation(out=Vb[:, :, :, cb:cb + 2], in_=pv.rearrange("p c r t -> p t r c"), func=AF.Copy)
                                                if (cb // 2) % 2 == 0 else e.tensor_copy(out=Vb[:, :, :, cb:cb + 2], in_=pv.rearrange("p c r t -> p t r c")))),
                      reads=[ups], writes=[uVb])
                yield
            TB = 512 // NT1 if NT1 * 16 > 512 else 16
            for tq in range(0, 128, TB):
                ps, ups = B.PS.get()
                for ti in range(TB):
                    t2_ = tq + ti
                    S.add("pe", (lambda e, ps=ps, ti=ti, t2_=t2_: e.matmul(ps[:C, ti * NT1:(ti + 1) * NT1], Vb[:, t2_, 0, :], To[:, t2_, 0, :], start=True, stop=False)),
                          reads=[uVb, uTo], writes=[ups])
                    S.add("pe", (lambda e, ps=ps, ti=ti, t2_=t2_: e.matmul(ps[:C, ti * NT1:(ti + 1) * NT1], Vb[:, t2_, 1, :], To[:, t2_, 1, :], start=False, stop=True)),
                          reads=[uVb, uTo], writes=[ups])
                S.add("act", (lambda e, ps=ps, tq=tq: e.activation(out=yb[:, :].rearrange("c (a b) -> c a b", b=128)[:, :, tq:tq + TB],
                                                                   in_=ps[:C, :TB * NT1].rearrange("c (b a) -> c a b", a=NT1), func=AF.Copy)),
                      reads=[ups], writes=[uyb])
                yield
            for q0 in range(0, n, 1024):
                qn = min(1024, n - q0)
                xz, uxz = xz_p.get()
                B.dma(xz[:, 0, :qn], Sx["x0T"][c0:c0 + C, base + q0:base + q0 + qn], reads=[Ux["x0T"]], writes=[uxz])
                B.dma(xz[:, 1, :qn], Sx["zzT"][c0:c0 + C, base + q0:base + q0 + qn], reads=[Ux["zzT"]], writes=[uxz])
                S.add("dve", (lambda e, xz=xz, q0=q0, qn=qn, c0=c0: e.scalar_tensor_tensor(out=xz[:, 1, :qn], in0=xz[:, 1, :qn], scalar=B.vs("hyb32")[:C, c0 // C:c0 // C + 1],
                                                                                      in1=yb[:, q0:q0 + qn], op0=ALU.mult, op1=ALU.add)),
                      reads=[uxz, uyb, B.u_vec], writes=[uxz])
                ob, uob = ob_p.get()
                S.add("dve", (lambda e, xz=xz, ob=ob, qn=qn: e.tensor_tensor(out=ob[:, :qn], in0=xz[:, 0, :qn], in1=xz[:, 1, :qn], op=ALU.mult)),
                      reads=[uxz], writes=[uob])
                B.dma(Sx["ohy"][c0:c0 + C, base + q0:base + q0 + qn], ob[:, :qn], reads=[uob], writes=[Ux["ohy"]], eng="pool")
                yield

        NG = D // C
        if B.stop and B.stop.endswith("hy1"):
            NG = 2
        for g in range(NG + 1):
            gens = []
            if g < NG:
                gens.append(fwd(g))
            if g >= 1:
                gens.append(inv(g - 1))
            while gens:
                nxt = []
                for gen in gens:
                    try:
                        next(gen)
                        nxt.append(gen)
                    except StopIteration:
                        pass
                gens = nxt
        S.barrier()


def load_wres(B, es, name, src, kparts, ncols, st_pool, eng="pool"):
    nc, S = B.nc, B.S
    w = es.enter_context(nc.sbuf_tensor(uname(name), [128, kparts, ncols], BF16))
    uw = Unit()
    srcv = src.rearrange("(k p) n -> p k n", p=128)
    for k0 in range(0, kparts, 4):
        kn = min(4, kparts - k0)
        for c0 in range(0, ncols, 512):
            st, ust = st_pool.get()
            B.dma(st[:, :kn, :], srcv[:, k0:k0 + kn, c0:c0 + 512], writes=[ust])
            S.add(eng, (lambda e, st=st, k0=k0, kn=kn, c0=c0: e.tensor_copy(out=w[:, k0:k0 + kn, c0:c0 + 512], in_=st[:, :kn, :])),
                  reads=[ust], writes=[uw])
    return w, uw


def merge_phase(B, l, es, xin, u_xin, tiles):
    nc, S, Sx, Ux, I = B.nc, B.S, B.Sx, B.Ux, B.I
    st_pool = TPool(nc, es, "mst", [128, 4, 512], F32, 2)
    Ws = []
    for nm in ("w_proj_dn", "w_proj_hy", "w_proj_lru", "w_out"):
        Ws.append(load_wres(B, es, nm + "_sb", I[nm][l], 8, D, st_pool))
    ot_p = TPool(nc, es, "mot", [128, 3, 8, 512], BF16, 1)
    gt_p = TPool(nc, es, "mgt", [128, 24, 512], BF16, 1)
    x_p = TPool(nc, es, "mx", [128, 8, 512], F32, 2)
    mg_p = TPool(nc, es, "mmg", [128, 8, 512], BF16, 1)
    t_p = TPool(nc, es, "mt", [128, 512], F32, 4)
    for (u0, n) in tiles:
        seg = 1 if u0 == 0 else 0
        ot, uot = ot_p.get()
        for bi, nm in enumerate(("odn", "ohy", "olru")):
            B.dma(ot[:, bi, :, :n], Sx[nm].rearrange("(k p) u -> p k u", p=128)[:, :, u0:u0 + n], reads=[Ux[nm]], writes=[uot])
        gt, ugt = gt_p.get()
        B.dma(gt[:, :, :n], Sx["gT"].rearrange("(k p) u -> p k u", p=128)[:, :, u0:u0 + n], reads=[Ux["gT"]], writes=[ugt])
        x, ux = x_p.get()
        B.dma(x[:, :, :n], xin.rearrange("(k p) u -> p k u", p=128)[:, :, u0:u0 + n], reads=[u_xin], writes=[ux])
        mg, umg = mg_p.get()
        for m in range(8):
            pss = []
            for bi in range(3):
                ps, ups = B.PS.get()
                w, uw = Ws[bi]
                for k in range(8):
                    S.add("pe", (lambda e, ps=ps, w=w, k=k, m=m, bi=bi, ot=ot, n=n: e.matmul(ps[:, :n], w[:, k, m * 128:(m + 1) * 128], ot[:, bi, k, :n],
                                                                                           start=(k == 0), stop=(k == 7))),
                          reads=[uw, uot], writes=[ups])
                pss.append((ps, ups))
            ta, uta = t_p.get()
            tb_, utb = t_p.get()
            S.add("dve", (lambda e, ta=ta, ps=pss[0][0], gt=gt, m=m, n=n: e.tensor_tensor(out=ta[:, :n], in0=ps[:, :n], in1=gt[:, m, :n], op=ALU.mult)),
                  reads=[pss[0][1], ugt], writes=[uta])
            S.add("dve", (lambda e, tb_=tb_, ps=pss[1][0], gt=gt, m=m, n=n: e.tensor_tensor(out=tb_[:, :n], in0=ps[:, :n], in1=gt[:, 8 + m, :n], op=ALU.mult)),
                  reads=[pss[1][1], ugt], writes=[utb])
            S.add("pool", (lambda e, ta=ta, tb_=tb_, n=n: e.tensor_tensor(out=ta[:, :n], in0=ta[:, :n], in1=tb_[:, :n], op=ALU.add)),
                  reads=[uta, utb], writes=[uta])
            tc_, utc = t_p.get()
            S.add("dve", (lambda e, tc_=tc_, ps=pss[2][0], gt=gt, m=m, n=n: e.tensor_tensor(out=tc_[:, :n], in0=ps[:, :n], in1=gt[:, 16 + m, :n], op=ALU.mult)),
                  reads=[pss[2][1], ugt], writes=[utc])
            S.add("pool", (lambda e, ta=ta, tc_=tc_, mg=mg, m=m, n=n: e.tensor_tensor(out=mg[:, m, :n], in0=ta[:, :n], in1=tc_[:, :n], op=ALU.add)),
                  reads=[uta, utc, umg], writes=[umg])
        w, uw = Ws[3]
        for nn in range(8):
            ps, ups = B.PS.get()
            for m in range(8):
                S.add("pe", (lambda e, ps=ps, m=m, nn=nn, mg=mg, n=n: e.matmul(ps[:, :n], w[:, m, nn * 128:(nn + 1) * 128], mg[:, m, :n],
                                                                              start=(m == 0), stop=(m == 7))),
                      reads=[uw, umg], writes=[ups])
            S.add("dve", (lambda e, ps=ps, x=x, nn=nn, n=n, seg=seg: e.scalar_tensor_tensor(out=x[:, nn, :n], in0=ps[:, :n], scalar=B.modcol(2, seg, nn),
                                                                                       in1=x[:, nn, :n], op0=ALU.mult, op1=ALU.add)),
                  reads=[ups, ux, B.u_modv], writes=[ux])
        B.dma(Sx["xA"].rearrange("(k p) u -> p k u", p=128)[:, :, u0:u0 + n], x[:, :, :n], reads=[ux], writes=[Ux["xA"]])


def ffn_phase(B, l, es, tiles):
    nc, S, Sx, Ux, I = B.nc, B.S, B.Sx, B.Ux, B.I
    with_ctx = tiles[0][0] == 0
    esu = ExitStack()
    hT = esu.enter_context(nc.sbuf_tensor(uname("hT2"), [128, 8, U], BF16))
    u_hT = [Unit() for _ in TT]
    with ExitStack() as es3:
        B.norm_to_hT(Sx["xA"], Ux["xA"], 1, hT, u_hT, tiles, es3, tile_ids=[TT.index(t) for t in tiles])
        S.barrier()
    with ExitStack() as es4:
        wst_pool = TPool(nc, es4, "fwst", [128, 8, 128], F32, 2)
        wbf_pool = TPool(nc, es4, "fwbf", [128, 8, 128], BF16, 2)
        up_p = TPool(nc, es4, "fup", [128, 66, 66], F32, 2)
        cp_p = TPool(nc, es4, "fcp", [128, 260], F32, 2)
        acc_p = TPool(nc, es4, "facc", [128, U], F32, 2)
        ab_p = TPool(nc, es4, "fab", [128, U], BF16, 2)
        for (t, ut) in up_p.t:
            S.add("pool", (lambda e, t=t: e.memset(t[:], 0.0)), writes=[ut])
        for (t, ut) in cp_p.t:
            S.add("pool", (lambda e, t=t: e.memset(t[:], 0.0)), writes=[ut])
        f_order = [part * (FH // 128) + j for j in range(FH // 128) for part in range(2)]
        f_next = [B.load_w_bf16(I["ffn_up"][l][:, f_order[0] * 128:(f_order[0] + 1) * 128], 128, wst_pool, wbf_pool)]
        f_i = 0
        for j in range(FH // 128):
            accs = []
            for part in range(2):
                cidx = part * (FH // 128) + j
                wb, uwb = f_next[0]
                f_i += 1
                if f_i < len(f_order):
                    nx = f_order[f_i]
                    f_next[0] = B.load_w_bf16(I["ffn_up"][l][:, nx * 128:(nx + 1) * 128], 128, wst_pool, wbf_pool)
                up, uup = up_p.get()
                cp, ucp = cp_p.get()
                for (u0, n) in tiles:
                    ti = TT.index((u0, n))
                    ps, ups = B.mm_tile(wb, uwb, 128, hT, u_hT, ti)
                    if u0 == 0:
                        S.add("act", (lambda e, ps=ps, cp=cp, n=n: e.activation(out=cp[:, 1:1 + n], in_=ps[:, :n], func=AF.Copy)), reads=[ups], writes=[ucp])
                    else:
                        r0 = (u0 - LT0) // 64
                        S.add("act", (lambda e, ps=ps, up=up, r0=r0: e.activation(out=up[:, 1 + r0:9 + r0, 1:65], in_=ps[:, :512].rearrange("p (r c) -> p r c", c=64),
                                                                                  func=AF.Copy)), reads=[ups], writes=[uup])
                acc, uacc = acc_p.get()
                cw = lambda tap, cidx=cidx: B.vs("ffncw", cidx * 9 + tap)
                av = acc[:, LT0:U].rearrange("p (r c) -> p r c", c=64)
                S.add("act", (lambda e, up=up, av=av, cw=cw: e.activation(out=av, in_=up[:, 0:64, 0:64], func=AF.Identity, scale=cw(0))),
                      reads=[uup, B.u_vec], writes=[uacc])
                for tap in range(1, 9):
                    di, dj = tap // 3, tap % 3
                    S.add("dve", (lambda e, up=up, av=av, cw=cw, tap=tap, di=di, dj=dj: e.scalar_tensor_tensor(
                        out=av, in0=up[:, di:di + 64, dj:dj + 64], scalar=cw(tap), in1=av, op0=ALU.mult, op1=ALU.add)),
                        reads=[uup, uacc, B.u_vec], writes=[uacc])
                if with_ctx:
                    S.add("act", (lambda e, cp=cp, acc=acc, cw=cw: e.activation(out=acc[:, 0:256], in_=cp[:, 0:256], func=AF.Identity, scale=cw(3))),
                          reads=[ucp, B.u_vec, uacc], writes=[uacc])
                    for tap in (4, 5):
                        S.add("dve", (lambda e, cp=cp, acc=acc, cw=cw, tap=tap: e.scalar_tensor_tensor(
                            out=acc[:, 0:256], in0=cp[:, tap - 3:tap - 3 + 256], scalar=cw(tap), in1=acc[:, 0:256], op0=ALU.mult, op1=ALU.add)),
                            reads=[ucp, uacc, B.u_vec], writes=[uacc])
                accs.append((acc, uacc))
            (ag, uag), (av_, uav) = accs
            lo = 0 if with_ctx else LT0
            S.add("act", (lambda e, ag=ag, lo=lo: e.activation(out=ag[:, lo:U], in_=ag[:, lo:U], func=AF.Silu)), reads=[uag], writes=[uag])
            ab, uab = ab_p.get()
            S.add("pool", (lambda e, ab=ab: e.memset(ab[:, 0:LT0], 0.0)), writes=[uab])
            S.add("dve", (lambda e, ag=ag, av_=av_, ab=ab, lo=lo: e.tensor_tensor(out=ab[:, lo:U], in0=ag[:, lo:U], in1=av_[:, lo:U], op=ALU.mult)),
                  reads=[uag, uav, uab], writes=[uab])
            B.dma(Sx["actT"][j * 128:(j + 1) * 128], ab[:], reads=[uab], writes=[Ux["actT"]])
        S.barrier()
    esu.close()
    st_pool = TPool(nc, es, "dst", [128, 4, 512], F32, 2)
    wd, uwd = load_wres(B, es, "wdown", I["ffn_down"][l], FH // 128, D, st_pool)
    at_p = TPool(nc, es, "dat", [128, FH // 128, 512], BF16, 2)
    x_p = TPool(nc, es, "dx", [128, 8, 512], F32, 2)
    for (u0, n) in tiles:
        seg = 1 if u0 == 0 else 0
        at, uat = at_p.get()
        B.dma(at[:, :, :n], Sx["actT"].rearrange("(k p) u -> p k u", p=128)[:, :, u0:u0 + n], reads=[Ux["actT"]], writes=[uat])
        x, ux = x_p.get()
        B.dma(x[:, :, :n], Sx["xA"].rearrange("(k p) u -> p k u", p=128)[:, :, u0:u0 + n], reads=[Ux["xA"]], writes=[ux])
        for nn in range(8):
            ps, ups = B.PS.get()
            for j in range(FH // 128):
                S.add("pe", (lambda e, ps=ps, j=j, nn=nn, at=at, n=n: e.matmul(ps[:, :n], wd[:, j, nn * 128:(nn + 1) * 128], at[:, j, :n],
                                                                              start=(j == 0), stop=(j == FH // 128 - 1))),
                      reads=[uwd, uat], writes=[ups])
            S.add("dve", (lambda e, ps=ps, x=x, nn=nn, n=n, seg=seg: e.scalar_tensor_tensor(out=x[:, nn, :n], in0=ps[:, :n], scalar=B.modcol(5, seg, nn),
                                                                                       in1=x[:, nn, :n], op0=ALU.mult, op1=ALU.add)),
                  reads=[ups, ux, B.u_modv], writes=[ux])
        B.dma(Sx["xB"].rearrange("(k p) u -> p k u", p=128)[:, :, u0:u0 + n], x[:, :, :n], reads=[ux], writes=[Ux["xB"]])


def final_phase(B, es):
    nc, S, Sx, Ux = B.nc, B.S, B.Sx, B.Ux
    xp = TPool(nc, es, "fx", [128, 8, 512], F32, 2)
    sqp = TPool(nc, es, "fsq", [128, 8, 512], F32R, 1)
    rsp = TPool(nc, es, "frs", [128, 512], F32, 2)
    for (u0, n) in TT[1:]:
        x, ux = xp.get()
        B.dma(x[:], Sx["xB"].rearrange("(k p) u -> p k u", p=128)[:, :, u0:u0 + n], reads=[Ux["xB"]], writes=[ux])
        sq, usq = sqp.get()
        S.add("act", (lambda e, x=x, sq=sq: e.activation(out=sq[:], in_=x[:], func=AF.Square)), reads=[ux], writes=[usq])
        ps, ups = B.PS.get()
        for k in range(8):
            S.add("pe", (lambda e, k=k, sq=sq, ps=ps: e.matmul(ps[:], B.ones[:], sq[:, k, :], start=(k == 0), stop=(k == 7))),
                  reads=[usq, B.u_ones], writes=[ups])
        rs, urs = rsp.get()
        S.add("act", (lambda e, rs=rs, ps=ps: e.activation(out=rs[:], in_=ps[:], func=AF.Sqrt, scale=1.0 / D, bias=B.eps6[:])), reads=[ups], writes=[urs])
        S.add("dve", (lambda e, rs=rs: e.reciprocal(out=rs[:], in_=rs[:])), reads=[urs], writes=[urs])
        S.add("dve", (lambda e, x=x, rs=rs: e.tensor_tensor(out=x[:], in0=x[:], in1=rs[:].unsqueeze(1).to_broadcast([128, 8, 512]), op=ALU.mult)),
              reads=[urs, ux], writes=[ux])
        S.add("dve", (lambda e, x=x: e.tensor_tensor(out=x[:], in0=x[:], in1=B.vs("fng").unsqueeze(2).to_broadcast([128, 8, 512]), op=ALU.mult)),
              reads=[ux, B.u_vec], writes=[ux])
        t0 = u0 - LT0
        B.dma(B.outT.rearrange("(k p) t -> p k t", p=128)[:, :, t0:t0 + n], x[:], reads=[ux])


_CACHE = {}


def kernel(**inputs):
    inp = {k: np.asarray(v) for k, v in inputs.items()}
    if "nc" not in _CACHE:
        b = Builder(debug=False)
        _CACHE["nc"] = b.build()
    nc = _CACHE["nc"]
    bsz = inp["x"].shape[0]
    in_maps = [prep_inputs(inp, b_) for b_ in range(bsz)]
    res = run_bass_kernel_spmd(nc, in_maps, core_ids=list(range(bsz)))
    out = np.stack([np.ascontiguousarray(np.asarray(r["outT"]).T) for r in res.results])
    return out.astype(np.float32)
```
